# Optimizing a Trainium2 kernel written in Bass

```python
import math
import jax, jax.numpy as jnp
from jax import lax
import numpy as np

D_MODEL = 1024
BATCH = 8
SEQ = 2048
DEPTH = 2
DEC_BATCH = 128
DEC_SEQ = 4
PAST_LEN = 16384
PAGE_SIZE = 128

N_META = 16
H_A = 4
DK_A = 128
DV_A = 128
QK_A = H_A * DK_A
W_A = H_A * DV_A
H_B = 4
DK_B = 128
DV_B = 128
W_B = H_B * DV_B
CONV_B = 4
D_FF = 2816
CONV_F = 3
CHUNK = 64
ALPHA = (2 * DEPTH) ** 0.25
BETA_INIT = (8 * DEPTH) ** -0.25
LN_EPS = 1e-5
NORM_EPS = 1e-6

A_Q = 0
A_K = A_Q + QK_A
A_V = A_K + QK_A
A_O = A_V + W_A
A_I = A_O + W_A
B_QKV = A_I + 2 * H_A
B_Z = B_QKV + 3 * W_B
B_BETA = B_Z + W_B
B_A = B_BETA + H_B
G_MERGE = B_A + H_B
N_IN = G_MERGE + 2 * D_MODEL

kernel_name = 'hybrid_mlstm_gdn_convffn_step'


def layer_norm(x, g, b):
    xf = x.astype(jnp.float32)
    mu = xf.mean(-1, keepdims=True)
    var = jnp.square(xf - mu).mean(-1, keepdims=True)
    return ((xf - mu) * lax.rsqrt(var + LN_EPS) * g + b).astype(x.dtype)


def head_layer_norm(h):
    mu = h.mean(-1, keepdims=True)
    var = jnp.square(h - mu).mean(-1, keepdims=True)
    return (h - mu) * lax.rsqrt(var + NORM_EPS)


def rms_norm(h):
    return h * lax.rsqrt(jnp.square(h).mean(-1, keepdims=True) + NORM_EPS)


def l2norm(h):
    return h * lax.rsqrt(jnp.square(h).sum(-1, keepdims=True) + NORM_EPS)


def causal_dwconv(u, buf, w):
    width = w.shape[0]
    L = u.shape[1]
    ext = jnp.concatenate([buf.astype(u.dtype), u], axis=1)
    out = ext[:, 0:L] * w[0]
    for j in range(1, width):
        out = out + ext[:, j:j + L] * w[j]
    return out, ext[:, L:]


def _to_chunks(a, c):
    B, H, L = a.shape[:3]
    a = a.reshape((B, H, L // c, c) + a.shape[3:])
    return jnp.moveaxis(a, 2, 0)


def _from_chunks(a):
    a = jnp.moveaxis(a, 0, 2)
    return a.reshape(a.shape[:2] + (a.shape[2] * a.shape[3],) + a.shape[4:])


def run_segments(step, carry, xs, seg_lens):
    outs = []
    start = 0
    for L in seg_lens:
        c = math.gcd(L, CHUNK)
        seg = tuple(_to_chunks(a[:, :, start:start + L], c) for a in xs)
        carry, o = lax.scan(step, carry, seg)
        outs.append(_from_chunks(o))
        start += L
    return jnp.concatenate(outs, axis=2), carry


def _mlstm_step(carry, xs):
    C, n, m = carry
    q, k, v, ig, lf = xs
    c = q.shape[2]
    incl = jnp.tril(jnp.ones((c, c), dtype=bool))
    b = jnp.cumsum(lf, axis=-1)
    D = jnp.where(incl, b[..., :, None] - b[..., None, :] + ig[..., None, :], -jnp.inf)
    inter = b + m[..., None]
    m_t = jnp.maximum(inter, D.max(-1))
    w_intra = jnp.exp(D - m_t[..., None])
    w_inter = jnp.exp(inter - m_t)
    s = jnp.einsum('bhtd,bhsd->bhts', q, k) * w_intra
    num = jnp.einsum('bhts,bhsv->bhtv', s, v) + w_inter[..., None] * jnp.einsum('bhtd,bhdv->bhtv', q, C)
    den = s.sum(-1) + w_inter * jnp.einsum('bhtd,bhd->bht', q, n)
    h = num / jnp.maximum(jnp.abs(den), jnp.exp(-m_t))[..., None]
    m_new = m_t[..., -1]
    w_src = jnp.exp(b[..., -1:] - b + ig - m_new[..., None])
    w_old = jnp.exp(b[..., -1] + m - m_new)
    kw = k * w_src[..., None]
    C_new = w_old[..., None, None] * C + jnp.einsum('bhsd,bhsv->bhdv', kw, v)
    n_new = w_old[..., None] * n + kw.sum(2)
    return (C_new, n_new, m_new), h


def _gdn_step(S, xs):
    q, k, v, beta, g = xs
    c = q.shape[2]
    incl = jnp.tril(jnp.ones((c, c), dtype=bool))
    strict = jnp.tril(jnp.ones((c, c), dtype=bool), -1)
    G = jnp.cumsum(g, axis=-1)
    decay = jnp.exp(jnp.where(incl, G[..., :, None] - G[..., None, :], -jnp.inf))
    A = jnp.where(strict, beta[..., None] * jnp.einsum('bhtd,bhsd->bhts', k, k) * decay, 0.0)
    rhs = jnp.concatenate([beta[..., None] * v, (beta * jnp.exp(G))[..., None] * k], axis=-1)
    sol = lax.linalg.triangular_solve(jnp.eye(c, dtype=A.dtype) + A, rhs, left_side=True, lower=True,
                                      unit_diagonal=True)
    dv = v.shape[-1]
    U = sol[..., :dv] - jnp.einsum('bhtd,bhdv->bhtv', sol[..., dv:], S)
    o = (jnp.exp(G)[..., None] * jnp.einsum('bhtd,bhdv->bhtv', q, S)
         + jnp.einsum('bhts,bhsv->bhtv', jnp.einsum('bhtd,bhsd->bhts', q, k) * decay, U))
    G_end = G[..., -1]
    S_new = (jnp.exp(G_end)[..., None, None] * S
             + jnp.einsum('bhsd,bhsv->bhdv', k * jnp.exp(G_end[..., None] - G)[..., None], U))
    return S_new, o


def token_mixers(x, state, w_in, gate_bias, a_norm_w, conv_w, A_log, dt_bias, b_norm_w,
                 w_pa, w_pb, w_out, seg_lens):
    C0, n0, m0, S0, conv_buf = state
    B, L, _ = x.shape
    f32 = jnp.float32
    p = x @ w_in

    def heads(a, h):
        return a.reshape(B, L, h, -1).transpose(0, 2, 1, 3).astype(f32)

    qa = heads(p[..., A_Q:A_K], H_A) * DK_A ** -0.5
    ka = heads(p[..., A_K:A_V], H_A)
    va = heads(p[..., A_V:A_O], H_A)
    gif = (p[..., A_I:B_QKV] + gate_bias).astype(f32).transpose(0, 2, 1)
    ig = gif[:, :H_A]
    lf = jax.nn.log_sigmoid(gif[:, H_A:])
    hA, (C, n, m) = run_segments(_mlstm_step, (C0.astype(f32), n0.astype(f32), m0.astype(f32)),
                                 (qa, ka, va, ig, lf), seg_lens)
    hA = head_layer_norm(hA.transpose(0, 2, 1, 3)).reshape(B, L, W_A) * a_norm_w
    hA = (jax.nn.sigmoid(p[..., A_O:A_I].astype(f32)) * hA).astype(x.dtype)

    qkv, conv_new = causal_dwconv(p[..., B_QKV:B_Z], conv_buf, conv_w)
    qkv = jax.nn.silu(qkv)
    qb = l2norm(heads(qkv[..., :W_B], H_B)) * DK_B ** -0.5
    kb = l2norm(heads(qkv[..., W_B:2 * W_B], H_B))
    vb = heads(qkv[..., 2 * W_B:], H_B)
    beta = jax.nn.sigmoid(p[..., B_BETA:B_A].astype(f32)).transpose(0, 2, 1)
    g = (-jnp.exp(A_log.astype(f32))
         * jax.nn.softplus(p[..., B_A:G_MERGE].astype(f32) + dt_bias)).transpose(0, 2, 1)
    hB, S = run_segments(_gdn_step, S0.astype(f32), (qb, kb, vb, beta, g), seg_lens)
    hB = rms_norm(hB.transpose(0, 2, 1, 3)) * b_norm_w
    hB = (hB.reshape(B, L, W_B) * jax.nn.silu(p[..., B_Z:B_BETA].astype(f32))).astype(x.dtype)

    mg = jax.nn.sigmoid(p[..., G_MERGE:])
    y = mg[..., :D_MODEL] * (hA @ w_pa) + mg[..., D_MODEL:] * (hB @ w_pb)
    return y @ w_out, (C, n, m, S, conv_new)


def conv_ffn(x, w_up, conv_w, w_down, buf):
    u = x @ w_up
    u, new_buf = causal_dwconv(u, buf, conv_w)
    h = jax.nn.silu(u[..., :D_FF]) * u[..., D_FF:]
    return h @ w_down, new_buf


def decoder_layer(x, st, lp, seg_lens):
    (w_in, gate_bias, a_norm_w, conv_w, A_log, dt_bias, b_norm_w, w_pa, w_pb, w_out,
     ln1_g, ln1_b, w_up, f_conv_w, w_down, ln2_g, ln2_b) = lp
    C0, n0, m0, S0, gbuf, fbuf = st
    mix, (C, n, m, S, gbuf_new) = token_mixers(x, (C0, n0, m0, S0, gbuf), w_in, gate_bias, a_norm_w,
                                               conv_w, A_log, dt_bias, b_norm_w, w_pa, w_pb, w_out,
                                               seg_lens)
    x = layer_norm(ALPHA * x + mix, ln1_g, ln1_b)
    ff, fbuf_new = conv_ffn(x, w_up, f_conv_w, w_down, fbuf)
    x = layer_norm(ALPHA * x + ff, ln2_g, ln2_b)
    return x, (C, n, m, S, gbuf_new, fbuf_new)


def setup_inputs(seed: int = 0) -> dict:
    key = jax.random.key(seed)
    ks = jax.random.split(key, 32)
    f32 = jnp.float32

    def nrm(k, shape, s):
        return jax.random.normal(k, shape, f32) * s

    dt = jax.random.uniform(ks[14], (DEPTH, H_B), f32, 0.001, 0.1)
    return {
        'x_prompt': nrm(ks[0], (BATCH, SEQ, D_MODEL), 1.0),
        'x_sample': nrm(ks[1], (DEC_BATCH, DEC_SEQ, D_MODEL), 1.0),
        'state_mlstm_C': nrm(ks[2], (DEPTH, DEC_BATCH, H_A, DK_A, DV_A), 0.1),
        'state_mlstm_n': nrm(ks[3], (DEPTH, DEC_BATCH, H_A, DK_A), 0.1),
        'state_mlstm_m': jax.random.uniform(ks[4], (DEPTH, DEC_BATCH, H_A), f32, 0.0, 3.0),
        'state_gdn_S': nrm(ks[5], (DEPTH, DEC_BATCH, H_B, DK_B, DV_B), DK_B ** -0.5),
        'state_gdn_conv': nrm(ks[6], (DEPTH, DEC_BATCH, CONV_B - 1, 3 * W_B), 1.0),
        'state_ffn_conv': nrm(ks[7], (DEPTH, DEC_BATCH, CONV_F - 1, 2 * D_FF), 1.0),
        'meta_tokens': nrm(ks[8], (N_META, D_MODEL), 1.0),
        'ln_emb_g': 1.0 + nrm(ks[9], (D_MODEL,), 0.01),
        'ln_emb_b': nrm(ks[10], (D_MODEL,), 0.01),
        'w_in': nrm(ks[11], (DEPTH, D_MODEL, N_IN), D_MODEL ** -0.5),
        'mlstm_gate_bias': jnp.concatenate([nrm(ks[12], (DEPTH, H_A), 0.1),
                                            3.0 + nrm(ks[13], (DEPTH, H_A), 0.1)], axis=-1),
        'mlstm_norm_w': 1.0 + nrm(ks[15], (DEPTH, W_A), 0.01),
        'gdn_conv_w': nrm(ks[16], (DEPTH, CONV_B, 3 * W_B), CONV_B ** -0.5),
        'gdn_A_log': jnp.log(jax.random.uniform(ks[17], (DEPTH, H_B), f32, 1.0, 16.0)),
        'gdn_dt_bias': jnp.log(jnp.expm1(dt)),
        'gdn_norm_w': 1.0 + nrm(ks[18], (DEPTH, DV_B), 0.01),
        'w_branch_a': nrm(ks[19], (DEPTH, W_A, D_MODEL), W_A ** -0.5),
        'w_branch_b': nrm(ks[20], (DEPTH, W_B, D_MODEL), W_B ** -0.5),
        'w_out': nrm(ks[21], (DEPTH, D_MODEL, D_MODEL), BETA_INIT * D_MODEL ** -0.5),
        'ln1_g': 1.0 + nrm(ks[22], (DEPTH, D_MODEL), 0.01),
        'ln1_b': nrm(ks[23], (DEPTH, D_MODEL), 0.01),
        'w_up': nrm(ks[24], (DEPTH, D_MODEL, 2 * D_FF), D_MODEL ** -0.5),
        'ffn_conv_w': nrm(ks[25], (DEPTH, CONV_F, 2 * D_FF), CONV_F ** -0.5),
        'w_down': nrm(ks[26], (DEPTH, D_FF, D_MODEL), BETA_INIT * D_FF ** -0.5),
        'ln2_g': 1.0 + nrm(ks[27], (DEPTH, D_MODEL), 0.01),
        'ln2_b': nrm(ks[28], (DEPTH, D_MODEL), 0.01),
    }


def reference(x_prompt, x_sample, state_mlstm_C, state_mlstm_n, state_mlstm_m, state_gdn_S,
              state_gdn_conv, state_ffn_conv, meta_tokens, ln_emb_g, ln_emb_b, w_in, mlstm_gate_bias,
              mlstm_norm_w, gdn_conv_w, gdn_A_log, gdn_dt_bias, gdn_norm_w, w_branch_a, w_branch_b,
              w_out, ln1_g, ln1_b, w_up, ffn_conv_w, w_down, ln2_g, ln2_b):
    f32 = jnp.float32
    B = x_prompt.shape[0]
    dt = x_prompt.dtype
    meta = jnp.broadcast_to(meta_tokens.astype(dt), (B, N_META, D_MODEL))
    xp = layer_norm(jnp.concatenate([meta, x_prompt], axis=1), ln_emb_g, ln_emb_b)
    xs = layer_norm(x_sample, ln_emb_g, ln_emb_b)
    prompt_segs = (N_META, x_prompt.shape[1])
    sample_segs = (x_sample.shape[1],)
    p_states = []
    s_states = []
    for l in range(DEPTH):
        lp = (w_in[l], mlstm_gate_bias[l], mlstm_norm_w[l], gdn_conv_w[l], gdn_A_log[l], gdn_dt_bias[l],
              gdn_norm_w[l], w_branch_a[l], w_branch_b[l], w_out[l], ln1_g[l], ln1_b[l], w_up[l],
              ffn_conv_w[l], w_down[l], ln2_g[l], ln2_b[l])
        zero_st = (jnp.zeros((B, H_A, DK_A, DV_A), f32), jnp.zeros((B, H_A, DK_A), f32),
                   jnp.zeros((B, H_A), f32), jnp.zeros((B, H_B, DK_B, DV_B), f32),
                   jnp.zeros((B, CONV_B - 1, 3 * W_B), dt), jnp.zeros((B, CONV_F - 1, 2 * D_FF), dt))
        samp_st = (state_mlstm_C[l], state_mlstm_n[l], state_mlstm_m[l], state_gdn_S[l],
                   state_gdn_conv[l], state_ffn_conv[l])
        xp, st_p = decoder_layer(xp, zero_st, lp, prompt_segs)
        xs, st_s = decoder_layer(xs, samp_st, lp, sample_segs)
        p_states.append(st_p)
        s_states.append(st_s)
    dtypes = (state_mlstm_C.dtype, state_mlstm_n.dtype, state_mlstm_m.dtype, state_gdn_S.dtype,
              state_gdn_conv.dtype, state_ffn_conv.dtype)

    def stack(states, i):
        return jnp.stack([s[i] for s in states], axis=0).astype(dtypes[i])

    y_prompt = xp[:, N_META:]
    y_sample = xs
    return (y_prompt, y_sample,
            stack(p_states, 0), stack(p_states, 1), stack(p_states, 2), stack(p_states, 3),
            stack(p_states, 4), stack(p_states, 5),
            stack(s_states, 0), stack(s_states, 1), stack(s_states, 2), stack(s_states, 3),
            stack(s_states, 4), stack(s_states, 5))
```

```python
import math
from contextlib import ExitStack
import numpy as np
import concourse.bass as bass
import concourse.mybir as mybir
from concourse.bass_utils import run_bass_kernel_spmd

F32 = mybir.dt.float32
BF16 = mybir.dt.bfloat16
AF = mybir.ActivationFunctionType
ALU = mybir.AluOpType
AX = mybir.AxisListType

NEG = -30000.0
LN_EPS = 1e-5
NORM_EPS = 1e-6
ALPHA = 4.0 ** 0.25
DEPTH = 2
D = 1024
KC = 8
NIN = 6160
A_Q, A_K, A_V, A_O, A_I = 0, 512, 1024, 1536, 2048
B_QKV, B_Z, B_BETA, B_A, G_MERGE = 2056, 3592, 4104, 4108, 4112
DFF = 2816
NS = 16


class V:
    __slots__ = ("t", "ap")

    def __init__(self, t, ap):
        self.t = t
        self.ap = ap

    def __getitem__(self, idx):
        return V(self.t, self.ap[idx])

    def re(self, pat, **kw):
        return V(self.t, self.ap.rearrange(pat, **kw))

    def bc(self, shape):
        return V(self.t, self.ap.to_broadcast(list(shape)))

    def unsq(self, axis):
        return V(self.t, self.ap.unsqueeze(axis))


class T:
    __slots__ = ("ap", "lw", "rd", "dsem", "name", "a0", "a1")

    def __init__(self, name, ap, rd):
        self.name = name
        self.ap = ap
        self.lw = None
        self.rd = dict(rd)
        self.dsem = None

    def __getitem__(self, idx):
        return V(self, self.ap[idx])

    @property
    def v(self):
        return V(self, self.ap)


class KB:
    def __init__(self, nc):
        self.nc = nc
        self.root = ExitStack()
        self.stacks = [self.root]
        self.scope_tiles = [[]]
        self.engs = {"pe": nc.tensor, "act": nc.scalar, "dve": nc.vector, "pool": nc.gpsimd, "sp": nc.sync}
        self.semh = {}
        self.own = {}
        self.cnt = {}
        for e in ("pe", "act", "dve", "pool"):
            s = self.root.enter_context(nc.semaphore("s_" + e))
            self.semh["s_" + e] = s
            self.own[e] = "s_" + e
            self.cnt["s_" + e] = 0
        self.pending = {e: False for e in self.own}
        self.dmasems = set()
        self.free_dsems = []
        self.seen = {e: {} for e in self.engs}
        self.grave = {}
        self.grave_ranges = []
        self.nid = 0
        self.ps_tiles = []
        self.ps_i = 0
        self.wb_tiles = []
        self.wb_i = 0

    def sb(self, name, shape, persist=False, dt=F32):
        self.nid += 1
        nm = "%s_%d" % (name, self.nid)
        st = self.root if persist else self.stacks[-1]
        shape = list(shape)
        nfree = 1
        for d_ in shape[1:]:
            nfree *= d_
        esz = 2 if dt == BF16 else 4
        per = 128 // esz
        npad = ((nfree + per - 1) // per) * per
        h = st.enter_context(self.nc.sbuf_tensor(nm, [shape[0], npad], dt))
        ml = self.nc.lookup_mloc(h)
        a0 = int(ml.addr)
        a1 = a0 + int(ml.dims[1])
        inh = {}
        keep = []
        for (g0, g1, evs) in self.grave_ranges:
            if g0 < a1 and a0 < g1:
                for sn, v in evs.items():
                    if inh.get(sn, 0) < v:
                        inh[sn] = v
                if a0 <= g0 and g1 <= a1:
                    continue
            keep.append((g0, g1, evs))
        self.grave_ranges = keep
        ap = h[:, 0:nfree]
        if len(shape) == 3:
            ap = ap.rearrange("p (a b) -> p a b", a=shape[1])
        elif len(shape) == 4:
            ap = ap.rearrange("p (a b c) -> p a b c", a=shape[1], b=shape[2])
        t = T(nm, ap, inh)
        t.a0, t.a1 = a0, a1
        self.min_rem = min(getattr(self, "min_rem", 1 << 30), self.nc.sbuf_bytes_remaining)
        if not persist:
            self.scope_tiles[-1].append(t)
        return t

    def psum(self, name, shape):
        h = self.root.enter_context(self.nc.psum_tensor(name, list(shape), F32))
        return T(name, h[:], {})

    def scope(self):
        kb = self

        class _S:
            def __enter__(s):
                kb.stacks.append(ExitStack())
                kb.scope_tiles.append([])

            def __exit__(s, *a):
                kb._free_tiles(kb.scope_tiles.pop())
                kb.stacks.pop().close()
                return False

        return _S()

    def push(self):
        self.stacks.append(ExitStack())
        self.scope_tiles.append([])

    def _free_tiles(self, tiles):
        for t in tiles:
            evs = dict(t.rd)
            if t.lw is not None and evs.get(t.lw[0], 0) < t.lw[1]:
                evs[t.lw[0]] = t.lw[1]
            if t.dsem is not None:
                if evs.get(t.dsem, 0) < self.cnt[t.dsem]:
                    evs[t.dsem] = self.cnt[t.dsem]
                self.free_dsems.append(t.dsem)
                t.dsem = None
            if evs:
                self.grave_ranges.append((t.a0, t.a1, evs))

    def pop(self):
        self._free_tiles(self.scope_tiles.pop())
        self.stacks.pop().close()

    def _g(self, ev):
        s, v = ev
        if self.grave.get(s, 0) < v:
            self.grave[s] = v

    def ps(self):
        t = self.ps_tiles[self.ps_i % len(self.ps_tiles)]
        self.ps_i += 1
        return t

    def wbuf(self):
        t = self.wb_tiles[self.wb_i % len(self.wb_tiles)]
        self.wb_i += 1
        return t

    def op(self, eng, fn, reads=(), writes=(), sig=True, dma_tile=None):
        e = self.engs[eng]
        is_load = dma_tile is not None and (dma_tile in writes) and dma_tile.lw is not None and dma_tile.lw[0] == dma_tile.dsem and not dma_tile.rd
        deps = {}

        def add(ev):
            if ev is None:
                return
            s, v = ev
            if deps.get(s, 0) < v:
                deps[s] = v

        for t in reads:
            add(t.lw)
        for t in writes:
            add(t.lw)
            for s, v in t.rd.items():
                add((s, v))
        own = self.own.get(eng) if dma_tile is None else None
        seen = self.seen[eng]
        for s, v in deps.items():
            if eng == "pe" and s == own:
                continue
            if dma_tile is not None and s == dma_tile.dsem and is_load:
                continue
            if s in self.dmasems:
                v = self.cnt[s]
            if seen.get(s, 0) >= v:
                continue
            e.wait_ge(self.semh[s], v)
            seen[s] = v
        inst = fn(e)
        if dma_tile is not None:
            if dma_tile.dsem is None:
                if self.free_dsems:
                    nm = self.free_dsems.pop(0)
                else:
                    nm = "d_%d" % len(self.dmasems)
                    self.semh[nm] = self.root.enter_context(self.nc.semaphore(nm))
                    self.cnt[nm] = 0
                    self.dmasems.add(nm)
                dma_tile.dsem = nm
            s = dma_tile.dsem
            inst.then_inc(self.semh[s], 16)
            self.cnt[s] += 16
            ev = (s, self.cnt[s])
        else:
            if sig:
                inst.then_inc(self.semh[own], 1)
                self.cnt[own] += 1
                ev = (own, self.cnt[own])
                self.pending[eng] = False
            else:
                ev = (own, self.cnt[own] + 1)
                self.pending[eng] = True
        for t in writes:
            t.lw = ev
            t.rd = {}
        for t in reads:
            if t.rd.get(ev[0], 0) < ev[1]:
                t.rd[ev[0]] = ev[1]
        return inst

    def finish(self):
        assert not any(self.pending.values())
        sp = self.engs["sp"]
        for s in sorted(self.dmasems):
            if self.cnt[s] > self.seen["sp"].get(s, 0):
                sp.wait_ge(self.semh[s], self.cnt[s])
        for e in ("pe", "act", "dve", "pool"):
            s = self.own[e]
            if self.cnt[s] > 0:
                sp.wait_ge(self.semh[s], self.cnt[s])
        self.root.close()

    def mm(self, out, lhsT, rhs, start=True, stop=True, sig=None):
        self.op("pe", lambda e: e.matmul(out.ap, lhsT=lhsT.ap, rhs=rhs.ap, start=start, stop=stop),
                reads=[lhsT.t, rhs.t], writes=[out.t], sig=(stop if sig is None else sig))

    def tr(self, out, in_, ident):
        if isinstance(ident, T):
            ident = ident.v
        r = in_.ap.shape[0]
        self.op("pe", lambda e: e.transpose(out=out.ap, in_=in_.ap, identity=ident.ap[0:r, 0:r]),
                reads=[in_.t, ident.t], writes=[out.t])

    def act(self, out, in_, func, bias=None, scale=None, eng="act"):
        kw = {}
        rd = [in_.t]
        if bias is not None:
            if isinstance(bias, V):
                kw["bias"] = bias.ap
                rd.append(bias.t)
            else:
                kw["bias"] = bias
        if scale is not None:
            if isinstance(scale, V):
                kw["scale"] = scale.ap
                rd.append(scale.t)
            else:
                kw["scale"] = scale
        self.op("act", lambda e: e.activation(out=out.ap, in_=in_.ap, func=func, **kw), reads=rd, writes=[out.t])

    def ts(self, out, in0, s1, op0, s2=None, op1=None, eng="dve"):
        rd = [in0.t]
        a1 = s1
        a2 = s2
        if isinstance(s1, V):
            rd.append(s1.t)
            a1 = s1.ap
        if isinstance(s2, V):
            rd.append(s2.t)
            a2 = s2.ap
        if op1 is None:
            self.op(eng, lambda e: e.tensor_scalar(out=out.ap, in0=in0.ap, scalar1=a1, scalar2=None, op0=op0),
                    reads=rd, writes=[out.t])
        else:
            self.op(eng, lambda e: e.tensor_scalar(out=out.ap, in0=in0.ap, scalar1=a1, scalar2=a2, op0=op0, op1=op1),
                    reads=rd, writes=[out.t])

    def tt(self, out, in0, in1, op, eng="dve"):
        self.op(eng, lambda e: e.tensor_tensor(out=out.ap, in0=in0.ap, in1=in1.ap, op=op),
                reads=[in0.t, in1.t], writes=[out.t])

    def stt(self, out, in0, scalar, in1, op0, op1, eng="dve"):
        rd = [in0.t, in1.t]
        a = scalar
        if isinstance(scalar, V):
            rd.append(scalar.t)
            a = scalar.ap
        self.op(eng, lambda e: e.scalar_tensor_tensor(out=out.ap, in0=in0.ap, scalar=a, in1=in1.ap, op0=op0, op1=op1),
                reads=rd, writes=[out.t])

    def cp(self, out, in_, eng="dve"):
        if eng == "act":
            self.op("act", lambda e: e.copy(out=out.ap, in_=in_.ap), reads=[in_.t], writes=[out.t])
        else:
            self.op(eng, lambda e: e.tensor_copy(out=out.ap, in_=in_.ap), reads=[in_.t], writes=[out.t])

    def memset(self, out, val, eng="dve"):
        self.op(eng, lambda e: e.memset(out.ap, val), writes=[out.t])

    def red(self, out, in_, op, eng="dve"):
        self.op(eng, lambda e: e.tensor_reduce(out=out.ap, in_=in_.ap, axis=AX.X, op=op), reads=[in_.t], writes=[out.t])

    def scan(self, out, in_, op0=ALU.add):
        self.op("dve", lambda e: e.tensor_tensor_scan(out=out.ap, data0=in_.ap, data1=in_.ap, initial=0.0 if op0 == ALU.add else -3.0e38,
                                                      op0=op0, op1=ALU.bypass), reads=[in_.t], writes=[out.t])

    def dma_in(self, out, in_ap, q="sp", slow=False):
        kw = {"allow_slow_non_contiguous": True} if slow else {}
        self.op(q, lambda e: e.dma_start(out=out.ap, in_=in_ap, **kw), writes=[out.t], dma_tile=out.t)

    def dma_out(self, out_ap, in_, q="sp", slow=False):
        kw = {"allow_slow_non_contiguous": True} if slow else {}
        self.op(q, lambda e: e.dma_start(out=out_ap, in_=in_.ap, **kw), reads=[in_.t], dma_tile=in_.t)


def build(TT=256, taps=()):
    nc = bass.Bass("TRN2", target_bir_lowering=False)
    k = KB(nc)
    taps = set(taps)
    tapouts = {}

    def din(name, shape):
        return nc.dram_tensor(name, list(shape), F32, kind="ExternalInput").ap()

    def dout(name, shape):
        return nc.dram_tensor(name, list(shape), F32, kind="ExternalOutput").ap()

    xp = din("xp", [2048, D])
    xs = din("xs", [64, D])
    meta = din("meta", [16, D])
    sC = din("sC", [DEPTH, NS * 4, 128, 128])
    sn = din("sn", [DEPTH, NS * 4, 128])
    sm = din("sm", [DEPTH, NS, 4])
    sS = din("sS", [DEPTH, NS * 4, 128, 128])
    sgc = din("sgc", [DEPTH, NS * 3, 1536])
    sfc = din("sfc", [DEPTH, NS * 2, 2 * DFF])
    ln_emb_g = din("ln_emb_g", [1, D])
    ln_emb_b = din("ln_emb_b", [1, D])
    w_in = din("w_in", [DEPTH, D, NIN])
    gate_bias = din("gate_bias", [DEPTH, 8, 1])
    mnorm_w = din("mnorm_w", [DEPTH, 1, 512])
    gconv_w = din("gconv_w", [DEPTH, 4, 1536])
    A_log = din("A_log", [DEPTH, 4, 1])
    dt_bias = din("dt_bias", [DEPTH, 4, 1])
    gnorm_w = din("gnorm_w", [DEPTH, 1, 128])
    w_pa = din("w_pa", [DEPTH, 512, D])
    w_pb = din("w_pb", [DEPTH, 512, D])
    w_out = din("w_out", [DEPTH, D, D])
    ln1_g = din("ln1_g", [DEPTH, 1, D])
    ln1_b = din("ln1_b", [DEPTH, 1, D])
    w_up = din("w_up", [DEPTH, D, 2 * DFF])
    fconv_w = din("fconv_w", [DEPTH, 3, 2 * DFF])
    w_down = din("w_down", [DEPTH, DFF, D])
    ln2_g = din("ln2_g", [DEPTH, 1, D])
    ln2_b = din("ln2_b", [DEPTH, 1, D])

    yp = dout("yp", [2048, D])
    ys = dout("ys", [64, D])
    pC = dout("pC", [DEPTH, 4, 128, 128])
    pn = dout("pn", [DEPTH, 4, 128])
    pm = dout("pm", [DEPTH, 4, 1])
    pS = dout("pS", [DEPTH, 4, 128, 128])
    pgc = dout("pgc", [DEPTH, 3, 1536])
    pfc = dout("pfc", [DEPTH, 2, 2 * DFF])
    oC = dout("oC", [DEPTH, NS * 4, 128, 128])
    on = dout("on", [DEPTH, NS * 4, 128])
    om = dout("om", [DEPTH, NS, 4])
    oS = dout("oS", [DEPTH, NS * 4, 128, 128])
    ogc = dout("ogc", [DEPTH, NS * 3, 1536])
    ofc = dout("ofc", [DEPTH, NS * 2, 2 * DFF])

    def tap(name, view):
        if name not in taps:
            return
        shp = list(view.ap.shape)
        o = nc.dram_tensor("tap_" + name, shp, view.ap.dtype, kind="ExternalOutput").ap()
        tapouts[name] = shp
        k.dma_out(o, view, q="sp", slow=True)

    k.ps_tiles = [k.psum("ps%d" % i, [128, 512]) for i in range(6)]
    psd = k.psum("psd", [128, 1024])
    k.wb_tiles = [k.sb("wb%d" % i, [128, 8, 512], persist=True, dt=BF16) for i in range(4)]

    ident = k.sb("ident", [128, 128], persist=True)
    k.memset(ident.v, 0.0, eng="pool")
    k.op("pool", lambda e: e.affine_select(out=ident.ap, in_=ident.ap, pattern=[[-1, 128]], compare_op=ALU.not_equal,
                                           fill=1.0, base=0, channel_multiplier=1), reads=[ident], writes=[ident])
    aident = k.sb("aident", [128, 128], persist=True)
    k.ts(aident.v, ident.v, ALPHA, ALU.mult)
    ones = k.sb("ones", [128, 128], persist=True)
    k.memset(ones.v, 1.0)
    cst = k.sb("cst", [128, 8], persist=True)
    k.memset(cst[:, 4:5], LN_EPS)
    k.memset(cst[:, 0:1], 1.0)
    k.memset(cst[:, 1:2], NORM_EPS)
    k.memset(cst[:, 2:3], math.log(128.0 ** -0.5))
    k.memset(cst[:, 3:4], 0.0)

    def aff(t, pattern, op, fill, base, cm):
        k.op("pool", lambda e: e.affine_select(out=t.ap, in_=t.ap, pattern=pattern, compare_op=op, fill=fill,
                                               base=base, channel_multiplier=cm), reads=[t], writes=[t])

    m01T = k.sb("m01T", [128, 128], persist=True)
    k.memset(m01T.v, 1.0, eng="pool")
    aff(m01T, [[1, 128]], ALU.is_ge, 0.0, 0, -1)
    mn_st = k.sb("mn_st", [128, 128], persist=True)
    k.memset(mn_st.v, 0.0, eng="pool")
    aff(mn_st, [[-1, 128]], ALU.is_gt, NEG, 0, 1)
    mn_stT = k.sb("mn_stT", [128, 128], persist=True)
    k.memset(mn_stT.v, 0.0, eng="pool")
    aff(mn_stT, [[1, 128]], ALU.is_gt, NEG, 0, -1)
    mn_inT = k.sb("mn_inT", [128, 128], persist=True)
    k.memset(mn_inT.v, 0.0, eng="pool")
    aff(mn_inT, [[1, 128]], ALU.is_ge, NEG, 0, -1)
    indT = k.sb("indT", [16, 64], persist=True)
    k.memset(indT.v, 1.0, eng="pool")
    aff(indT, [[1, 64]], ALU.is_ge, 0.0, 0, -4)
    aff(indT, [[-1, 64]], ALU.is_ge, 0.0, 3, 4)
    ind = k.sb("ind", [64, 16], persist=True)
    k.memset(ind.v, 1.0, eng="pool")
    aff(ind, [[-4, 16]], ALU.is_ge, 0.0, 0, 1)
    aff(ind, [[4, 16]], ALU.is_ge, 0.0, 3, -1)
    same = k.sb("same", [64, 64], persist=True)
    p0 = k.ps()
    k.mm(p0[0:64, 0:64], indT.v, indT.v)
    k.cp(same.v, p0[0:64, 0:64])
    sneg = k.sb("sneg", [64, 64], persist=True)
    k.ts(sneg.v, same.v, -NEG, ALU.mult, NEG, ALU.add)
    b01T = k.sb("b01T", [64, 64], persist=True)
    k.tt(b01T.v, m01T[0:64, 0:64], same.v, ALU.mult)
    bn_st = k.sb("bn_st", [64, 64], persist=True)
    k.tt(bn_st.v, mn_st[0:64, 0:64], sneg.v, ALU.min)
    bn_stT = k.sb("bn_stT", [64, 64], persist=True)
    k.tt(bn_stT.v, mn_stT[0:64, 0:64], sneg.v, ALU.min)
    bn_inT = k.sb("bn_inT", [64, 64], persist=True)
    k.tt(bn_inT.v, mn_inT[0:64, 0:64], sneg.v, ALU.min)
    MASKS = {"c": (m01T, mn_st, mn_stT, mn_inT), "b": (b01T, bn_st, bn_stT, bn_inT)}
    oh = k.sb("oh", [4, 4, 128], persist=True)
    k.memset(oh.v, 0.0, eng="pool")
    aff(oh, [[-1, 4], [0, 128]], ALU.not_equal, 1.0, 0, 1)

    def load_fm(name, src2d, R, C):
        dst = k.sb(name, [128, C, R], persist=True)
        with k.scope():
            tmp = k.sb("ldfm", [R, C * 128])
            k.dma_in(tmp.v, src2d)
            c0 = 0
            while c0 < C:
                n = min(C - c0, 512 // R)
                p = k.ps()
                for c in range(n):
                    k.tr(p[:, c * R:(c + 1) * R], tmp[:, (c0 + c) * 128:(c0 + c + 1) * 128], ident)
                k.cp(dst[:, c0:c0 + n, :], p[:, 0:n * R].re("p (c r) -> p c r", r=R))
                c0 += n
        return dst

    def load_bc(name, src_row, n):
        dst = k.sb(name, [128, n], persist=True)
        k.dma_in(dst.v, src_row.to_broadcast([128, n]), slow=True)
        return dst

    g_emb = load_fm("g_emb", ln_emb_g, 1, 8)
    b_emb = load_fm("b_emb", ln_emb_b, 1, 8)
    LP = []
    for l in range(DEPTH):
        P = {}
        P["g1"] = load_fm("g1", ln1_g[l], 1, 8)
        P["b1"] = load_fm("b1", ln1_b[l], 1, 8)
        P["g2"] = load_fm("g2", ln2_g[l], 1, 8)
        P["b2"] = load_fm("b2", ln2_b[l], 1, 8)
        P["cwg"] = load_fm("cwg", gconv_w[l], 4, 12)
        P["cwf"] = load_fm("cwf", fconv_w[l], 3, 44)
        P["anw"] = load_bc("anw", mnorm_w[l], 512)
        P["gnw"] = load_bc("gnw", gnorm_w[l], 128)
        gb = k.sb("gb", [4, 6], persist=True)
        k.dma_in(gb[:, 0:1], gate_bias[l][0:4, :], slow=True)
        k.dma_in(gb[:, 4:5], gate_bias[l][4:8, :], slow=True)
        k.dma_in(gb[:, 5:6], A_log[l], slow=True)
        k.dma_in(gb[:, 3:4], dt_bias[l], slow=True)
        k.ts(gb[:, 1:2], gb[:, 4:5], -1.0, ALU.mult)
        k.act(gb[:, 2:3], gb[:, 5:6], AF.Exp)
        k.ts(gb[:, 2:3], gb[:, 2:3], -1.0, ALU.mult)
        P["gb"] = gb
        wg = k.sb("wg", [128, 8, 16], persist=True, dt=BF16)
        k.dma_in(wg[:, :, 0:8], w_in[l][:, A_I:A_I + 8].rearrange("(k p) n -> p k n", p=128), slow=True, q="pool")
        k.dma_in(wg[:, :, 8:16], w_in[l][:, B_BETA:B_BETA + 8].rearrange("(k p) n -> p k n", p=128), slow=True, q="pool")
        P["wg"] = wg
        Um = k.sb("Um", [128, 4, 129], persist=True)
        k.memset(Um.v, 0.0)
        mcur = k.sb("mcur", [4, 1], persist=True)
        k.memset(mcur.v, 0.0)
        S = k.sb("S", [128, 4, 128], persist=True)
        k.memset(S.v, 0.0)
        cg = k.sb("cg", [128, 12, 3], persist=True)
        k.memset(cg.v, 0.0)
        cf = k.sb("cf", [128, 44, 2], persist=True)
        k.memset(cf.v, 0.0)
        P.update(Um=Um, mcur=mcur, S=S, cg=cg, cf=cf)
        LP.append(P)
    g2bc = load_bc("g2bc", ln2_g[DEPTH - 1], D)
    b2bc = load_bc("b2bc", ln2_b[DEPTH - 1], D)

    tiles = [dict(TT=80, first=True, chunks=[dict(c0=0, c=16, kind="meta"), dict(c0=16, c=64, kind="samp")],
                  groups=[(0, 16)] + [(16 + 4 * i, 4) for i in range(16)], tgs=[(0, 80)], tok0=0,
                  pgroups=[(0, 16, [(0, 0, 16)]), (16, 64, [(1, 0, 64)])])]
    for i in range(2048 // TT):
        tiles.append(dict(TT=TT, first=False, chunks=[dict(c0=128 * j, c=128, kind="p", tok0=i * TT + 128 * j) for j in range(TT // 128)],
                          groups=[(128 * j, 128) for j in range(TT // 128)], tgs=[(128 * g, 128) for g in range(TT // 128)], tok0=i * TT,
                          pgroups=[(128 * g, 128, [(g, 0, 128)]) for g in range(TT // 128)]))

    scratch = {}
    cur = {"ti": 0, "n": 0}

    def wload_parts(parts, ncols):
        wb = k.wbuf()
        ktot = sum(kc for _, kc in parts)
        key = cur["n"]
        cur["n"] += 1
        if cur["ti"] == 0:
            stg = cur["stage"].get()
            k0 = 0
            for (src, kc) in parts:
                k.dma_in(stg[:, k0:k0 + kc, 0:ncols], src.rearrange("(k p) n -> p k n", p=128), q="sp")
                k0 += kc
            k.cp(wb[:, 0:ktot, 0:ncols], stg[:, 0:ktot, 0:ncols], eng="pool")
            scr = nc.dram_tensor("wscr_%d" % key, [128, ktot, ncols], BF16).ap()
            st = T("scr%d" % key, None, {})
            scratch[key] = (scr, st)
            k.op("sp", lambda e: e.dma_start(out=scr, in_=wb.ap[:, 0:ktot, 0:ncols]), reads=[wb], writes=[st], dma_tile=wb)
        else:
            scr, st = scratch[key]
            k.op("sp", lambda e: e.dma_start(out=wb.ap[:, 0:ktot, 0:ncols], in_=scr), reads=[st], writes=[wb], dma_tile=wb)
        return wb

    def wload(src2d, kc, ncols):
        return wload_parts([(src2d, kc)], ncols)

    def proj_fm(wb, cb, xT, n, nk=8, k0=0):
        p = k.ps()
        for kk in range(nk):
            k.mm(p[:, 0:n], wb[:, k0 + kk, cb:cb + 128], xT[:, kk, 0:n], start=(kk == 0), stop=(kk == nk - 1))
        return p

    def proj_tm(wb, ncols, xT, c0, c):
        p = k.ps()
        for kk in range(8):
            k.mm(p[0:c, 0:ncols], xT[:, kk, c0:c0 + c], wb[:, kk, 0:ncols], start=(kk == 0), stop=(kk == 7))
        return p

    def rsqrt_small(out, in_, eps):
        n = in_.ap.shape[0]
        col = 1 if eps == NORM_EPS else 4
        k.act(out, in_, AF.Ln, bias=cst[0:n, col:col + 1])
        k.act(out, out, AF.Exp, scale=-0.5)

    def ln_chunk(src, c, g_fm, b_fm, dstT, c0, final_dst=None, dstB=None):
        with k.scope():
            st = k.sb("lnst", [64, 12])
            mv = k.sb("lnmv", [64, 2])
            rs = k.sb("lnrs", [64, 1])
            xn = k.sb("lnxn", [64, 1024])
            k.op("dve", lambda e: e.bn_stats(out=st.ap[0:c, 0:6], in_=src.ap[:, 0:512]), reads=[src.t], writes=[st])
            k.op("dve", lambda e: e.bn_stats(out=st.ap[0:c, 6:12], in_=src.ap[:, 512:1024]), reads=[src.t], writes=[st])
            k.op("dve", lambda e: e.bn_aggr(out=mv.ap[0:c, :], in_=st.ap[0:c, :]), reads=[st], writes=[mv])
            rsqrt_small(rs[0:c, :], mv[0:c, 1:2], LN_EPS)
            k.ts(xn[0:c, :], src, mv[0:c, 0:1], ALU.subtract, rs[0:c, 0:1], ALU.mult)
            if final_dst is not None:
                k.tt(xn[0:c, :], xn[0:c, :], g2bc[0:c, :], ALU.mult)
                k.tt(xn[0:c, :], xn[0:c, :], b2bc[0:c, :], ALU.add)
                k.dma_out(final_dst, xn[0:c, :])
            else:
                p = k.ps()
                for cc in range(8):
                    k.tr(p[:, cc * c:(cc + 1) * c], xn[0:c, cc * 128:(cc + 1) * 128], ident)
                for cc in range(8):
                    if cc % 2 == 0:
                        k.act(dstT[:, cc, c0:c0 + c], p[:, cc * c:(cc + 1) * c], AF.Identity,
                              bias=b_fm[:, cc, 0:1], scale=g_fm[:, cc, 0:1])
                    else:
                        k.ts(dstT[:, cc, c0:c0 + c], p[:, cc * c:(cc + 1) * c], g_fm[:, cc, 0:1], ALU.mult,
                             b_fm[:, cc, 0:1], ALU.add)
                k.cp(dstB[:, :, c0:c0 + c], dstT[:, :, c0:c0 + c])

    def fm_to_rows(src3, R, C, dram2d):
        with k.scope():
            ob = k.sb("f2r", [R, C * 128])
            c0 = 0
            while c0 < C:
                n = min(4, C - c0)
                p = k.ps()
                for c in range(n):
                    k.tr(p[0:R, c * 128:(c + 1) * 128], src3[:, c0 + c, :], ident)
                k.cp(ob[:, c0 * 128:(c0 + n) * 128], p[0:R, 0:n * 128])
                c0 += n
            k.dma_out(dram2d, ob.v)

    def bcast_rows(row4, n, name):
        R = k.sb("bcR", [4, n, 4])
        k.tt(R.v, row4.unsq(2).bc([4, n, 4]), ident[0:4, 0:4].unsq(1).bc([4, n, 4]), ALU.mult)
        p = k.ps()
        k.mm(p[:, 0:n * 4], ones[0:4, :], R.v.re("p a b -> p (a b)"))
        dst = k.sb(name, [128, n, 4])
        k.cp(dst.v, p[:, 0:n * 4].re("p (a b) -> p a b", b=4))
        return dst

    class RPool:
        def __init__(self, name, shape, n, dt=F32):
            self.tiles = [k.sb(name, shape, dt=dt) for _ in range(n)]
            self.i = 0

        def get(self):
            t = self.tiles[self.i % len(self.tiles)]
            self.i += 1
            return t

    def interleave(gens):
        gens = list(gens)
        while gens:
            nxt = []
            for g in gens:
                try:
                    next(g)
                    nxt.append(g)
                except StopIteration:
                    pass
            gens = nxt

    def rows_to_tm(rows, c0, c, name):
        p = k.ps()
        for i, r in enumerate(rows):
            k.tr(p[0:c, 4 * i:4 * i + 4], r[:, c0:c0 + c], ident)
        dst = k.sb(name, [128, 4 * len(rows)])
        k.cp(dst[0:c, :], p[0:c, 0:4 * len(rows)])
        return dst

    def mlstm_phase(l, tl, xT, hAT):
        P = LP[l]
        TTc = tl["TT"]
        chunks = tl["chunks"]
        CP = max(ch["c"] for ch in chunks)
        Um, mcur, gb, wg = P["Um"], P["mcur"], P["gb"], P["wg"]
        qT = k.sb("qT", [128, 4, TTc])
        kT = k.sb("kT", [128, 4, TTc])
        wb = wload(w_in[l][:, A_Q:A_Q + 512], 8, 512)
        for h in range(4):
            p = proj_fm(wb, h * 128, xT, TTc)
            k.act(qT[:, h, :], p[:, 0:TTc], AF.Identity, scale=128.0 ** -0.5)
        wb = wload(w_in[l][:, A_K:A_K + 512], 8, 512)
        for h in range(4):
            p = proj_fm(wb, h * 128, xT, TTc)
            k.cp(kT[:, h, :], p[:, 0:TTc])
        kc = [k.sb("kc", [CP, 4, 128]) for _ in chunks]
        va = [k.sb("va", [CP, 4, 129]) for _ in chunks]
        ow = [k.sb("ow", [CP, 512]) for _ in chunks]
        for (g0, gn, mem) in tl["pgroups"]:
            p = proj_tm(wb, 512, xT, g0, gn)
            for (ci, r0, c) in mem:
                k.cp(kc[ci][0:c], p[r0:r0 + c, :].re("p (h d) -> p h d", h=4), eng="act")
        wb = wload(w_in[l][:, A_V:A_V + 512], 8, 512)
        for (g0, gn, mem) in tl["pgroups"]:
            p = proj_tm(wb, 512, xT, g0, gn)
            for (ci, r0, c) in mem:
                k.cp(va[ci][0:c, :, 0:128], p[r0:r0 + c, :].re("p (h d) -> p h d", h=4))
                k.memset(va[ci][0:c, :, 128:129], 1.0)
        wb = wload(w_in[l][:, A_O:A_O + 512], 8, 512)
        for (g0, gn, mem) in tl["pgroups"]:
            p = proj_tm(wb, 512, xT, g0, gn)
            for (ci, r0, c) in mem:
                k.act(ow[ci][0:c, :], p[r0:r0 + c, :], AF.Sigmoid)
                k.tt(ow[ci][0:c, :], ow[ci][0:c, :], P["anw"][0:c, :], ALU.mult)
        pi = k.ps()
        for kk in range(8):
            k.mm(pi[0:4, 0:TTc], wg[:, kk, 0:4], xT[:, kk, 0:TTc], start=(kk == 0), stop=(kk == 7))
        pf = k.ps()
        for kk in range(8):
            k.mm(pf[0:4, 0:TTc], wg[:, kk, 4:8], xT[:, kk, 0:TTc], start=(kk == 0), stop=(kk == 7))
        igr = k.sb("igr", [4, TTc])
        lf = k.sb("lf", [4, TTc])
        b = k.sb("b", [4, TTc])
        a = k.sb("a", [4, TTc])
        Er = k.sb("Er", [4, TTc])
        Th = k.sb("Th", [4, TTc])
        k.act(igr.v, pi[0:4, 0:TTc], AF.Identity, bias=gb[:, 0:1])
        k.act(lf.v, pf[0:4, 0:TTc], AF.Exp, bias=gb[:, 1:2], scale=-1.0)
        k.act(lf.v, lf.v, AF.Ln, bias=cst[0:4, 0:1])
        k.ts(lf.v, lf.v, -1.0, ALU.mult)
        for (g0, gl) in tl["groups"]:
            k.scan(b[:, g0:g0 + gl], lf[:, g0:g0 + gl])
        k.tt(a.v, igr.v, b.v, ALU.subtract)
        nb = k.sb("nb", [4, TTc])
        k.ts(nb.v, b.v, -1.0, ALU.mult)
        nch = len(chunks)
        rr = k.sb("rr", [4, nch + 16])
        nrr = k.sb("nrr", [4, nch + 16])
        mprev = k.sb("mprev", [4, nch + 16])
        am = k.sb("am", [4, nch + 16])
        wrow = k.sb("wrow", [4, nch + 16])
        for j, ch in enumerate(chunks):
            c0, c = ch["c0"], ch["c"]
            if ch["kind"] == "samp":
                av = a[:, c0:c0 + 64].re("p (s j) -> p s j", j=4)
                bv = b[:, c0:c0 + 64].re("p (s j) -> p s j", j=4)
                k.red(am[:, j:j + 16], av, ALU.max)
                k.dma_in(mprev[:, j:j + 16], sm[l].rearrange("s h -> h s"), slow=True)
                k.tt(rr[:, j:j + 16], mprev[:, j:j + 16], am[:, j:j + 16], ALU.max)
                mnew = k.sb("mnew", [4, 16])
                k.tt(mnew.v, rr[:, j:j + 16], bv[:, :, 3], ALU.add)
                k.dma_out(om[l].rearrange("s h -> h s"), mnew.v, slow=True)
                k.ts(nrr[:, j:j + 16], rr[:, j:j + 16], -1.0, ALU.mult)
                k.tt(Er[:, c0:c0 + 64].re("p (s j) -> p s j", j=4), av, rr[:, j:j + 16].unsq(2).bc([4, 16, 4]), ALU.subtract)
                k.tt(Th[:, c0:c0 + 64].re("p (s j) -> p s j", j=4), nb[:, c0:c0 + 64].re("p (s j) -> p s j", j=4),
                     rr[:, j:j + 16].unsq(2).bc([4, 16, 4]), ALU.subtract)
                k.act(Er[:, c0:c0 + 64], Er[:, c0:c0 + 64], AF.Exp)
                k.act(Th[:, c0:c0 + 64], Th[:, c0:c0 + 64], AF.Exp)
                k.tt(wrow[:, j:j + 16], mprev[:, j:j + 16], rr[:, j:j + 16], ALU.subtract)
            else:
                k.red(am[:, j:j + 1], a[:, c0:c0 + c], ALU.max)
                k.cp(mprev[:, j:j + 1], mcur.v)
                k.tt(rr[:, j:j + 1], mcur.v, am[:, j:j + 1], ALU.max)
                k.tt(mcur.v, rr[:, j:j + 1], b[:, c0 + c - 1:c0 + c], ALU.add)
                k.ts(nrr[:, j:j + 1], rr[:, j:j + 1], -1.0, ALU.mult)
                k.act(Er[:, c0:c0 + c], a[:, c0:c0 + c], AF.Exp, bias=nrr[:, j:j + 1])
                k.act(Th[:, c0:c0 + c], nb[:, c0:c0 + c], AF.Exp, bias=nrr[:, j:j + 1])
                k.tt(wrow[:, j:j + 1], mprev[:, j:j + 1], rr[:, j:j + 1], ALU.subtract)
        nw = nch + 15 if chunks[-1]["kind"] == "samp" else nch
        k.act(wrow[:, 0:nw], wrow[:, 0:nw], AF.Exp)
        wbc = bcast_rows(wrow[:, 0:nw], nw, "wbc")
        tap("Er%d" % l, Er.v)
        tap("Th%d" % l, Th.v)
        for j, ch in enumerate(chunks):
            c0, c, kind = ch["c0"], ch["c"], ch["kind"]
            m01 = MASKS["b" if kind == "samp" else "c"][0]
            with k.scope():
                sc = rows_to_tm([Er.v, Th.v], c0, c, "scm")
                vE = k.sb("vE", [CP, 4, 129])
                k.tt(vE[0:c], va[j][0:c], sc[0:c, 0:4].unsq(2).bc([c, 4, 129]), ALU.mult)
                if kind != "samp":
                    offs = [(h // 2) * 512 + (h % 2) * 129 for h in range(4)]
                    psts = []
                    for h in range(4):
                        pst = k.ps()
                        k.mm(pst[0:c, 0:c], kT[:, h, c0:c0 + c], qT[:, h, c0:c0 + c])
                        psts.append(pst)
                    STl, Chl = [], []
                    for h in range(4):
                        STs = k.sb("STs", [CP, CP])
                        k.stt(STs[0:c, 0:c], psts[h][0:c, 0:c], sc[0:c, h:h + 1], m01[0:c, 0:c], ALU.mult, ALU.mult)
                        STl.append(STs)
                        Ch = k.sb("Ch", [128, 129])
                        k.ts(Ch.v, Um[:, h, :], wbc[:, j, h:h + 1], ALU.mult)
                        Chl.append(Ch)
                    for h in range(4):
                        k.mm(psd[0:c, offs[h]:offs[h] + 129], STl[h][0:c, 0:c], va[j][0:c, h, :], start=True, stop=False)
                        k.mm(psd[0:c, offs[h]:offs[h] + 129], qT[:, h, c0:c0 + c], Chl[h].v, start=False, stop=True)
                    pps = []
                    for h in range(4):
                        pp = k.ps()
                        k.mm(pp[:, 0:129], kc[j][0:c, h, :], vE[0:c, h, :])
                        pps.append(pp)
                    for h in range(4):
                        k.tt(Um[:, h, :], Chl[h].v, pps[h][:, 0:129], ALU.add)
                for h in (range(4) if kind == "samp" else []):
                    k.push()
                    off = (h // 2) * 512 + (h % 2) * 129
                    pst = k.ps()
                    k.mm(pst[0:c, 0:c], kT[:, h, c0:c0 + c], qT[:, h, c0:c0 + c])
                    STs = k.sb("STs", [CP, CP])
                    k.stt(STs[0:c, 0:c], pst[0:c, 0:c], sc[0:c, h:h + 1], m01[0:c, 0:c], ALU.mult, ALU.mult)
                    if kind != "samp":
                        pass
                    else:
                        qTm = k.sb("qTm", [128, 16, 64])
                        k.memset(qTm.v, 0.0)
                        for i in range(16):
                            k.cp(qTm[:, i, 4 * i:4 * i + 4], qT[:, h, c0 + 4 * i:c0 + 4 * i + 4])
                        k.mm(psd[0:c, off:off + 129], STs[0:c, 0:c], va[j][0:c, h, :], start=True, stop=False)
                        for g in range(4):
                            with k.scope():
                                Cg = k.sb("Cg", [128, 4, 129])
                                k.dma_in(Cg[:, :, 0:128], sC[l].rearrange("(s h) d v -> h d s v", h=4)[h][:, 4 * g:4 * g + 4, :])
                                nrow = k.sb("nrow", [4, 128])
                                k.dma_in(nrow.v, sn[l].rearrange("(s h) d -> h s d", h=4)[h][4 * g:4 * g + 4, :])
                                pn_ = k.ps()
                                k.tr(pn_[:, 0:4], nrow.v, ident)
                                k.cp(Cg[:, :, 128], pn_[:, 0:4])
                                wv = wbc[:, j + 4 * g:j + 4 * g + 4, h:h + 1]
                                k.tt(Cg.v, Cg.v, wv.bc([128, 4, 129]), ALU.mult)
                                for ii in range(4):
                                    i = 4 * g + ii
                                    k.mm(psd[0:c, off:off + 129], qTm[:, i, :], Cg[:, ii, :], start=False,
                                         stop=(i == 15))
                                vEx = k.sb("vEx", [64, 4, 129])
                                k.tt(vEx.v, vE[0:64, h, :].unsq(1).bc([64, 4, 129]),
                                     ind[:, 4 * g:4 * g + 4].unsq(2).bc([64, 4, 129]), ALU.mult)
                                vf = vEx.v.re("p a b -> p (a b)")
                                pu = k.ps()
                                pu2 = k.ps()
                                k.mm(pu[:, 0:512], kc[j][0:64, h, :], vf[:, 0:512])
                                k.mm(pu2[:, 0:4], kc[j][0:64, h, :], vf[:, 512:516])
                                Cf = Cg.v.re("p a b -> p (a b)")
                                k.tt(Cf[:, 0:512], Cf[:, 0:512], pu[:, 0:512], ALU.add)
                                k.tt(Cf[:, 512:516], Cf[:, 512:516], pu2[:, 0:4], ALU.add)
                                k.dma_out(oC[l].rearrange("(s h) d v -> h d s v", h=4)[h][:, 4 * g:4 * g + 4, :], Cg[:, :, 0:128])
                                pn2 = k.ps()
                                k.tr(pn2[0:4, 0:128], Cg[:, :, 128], ident)
                                nout = k.sb("nout", [4, 128])
                                k.cp(nout.v, pn2[0:4, 0:128])
                                k.dma_out(on[l].rearrange("(s h) d -> h s d", h=4)[h][4 * g:4 * g + 4, :], nout.v)
                    k.pop()
                numv = psd[0:c, :].re("p (a r) -> p a r", a=2)[:, :, 0:258].re("p a (h q) -> p a h q", h=2)
                den = k.sb("den", [CP, 2, 2])
                k.cp(den[0:c], numv[:, :, :, 128])
                dn = k.sb("dn", [CP, 4])
                denf = den[0:c].re("p a b -> p (a b)")
                k.ts(dn[0:c], denf, -1.0, ALU.mult)
                k.tt(dn[0:c], dn[0:c], denf, ALU.max)
                k.tt(dn[0:c], dn[0:c], sc[0:c, 4:8], ALU.max)
                rden = k.sb("rden", [CP, 4])
                k.op("dve", lambda e: e.reciprocal(out=rden.ap[0:c], in_=dn.ap[0:c]), reads=[dn], writes=[rden])
                st = k.sb("hst", [CP, 4, 6])
                mv = k.sb("hmv", [CP, 4, 2])
                for h in range(4):
                    off = (h // 2) * 512 + (h % 2) * 129
                    k.op("dve", lambda e, h=h, off=off: e.bn_stats(out=st.ap[0:c, h, :], in_=psd.ap[0:c, off:off + 128]),
                         reads=[psd], writes=[st])
                for h in range(4):
                    k.op("dve", lambda e, h=h: e.bn_aggr(out=mv.ap[0:c, h, :], in_=st.ap[0:c, h, :]), reads=[st], writes=[mv])
                t1 = k.sb("t1", [CP, 4])
                k.tt(t1[0:c], rden[0:c], rden[0:c], ALU.mult)
                k.tt(t1[0:c], t1[0:c], mv[0:c, :, 1], ALU.mult)
                rsq = k.sb("rsq", [CP, 4])
                rsqrt_small(rsq[0:c], t1[0:c], NORM_EPS)
                k.tt(rsq[0:c], rsq[0:c], rden[0:c], ALU.mult)
                hA = k.sb("hA", [CP, 512])
                for h in range(4):
                    off = (h // 2) * 512 + (h % 2) * 129
                    k.ts(hA[0:c, h * 128:(h + 1) * 128], psd[0:c, off:off + 128], mv[0:c, h, 0:1], ALU.subtract,
                         rsq[0:c, h:h + 1], ALU.mult)
                k.tt(hA[0:c], hA[0:c], ow[j][0:c], ALU.mult)
                pt = k.ps()
                for h in range(4):
                    k.tr(pt[:, h * c:(h + 1) * c], hA[0:c, h * 128:(h + 1) * 128], ident)
                k.cp(hAT[:, :, c0:c0 + c], pt[:, 0:4 * c].re("p (h t) -> p h t", h=4), eng="act")

    def gdn_phase(l, tl, xT, hBT):
        P = LP[l]
        TTc = tl["TT"]
        chunks = tl["chunks"]
        CP = max(ch["c"] for ch in chunks)
        S, gb, wg, cg, cwg = P["S"], P["gb"], P["wg"], P["cg"], P["cwg"]
        qkvT = k.sb("qkvT", [128, 12, TTc])
        k.push()
        pools = None if tl["first"] else (RPool("cext", [128, TTc + 3], 3), RPool("cacc", [128, TTc], 3))
        sqp = RPool("csq", [128, TTc], 2)
        for blk in range(3):
            wb = wload(w_in[l][:, B_QKV + 512 * blk:B_QKV + 512 * (blk + 1)], 8, 512)
            for q4 in range(4):
                cc = blk * 4 + q4
                p = proj_fm(wb, q4 * 128, xT, TTc)
                with k.scope():
                    acc = conv_fm(p, cc, 4, cg, cwg, sgc[l], ogc[l], tl, "c", pools)
                    k.act(qkvT[:, cc, :], acc.v, AF.Silu)
                    if cc < 8:
                        sq = sqp.get()
                        k.tt(sq.v, qkvT[:, cc, :], qkvT[:, cc, :], ALU.mult)
                        pq = k.ps()
                        k.mm(pq[:, 0:TTc], ones.v, sq.v)
                        k.act(sq.v, pq[:, 0:TTc], AF.Ln, bias=cst[:, 1:2])
                        k.act(sq.v, sq.v, AF.Exp, scale=-0.5, bias=(cst[:, 2:3] if cc < 4 else cst[:, 3:4]))
                        k.tt(qkvT[:, cc, :], qkvT[:, cc, :], sq.v, ALU.mult)
        k.pop()
        wb = wload(w_in[l][:, B_Z:B_Z + 512], 8, 512)
        wz = [k.sb("wz", [CP, 4, 128]) for _ in chunks]
        for (g0, gn, mem) in tl["pgroups"]:
            p = proj_tm(wb, 512, xT, g0, gn)
            for (ci, r0, c) in mem:
                k.act(wz[ci][0:c].re("p h d -> p (h d)"), p[r0:r0 + c, :], AF.Silu)
                k.tt(wz[ci][0:c], wz[ci][0:c], P["gnw"][0:c, :].unsq(1).bc([c, 4, 128]), ALU.mult)
        pbt = k.ps()
        for kk in range(8):
            k.mm(pbt[0:4, 0:TTc], wg[:, kk, 8:12], xT[:, kk, 0:TTc], start=(kk == 0), stop=(kk == 7))
        pa_ = k.ps()
        for kk in range(8):
            k.mm(pa_[0:4, 0:TTc], wg[:, kk, 12:16], xT[:, kk, 0:TTc], start=(kk == 0), stop=(kk == 7))
        spb = k.sb("spb", [4, TTc])
        k.act(spb.v, pbt[0:4, 0:TTc], AF.Exp, scale=-1.0)
        k.act(spb.v, spb.v, AF.Ln, bias=cst[0:4, 0:1])
        g = k.sb("g", [4, TTc])
        k.act(g.v, pa_[0:4, 0:TTc], AF.Exp, bias=gb[:, 3:4])
        k.act(g.v, g.v, AF.Ln, bias=cst[0:4, 0:1])
        k.ts(g.v, g.v, gb[:, 2:3], ALU.mult)
        G = k.sb("G", [4, TTc])
        for (g0, gl) in tl["groups"]:
            k.scan(G[:, g0:g0 + gl], g[:, g0:g0 + gl])
        Gb = k.sb("Gb", [4, TTc])
        k.tt(Gb.v, G.v, spb.v, ALU.subtract)
        nG = k.sb("nG", [4, TTc])
        k.ts(nG.v, G.v, -1.0, ALU.mult)
        r_beta = k.sb("r_beta", [4, TTc])
        k.act(r_beta.v, spb.v, AF.Exp, scale=-1.0)
        r_bg = k.sb("r_bg", [4, TTc])
        k.act(r_bg.v, Gb.v, AF.Exp)
        r_eg = k.sb("r_eg", [4, TTc])
        k.act(r_eg.v, G.v, AF.Exp)
        r_kd = k.sb("r_kd", [4, TTc])
        nch = len(chunks)
        ge = k.sb("ge", [4, nch + 16])
        for j, ch in enumerate(chunks):
            c0, c = ch["c0"], ch["c"]
            if ch["kind"] == "samp":
                Gv = G[:, c0:c0 + 64].re("p (s j) -> p s j", j=4)
                k.cp(ge[:, j:j + 16], Gv[:, :, 3])
                k.tt(r_kd[:, c0:c0 + 64].re("p (s j) -> p s j", j=4), nG[:, c0:c0 + 64].re("p (s j) -> p s j", j=4),
                     ge[:, j:j + 16].unsq(2).bc([4, 16, 4]), ALU.add)
            else:
                k.cp(ge[:, j:j + 1], G[:, c0 + c - 1:c0 + c])
                k.ts(r_kd[:, c0:c0 + c], nG[:, c0:c0 + c], ge[:, j:j + 1], ALU.add)
        k.act(r_kd.v, r_kd.v, AF.Exp)
        nw = nch + 15 if chunks[-1]["kind"] == "samp" else nch
        k.act(ge[:, 0:nw], ge[:, 0:nw], AF.Exp)
        gbc = bcast_rows(ge[:, 0:nw], nw, "gbc")
        for j, ch in enumerate(chunks):
            c0, c, kind = ch["c0"], ch["c"], ch["kind"]
            _, m_st, m_stT, m_inT = MASKS["b" if kind == "samp" else "c"]
            nsq = {128: 6, 64: 5, 16: 3}[c] if kind != "samp" else 1
            with k.scope():
                sc = rows_to_tm([r_beta.v, r_bg.v, r_eg.v, r_kd.v], c0, c, "scg")
                kcn = k.sb("kcn", [CP, 4, 128])
                vcn = k.sb("vcn", [CP, 4, 128])
                p = k.ps()
                for h in range(4):
                    k.tr(p[0:c, h * 128:(h + 1) * 128], qkvT[:, 4 + h, c0:c0 + c], ident)
                k.cp(kcn[0:c].re("p h d -> p (h d)"), p[0:c, :], eng="act")
                p = k.ps()
                for h in range(4):
                    k.tr(p[0:c, h * 128:(h + 1) * 128], qkvT[:, 8 + h, c0:c0 + c], ident)
                k.cp(vcn[0:c].re("p h d -> p (h d)"), p[0:c, :])
                rv = k.sb("rv", [CP, 4, 128])
                rk = k.sb("rk", [CP, 4, 128])
                kd = k.sb("kd", [CP, 4, 128])
                k.tt(rv[0:c], vcn[0:c], sc[0:c, 0:4].unsq(2).bc([c, 4, 128]), ALU.mult)
                k.tt(rk[0:c], kcn[0:c], sc[0:c, 4:8].unsq(2).bc([c, 4, 128]), ALU.mult)
                k.tt(kd[0:c], kcn[0:c], sc[0:c, 12:16].unsq(2).bc([c, 4, 128]), ALU.mult)
                ob = k.sb("ob", [CP, 4, 128])

                def head(h):
                    if kind == "samp":
                        k.push()
                    qh = qkvT[:, h, c0:c0 + c]
                    kh = qkvT[:, 4 + h, c0:c0 + c]
                    pe_ = k.ps()
                    k.mm(pe_[0:c, 0:c], Gb[:, c0:c0 + c], oh[:, h, 0:c], start=True, stop=False)
                    k.mm(pe_[0:c, 0:c], oh[:, h, 0:c], nG[:, c0:c0 + c], start=False, stop=False)
                    k.mm(pe_[0:c, 0:c], ident[0:c, 0:c], m_st[0:c, 0:c], start=False, stop=True)
                    k.mm(pe_[0:c, c:2 * c], oh[:, h, 0:c], Gb[:, c0:c0 + c], start=True, stop=False)
                    k.mm(pe_[0:c, c:2 * c], nG[:, c0:c0 + c], oh[:, h, 0:c], start=False, stop=False)
                    k.mm(pe_[0:c, c:2 * c], ident[0:c, 0:c], m_stT[0:c, 0:c], start=False, stop=True)
                    k.mm(pe_[0:c, 2 * c:3 * c], oh[:, h, 0:c], G[:, c0:c0 + c], start=True, stop=False)
                    k.mm(pe_[0:c, 2 * c:3 * c], nG[:, c0:c0 + c], oh[:, h, 0:c], start=False, stop=False)
                    k.mm(pe_[0:c, 2 * c:3 * c], ident[0:c, 0:c], m_inT[0:c, 0:c], start=False, stop=True)
                    Ex = k.sb("Ex", [CP, 3 * CP])
                    k.act(Ex[0:c, 0:3 * c], pe_[0:c, 0:3 * c], AF.Exp)
                    yield
                    psc = k.ps()
                    k.mm(psc[0:c, 0:c], kh, kh)
                    k.mm(psc[0:c, c:2 * c], kh, kh)
                    k.mm(psc[0:c, 2 * c:3 * c], kh, qh)
                    M3 = k.sb("M3", [CP, 3 * CP])
                    k.tt(M3[0:c, 0:3 * c], psc[0:c, 0:3 * c], Ex[0:c, 0:3 * c], ALU.mult)
                    TTs = [k.sb("TTm", [CP, CP]), k.sb("TTm", [CP, CP])]
                    P2s = [k.sb("P2", [CP, 2 * CP]), k.sb("P2", [CP, 2 * CP])]
                    TTm = TTs[0]
                    k.tt(TTm[0:c, 0:c], ident[0:c, 0:c], M3[0:c, c:2 * c], ALU.subtract)
                    yield
                    Pw = M3
                    for q in range(nsq):
                        pp = k.ps()
                        k.mm(pp[0:c, 0:c], Pw[0:c, c:2 * c], Pw[0:c, 0:c])
                        P2 = P2s[q % 2]
                        if q < nsq - 1:
                            k.mm(pp[0:c, c:2 * c], Pw[0:c, 0:c], Pw[0:c, c:2 * c])
                            k.cp(P2[0:c, 0:2 * c], pp[0:c, 0:2 * c], eng="act")
                        else:
                            k.cp(P2[0:c, 0:c], pp[0:c, 0:c], eng="act")
                        yield
                        pt_ = k.ps()
                        k.mm(pt_[0:c, 0:c], P2[0:c, 0:c], TTm[0:c, 0:c])
                        Tn = TTs[(q + 1) % 2]
                        k.tt(Tn[0:c, 0:c], TTm[0:c, 0:c], pt_[0:c, 0:c], ALU.add)
                        TTm = Tn
                        Pw = P2
                        yield
                    pw = k.ps()
                    k.mm(pw[:, 0:c], rk[0:c, h, :], TTm[0:c, 0:c])
                    WTn = k.sb("WTn", [128, CP])
                    k.ts(WTn[:, 0:c], pw[:, 0:c], -1.0, ALU.mult)
                    yield
                    pu = k.ps()
                    po = k.ps()
                    if kind != "samp":
                        k.mm(pu[0:c, 0:128], TTm[0:c, 0:c], rv[0:c, h, :], start=True, stop=False)
                        k.mm(pu[0:c, 0:128], WTn[:, 0:c], S[:, h, :], start=False, stop=True)
                        Us = k.sb("Us", [CP, 128])
                        k.cp(Us[0:c], pu[0:c, 0:128], eng="act")
                        k.mm(po[0:c, 0:128], qh, S[:, h, :])
                        o1 = k.sb("o1", [CP, 128])
                        k.act(o1[0:c], po[0:c, 0:128], AF.Identity, scale=sc[0:c, 8 + h:9 + h])
                        yield
                        po2 = k.ps()
                        k.mm(po2[0:c, 0:128], M3[0:c, 2 * c:3 * c], Us[0:c])
                        k.tt(ob[0:c, h, :], o1[0:c], po2[0:c, 0:128], ALU.add)
                        pS_ = k.ps()
                        k.mm(pS_[:, 0:128], kd[0:c, h, :], Us[0:c])
                        k.stt(S[:, h, :], S[:, h, :], gbc[:, j, h:h + 1], pS_[:, 0:128], ALU.mult, ALU.add)
                    else:
                        WTm = k.sb("WTm", [128, 16, 64])
                        qTm = k.sb("qTm", [128, 16, 64])
                        k.memset(WTm.v, 0.0)
                        k.memset(qTm.v, 0.0)
                        for i in range(16):
                            k.cp(WTm[:, i, 4 * i:4 * i + 4], WTn[:, 4 * i:4 * i + 4])
                            k.cp(qTm[:, i, 4 * i:4 * i + 4], qkvT[:, h, c0 + 4 * i:c0 + 4 * i + 4], eng="act")
                        Sg = []
                        for gq in range(4):
                            t = k.sb("Sg", [128, 4, 128])
                            k.dma_in(t.v, sS[l].rearrange("(s h) d v -> h d s v", h=4)[h][:, 4 * gq:4 * gq + 4, :])
                            Sg.append(t)
                        k.mm(pu[0:c, 0:128], TTm[0:c, 0:c], rv[0:c, h, :], start=True, stop=False)
                        for i in range(16):
                            k.mm(pu[0:c, 0:128], WTm[:, i, :], Sg[i // 4][:, i % 4, :], start=False, stop=(i == 15))
                        Us = k.sb("Us", [CP, 128])
                        k.cp(Us[0:c], pu[0:c, 0:128], eng="act")
                        for i in range(16):
                            k.mm(po[0:c, 0:128], qTm[:, i, :], Sg[i // 4][:, i % 4, :], start=(i == 0), stop=(i == 15))
                        o1 = k.sb("o1", [CP, 128])
                        k.act(o1[0:c], po[0:c, 0:128], AF.Identity, scale=sc[0:c, 8 + h:9 + h])
                        po2 = k.ps()
                        k.mm(po2[0:c, 0:128], M3[0:c, 2 * c:3 * c], Us[0:c])
                        k.tt(ob[0:c, h, :], o1[0:c], po2[0:c, 0:128], ALU.add)
                        for gq in range(4):
                            Ux = k.sb("Ux", [64, 4, 128])
                            k.tt(Ux.v, Us.v.unsq(1).bc([64, 4, 128]), ind[:, 4 * gq:4 * gq + 4].unsq(2).bc([64, 4, 128]), ALU.mult)
                            pS_ = k.ps()
                            k.mm(pS_[:, 0:512], kd[0:64, h, :], Ux.v.re("p a b -> p (a b)"))
                            k.tt(Sg[gq].v, Sg[gq].v, gbc[:, j + 4 * gq:j + 4 * gq + 4, h:h + 1].bc([128, 4, 128]), ALU.mult)
                            k.tt(Sg[gq].v, Sg[gq].v, pS_[:, 0:512].re("p (a b) -> p a b", a=4), ALU.add)
                            k.dma_out(oS[l].rearrange("(s h) d v -> h d s v", h=4)[h][:, 4 * gq:4 * gq + 4, :], Sg[gq].v)
                    if kind == "samp":
                        k.pop()

                if kind == "samp":
                    for h in range(4):
                        for _ in head(h):
                            pass
                else:
                    interleave([head(h) for h in range(4)])
                sq = k.sb("osq", [CP, 4, 128])
                k.tt(sq[0:c], ob[0:c], ob[0:c], ALU.mult)
                ss = k.sb("oss", [CP, 4])
                k.red(ss[0:c], sq[0:c], ALU.add)
                k.ts(ss[0:c], ss[0:c], 1.0 / 128.0, ALU.mult)
                rs = k.sb("ors", [CP, 4])
                rsqrt_small(rs[0:c], ss[0:c], NORM_EPS)
                k.tt(ob[0:c], ob[0:c], rs[0:c].unsq(2).bc([c, 4, 128]), ALU.mult)
                k.tt(ob[0:c], ob[0:c], wz[j][0:c], ALU.mult)
                pt = k.ps()
                for h in range(4):
                    k.tr(pt[:, h * c:(h + 1) * c], ob[0:c, h, :], ident)
                k.cp(hBT[:, :, c0:c0 + c], pt[:, 0:4 * c].re("p (h t) -> p h t", h=4), eng="act")

    def ln_rows(src, n, g_fm, b_fm, dst32, dstB, c0, final=None):
        with k.scope():
            st = k.sb("lnst", [128, 12])
            mv = k.sb("lnmv", [128, 2])
            rs = k.sb("lnrs", [128, 1])
            xn = k.sb("lnxn", [128, 1024])
            k.op("dve", lambda e: e.bn_stats(out=st.ap[0:n, 0:6], in_=src.ap[:, 0:512]), reads=[src.t], writes=[st])
            k.op("dve", lambda e: e.bn_stats(out=st.ap[0:n, 6:12], in_=src.ap[:, 512:1024]), reads=[src.t], writes=[st])
            k.op("dve", lambda e: e.bn_aggr(out=mv.ap[0:n, :], in_=st.ap[0:n, :]), reads=[st], writes=[mv])
            rsqrt_small(rs[0:n, :], mv[0:n, 1:2], LN_EPS)
            k.ts(xn[0:n, :], src, mv[0:n, 0:1], ALU.subtract, rs[0:n, 0:1], ALU.mult)
            if final is not None:
                k.tt(xn[0:n, :], xn[0:n, :], g2bc[0:n, :], ALU.mult)
                k.tt(xn[0:n, :], xn[0:n, :], b2bc[0:n, :], ALU.add)
                for (dst, r0, r1) in final:
                    k.dma_out(dst, xn[r0:r1, :])
            else:
                for half in range(2):
                    p = k.ps()
                    for q in range(4):
                        cc = half * 4 + q
                        k.tr(p[:, q * n:(q + 1) * n], xn[0:n, cc * 128:(cc + 1) * 128], ident)
                    for q in range(4):
                        cc = half * 4 + q
                        if q % 2 == 0:
                            k.act(dst32[:, cc, c0:c0 + n], p[:, q * n:(q + 1) * n], AF.Identity,
                                  bias=b_fm[:, cc, 0:1], scale=g_fm[:, cc, 0:1])
                        else:
                            k.ts(dst32[:, cc, c0:c0 + n], p[:, q * n:(q + 1) * n], g_fm[:, cc, 0:1], ALU.mult,
                                 b_fm[:, cc, 0:1], ALU.add)
                k.cp(dstB[:, :, c0:c0 + n], dst32[:, :, c0:c0 + n], eng="act")

    def merge_phase(l, tl, xT, xT32, hAT, hBT, x1T32, x1T):
        P = LP[l]
        TTc = tl["TT"]
        yT = k.sb("yT", [128, 8, TTc], dt=BF16)
        for hh in range(2):
            wpp = wload_parts([(w_pa[l][:, hh * 512:(hh + 1) * 512], 4), (w_pb[l][:, hh * 512:(hh + 1) * 512], 4)], 512)
            wga = wload(w_in[l][:, G_MERGE + hh * 512:G_MERGE + (hh + 1) * 512], 8, 512)
            ga = []
            for q4 in range(4):
                p = proj_fm(wga, q4 * 128, xT, TTc)
                t = k.sb("ga", [128, TTc])
                k.act(t.v, p[:, 0:TTc], AF.Sigmoid)
                ppa = k.ps()
                for kk in range(4):
                    k.mm(ppa[:, 0:TTc], wpp[:, kk, q4 * 128:(q4 + 1) * 128], hAT[:, kk, :], start=(kk == 0), stop=(kk == 3))
                k.tt(t.v, t.v, ppa[:, 0:TTc], ALU.mult)
                ga.append(t)
            wgb = wload(w_in[l][:, G_MERGE + D + hh * 512:G_MERGE + D + (hh + 1) * 512], 8, 512)
            for q4 in range(4):
                n = hh * 4 + q4
                p = proj_fm(wgb, q4 * 128, xT, TTc)
                t = k.sb("gbm", [128, TTc])
                k.act(t.v, p[:, 0:TTc], AF.Sigmoid)
                ppb = k.ps()
                for kk in range(4):
                    k.mm(ppb[:, 0:TTc], wpp[:, 4 + kk, q4 * 128:(q4 + 1) * 128], hBT[:, kk, :], start=(kk == 0), stop=(kk == 3))
                k.tt(t.v, t.v, ppb[:, 0:TTc], ALU.mult)
                k.tt(yT[:, n, :], t.v, ga[q4].v, ALU.add)
        tgs = tl["tgs"]
        osb = [k.sb("osb", [128, 1024]) for _ in tgs]
        for hh in range(2):
            wb = wload(w_out[l][:, hh * 512:(hh + 1) * 512], 8, 512)
            for j, (c0, n) in enumerate(tgs):
                p = k.ps()
                for n8 in range(8):
                    k.mm(p[0:n, 0:512], yT[:, n8, c0:c0 + n], wb[:, n8, :], start=(n8 == 0), stop=False)
                for q4 in range(4):
                    k.mm(p[0:n, q4 * 128:(q4 + 1) * 128], xT32[:, hh * 4 + q4, c0:c0 + n], aident.v, start=False, stop=(q4 == 3))
                k.cp(osb[j][0:n, hh * 512:(hh + 1) * 512], p[0:n, 0:512], eng="act")
        for j, (c0, n) in enumerate(tgs):
            ln_rows(osb[j][0:n, :], n, P["g1"], P["b1"], x1T32, x1T, c0)

    def conv_fm(p, cc, nt, carry, cw, s_in, s_out, tl, tag, pools=None):
        TTc = tl["TT"]
        hh = nt - 1
        acc = pools[1].get() if pools is not None else k.sb(tag + "acc", [128, TTc])
        if tl["first"]:
            ext = k.sb(tag + "ext", [128, 16 + hh])
            k.cp(ext[:, 0:hh], carry[:, cc, :])
            k.cp(ext[:, hh:16 + hh], p[:, 0:16], eng="act")
            exs = k.sb(tag + "exs", [128, 16, 4 + hh])
            hist = k.sb(tag + "hist", [16 * hh, 128])
            k.dma_in(hist.v, s_in[:, cc * 128:(cc + 1) * 128])
            ph = k.ps()
            k.tr(ph[:, 0:16 * hh], hist.v, ident)
            k.cp(exs[:, :, 0:hh], ph[:, 0:16 * hh].re("p (s j) -> p s j", j=hh))
            k.cp(exs[:, :, hh:4 + hh], p[:, 16:80].re("p (s j) -> p s j", j=4), eng="act")
            k.cp(carry[:, cc, :], ext[:, 16:16 + hh])
            pc = k.ps()
            nst = k.sb(tag + "nst", [128, 16, hh])
            k.cp(nst.v, exs[:, :, 4:4 + hh])
            k.tr(pc[0:16 * hh, 0:128], nst.v.re("p s j -> p (s j)"), ident)
            ob = k.sb(tag + "ob", [16 * hh, 128])
            k.cp(ob.v, pc[0:16 * hh, 0:128])
            k.dma_out(s_out[:, cc * 128:(cc + 1) * 128], ob.v)
            av = acc[:, 0:16]
            k.ts(av, ext[:, 0:16], cw[:, cc, 0:1], ALU.mult)
            for jj in range(1, nt):
                k.stt(av, ext[:, jj:jj + 16], cw[:, cc, jj:jj + 1], av, ALU.mult, ALU.add)
            av = acc[:, 16:80].re("p (s j) -> p s j", j=4)
            k.ts(av, exs[:, :, 0:4], cw[:, cc, 0:1], ALU.mult)
            for jj in range(1, nt):
                k.stt(av, exs[:, :, jj:jj + 4], cw[:, cc, jj:jj + 1], av, ALU.mult, ALU.add)
        else:
            ext = pools[0].get()
            k.cp(ext[:, 0:hh], carry[:, cc, :])
            k.cp(ext[:, hh:TTc + hh], p[:, 0:TTc], eng="act")
            k.act(acc.v, p[:, 0:TTc], AF.Identity, scale=cw[:, cc, hh:hh + 1])
            k.cp(carry[:, cc, :], ext[:, TTc:TTc + hh])
            for jj in range(0, hh):
                k.stt(acc.v, ext[:, jj:jj + TTc], cw[:, cc, jj:jj + 1], acc.v, ALU.mult, ALU.add)
        return acc

    def ffn_phase(l, tl, x1T, x1T32, x2T32, x2T, last):
        P = LP[l]
        TTc = tl["TT"]
        cf, cwf = P["cf"], P["cwf"]
        hT = k.sb("hT", [128, 22, TTc], dt=BF16)
        k.push()
        pools = None if tl["first"] else (RPool("fext", [128, TTc + 2], 4), RPool("facc", [128, TTc], 4))
        for blk in range(6):
            ncols = 512 if blk < 5 else 256
            wb1 = wload(w_up[l][:, blk * 512:blk * 512 + ncols], 8, ncols)
            wb2 = wload(w_up[l][:, DFF + blk * 512:DFF + blk * 512 + ncols], 8, ncols)
            for q4 in range(ncols // 128):
                f = blk * 4 + q4
                with k.scope():
                    p1 = proj_fm(wb1, q4 * 128, x1T, TTc)
                    a1 = conv_fm(p1, f, 3, cf, cwf, sfc[l], ofc[l], tl, "f", pools)
                    k.act(a1.v, a1.v, AF.Silu)
                    p2 = proj_fm(wb2, q4 * 128, x1T, TTc)
                    a2 = conv_fm(p2, 22 + f, 3, cf, cwf, sfc[l], ofc[l], tl, "f", pools)
                    k.tt(hT[:, f, :], a1.v, a2.v, ALU.mult)
        k.pop()
        tgs = tl["tgs"]
        osb = [k.sb("osb2", [128, 1024]) for _ in tgs]
        kgs = [(0, 8), (8, 8), (16, 6)]
        for hh in range(2):
            pts = [k.ps() for _ in tgs]
            for (k0, nk) in kgs:
                wb = wload(w_down[l][k0 * 128:(k0 + nk) * 128, hh * 512:(hh + 1) * 512], nk, 512)
                for j, (c0, n) in enumerate(tgs):
                    for kk in range(nk):
                        k.mm(pts[j][0:n, 0:512], hT[:, k0 + kk, c0:c0 + n], wb[:, kk, :], start=(k0 + kk == 0), stop=False,
                             sig=(kk == nk - 1))
            for j, (c0, n) in enumerate(tgs):
                for q4 in range(4):
                    k.mm(pts[j][0:n, q4 * 128:(q4 + 1) * 128], x1T32[:, hh * 4 + q4, c0:c0 + n], aident.v, start=False, stop=(q4 == 3))
                k.cp(osb[j][0:n, hh * 512:(hh + 1) * 512], pts[j][0:n, 0:512], eng="act")
        for j, (c0, n) in enumerate(tgs):
            fin = None
            if last:
                if tl["first"]:
                    fin = [(ys[:, :], 16, 80)]
                else:
                    tok0 = tl["tok0"] + c0
                    fin = [(yp[tok0:tok0 + n, :], 0, n)]
            ln_rows(osb[j][0:n, :], n, P["g2"], P["b2"], x2T32, x2T, c0, final=fin)

    for ti, tl in enumerate(tiles):
        TTc = tl["TT"]
        cur["ti"] = ti
        cur["n"] = 0
        with k.scope():
            if ti == 0:
                cur["stage"] = RPool("wstage", [128, 8, 512], 2)
            xT32 = k.sb("xT32", [128, 8, TTc])
            xT = k.sb("xT", [128, 8, TTc], dt=BF16)
            for (c0, n) in tl["tgs"]:
                with k.scope():
                    xin = k.sb("xin", [128, 1024])
                    if tl["first"]:
                        k.dma_in(xin[0:16, :], meta[:, :])
                        k.dma_in(xin[16:80, :], xs[:, :])
                    else:
                        k.dma_in(xin[0:n, :], xp[tl["tok0"] + c0:tl["tok0"] + c0 + n, :])
                    ln_rows(xin[0:n, :], n, g_emb, b_emb, xT32, xT, c0)
            if ti == 0:
                tap("xT0", xT32.v)
            for l in range(DEPTH):
                x1T32 = k.sb("x1T32", [128, 8, TTc])
                x1T = k.sb("x1T", [128, 8, TTc], dt=BF16)
                x2T32 = k.sb("x2T32", [128, 8, TTc])
                x2T = k.sb("x2T", [128, 8, TTc], dt=BF16)
                with k.scope():
                    hAT = k.sb("hAT", [128, 4, TTc], dt=BF16)
                    hBT = k.sb("hBT", [128, 4, TTc], dt=BF16)
                    with k.scope():
                        mlstm_phase(l, tl, xT, hAT)
                    if ti == 0:
                        tap("hAT%d" % l, hAT.v)
                    with k.scope():
                        gdn_phase(l, tl, xT, hBT)
                    if ti == 0:
                        tap("hBT%d" % l, hBT.v)
                    with k.scope():
                        merge_phase(l, tl, xT, xT32, hAT, hBT, x1T32, x1T)
                if ti == 0:
                    tap("x1T%d" % l, x1T32.v)
                with k.scope():
                    ffn_phase(l, tl, x1T, x1T32, x2T32, x2T, last=(l == DEPTH - 1))
                if ti == 0 and l == 0:
                    tap("x2T%d" % l, x2T32.v)
                xT, xT32 = x2T, x2T32
    for l in range(DEPTH):
        P = LP[l]
        k.dma_out(pC[l].rearrange("h d v -> d h v"), P["Um"][:, :, 0:128])
        with k.scope():
            pt = k.ps()
            k.tr(pt[0:4, 0:128], P["Um"][:, :, 128], ident)
            nr = k.sb("pnr", [4, 128])
            k.cp(nr.v, pt[0:4, 0:128])
            k.dma_out(pn[l], nr.v)
        k.dma_out(pm[l], P["mcur"].v, slow=True)
        k.dma_out(pS[l].rearrange("h d v -> d h v"), P["S"].v)
        fm_to_rows(P["cg"].v, 3, 12, pgc[l])
        fm_to_rows(P["cf"].v, 2, 44, pfc[l])
    k.finish()
    return nc, tapouts


_CACHE = {}


def _prep_inputs(inp, c):
    f = lambda a: np.ascontiguousarray(np.asarray(a, dtype=np.float32))
    s0, s1 = NS * c, NS * (c + 1)
    m = {
        "xp": f(inp["x_prompt"][c]),
        "xs": f(inp["x_sample"][s0:s1]).reshape(64, D),
        "meta": f(inp["meta_tokens"]),
        "sC": f(inp["state_mlstm_C"][:, s0:s1]).reshape(DEPTH, NS * 4, 128, 128),
        "sn": f(inp["state_mlstm_n"][:, s0:s1]).reshape(DEPTH, NS * 4, 128),
        "sm": f(inp["state_mlstm_m"][:, s0:s1]),
        "sS": f(inp["state_gdn_S"][:, s0:s1]).reshape(DEPTH, NS * 4, 128, 128),
        "sgc": f(inp["state_gdn_conv"][:, s0:s1]).reshape(DEPTH, NS * 3, 1536),
        "sfc": f(inp["state_ffn_conv"][:, s0:s1]).reshape(DEPTH, NS * 2, 2 * DFF),
        "ln_emb_g": f(inp["ln_emb_g"]).reshape(1, D),
        "ln_emb_b": f(inp["ln_emb_b"]).reshape(1, D),
        "w_in": f(inp["w_in"]),
        "gate_bias": f(inp["mlstm_gate_bias"]).reshape(DEPTH, 8, 1),
        "mnorm_w": f(inp["mlstm_norm_w"]).reshape(DEPTH, 1, 512),
        "gconv_w": f(inp["gdn_conv_w"]),
        "A_log": f(inp["gdn_A_log"]).reshape(DEPTH, 4, 1),
        "dt_bias": f(inp["gdn_dt_bias"]).reshape(DEPTH, 4, 1),
        "gnorm_w": f(inp["gdn_norm_w"]).reshape(DEPTH, 1, 128),
        "w_pa": f(inp["w_branch_a"]),
        "w_pb": f(inp["w_branch_b"]),
        "w_out": f(inp["w_out"]),
        "ln1_g": f(inp["ln1_g"]).reshape(DEPTH, 1, D),
        "ln1_b": f(inp["ln1_b"]).reshape(DEPTH, 1, D),
        "w_up": f(inp["w_up"]),
        "fconv_w": f(inp["ffn_conv_w"]),
        "w_down": f(inp["w_down"]),
        "ln2_g": f(inp["ln2_g"]).reshape(DEPTH, 1, D),
        "ln2_b": f(inp["ln2_b"]).reshape(DEPTH, 1, D),
    }
    return m


def kernel(**inp):
    if "nc" not in _CACHE:
        _CACHE["nc"] = build()[0]
    nc = _CACHE["nc"]
    in_maps = [_prep_inputs(inp, c) for c in range(8)]
    res = run_bass_kernel_spmd(nc, in_maps, core_ids=list(range(8)))
    R = res.results
    st = lambda name: np.stack([np.asarray(R[c][name]) for c in range(8)], axis=1)
    cat = lambda name, shp: np.concatenate([np.asarray(R[c][name]).reshape(shp) for c in range(8)], axis=1)
    y_prompt = np.stack([np.asarray(R[c]["yp"]) for c in range(8)], axis=0)
    y_sample = np.concatenate([np.asarray(R[c]["ys"]).reshape(NS, 4, D) for c in range(8)], axis=0)
    outs = (
        y_prompt, y_sample,
        st("pC"), st("pn"), st("pm").reshape(DEPTH, 8, 4), st("pS"), st("pgc"), st("pfc"),
        cat("oC", (DEPTH, NS, 4, 128, 128)), cat("on", (DEPTH, NS, 4, 128)), cat("om", (DEPTH, NS, 4)),
        cat("oS", (DEPTH, NS, 4, 128, 128)), cat("ogc", (DEPTH, NS, 3, 1536)), cat("ofc", (DEPTH, NS, 2, 2 * DFF)),
    )
    return tuple(np.ascontiguousarray(o, dtype=np.float32) for o in outs)
```

```python
import math
from contextlib import ExitStack
import numpy as np
import concourse.bass as bass
import concourse.mybir as mybir
from concourse.bass_utils import run_bass_kernel_spmd

F32 = mybir.dt.float32
BF16 = mybir.dt.bfloat16
AF = mybir.ActivationFunctionType
ALU = mybir.AluOpType
AX = mybir.AxisListType

NEG = -30000.0
LN_EPS = 1e-5
NORM_EPS = 1e-6
ALPHA = 4.0 ** 0.25
DEPTH = 2
D = 1024
KC = 8
NIN = 6160
A_Q, A_K, A_V, A_O, A_I = 0, 512, 1024, 1536, 2048
B_QKV, B_Z, B_BETA, B_A, G_MERGE = 2056, 3592, 4104, 4108, 4112
DFF = 2816
NS = 16


class V:
    __slots__ = ("t", "ap")

    def __init__(self, t, ap):
        self.t = t
        self.ap = ap

    def __getitem__(self, idx):
        return V(self.t, self.ap[idx])

    def re(self, pat, **kw):
        return V(self.t, self.ap.rearrange(pat, **kw))

    def bc(self, shape):
        return V(self.t, self.ap.to_broadcast(list(shape)))

    def unsq(self, axis):
        return V(self.t, self.ap.unsqueeze(axis))


class T:
    __slots__ = ("ap", "lw", "rd", "dsem", "name", "a0", "a1")

    def __init__(self, name, ap, rd):
        self.name = name
        self.ap = ap
        self.lw = None
        self.rd = dict(rd)
        self.dsem = None

    def __getitem__(self, idx):
        return V(self, self.ap[idx])

    @property
    def v(self):
        return V(self, self.ap)


class KB:
    def __init__(self, nc):
        self.nc = nc
        self.root = ExitStack()
        self.stacks = [self.root]
        self.scope_tiles = [[]]
        self.engs = {"pe": nc.tensor, "act": nc.scalar, "dve": nc.vector, "pool": nc.gpsimd, "sp": nc.sync}
        self.semh = {}
        self.own = {}
        self.cnt = {}
        for e in ("pe", "act", "dve", "pool"):
            s = self.root.enter_context(nc.semaphore("s_" + e))
            self.semh["s_" + e] = s
            self.own[e] = "s_" + e
            self.cnt["s_" + e] = 0
        self.pending = {e: False for e in self.own}
        self.dmasems = set()
        self.free_dsems = []
        self.seen = {e: {} for e in self.engs}
        self.grave = {}
        self.grave_ranges = []
        self.conservative = True
        with nc.sbuf_tensor("probe0", [128, 8], F32) as h0:
            a_ = int(nc.lookup_mloc(h0).addr)
        fill = (512 - a_ % 512) % 512
        if fill:
            self.root.enter_context(nc.sbuf_tensor("fill0", [128, fill // 4], F32))
        self.nid = 0
        self.ps_tiles = []
        self.ps_i = 0
        self.wb_tiles = []
        self.wb_i = 0

    def sb(self, name, shape, persist=False, dt=F32):
        self.nid += 1
        nm = "%s_%d" % (name, self.nid)
        st = self.root if persist else self.stacks[-1]
        shape = list(shape)
        nfree = 1
        for d_ in shape[1:]:
            nfree *= d_
        esz = 2 if dt == BF16 else 4
        per = 512 // esz
        npad = ((nfree + per - 1) // per) * per
        h = st.enter_context(self.nc.sbuf_tensor(nm, [shape[0], npad], dt))
        ml = self.nc.lookup_mloc(h)
        a0 = int(ml.addr)
        a1 = a0 + int(ml.dims[1])
        inh = {}
        keep = []
        for (g0, g1, evs) in self.grave_ranges:
            if g0 < a1 and a0 < g1:
                for sn, v in evs.items():
                    if inh.get(sn, 0) < v:
                        inh[sn] = v
                if a0 <= g0 and g1 <= a1:
                    continue
            keep.append((g0, g1, evs))
        self.grave_ranges = keep
        if self.conservative:
            for sn, v in self.grave.items():
                if inh.get(sn, 0) < v:
                    inh[sn] = v
        ap = h[:, 0:nfree]
        if len(shape) == 3:
            ap = ap.rearrange("p (a b) -> p a b", a=shape[1])
        elif len(shape) == 4:
            ap = ap.rearrange("p (a b c) -> p a b c", a=shape[1], b=shape[2])
        t = T(nm, ap, inh)
        t.a0, t.a1 = a0, a1
        self.min_rem = min(getattr(self, "min_rem", 1 << 30), self.nc.sbuf_bytes_remaining)
        if not persist:
            self.scope_tiles[-1].append(t)
        return t

    def psum(self, name, shape):
        h = self.root.enter_context(self.nc.psum_tensor(name, list(shape), F32))
        return T(name, h[:], {})

    def scope(self):
        kb = self

        class _S:
            def __enter__(s):
                kb.stacks.append(ExitStack())
                kb.scope_tiles.append([])

            def __exit__(s, *a):
                kb._free_tiles(kb.scope_tiles.pop())
                kb.stacks.pop().close()
                return False

        return _S()

    def push(self):
        self.stacks.append(ExitStack())
        self.scope_tiles.append([])

    def _free_tiles(self, tiles):
        for t in tiles:
            evs = dict(t.rd)
            if t.lw is not None and evs.get(t.lw[0], 0) < t.lw[1]:
                evs[t.lw[0]] = t.lw[1]
            if t.dsem is not None:
                if evs.get(t.dsem, 0) < self.cnt[t.dsem]:
                    evs[t.dsem] = self.cnt[t.dsem]
                self.free_dsems.append(t.dsem)
                t.dsem = None
            if evs:
                self.grave_ranges.append((t.a0, t.a1, evs))
                for sn, v in evs.items():
                    if self.grave.get(sn, 0) < v:
                        self.grave[sn] = v

    def pop(self):
        self._free_tiles(self.scope_tiles.pop())
        self.stacks.pop().close()

    def _g(self, ev):
        s, v = ev
        if self.grave.get(s, 0) < v:
            self.grave[s] = v

    def ps(self):
        t = self.ps_tiles[self.ps_i % len(self.ps_tiles)]
        self.ps_i += 1
        return t

    def wbuf(self):
        t = self.wb_tiles[self.wb_i % len(self.wb_tiles)]
        self.wb_i += 1
        return t

    def op(self, eng, fn, reads=(), writes=(), sig=True, dma_tile=None):
        e = self.engs[eng]
        is_load = dma_tile is not None and (dma_tile in writes) and dma_tile.lw is not None and dma_tile.lw[0] == dma_tile.dsem and not dma_tile.rd
        deps = {}

        def add(ev):
            if ev is None:
                return
            s, v = ev
            if deps.get(s, 0) < v:
                deps[s] = v

        for t in reads:
            add(t.lw)
        for t in writes:
            add(t.lw)
            for s, v in t.rd.items():
                add((s, v))
        own = self.own.get(eng) if dma_tile is None else None
        seen = self.seen[eng]
        for s, v in deps.items():
            if eng == "pe" and s == own:
                continue
            if dma_tile is not None and s == dma_tile.dsem and is_load:
                continue
            if s in self.dmasems:
                v = self.cnt[s]
            if seen.get(s, 0) >= v:
                continue
            e.wait_ge(self.semh[s], v)
            seen[s] = v
        if dma_tile is not None and dma_tile.dsem is None:
            if self.free_dsems:
                nm = self.free_dsems.pop(0)
                if seen.get(nm, 0) < self.cnt[nm]:
                    e.wait_ge(self.semh[nm], self.cnt[nm])
                    seen[nm] = self.cnt[nm]
            else:
                nm = "d_%d" % len(self.dmasems)
                self.semh[nm] = self.root.enter_context(self.nc.semaphore(nm))
                self.cnt[nm] = 0
                self.dmasems.add(nm)
            dma_tile.dsem = nm
        inst = fn(e)
        if dma_tile is not None:
            s = dma_tile.dsem
            inst.then_inc(self.semh[s], 16)
            self.cnt[s] += 16
            ev = (s, self.cnt[s])
        else:
            if sig:
                inst.then_inc(self.semh[own], 1)
                self.cnt[own] += 1
                ev = (own, self.cnt[own])
                self.pending[eng] = False
            else:
                ev = (own, self.cnt[own] + 1)
                self.pending[eng] = True
        for t in writes:
            t.lw = ev
            t.rd = {}
        for t in reads:
            if t.rd.get(ev[0], 0) < ev[1]:
                t.rd[ev[0]] = ev[1]
        return inst

    def finish(self):
        assert not any(self.pending.values())
        sp = self.engs["sp"]
        for s in sorted(self.dmasems):
            if self.cnt[s] > self.seen["sp"].get(s, 0):
                sp.wait_ge(self.semh[s], self.cnt[s])
        for e in ("pe", "act", "dve", "pool"):
            s = self.own[e]
            if self.cnt[s] > 0:
                sp.wait_ge(self.semh[s], self.cnt[s])
        self.root.close()

    def mm(self, out, lhsT, rhs, start=True, stop=True, sig=None):
        self.op("pe", lambda e: e.matmul(out.ap, lhsT=lhsT.ap, rhs=rhs.ap, start=start, stop=stop),
                reads=[lhsT.t, rhs.t], writes=[out.t], sig=(stop if sig is None else sig))

    def tr(self, out, in_, ident):
        if isinstance(ident, T):
            ident = ident.v
        r = in_.ap.shape[0]
        self.op("pe", lambda e: e.transpose(out=out.ap, in_=in_.ap, identity=ident.ap[0:r, 0:r]),
                reads=[in_.t, ident.t], writes=[out.t])

    def act(self, out, in_, func, bias=None, scale=None, eng="act"):
        kw = {}
        rd = [in_.t]
        if bias is not None:
            if isinstance(bias, V):
                kw["bias"] = bias.ap
                rd.append(bias.t)
            else:
                kw["bias"] = bias
        if scale is not None:
            if isinstance(scale, V):
                kw["scale"] = scale.ap
                rd.append(scale.t)
            else:
                kw["scale"] = scale
        self.op("act", lambda e: e.activation(out=out.ap, in_=in_.ap, func=func, **kw), reads=rd, writes=[out.t])

    def ts(self, out, in0, s1, op0, s2=None, op1=None, eng="dve"):
        rd = [in0.t]
        a1 = s1
        a2 = s2
        if isinstance(s1, V):
            rd.append(s1.t)
            a1 = s1.ap
        if isinstance(s2, V):
            rd.append(s2.t)
            a2 = s2.ap
        if op1 is None:
            self.op(eng, lambda e: e.tensor_scalar(out=out.ap, in0=in0.ap, scalar1=a1, scalar2=None, op0=op0),
                    reads=rd, writes=[out.t])
        else:
            self.op(eng, lambda e: e.tensor_scalar(out=out.ap, in0=in0.ap, scalar1=a1, scalar2=a2, op0=op0, op1=op1),
                    reads=rd, writes=[out.t])

    def tt(self, out, in0, in1, op, eng="dve"):
        self.op(eng, lambda e: e.tensor_tensor(out=out.ap, in0=in0.ap, in1=in1.ap, op=op),
                reads=[in0.t, in1.t], writes=[out.t])

    def stt(self, out, in0, scalar, in1, op0, op1, eng="dve"):
        rd = [in0.t, in1.t]
        a = scalar
        if isinstance(scalar, V):
            rd.append(scalar.t)
            a = scalar.ap
        self.op(eng, lambda e: e.scalar_tensor_tensor(out=out.ap, in0=in0.ap, scalar=a, in1=in1.ap, op0=op0, op1=op1),
                reads=rd, writes=[out.t])

    def cp(self, out, in_, eng="dve"):
        if eng == "act":
            self.op("act", lambda e: e.copy(out=out.ap, in_=in_.ap), reads=[in_.t], writes=[out.t])
        else:
            self.op(eng, lambda e: e.tensor_copy(out=out.ap, in_=in_.ap), reads=[in_.t], writes=[out.t])

    def memset(self, out, val, eng="dve"):
        self.op(eng, lambda e: e.memset(out.ap, val), writes=[out.t])

    def red(self, out, in_, op, eng="dve"):
        self.op(eng, lambda e: e.tensor_reduce(out=out.ap, in_=in_.ap, axis=AX.X, op=op), reads=[in_.t], writes=[out.t])

    def scan(self, out, in_, op0=ALU.add):
        self.op("dve", lambda e: e.tensor_tensor_scan(out=out.ap, data0=in_.ap, data1=in_.ap, initial=0.0 if op0 == ALU.add else -3.0e38,
                                                      op0=op0, op1=ALU.bypass), reads=[in_.t], writes=[out.t])

    def dma_in(self, out, in_ap, q="sp", slow=False):
        kw = {"allow_slow_non_contiguous": True} if slow else {}
        self.op(q, lambda e: e.dma_start(out=out.ap, in_=in_ap, **kw), writes=[out.t], dma_tile=out.t)

    def dma_out(self, out_ap, in_, q="sp", slow=False):
        kw = {"allow_slow_non_contiguous": True} if slow else {}
        self.op(q, lambda e: e.dma_start(out=out_ap, in_=in_.ap, **kw), reads=[in_.t], dma_tile=in_.t)


def build(TT=256, taps=()):
    nc = bass.Bass("TRN2", target_bir_lowering=False)
    k = KB(nc)
    taps = set(taps)
    tapouts = {}

    def din(name, shape):
        return nc.dram_tensor(name, list(shape), F32, kind="ExternalInput").ap()

    def dout(name, shape):
        return nc.dram_tensor(name, list(shape), F32, kind="ExternalOutput").ap()

    xp = din("xp", [2048, D])
    xs = din("xs", [64, D])
    meta = din("meta", [16, D])
    sC = din("sC", [DEPTH, NS * 4, 128, 128])
    sn = din("sn", [DEPTH, NS * 4, 128])
    sm = din("sm", [DEPTH, NS, 4])
    sS = din("sS", [DEPTH, NS * 4, 128, 128])
    sgc = din("sgc", [DEPTH, NS * 3, 1536])
    sfc = din("sfc", [DEPTH, NS * 2, 2 * DFF])
    ln_emb_g = din("ln_emb_g", [1, D])
    ln_emb_b = din("ln_emb_b", [1, D])
    w_in = din("w_in", [DEPTH, D, NIN])
    gate_bias = din("gate_bias", [DEPTH, 8, 1])
    mnorm_w = din("mnorm_w", [DEPTH, 1, 512])
    gconv_w = din("gconv_w", [DEPTH, 4, 1536])
    A_log = din("A_log", [DEPTH, 4, 1])
    dt_bias = din("dt_bias", [DEPTH, 4, 1])
    gnorm_w = din("gnorm_w", [DEPTH, 1, 128])
    w_pa = din("w_pa", [DEPTH, 512, D])
    w_pb = din("w_pb", [DEPTH, 512, D])
    w_out = din("w_out", [DEPTH, D, D])
    ln1_g = din("ln1_g", [DEPTH, 1, D])
    ln1_b = din("ln1_b", [DEPTH, 1, D])
    w_up = din("w_up", [DEPTH, D, 2 * DFF])
    fconv_w = din("fconv_w", [DEPTH, 3, 2 * DFF])
    w_down = din("w_down", [DEPTH, DFF, D])
    ln2_g = din("ln2_g", [DEPTH, 1, D])
    ln2_b = din("ln2_b", [DEPTH, 1, D])

    yp = dout("yp", [2048, D])
    ys = dout("ys", [64, D])
    pC = dout("pC", [DEPTH, 4, 128, 128])
    pn = dout("pn", [DEPTH, 4, 128])
    pm = dout("pm", [DEPTH, 4, 1])
    pS = dout("pS", [DEPTH, 4, 128, 128])
    pgc = dout("pgc", [DEPTH, 3, 1536])
    pfc = dout("pfc", [DEPTH, 2, 2 * DFF])
    oC = dout("oC", [DEPTH, NS * 4, 128, 128])
    on = dout("on", [DEPTH, NS * 4, 128])
    om = dout("om", [DEPTH, NS, 4])
    oS = dout("oS", [DEPTH, NS * 4, 128, 128])
    ogc = dout("ogc", [DEPTH, NS * 3, 1536])
    ofc = dout("ofc", [DEPTH, NS * 2, 2 * DFF])

    def tap(name, view):
        if name not in taps:
            return
        shp = list(view.ap.shape)
        o = nc.dram_tensor("tap_" + name, shp, view.ap.dtype, kind="ExternalOutput").ap()
        tapouts[name] = shp
        k.dma_out(o, view, q="sp", slow=True)

    k.ps_tiles = [k.psum("ps%d" % i, [128, 512]) for i in range(6)]
    psd = k.psum("psd", [128, 1024])
    k.wb_tiles = [k.sb("wb%d" % i, [128, 8, 512], persist=True, dt=BF16) for i in range(4)]

    ident = k.sb("ident", [128, 128], persist=True)
    k.memset(ident.v, 0.0, eng="pool")
    k.op("pool", lambda e: e.affine_select(out=ident.ap, in_=ident.ap, pattern=[[-1, 128]], compare_op=ALU.not_equal,
                                           fill=1.0, base=0, channel_multiplier=1), reads=[ident], writes=[ident])
    aident = k.sb("aident", [128, 128], persist=True)
    k.ts(aident.v, ident.v, ALPHA, ALU.mult)
    ones = k.sb("ones", [128, 128], persist=True)
    k.memset(ones.v, 1.0)
    cst = k.sb("cst", [128, 8], persist=True)
    k.memset(cst[:, 4:5], LN_EPS)
    k.memset(cst[:, 0:1], 1.0)
    k.memset(cst[:, 1:2], NORM_EPS)
    k.memset(cst[:, 2:3], math.log(128.0 ** -0.5))
    k.memset(cst[:, 3:4], 0.0)

    def aff(t, pattern, op, fill, base, cm):
        k.op("pool", lambda e: e.affine_select(out=t.ap, in_=t.ap, pattern=pattern, compare_op=op, fill=fill,
                                               base=base, channel_multiplier=cm), reads=[t], writes=[t])

    m01T = k.sb("m01T", [128, 128], persist=True)
    k.memset(m01T.v, 1.0, eng="pool")
    aff(m01T, [[1, 128]], ALU.is_ge, 0.0, 0, -1)
    mn_st = k.sb("mn_st", [128, 128], persist=True)
    k.memset(mn_st.v, 0.0, eng="pool")
    aff(mn_st, [[-1, 128]], ALU.is_gt, NEG, 0, 1)
    mn_stT = k.sb("mn_stT", [128, 128], persist=True)
    k.memset(mn_stT.v, 0.0, eng="pool")
    aff(mn_stT, [[1, 128]], ALU.is_gt, NEG, 0, -1)
    mn_inT = k.sb("mn_inT", [128, 128], persist=True)
    k.memset(mn_inT.v, 0.0, eng="pool")
    aff(mn_inT, [[1, 128]], ALU.is_ge, NEG, 0, -1)
    indT = k.sb("indT", [16, 64], persist=True)
    k.memset(indT.v, 1.0, eng="pool")
    aff(indT, [[1, 64]], ALU.is_ge, 0.0, 0, -4)
    aff(indT, [[-1, 64]], ALU.is_ge, 0.0, 3, 4)
    ind = k.sb("ind", [64, 16], persist=True)
    k.memset(ind.v, 1.0, eng="pool")
    aff(ind, [[-4, 16]], ALU.is_ge, 0.0, 0, 1)
    aff(ind, [[4, 16]], ALU.is_ge, 0.0, 3, -1)
    same = k.sb("same", [64, 64], persist=True)
    p0 = k.ps()
    k.mm(p0[0:64, 0:64], indT.v, indT.v)
    k.cp(same.v, p0[0:64, 0:64])
    sneg = k.sb("sneg", [64, 64], persist=True)
    k.ts(sneg.v, same.v, -NEG, ALU.mult, NEG, ALU.add)
    b01T = k.sb("b01T", [64, 64], persist=True)
    k.tt(b01T.v, m01T[0:64, 0:64], same.v, ALU.mult)
    bn_st = k.sb("bn_st", [64, 64], persist=True)
    k.tt(bn_st.v, mn_st[0:64, 0:64], sneg.v, ALU.min)
    bn_stT = k.sb("bn_stT", [64, 64], persist=True)
    k.tt(bn_stT.v, mn_stT[0:64, 0:64], sneg.v, ALU.min)
    bn_inT = k.sb("bn_inT", [64, 64], persist=True)
    k.tt(bn_inT.v, mn_inT[0:64, 0:64], sneg.v, ALU.min)
    MASKS = {"c": (m01T, mn_st, mn_stT, mn_inT), "b": (b01T, bn_st, bn_stT, bn_inT)}
    oh = k.sb("oh", [4, 4, 128], persist=True)
    k.memset(oh.v, 0.0, eng="pool")
    aff(oh, [[-1, 4], [0, 128]], ALU.not_equal, 1.0, 0, 1)

    def load_fm(name, src2d, R, C):
        dst = k.sb(name, [128, C, R], persist=True)
        with k.scope():
            tmp = k.sb("ldfm", [R, C * 128])
            k.dma_in(tmp.v, src2d)
            c0 = 0
            while c0 < C:
                n = min(C - c0, 512 // R)
                p = k.ps()
                for c in range(n):
                    k.tr(p[:, c * R:(c + 1) * R], tmp[:, (c0 + c) * 128:(c0 + c + 1) * 128], ident)
                k.cp(dst[:, c0:c0 + n, :], p[:, 0:n * R].re("p (c r) -> p c r", r=R))
                c0 += n
        return dst

    def load_bc(name, src_row, n):
        dst = k.sb(name, [128, n], persist=True)
        k.dma_in(dst.v, src_row.to_broadcast([128, n]), slow=True)
        return dst

    g_emb = load_fm("g_emb", ln_emb_g, 1, 8)
    b_emb = load_fm("b_emb", ln_emb_b, 1, 8)
    LP = []
    for l in range(DEPTH):
        P = {}
        P["g1"] = load_fm("g1", ln1_g[l], 1, 8)
        P["b1"] = load_fm("b1", ln1_b[l], 1, 8)
        P["g2"] = load_fm("g2", ln2_g[l], 1, 8)
        P["b2"] = load_fm("b2", ln2_b[l], 1, 8)
        P["cwg"] = load_fm("cwg", gconv_w[l], 4, 12)
        P["cwf"] = load_fm("cwf", fconv_w[l], 3, 44)
        P["anw"] = load_bc("anw", mnorm_w[l], 512)
        P["gnw"] = load_bc("gnw", gnorm_w[l], 128)
        gbi = k.sb("gbi", [4, 1], persist=True)
        gbf = k.sb("gbf", [4, 1], persist=True)
        gal = k.sb("gal", [4, 1], persist=True)
        gdt = k.sb("gdt", [4, 1], persist=True)
        k.dma_in(gbi.v, gate_bias[l][0:4, :], slow=True)
        k.dma_in(gbf.v, gate_bias[l][4:8, :], slow=True)
        k.dma_in(gal.v, A_log[l], slow=True)
        k.dma_in(gdt.v, dt_bias[l], slow=True)
        k.ts(gbf.v, gbf.v, -1.0, ALU.mult)
        k.act(gal.v, gal.v, AF.Exp)
        k.ts(gal.v, gal.v, -1.0, ALU.mult)
        P["gb"] = (gbi, gbf, gal, gdt)
        Um = k.sb("Um", [128, 4, 129], persist=True)
        k.memset(Um.v, 0.0)
        mcur = k.sb("mcur", [4, 1], persist=True)
        k.memset(mcur.v, 0.0)
        S = k.sb("S", [128, 4, 128], persist=True)
        k.memset(S.v, 0.0)
        cg = k.sb("cg", [128, 12, 3], persist=True)
        k.memset(cg.v, 0.0)
        cf = k.sb("cf", [128, 44, 2], persist=True)
        k.memset(cf.v, 0.0)
        P.update(Um=Um, mcur=mcur, S=S, cg=cg, cf=cf)
        LP.append(P)
    g2bc = load_bc("g2bc", ln2_g[DEPTH - 1], D)
    b2bc = load_bc("b2bc", ln2_b[DEPTH - 1], D)

    tiles = [dict(TT=80, first=True, chunks=[dict(c0=0, c=16, kind="meta"), dict(c0=16, c=64, kind="samp")],
                  groups=[(0, 16)] + [(16 + 4 * i, 4) for i in range(16)], tgs=[(0, 80)], tok0=0,
                  pgroups=[(0, 16, [(0, 0, 16)]), (16, 64, [(1, 0, 64)])])]
    for i in range(2048 // TT):
        tiles.append(dict(TT=TT, first=False, chunks=[dict(c0=128 * j, c=128, kind="p", tok0=i * TT + 128 * j) for j in range(TT // 128)],
                          groups=[(128 * j, 128) for j in range(TT // 128)], tgs=[(128 * g, 128) for g in range(TT // 128)], tok0=i * TT,
                          pgroups=[(128 * g, 128, [(g, 0, 128)]) for g in range(TT // 128)]))

    scratch = {}
    cur = {"ti": 0, "n": 0}

    def wload_parts(parts, ncols):
        wb = k.wbuf()
        ktot = sum(kc for _, kc in parts)
        key = cur["n"]
        cur["n"] += 1
        if cur["ti"] == 0:
            k0 = 0
            for (src, kc) in parts:
                k.dma_in(wb[:, k0:k0 + kc, 0:ncols], src.rearrange("(k p) n -> p k n", p=128), q="pool")
                k0 += kc
            scr = nc.dram_tensor("wscr_%d" % key, [128, ktot, ncols], BF16).ap()
            st = T("scr%d" % key, None, {})
            scratch[key] = (scr, st)
            k.op("sp", lambda e: e.dma_start(out=scr, in_=wb.ap[:, 0:ktot, 0:ncols]), reads=[wb], writes=[st], dma_tile=wb)
        else:
            scr, st = scratch[key]
            k.op("sp", lambda e: e.dma_start(out=wb.ap[:, 0:ktot, 0:ncols], in_=scr), reads=[st], writes=[wb], dma_tile=wb)
        return wb

    def wload(src2d, kc, ncols):
        return wload_parts([(src2d, kc)], ncols)

    def proj_fm(wb, cb, xT, n, nk=8, k0=0):
        p = k.ps()
        for kk in range(nk):
            k.mm(p[:, 0:n], wb[:, k0 + kk, cb:cb + 128], xT[:, kk, 0:n], start=(kk == 0), stop=(kk == nk - 1))
        return p

    def proj_tm(wb, ncols, xT, c0, c):
        p = k.ps()
        for kk in range(8):
            k.mm(p[0:c, 0:ncols], xT[:, kk, c0:c0 + c], wb[:, kk, 0:ncols], start=(kk == 0), stop=(kk == 7))
        return p

    def rsqrt_small(out, in_, eps):
        n = in_.ap.shape[0]
        col = 1 if eps == NORM_EPS else 4
        k.act(out, in_, AF.Ln, bias=cst[0:n, col:col + 1])
        k.act(out, out, AF.Exp, scale=-0.5)

    def ln_chunk(src, c, g_fm, b_fm, dstT, c0, final_dst=None, dstB=None):
        with k.scope():
            st = k.sb("lnst", [64, 12])
            mv = k.sb("lnmv", [64, 2])
            rs = k.sb("lnrs", [64, 1])
            xn = k.sb("lnxn", [64, 1024])
            k.op("dve", lambda e: e.bn_stats(out=st.ap[0:c, 0:6], in_=src.ap[:, 0:512]), reads=[src.t], writes=[st])
            k.op("dve", lambda e: e.bn_stats(out=st.ap[0:c, 6:12], in_=src.ap[:, 512:1024]), reads=[src.t], writes=[st])
            k.op("dve", lambda e: e.bn_aggr(out=mv.ap[0:c, :], in_=st.ap[0:c, :]), reads=[st], writes=[mv])
            rsqrt_small(rs[0:c, :], mv[0:c, 1:2], LN_EPS)
            k.ts(xn[0:c, :], src, mv[0:c, 0:1], ALU.subtract, rs[0:c, 0:1], ALU.mult)
            if final_dst is not None:
                k.tt(xn[0:c, :], xn[0:c, :], g2bc[0:c, :], ALU.mult)
                k.tt(xn[0:c, :], xn[0:c, :], b2bc[0:c, :], ALU.add)
                k.dma_out(final_dst, xn[0:c, :])
            else:
                p = k.ps()
                for cc in range(8):
                    k.tr(p[:, cc * c:(cc + 1) * c], xn[0:c, cc * 128:(cc + 1) * 128], ident)
                for cc in range(8):
                    if cc % 2 == 0:
                        k.act(dstT[:, cc, c0:c0 + c], p[:, cc * c:(cc + 1) * c], AF.Identity,
                              bias=b_fm[:, cc, 0:1], scale=g_fm[:, cc, 0:1])
                    else:
                        k.ts(dstT[:, cc, c0:c0 + c], p[:, cc * c:(cc + 1) * c], g_fm[:, cc, 0:1], ALU.mult,
                             b_fm[:, cc, 0:1], ALU.add)
                k.cp(dstB[:, :, c0:c0 + c], dstT[:, :, c0:c0 + c])

    def fm_to_rows(src3, R, C, dram2d):
        with k.scope():
            ob = k.sb("f2r", [R, C * 128])
            c0 = 0
            while c0 < C:
                n = min(4, C - c0)
                p = k.ps()
                for c in range(n):
                    k.tr(p[0:R, c * 128:(c + 1) * 128], src3[:, c0 + c, :], ident)
                k.cp(ob[:, c0 * 128:(c0 + n) * 128], p[0:R, 0:n * 128])
                c0 += n
            k.dma_out(dram2d, ob.v)

    def bcast_rows(row4, n, name):
        R = k.sb("bcR", [4, n, 4])
        k.tt(R.v, row4.unsq(2).bc([4, n, 4]), ident[0:4, 0:4].unsq(1).bc([4, n, 4]), ALU.mult)
        p = k.ps()
        k.mm(p[:, 0:n * 4], ones[0:4, :], R.v.re("p a b -> p (a b)"))
        dst = k.sb(name, [128, n, 4])
        k.cp(dst.v, p[:, 0:n * 4].re("p (a b) -> p a b", b=4))
        return dst

    class RPool:
        def __init__(self, name, shape, n, dt=F32):
            self.tiles = [k.sb(name, shape, dt=dt) for _ in range(n)]
            self.i = 0

        def get(self):
            t = self.tiles[self.i % len(self.tiles)]
            self.i += 1
            return t

    def interleave(gens):
        gens = list(gens)
        while gens:
            nxt = []
            for g in gens:
                try:
                    next(g)
                    nxt.append(g)
                except StopIteration:
                    pass
            gens = nxt

    def rows_to_tm(rows, c0, c, name):
        p = k.ps()
        for i, r in enumerate(rows):
            k.tr(p[0:c, 4 * i:4 * i + 4], r[:, c0:c0 + c], ident)
        dst = k.sb(name, [128, 4 * len(rows)])
        k.cp(dst[0:c, :], p[0:c, 0:4 * len(rows)])
        return dst

    def mlstm_phase(l, tl, xT, hAT):
        P = LP[l]
        TTc = tl["TT"]
        chunks = tl["chunks"]
        CP = max(ch["c"] for ch in chunks)
        Um, mcur = P["Um"], P["mcur"]
        gbi, gbf, gal, gdt = P["gb"]
        qT = k.sb("qT", [128, 4, TTc])
        kT = k.sb("kT", [128, 4, TTc])
        wb = wload(w_in[l][:, A_Q:A_Q + 512], 8, 512)
        for h in range(4):
            p = proj_fm(wb, h * 128, xT, TTc)
            k.act(qT[:, h, :], p[:, 0:TTc], AF.Identity, scale=128.0 ** -0.5)
        wb = wload(w_in[l][:, A_K:A_K + 512], 8, 512)
        for h in range(4):
            p = proj_fm(wb, h * 128, xT, TTc)
            k.cp(kT[:, h, :], p[:, 0:TTc])
        kc = [k.sb("kc", [CP, 4, 128]) for _ in chunks]
        va = [k.sb("va", [CP, 4, 129]) for _ in chunks]
        ow = [k.sb("ow", [CP, 512]) for _ in chunks]
        for (g0, gn, mem) in tl["pgroups"]:
            p = proj_tm(wb, 512, xT, g0, gn)
            for (ci, r0, c) in mem:
                k.cp(kc[ci][0:c], p[r0:r0 + c, :].re("p (h d) -> p h d", h=4), eng="act")
        wb = wload(w_in[l][:, A_V:A_V + 512], 8, 512)
        for (g0, gn, mem) in tl["pgroups"]:
            p = proj_tm(wb, 512, xT, g0, gn)
            for (ci, r0, c) in mem:
                k.cp(va[ci][0:c, :, 0:128], p[r0:r0 + c, :].re("p (h d) -> p h d", h=4))
                k.memset(va[ci][0:c, :, 128:129], 1.0)
        wb = wload(w_in[l][:, A_O:A_O + 512], 8, 512)
        for (g0, gn, mem) in tl["pgroups"]:
            p = proj_tm(wb, 512, xT, g0, gn)
            for (ci, r0, c) in mem:
                k.act(ow[ci][0:c, :], p[r0:r0 + c, :], AF.Sigmoid)
                k.tt(ow[ci][0:c, :], ow[ci][0:c, :], P["anw"][0:c, :], ALU.mult)
        wgt = wload(w_in[l][:, A_I - 248:A_I + 8], 8, 256)
        pi = k.ps()
        for kk in range(8):
            k.mm(pi[0:4, 0:TTc], wgt[:, kk, 248:252], xT[:, kk, 0:TTc], start=(kk == 0), stop=(kk == 7))
        pf = k.ps()
        for kk in range(8):
            k.mm(pf[0:4, 0:TTc], wgt[:, kk, 252:256], xT[:, kk, 0:TTc], start=(kk == 0), stop=(kk == 7))
        igr = k.sb("igr", [4, TTc])
        lf = k.sb("lf", [4, TTc])
        b = k.sb("b", [4, TTc])
        a = k.sb("a", [4, TTc])
        Er = k.sb("Er", [4, TTc])
        Th = k.sb("Th", [4, TTc])
        k.act(igr.v, pi[0:4, 0:TTc], AF.Identity, bias=gbi[:, 0:1])
        k.act(lf.v, pf[0:4, 0:TTc], AF.Exp, bias=gbf[:, 0:1], scale=-1.0)
        k.act(lf.v, lf.v, AF.Ln, bias=cst[0:4, 0:1])
        k.ts(lf.v, lf.v, -1.0, ALU.mult)
        for (g0, gl) in tl["groups"]:
            k.scan(b[:, g0:g0 + gl], lf[:, g0:g0 + gl])
        k.tt(a.v, igr.v, b.v, ALU.subtract)
        nb = k.sb("nb", [4, TTc])
        k.ts(nb.v, b.v, -1.0, ALU.mult)
        nch = len(chunks)
        rr = k.sb("rr", [4, nch + 16])
        nrr = k.sb("nrr", [4, nch + 16])
        mprev = k.sb("mprev", [4, nch + 16])
        am = k.sb("am", [4, nch + 16])
        wrow = k.sb("wrow", [4, nch + 16])
        for j, ch in enumerate(chunks):
            c0, c = ch["c0"], ch["c"]
            if ch["kind"] == "samp":
                av = a[:, c0:c0 + 64].re("p (s j) -> p s j", j=4)
                bv = b[:, c0:c0 + 64].re("p (s j) -> p s j", j=4)
                k.red(am[:, j:j + 16], av, ALU.max)
                m0t = k.sb("m0t", [16, 4])
                k.dma_in(m0t.v, sm[l])
                pm0 = k.ps()
                k.tr(pm0[0:4, 0:16], m0t.v, ident)
                k.cp(mprev[:, j:j + 16], pm0[0:4, 0:16])
                k.tt(rr[:, j:j + 16], mprev[:, j:j + 16], am[:, j:j + 16], ALU.max)
                mnew = k.sb("mnew", [4, 16])
                k.tt(mnew.v, rr[:, j:j + 16], bv[:, :, 3], ALU.add)
                mex = k.sb("mex", [4, 16, 4])
                k.tt(mex.v, mnew.v.unsq(2).bc([4, 16, 4]), ident[0:4, 0:4].unsq(1).bc([4, 16, 4]), ALU.mult)
                pmo = k.ps()
                k.mm(pmo[0:1, 0:64], ones[0:4, 0:1], mex.v.re("p a b -> p (a b)"))
                mrow = k.sb("mrow", [1, 64])
                k.cp(mrow.v, pmo[0:1, 0:64])
                k.dma_out(om[l].rearrange("s h -> (s h)").rearrange("(o n) -> o n", o=1), mrow.v)
                k.ts(nrr[:, j:j + 16], rr[:, j:j + 16], -1.0, ALU.mult)
                k.tt(Er[:, c0:c0 + 64].re("p (s j) -> p s j", j=4), av, rr[:, j:j + 16].unsq(2).bc([4, 16, 4]), ALU.subtract)
                k.tt(Th[:, c0:c0 + 64].re("p (s j) -> p s j", j=4), nb[:, c0:c0 + 64].re("p (s j) -> p s j", j=4),
                     rr[:, j:j + 16].unsq(2).bc([4, 16, 4]), ALU.subtract)
                k.act(Er[:, c0:c0 + 64], Er[:, c0:c0 + 64], AF.Exp)
                k.act(Th[:, c0:c0 + 64], Th[:, c0:c0 + 64], AF.Exp)
                k.tt(wrow[:, j:j + 16], mprev[:, j:j + 16], rr[:, j:j + 16], ALU.subtract)
            else:
                k.red(am[:, j:j + 1], a[:, c0:c0 + c], ALU.max)
                k.cp(mprev[:, j:j + 1], mcur.v)
                k.tt(rr[:, j:j + 1], mcur.v, am[:, j:j + 1], ALU.max)
                k.tt(mcur.v, rr[:, j:j + 1], b[:, c0 + c - 1:c0 + c], ALU.add)
                k.ts(nrr[:, j:j + 1], rr[:, j:j + 1], -1.0, ALU.mult)
                k.act(Er[:, c0:c0 + c], a[:, c0:c0 + c], AF.Exp, bias=nrr[:, j:j + 1])
                k.act(Th[:, c0:c0 + c], nb[:, c0:c0 + c], AF.Exp, bias=nrr[:, j:j + 1])
                k.tt(wrow[:, j:j + 1], mprev[:, j:j + 1], rr[:, j:j + 1], ALU.subtract)
        nw = nch + 15 if chunks[-1]["kind"] == "samp" else nch
        k.act(wrow[:, 0:nw], wrow[:, 0:nw], AF.Exp)
        wbc = bcast_rows(wrow[:, 0:nw], nw, "wbc")
        tap("Er%d" % l, Er.v)
        tap("Th%d" % l, Th.v)
        for j, ch in enumerate(chunks):
            c0, c, kind = ch["c0"], ch["c"], ch["kind"]
            m01 = MASKS["b" if kind == "samp" else "c"][0]
            with k.scope():
                sc = rows_to_tm([Er.v, Th.v], c0, c, "scm")
                vE = k.sb("vE", [CP, 4, 129])
                k.tt(vE[0:c], va[j][0:c], sc[0:c, 0:4].unsq(2).bc([c, 4, 129]), ALU.mult)
                if kind != "samp":
                    offs = [(h // 2) * 512 + (h % 2) * 129 for h in range(4)]
                    psts = []
                    for h in range(4):
                        pst = k.ps()
                        k.mm(pst[0:c, 0:c], kT[:, h, c0:c0 + c], qT[:, h, c0:c0 + c])
                        psts.append(pst)
                    STl, Chl = [], []
                    for h in range(4):
                        STs = k.sb("STs", [CP, CP])
                        k.stt(STs[0:c, 0:c], psts[h][0:c, 0:c], sc[0:c, h:h + 1], m01[0:c, 0:c], ALU.mult, ALU.mult)
                        STl.append(STs)
                        Ch = k.sb("Ch", [128, 129])
                        k.ts(Ch.v, Um[:, h, :], wbc[:, j, h:h + 1], ALU.mult)
                        Chl.append(Ch)
                    for h in range(4):
                        k.mm(psd[0:c, offs[h]:offs[h] + 129], STl[h][0:c, 0:c], va[j][0:c, h, :], start=True, stop=False)
                        k.mm(psd[0:c, offs[h]:offs[h] + 129], qT[:, h, c0:c0 + c], Chl[h].v, start=False, stop=True)
                    pps = []
                    for h in range(4):
                        pp = k.ps()
                        k.mm(pp[:, 0:129], kc[j][0:c, h, :], vE[0:c, h, :])
                        pps.append(pp)
                    for h in range(4):
                        k.tt(Um[:, h, :], Chl[h].v, pps[h][:, 0:129], ALU.add)
                for h in (range(4) if kind == "samp" else []):
                    k.push()
                    off = (h // 2) * 512 + (h % 2) * 129
                    pst = k.ps()
                    k.mm(pst[0:c, 0:c], kT[:, h, c0:c0 + c], qT[:, h, c0:c0 + c])
                    STs = k.sb("STs", [CP, CP])
                    k.stt(STs[0:c, 0:c], pst[0:c, 0:c], sc[0:c, h:h + 1], m01[0:c, 0:c], ALU.mult, ALU.mult)
                    if kind != "samp":
                        pass
                    else:
                        qTm = k.sb("qTm", [128, 16, 64])
                        k.memset(qTm.v, 0.0)
                        for i in range(16):
                            k.cp(qTm[:, i, 4 * i:4 * i + 4], qT[:, h, c0 + 4 * i:c0 + 4 * i + 4])
                        k.mm(psd[0:c, off:off + 129], STs[0:c, 0:c], va[j][0:c, h, :], start=True, stop=False)
                        for g in range(4):
                            with k.scope():
                                Cg = k.sb("Cg", [128, 4, 129])
                                Cst = k.sb("Cst", [128, 4, 128])
                                k.dma_in(Cst.v, sC[l].rearrange("(s h) d v -> h d s v", h=4)[h][:, 4 * g:4 * g + 4, :])
                                k.cp(Cg[:, :, 0:128], Cst.v, eng="act")
                                nrow = k.sb("nrow", [4, 128])
                                k.dma_in(nrow.v, sn[l].rearrange("(s h) d -> h s d", h=4)[h][4 * g:4 * g + 4, :])
                                pn_ = k.ps()
                                k.tr(pn_[:, 0:4], nrow.v, ident)
                                k.cp(Cg[:, :, 128], pn_[:, 0:4])
                                wv = wbc[:, j + 4 * g:j + 4 * g + 4, h:h + 1]
                                k.tt(Cg.v, Cg.v, wv.bc([128, 4, 129]), ALU.mult)
                                for ii in range(4):
                                    i = 4 * g + ii
                                    k.mm(psd[0:c, off:off + 129], qTm[:, i, :], Cg[:, ii, :], start=False,
                                         stop=(i == 15))
                                vEx = k.sb("vEx", [64, 4, 129])
                                k.tt(vEx.v, vE[0:64, h, :].unsq(1).bc([64, 4, 129]),
                                     ind[:, 4 * g:4 * g + 4].unsq(2).bc([64, 4, 129]), ALU.mult)
                                vf = vEx.v.re("p a b -> p (a b)")
                                pu = k.ps()
                                pu2 = k.ps()
                                k.mm(pu[:, 0:512], kc[j][0:64, h, :], vf[:, 0:512])
                                k.mm(pu2[:, 0:4], kc[j][0:64, h, :], vf[:, 512:516])
                                Cf = Cg.v.re("p a b -> p (a b)")
                                k.tt(Cf[:, 0:512], Cf[:, 0:512], pu[:, 0:512], ALU.add)
                                k.tt(Cf[:, 512:516], Cf[:, 512:516], pu2[:, 0:4], ALU.add)
                                k.dma_out(oC[l].rearrange("(s h) d v -> h d s v", h=4)[h][:, 4 * g:4 * g + 4, :], Cg[:, :, 0:128])
                                pn2 = k.ps()
                                k.tr(pn2[0:4, 0:128], Cg[:, :, 128], ident)
                                nout = k.sb("nout", [4, 128])
                                k.cp(nout.v, pn2[0:4, 0:128])
                                k.dma_out(on[l].rearrange("(s h) d -> h s d", h=4)[h][4 * g:4 * g + 4, :], nout.v)
                    k.pop()
                numv = psd[0:c, :].re("p (a r) -> p a r", a=2)[:, :, 0:258].re("p a (h q) -> p a h q", h=2)
                den = k.sb("den", [CP, 2, 2])
                k.cp(den[0:c], numv[:, :, :, 128])
                dn = k.sb("dn", [CP, 4])
                denf = den[0:c].re("p a b -> p (a b)")
                k.ts(dn[0:c], denf, -1.0, ALU.mult)
                k.tt(dn[0:c], dn[0:c], denf, ALU.max)
                k.tt(dn[0:c], dn[0:c], sc[0:c, 4:8], ALU.max)
                rden = k.sb("rden", [CP, 4])
                k.op("dve", lambda e: e.reciprocal(out=rden.ap[0:c], in_=dn.ap[0:c]), reads=[dn], writes=[rden])
                st = k.sb("hst", [CP, 4, 6])
                mv = k.sb("hmv", [CP, 4, 2])
                for h in range(4):
                    off = (h // 2) * 512 + (h % 2) * 129
                    k.op("dve", lambda e, h=h, off=off: e.bn_stats(out=st.ap[0:c, h, :], in_=psd.ap[0:c, off:off + 128]),
                         reads=[psd], writes=[st])
                for h in range(4):
                    k.op("dve", lambda e, h=h: e.bn_aggr(out=mv.ap[0:c, h, :], in_=st.ap[0:c, h, :]), reads=[st], writes=[mv])
                t1 = k.sb("t1", [CP, 4])
                k.tt(t1[0:c], rden[0:c], rden[0:c], ALU.mult)
                k.tt(t1[0:c], t1[0:c], mv[0:c, :, 1], ALU.mult)
                rsq = k.sb("rsq", [CP, 4])
                rsqrt_small(rsq[0:c], t1[0:c], NORM_EPS)
                k.tt(rsq[0:c], rsq[0:c], rden[0:c], ALU.mult)
                hA = k.sb("hA", [CP, 512])
                for h in range(4):
                    off = (h // 2) * 512 + (h % 2) * 129
                    k.ts(hA[0:c, h * 128:(h + 1) * 128], psd[0:c, off:off + 128], mv[0:c, h, 0:1], ALU.subtract,
                         rsq[0:c, h:h + 1], ALU.mult)
                k.tt(hA[0:c], hA[0:c], ow[j][0:c], ALU.mult)
                pt = k.ps()
                for h in range(4):
                    k.tr(pt[:, h * c:(h + 1) * c], hA[0:c, h * 128:(h + 1) * 128], ident)
                k.cp(hAT[:, :, c0:c0 + c], pt[:, 0:4 * c].re("p (h t) -> p h t", h=4), eng="act")

    def gdn_phase(l, tl, xT, hBT):
        P = LP[l]
        TTc = tl["TT"]
        chunks = tl["chunks"]
        CP = max(ch["c"] for ch in chunks)
        S, cg, cwg = P["S"], P["cg"], P["cwg"]
        gbi, gbf, gal, gdt = P["gb"]
        qkvT = k.sb("qkvT", [128, 12, TTc])
        k.push()
        pools = None if tl["first"] else (RPool("cext", [128, TTc + 3], 3), RPool("cacc", [128, TTc], 3))
        sqp = RPool("csq", [128, TTc], 2)
        for blk in range(3):
            wb = wload(w_in[l][:, B_QKV + 512 * blk:B_QKV + 512 * (blk + 1)], 8, 512)
            for q4 in range(4):
                cc = blk * 4 + q4
                p = proj_fm(wb, q4 * 128, xT, TTc)
                with k.scope():
                    acc = conv_fm(p, cc, 4, cg, cwg, sgc[l], ogc[l], tl, "c", pools)
                    k.act(qkvT[:, cc, :], acc.v, AF.Silu)
                    if cc < 8:
                        sq = sqp.get()
                        k.tt(sq.v, qkvT[:, cc, :], qkvT[:, cc, :], ALU.mult, eng=("dve" if tl["first"] else "pool"))
                        pq = k.ps()
                        k.mm(pq[:, 0:TTc], ones.v, sq.v)
                        k.act(sq.v, pq[:, 0:TTc], AF.Ln, bias=cst[:, 1:2])
                        k.act(sq.v, sq.v, AF.Exp, scale=-0.5, bias=(cst[:, 2:3] if cc < 4 else cst[:, 3:4]))
                        k.tt(qkvT[:, cc, :], qkvT[:, cc, :], sq.v, ALU.mult)
        k.pop()
        wb = wload(w_in[l][:, B_Z:B_Z + 512], 8, 512)
        wz = [k.sb("wz", [CP, 4, 128]) for _ in chunks]
        for (g0, gn, mem) in tl["pgroups"]:
            p = proj_tm(wb, 512, xT, g0, gn)
            for (ci, r0, c) in mem:
                k.act(wz[ci][0:c].re("p h d -> p (h d)"), p[r0:r0 + c, :], AF.Silu)
                k.tt(wz[ci][0:c], wz[ci][0:c], P["gnw"][0:c, :].unsq(1).bc([c, 4, 128]), ALU.mult)
        wgt = wload(w_in[l][:, B_BETA - 248:B_BETA + 8], 8, 256)
        pbt = k.ps()
        for kk in range(8):
            k.mm(pbt[0:4, 0:TTc], wgt[:, kk, 248:252], xT[:, kk, 0:TTc], start=(kk == 0), stop=(kk == 7))
        pa_ = k.ps()
        for kk in range(8):
            k.mm(pa_[0:4, 0:TTc], wgt[:, kk, 252:256], xT[:, kk, 0:TTc], start=(kk == 0), stop=(kk == 7))
        spb = k.sb("spb", [4, TTc])
        k.act(spb.v, pbt[0:4, 0:TTc], AF.Exp, scale=-1.0)
        k.act(spb.v, spb.v, AF.Ln, bias=cst[0:4, 0:1])
        g = k.sb("g", [4, TTc])
        k.act(g.v, pa_[0:4, 0:TTc], AF.Exp, bias=gdt[:, 0:1])
        k.act(g.v, g.v, AF.Ln, bias=cst[0:4, 0:1])
        k.ts(g.v, g.v, gal[:, 0:1], ALU.mult)
        G = k.sb("G", [4, TTc])
        for (g0, gl) in tl["groups"]:
            k.scan(G[:, g0:g0 + gl], g[:, g0:g0 + gl])
        Gb = k.sb("Gb", [4, TTc])
        k.tt(Gb.v, G.v, spb.v, ALU.subtract)
        nG = k.sb("nG", [4, TTc])
        k.ts(nG.v, G.v, -1.0, ALU.mult)
        r_beta = k.sb("r_beta", [4, TTc])
        k.act(r_beta.v, spb.v, AF.Exp, scale=-1.0)
        r_bg = k.sb("r_bg", [4, TTc])
        k.act(r_bg.v, Gb.v, AF.Exp)
        r_eg = k.sb("r_eg", [4, TTc])
        k.act(r_eg.v, G.v, AF.Exp)
        r_kd = k.sb("r_kd", [4, TTc])
        nch = len(chunks)
        ge = k.sb("ge", [4, nch + 16])
        for j, ch in enumerate(chunks):
            c0, c = ch["c0"], ch["c"]
            if ch["kind"] == "samp":
                Gv = G[:, c0:c0 + 64].re("p (s j) -> p s j", j=4)
                k.cp(ge[:, j:j + 16], Gv[:, :, 3])
                k.tt(r_kd[:, c0:c0 + 64].re("p (s j) -> p s j", j=4), nG[:, c0:c0 + 64].re("p (s j) -> p s j", j=4),
                     ge[:, j:j + 16].unsq(2).bc([4, 16, 4]), ALU.add)
            else:
                k.cp(ge[:, j:j + 1], G[:, c0 + c - 1:c0 + c])
                k.ts(r_kd[:, c0:c0 + c], nG[:, c0:c0 + c], ge[:, j:j + 1], ALU.add)
        k.act(r_kd.v, r_kd.v, AF.Exp)
        nw = nch + 15 if chunks[-1]["kind"] == "samp" else nch
        k.act(ge[:, 0:nw], ge[:, 0:nw], AF.Exp)
        gbc = bcast_rows(ge[:, 0:nw], nw, "gbc")
        for j, ch in enumerate(chunks):
            c0, c, kind = ch["c0"], ch["c"], ch["kind"]
            _, m_st, m_stT, m_inT = MASKS["b" if kind == "samp" else "c"]
            nsq = {128: 6, 64: 5, 16: 3}[c] if kind != "samp" else 1
            with k.scope():
                sc = rows_to_tm([r_beta.v, r_bg.v, r_eg.v, r_kd.v], c0, c, "scg")
                kcn = k.sb("kcn", [CP, 4, 128])
                vcn = k.sb("vcn", [CP, 4, 128])
                p = k.ps()
                for h in range(4):
                    k.tr(p[0:c, h * 128:(h + 1) * 128], qkvT[:, 4 + h, c0:c0 + c], ident)
                k.cp(kcn[0:c].re("p h d -> p (h d)"), p[0:c, :], eng="act")
                p = k.ps()
                for h in range(4):
                    k.tr(p[0:c, h * 128:(h + 1) * 128], qkvT[:, 8 + h, c0:c0 + c], ident)
                k.cp(vcn[0:c].re("p h d -> p (h d)"), p[0:c, :])
                rv = k.sb("rv", [CP, 4, 128])
                rk = k.sb("rk", [CP, 4, 128])
                kd = k.sb("kd", [CP, 4, 128])
                k.tt(rv[0:c], vcn[0:c], sc[0:c, 0:4].unsq(2).bc([c, 4, 128]), ALU.mult)
                k.tt(rk[0:c], kcn[0:c], sc[0:c, 4:8].unsq(2).bc([c, 4, 128]), ALU.mult)
                k.tt(kd[0:c], kcn[0:c], sc[0:c, 12:16].unsq(2).bc([c, 4, 128]), ALU.mult)
                ob = k.sb("ob", [CP, 4, 128])

                def head(h):
                    if kind == "samp":
                        k.push()
                    qh = qkvT[:, h, c0:c0 + c]
                    kh = qkvT[:, 4 + h, c0:c0 + c]
                    pe_ = k.ps()
                    k.mm(pe_[0:c, 0:c], Gb[:, c0:c0 + c], oh[:, h, 0:c], start=True, stop=False)
                    k.mm(pe_[0:c, 0:c], oh[:, h, 0:c], nG[:, c0:c0 + c], start=False, stop=False)
                    k.mm(pe_[0:c, 0:c], ident[0:c, 0:c], m_st[0:c, 0:c], start=False, stop=True)
                    k.mm(pe_[0:c, c:2 * c], oh[:, h, 0:c], Gb[:, c0:c0 + c], start=True, stop=False)
                    k.mm(pe_[0:c, c:2 * c], nG[:, c0:c0 + c], oh[:, h, 0:c], start=False, stop=False)
                    k.mm(pe_[0:c, c:2 * c], ident[0:c, 0:c], m_stT[0:c, 0:c], start=False, stop=True)
                    k.mm(pe_[0:c, 2 * c:3 * c], oh[:, h, 0:c], G[:, c0:c0 + c], start=True, stop=False)
                    k.mm(pe_[0:c, 2 * c:3 * c], nG[:, c0:c0 + c], oh[:, h, 0:c], start=False, stop=False)
                    k.mm(pe_[0:c, 2 * c:3 * c], ident[0:c, 0:c], m_inT[0:c, 0:c], start=False, stop=True)
                    Ex = k.sb("Ex", [CP, 3 * CP])
                    k.act(Ex[0:c, 0:3 * c], pe_[0:c, 0:3 * c], AF.Exp)
                    yield
                    psc = k.ps()
                    k.mm(psc[0:c, 0:c], kh, kh)
                    k.mm(psc[0:c, c:2 * c], kh, kh)
                    k.mm(psc[0:c, 2 * c:3 * c], kh, qh)
                    M3 = k.sb("M3", [CP, 3 * CP])
                    k.tt(M3[0:c, 0:3 * c], psc[0:c, 0:3 * c], Ex[0:c, 0:3 * c], ALU.mult)
                    TTs = [k.sb("TTm", [CP, CP]), k.sb("TTm", [CP, CP])]
                    P2s = [k.sb("P2", [CP, 2 * CP]), k.sb("P2", [CP, 2 * CP])]
                    TTm = TTs[0]
                    k.tt(TTm[0:c, 0:c], ident[0:c, 0:c], M3[0:c, c:2 * c], ALU.subtract)
                    yield
                    Pw = M3
                    for q in range(nsq):
                        pp = k.ps()
                        k.mm(pp[0:c, 0:c], Pw[0:c, c:2 * c], Pw[0:c, 0:c])
                        P2 = P2s[q % 2]
                        if q < nsq - 1:
                            k.mm(pp[0:c, c:2 * c], Pw[0:c, 0:c], Pw[0:c, c:2 * c])
                            k.cp(P2[0:c, 0:2 * c], pp[0:c, 0:2 * c], eng="act")
                        else:
                            k.cp(P2[0:c, 0:c], pp[0:c, 0:c], eng="act")
                        yield
                        pt_ = k.ps()
                        k.mm(pt_[0:c, 0:c], P2[0:c, 0:c], TTm[0:c, 0:c])
                        Tn = TTs[(q + 1) % 2]
                        k.tt(Tn[0:c, 0:c], TTm[0:c, 0:c], pt_[0:c, 0:c], ALU.add)
                        TTm = Tn
                        Pw = P2
                        yield
                    pw = k.ps()
                    k.mm(pw[:, 0:c], rk[0:c, h, :], TTm[0:c, 0:c])
                    WTn = k.sb("WTn", [128, CP])
                    k.ts(WTn[:, 0:c], pw[:, 0:c], -1.0, ALU.mult)
                    yield
                    pu = k.ps()
                    po = k.ps()
                    if kind != "samp":
                        k.mm(pu[0:c, 0:128], TTm[0:c, 0:c], rv[0:c, h, :], start=True, stop=False)
                        k.mm(pu[0:c, 0:128], WTn[:, 0:c], S[:, h, :], start=False, stop=True)
                        Us = k.sb("Us", [CP, 128])
                        k.cp(Us[0:c], pu[0:c, 0:128], eng="act")
                        k.mm(po[0:c, 0:128], qh, S[:, h, :])
                        o1 = k.sb("o1", [CP, 128])
                        k.act(o1[0:c], po[0:c, 0:128], AF.Identity, scale=sc[0:c, 8 + h:9 + h])
                        yield
                        po2 = k.ps()
                        k.mm(po2[0:c, 0:128], M3[0:c, 2 * c:3 * c], Us[0:c])
                        k.tt(ob[0:c, h, :], o1[0:c], po2[0:c, 0:128], ALU.add)
                        pS_ = k.ps()
                        k.mm(pS_[:, 0:128], kd[0:c, h, :], Us[0:c])
                        k.stt(S[:, h, :], S[:, h, :], gbc[:, j, h:h + 1], pS_[:, 0:128], ALU.mult, ALU.add)
                    else:
                        WTm = k.sb("WTm", [128, 16, 64])
                        qTm = k.sb("qTm", [128, 16, 64])
                        k.memset(WTm.v, 0.0)
                        k.memset(qTm.v, 0.0)
                        for i in range(16):
                            k.cp(WTm[:, i, 4 * i:4 * i + 4], WTn[:, 4 * i:4 * i + 4])
                            k.cp(qTm[:, i, 4 * i:4 * i + 4], qkvT[:, h, c0 + 4 * i:c0 + 4 * i + 4], eng="act")
                        Sg = []
                        for gq in range(4):
                            t = k.sb("Sg", [128, 4, 128])
                            k.dma_in(t.v, sS[l].rearrange("(s h) d v -> h d s v", h=4)[h][:, 4 * gq:4 * gq + 4, :])
                            Sg.append(t)
                        k.mm(pu[0:c, 0:128], TTm[0:c, 0:c], rv[0:c, h, :], start=True, stop=False)
                        for i in range(16):
                            k.mm(pu[0:c, 0:128], WTm[:, i, :], Sg[i // 4][:, i % 4, :], start=False, stop=(i == 15))
                        Us = k.sb("Us", [CP, 128])
                        k.cp(Us[0:c], pu[0:c, 0:128], eng="act")
                        for i in range(16):
                            k.mm(po[0:c, 0:128], qTm[:, i, :], Sg[i // 4][:, i % 4, :], start=(i == 0), stop=(i == 15))
                        o1 = k.sb("o1", [CP, 128])
                        k.act(o1[0:c], po[0:c, 0:128], AF.Identity, scale=sc[0:c, 8 + h:9 + h])
                        po2 = k.ps()
                        k.mm(po2[0:c, 0:128], M3[0:c, 2 * c:3 * c], Us[0:c])
                        k.tt(ob[0:c, h, :], o1[0:c], po2[0:c, 0:128], ALU.add)
                        for gq in range(4):
                            Ux = k.sb("Ux", [64, 4, 128])
                            k.tt(Ux.v, Us.v.unsq(1).bc([64, 4, 128]), ind[:, 4 * gq:4 * gq + 4].unsq(2).bc([64, 4, 128]), ALU.mult)
                            pS_ = k.ps()
                            k.mm(pS_[:, 0:512], kd[0:64, h, :], Ux.v.re("p a b -> p (a b)"))
                            k.tt(Sg[gq].v, Sg[gq].v, gbc[:, j + 4 * gq:j + 4 * gq + 4, h:h + 1].bc([128, 4, 128]), ALU.mult)
                            k.tt(Sg[gq].v, Sg[gq].v, pS_[:, 0:512].re("p (a b) -> p a b", a=4), ALU.add)
                            k.dma_out(oS[l].rearrange("(s h) d v -> h d s v", h=4)[h][:, 4 * gq:4 * gq + 4, :], Sg[gq].v)
                    if kind == "samp":
                        k.pop()

                if kind == "samp":
                    for h in range(4):
                        for _ in head(h):
                            pass
                else:
                    interleave([head(h) for h in range(4)])
                sq = k.sb("osq", [CP, 4, 128])
                k.tt(sq[0:c], ob[0:c], ob[0:c], ALU.mult)
                ss = k.sb("oss", [CP, 4])
                k.red(ss[0:c], sq[0:c], ALU.add)
                k.ts(ss[0:c], ss[0:c], 1.0 / 128.0, ALU.mult)
                rs = k.sb("ors", [CP, 4])
                rsqrt_small(rs[0:c], ss[0:c], NORM_EPS)
                k.tt(ob[0:c], ob[0:c], rs[0:c].unsq(2).bc([c, 4, 128]), ALU.mult)
                k.tt(ob[0:c], ob[0:c], wz[j][0:c], ALU.mult)
                pt = k.ps()
                for h in range(4):
                    k.tr(pt[:, h * c:(h + 1) * c], ob[0:c, h, :], ident)
                k.cp(hBT[:, :, c0:c0 + c], pt[:, 0:4 * c].re("p (h t) -> p h t", h=4), eng="act")

    def ln_rows(src, n, g_fm, b_fm, dst32, dstB, c0, final=None):
        with k.scope():
            st = k.sb("lnst", [128, 12])
            mv = k.sb("lnmv", [128, 2])
            rs = k.sb("lnrs", [128, 1])
            xn = k.sb("lnxn", [128, 1024])
            k.op("dve", lambda e: e.bn_stats(out=st.ap[0:n, 0:6], in_=src.ap[:, 0:512]), reads=[src.t], writes=[st])
            k.op("dve", lambda e: e.bn_stats(out=st.ap[0:n, 6:12], in_=src.ap[:, 512:1024]), reads=[src.t], writes=[st])
            k.op("dve", lambda e: e.bn_aggr(out=mv.ap[0:n, :], in_=st.ap[0:n, :]), reads=[st], writes=[mv])
            rsqrt_small(rs[0:n, :], mv[0:n, 1:2], LN_EPS)
            k.ts(xn[0:n, :], src, mv[0:n, 0:1], ALU.subtract, rs[0:n, 0:1], ALU.mult)
            if final is not None:
                k.tt(xn[0:n, :], xn[0:n, :], g2bc[0:n, :], ALU.mult)
                k.tt(xn[0:n, :], xn[0:n, :], b2bc[0:n, :], ALU.add)
                for (dst, r0, r1) in final:
                    k.dma_out(dst, xn[r0:r1, :])
            else:
                for half in range(2):
                    p = k.ps()
                    for q in range(4):
                        cc = half * 4 + q
                        k.tr(p[:, q * n:(q + 1) * n], xn[0:n, cc * 128:(cc + 1) * 128], ident)
                    for q in range(4):
                        cc = half * 4 + q
                        if q % 2 == 0:
                            k.act(dst32[:, cc, c0:c0 + n], p[:, q * n:(q + 1) * n], AF.Identity,
                                  bias=b_fm[:, cc, 0:1], scale=g_fm[:, cc, 0:1])
                        else:
                            k.ts(dst32[:, cc, c0:c0 + n], p[:, q * n:(q + 1) * n], g_fm[:, cc, 0:1], ALU.mult,
                                 b_fm[:, cc, 0:1], ALU.add)
                k.cp(dstB[:, :, c0:c0 + n], dst32[:, :, c0:c0 + n], eng="act")

    def merge_phase(l, tl, xT, xT32, hAT, hBT, x1T32, x1T):
        P = LP[l]
        TTc = tl["TT"]
        yT = k.sb("yT", [128, 8, TTc], dt=BF16)
        for hh in range(2):
            wpp = wload_parts([(w_pa[l][:, hh * 512:(hh + 1) * 512], 4), (w_pb[l][:, hh * 512:(hh + 1) * 512], 4)], 512)
            wga = wload(w_in[l][:, G_MERGE + hh * 512:G_MERGE + (hh + 1) * 512], 8, 512)
            ga = []
            for q4 in range(4):
                p = proj_fm(wga, q4 * 128, xT, TTc)
                t = k.sb("ga", [128, TTc])
                k.act(t.v, p[:, 0:TTc], AF.Sigmoid)
                ppa = k.ps()
                for kk in range(4):
                    k.mm(ppa[:, 0:TTc], wpp[:, kk, q4 * 128:(q4 + 1) * 128], hAT[:, kk, :], start=(kk == 0), stop=(kk == 3))
                k.tt(t.v, t.v, ppa[:, 0:TTc], ALU.mult)
                ga.append(t)
            wgb = wload(w_in[l][:, G_MERGE + D + hh * 512:G_MERGE + D + (hh + 1) * 512], 8, 512)
            for q4 in range(4):
                n = hh * 4 + q4
                p = proj_fm(wgb, q4 * 128, xT, TTc)
                t = k.sb("gbm", [128, TTc])
                k.act(t.v, p[:, 0:TTc], AF.Sigmoid)
                ppb = k.ps()
                for kk in range(4):
                    k.mm(ppb[:, 0:TTc], wpp[:, 4 + kk, q4 * 128:(q4 + 1) * 128], hBT[:, kk, :], start=(kk == 0), stop=(kk == 3))
                k.tt(t.v, t.v, ppb[:, 0:TTc], ALU.mult)
                k.tt(yT[:, n, :], t.v, ga[q4].v, ALU.add)
        tgs = tl["tgs"]
        osb = [k.sb("osb", [128, 1024]) for _ in tgs]
        for hh in range(2):
            wb = wload(w_out[l][:, hh * 512:(hh + 1) * 512], 8, 512)
            for j, (c0, n) in enumerate(tgs):
                p = k.ps()
                for n8 in range(8):
                    k.mm(p[0:n, 0:512], yT[:, n8, c0:c0 + n], wb[:, n8, :], start=(n8 == 0), stop=False)
                for q4 in range(4):
                    k.mm(p[0:n, q4 * 128:(q4 + 1) * 128], xT32[:, hh * 4 + q4, c0:c0 + n], aident.v, start=False, stop=(q4 == 3))
                k.cp(osb[j][0:n, hh * 512:(hh + 1) * 512], p[0:n, 0:512], eng="act")
        for j, (c0, n) in enumerate(tgs):
            ln_rows(osb[j][0:n, :], n, P["g1"], P["b1"], x1T32, x1T, c0)

    def conv_fm(p, cc, nt, carry, cw, s_in, s_out, tl, tag, pools=None):
        TTc = tl["TT"]
        hh = nt - 1
        acc = pools[1].get() if pools is not None else k.sb(tag + "acc", [128, TTc])
        if tl["first"]:
            ext = k.sb(tag + "ext", [128, 16 + hh])
            k.cp(ext[:, 0:hh], carry[:, cc, :])
            k.cp(ext[:, hh:16 + hh], p[:, 0:16], eng="act")
            exs = k.sb(tag + "exs", [128, 16, 4 + hh])
            hist = k.sb(tag + "hist", [16 * hh, 128])
            k.dma_in(hist.v, s_in[:, cc * 128:(cc + 1) * 128])
            ph = k.ps()
            k.tr(ph[:, 0:16 * hh], hist.v, ident)
            k.cp(exs[:, :, 0:hh], ph[:, 0:16 * hh].re("p (s j) -> p s j", j=hh))
            k.cp(exs[:, :, hh:4 + hh], p[:, 16:80].re("p (s j) -> p s j", j=4), eng="act")
            k.cp(carry[:, cc, :], ext[:, 16:16 + hh])
            pc = k.ps()
            nst = k.sb(tag + "nst", [128, 16, hh])
            k.cp(nst.v, exs[:, :, 4:4 + hh])
            k.tr(pc[0:16 * hh, 0:128], nst.v.re("p s j -> p (s j)"), ident)
            ob = k.sb(tag + "ob", [16 * hh, 128])
            k.cp(ob.v, pc[0:16 * hh, 0:128])
            k.dma_out(s_out[:, cc * 128:(cc + 1) * 128], ob.v)
            av = acc[:, 0:16]
            k.ts(av, ext[:, 0:16], cw[:, cc, 0:1], ALU.mult)
            for jj in range(1, nt):
                k.stt(av, ext[:, jj:jj + 16], cw[:, cc, jj:jj + 1], av, ALU.mult, ALU.add)
            av = acc[:, 16:80].re("p (s j) -> p s j", j=4)
            k.ts(av, exs[:, :, 0:4], cw[:, cc, 0:1], ALU.mult)
            for jj in range(1, nt):
                k.stt(av, exs[:, :, jj:jj + 4], cw[:, cc, jj:jj + 1], av, ALU.mult, ALU.add)
        else:
            ext = pools[0].get()
            k.cp(ext[:, 0:hh], carry[:, cc, :], eng="pool")
            k.cp(ext[:, hh:TTc + hh], p[:, 0:TTc], eng="act")
            k.act(acc.v, p[:, 0:TTc], AF.Identity, scale=cw[:, cc, hh:hh + 1])
            k.cp(carry[:, cc, :], ext[:, TTc:TTc + hh], eng="pool")
            for jj in range(0, hh):
                k.stt(acc.v, ext[:, jj:jj + TTc], cw[:, cc, jj:jj + 1], acc.v, ALU.mult, ALU.add)
        return acc

    def ffn_phase(l, tl, x1T, x1T32, x2T32, x2T, last):
        P = LP[l]
        TTc = tl["TT"]
        cf, cwf = P["cf"], P["cwf"]
        hT = k.sb("hT", [128, 22, TTc], dt=BF16)
        k.push()
        pools = None if tl["first"] else (RPool("fext", [128, TTc + 2], 4), RPool("facc", [128, TTc], 4))
        for blk in range(6):
            ncols = 512 if blk < 5 else 256
            wb1 = wload(w_up[l][:, blk * 512:blk * 512 + ncols], 8, ncols)
            wb2 = wload(w_up[l][:, DFF + blk * 512:DFF + blk * 512 + ncols], 8, ncols)
            for q4 in range(ncols // 128):
                f = blk * 4 + q4
                with k.scope():
                    p1 = proj_fm(wb1, q4 * 128, x1T, TTc)
                    a1 = conv_fm(p1, f, 3, cf, cwf, sfc[l], ofc[l], tl, "f", pools)
                    k.act(a1.v, a1.v, AF.Silu)
                    p2 = proj_fm(wb2, q4 * 128, x1T, TTc)
                    a2 = conv_fm(p2, 22 + f, 3, cf, cwf, sfc[l], ofc[l], tl, "f", pools)
                    k.tt(hT[:, f, :], a1.v, a2.v, ALU.mult, eng=("dve" if tl["first"] else "pool"))
        k.pop()
        tgs = tl["tgs"]
        osb = [k.sb("osb2", [128, 1024]) for _ in tgs]
        kgs = [(0, 8), (8, 8), (16, 6)]
        for hh in range(2):
            pts = [k.ps() for _ in tgs]
            for (k0, nk) in kgs:
                wb = wload(w_down[l][k0 * 128:(k0 + nk) * 128, hh * 512:(hh + 1) * 512], nk, 512)
                for j, (c0, n) in enumerate(tgs):
                    for kk in range(nk):
                        k.mm(pts[j][0:n, 0:512], hT[:, k0 + kk, c0:c0 + n], wb[:, kk, :], start=(k0 + kk == 0), stop=False,
                             sig=(kk == nk - 1))
            for j, (c0, n) in enumerate(tgs):
                for q4 in range(4):
                    k.mm(pts[j][0:n, q4 * 128:(q4 + 1) * 128], x1T32[:, hh * 4 + q4, c0:c0 + n], aident.v, start=False, stop=(q4 == 3))
                k.cp(osb[j][0:n, hh * 512:(hh + 1) * 512], pts[j][0:n, 0:512], eng="act")
        for j, (c0, n) in enumerate(tgs):
            fin = None
            if last:
                if tl["first"]:
                    fin = [(ys[:, :], 16, 80)]
                else:
                    tok0 = tl["tok0"] + c0
                    fin = [(yp[tok0:tok0 + n, :], 0, n)]
            ln_rows(osb[j][0:n, :], n, P["g2"], P["b2"], x2T32, x2T, c0, final=fin)

    for ti, tl in enumerate(tiles):
        TTc = tl["TT"]
        cur["ti"] = ti
        cur["n"] = 0
        k.conservative = False
        with k.scope():
            xT32 = k.sb("xT32", [128, 8, TTc])
            xT = k.sb("xT", [128, 8, TTc], dt=BF16)
            for (c0, n) in tl["tgs"]:
                with k.scope():
                    xin = k.sb("xin", [128, 1024])
                    if tl["first"]:
                        k.dma_in(xin[0:16, :], meta[:, :])
                        k.dma_in(xin[16:80, :], xs[:, :])
                    else:
                        k.dma_in(xin[0:n, :], xp[tl["tok0"] + c0:tl["tok0"] + c0 + n, :])
                    ln_rows(xin[0:n, :], n, g_emb, b_emb, xT32, xT, c0)
            if ti == 0:
                tap("xT0", xT32.v)
                tap("xTb0", xT.v)
            xpairs = [(xT32, xT), (k.sb("xB32", [128, 8, TTc]), k.sb("xB", [128, 8, TTc], dt=BF16)),
                      (k.sb("xC32", [128, 8, TTc]), k.sb("xC", [128, 8, TTc], dt=BF16))]
            for l in range(DEPTH):
                (x1T32, x1T) = xpairs[1]
                (x2T32, x2T) = xpairs[2] if l % 2 == 0 else xpairs[0]
                with k.scope():
                    hAT = k.sb("hAT", [128, 4, TTc], dt=BF16)
                    hBT = k.sb("hBT", [128, 4, TTc], dt=BF16)
                    with k.scope():
                        mlstm_phase(l, tl, xT, hAT)
                    if ti == 0:
                        tap("hAT%d" % l, hAT.v)
                    with k.scope():
                        gdn_phase(l, tl, xT, hBT)
                    if ti == 0:
                        tap("hBT%d" % l, hBT.v)
                    with k.scope():
                        merge_phase(l, tl, xT, xT32, hAT, hBT, x1T32, x1T)
                if ti == 0:
                    tap("x1T%d" % l, x1T32.v)
                with k.scope():
                    ffn_phase(l, tl, x1T, x1T32, x2T32, x2T, last=(l == DEPTH - 1))
                if ti == 0 and l == 0:
                    tap("x2T%d" % l, x2T32.v)
                xT, xT32 = x2T, x2T32
    for l in range(DEPTH):
        P = LP[l]
        k.dma_out(pC[l].rearrange("h d v -> d h v"), P["Um"][:, :, 0:128])
        with k.scope():
            pt = k.ps()
            k.tr(pt[0:4, 0:128], P["Um"][:, :, 128], ident)
            nr = k.sb("pnr", [4, 128])
            k.cp(nr.v, pt[0:4, 0:128])
            k.dma_out(pn[l], nr.v)
        with k.scope():
            ptm = k.ps()
            k.tr(ptm[0:1, 0:4], P["mcur"].v, ident)
            mr = k.sb("pmr", [1, 4])
            k.cp(mr.v, ptm[0:1, 0:4])
            k.dma_out(pm[l].rearrange("h o -> o h"), mr.v)
        k.dma_out(pS[l].rearrange("h d v -> d h v"), P["S"].v)
        fm_to_rows(P["cg"].v, 3, 12, pgc[l])
        fm_to_rows(P["cf"].v, 2, 44, pfc[l])
    k.finish()
    return nc, tapouts


_CACHE = {}


def _prep_inputs(inp, c):
    f = lambda a: np.ascontiguousarray(np.asarray(a, dtype=np.float32))
    s0, s1 = NS * c, NS * (c + 1)
    m = {
        "xp": f(inp["x_prompt"][c]),
        "xs": f(inp["x_sample"][s0:s1]).reshape(64, D),
        "meta": f(inp["meta_tokens"]),
        "sC": f(inp["state_mlstm_C"][:, s0:s1]).reshape(DEPTH, NS * 4, 128, 128),
        "sn": f(inp["state_mlstm_n"][:, s0:s1]).reshape(DEPTH, NS * 4, 128),
        "sm": f(inp["state_mlstm_m"][:, s0:s1]),
        "sS": f(inp["state_gdn_S"][:, s0:s1]).reshape(DEPTH, NS * 4, 128, 128),
        "sgc": f(inp["state_gdn_conv"][:, s0:s1]).reshape(DEPTH, NS * 3, 1536),
        "sfc": f(inp["state_ffn_conv"][:, s0:s1]).reshape(DEPTH, NS * 2, 2 * DFF),
        "ln_emb_g": f(inp["ln_emb_g"]).reshape(1, D),
        "ln_emb_b": f(inp["ln_emb_b"]).reshape(1, D),
        "w_in": f(inp["w_in"]),
        "gate_bias": f(inp["mlstm_gate_bias"]).reshape(DEPTH, 8, 1),
        "mnorm_w": f(inp["mlstm_norm_w"]).reshape(DEPTH, 1, 512),
        "gconv_w": f(inp["gdn_conv_w"]),
        "A_log": f(inp["gdn_A_log"]).reshape(DEPTH, 4, 1),
        "dt_bias": f(inp["gdn_dt_bias"]).reshape(DEPTH, 4, 1),
        "gnorm_w": f(inp["gdn_norm_w"]).reshape(DEPTH, 1, 128),
        "w_pa": f(inp["w_branch_a"]),
        "w_pb": f(inp["w_branch_b"]),
        "w_out": f(inp["w_out"]),
        "ln1_g": f(inp["ln1_g"]).reshape(DEPTH, 1, D),
        "ln1_b": f(inp["ln1_b"]).reshape(DEPTH, 1, D),
        "w_up": f(inp["w_up"]),
        "fconv_w": f(inp["ffn_conv_w"]),
        "w_down": f(inp["w_down"]),
        "ln2_g": f(inp["ln2_g"]).reshape(DEPTH, 1, D),
        "ln2_b": f(inp["ln2_b"]).reshape(DEPTH, 1, D),
    }
    return m


def kernel(**inp):
    if "nc" not in _CACHE:
        _CACHE["nc"] = build()[0]
    nc = _CACHE["nc"]
    in_maps = [_prep_inputs(inp, c) for c in range(8)]
    res = run_bass_kernel_spmd(nc, in_maps, core_ids=list(range(8)))
    R = res.results
    st = lambda name: np.stack([np.asarray(R[c][name]) for c in range(8)], axis=1)
    cat = lambda name, shp: np.concatenate([np.asarray(R[c][name]).reshape(shp) for c in range(8)], axis=1)
    y_prompt = np.stack([np.asarray(R[c]["yp"]) for c in range(8)], axis=0)
    y_sample = np.concatenate([np.asarray(R[c]["ys"]).reshape(NS, 4, D) for c in range(8)], axis=0)
    outs = (
        y_prompt, y_sample,
        st("pC"), st("pn"), st("pm").reshape(DEPTH, 8, 4), st("pS"), st("pgc"), st("pfc"),
        cat("oC", (DEPTH, NS, 4, 128, 128)), cat("on", (DEPTH, NS, 4, 128)), cat("om", (DEPTH, NS, 4)),
        cat("oS", (DEPTH, NS, 4, 128, 128)), cat("ogc", (DEPTH, NS, 3, 1536)), cat("ofc", (DEPTH, NS, 2, 2 * DFF)),
    )
    return tuple(np.ascontiguousarray(o, dtype=np.float32) for o in outs)
```

```python
import math
from contextlib import ExitStack
import numpy as np
import concourse.bass as bass
import concourse.mybir as mybir
from concourse.bass_utils import run_bass_kernel_spmd

F32 = mybir.dt.float32
BF16 = mybir.dt.bfloat16
AF = mybir.ActivationFunctionType
ALU = mybir.AluOpType
AX = mybir.AxisListType

NEG = -30000.0
LN_EPS = 1e-5
NORM_EPS = 1e-6
ALPHA = 4.0 ** 0.25
DEPTH = 2
D = 1024
KC = 8
NIN = 6160
A_Q, A_K, A_V, A_O, A_I = 0, 512, 1024, 1536, 2048
B_QKV, B_Z, B_BETA, B_A, G_MERGE = 2056, 3592, 4104, 4108, 4112
DFF = 2816
NS = 16


class V:
    __slots__ = ("t", "ap")

    def __init__(self, t, ap):
        self.t = t
        self.ap = ap

    def __getitem__(self, idx):
        return V(self.t, self.ap[idx])

    def re(self, pat, **kw):
        return V(self.t, self.ap.rearrange(pat, **kw))

    def bc(self, shape):
        return V(self.t, self.ap.to_broadcast(list(shape)))

    def unsq(self, axis):
        return V(self.t, self.ap.unsqueeze(axis))


class T:
    __slots__ = ("ap", "lw", "rd", "dsem", "name", "a0", "a1")

    def __init__(self, name, ap, rd):
        self.name = name
        self.ap = ap
        self.lw = None
        self.rd = dict(rd)
        self.dsem = None

    def __getitem__(self, idx):
        return V(self, self.ap[idx])

    @property
    def v(self):
        return V(self, self.ap)


class KB:
    def __init__(self, nc):
        self.nc = nc
        self.root = ExitStack()
        self.stacks = [self.root]
        self.scope_tiles = [[]]
        self.engs = {"pe": nc.tensor, "act": nc.scalar, "dve": nc.vector, "pool": nc.gpsimd, "sp": nc.sync}
        self.semh = {}
        self.own = {}
        self.cnt = {}
        for e in ("pe", "act", "dve", "pool"):
            s = self.root.enter_context(nc.semaphore("s_" + e))
            self.semh["s_" + e] = s
            self.own[e] = "s_" + e
            self.cnt["s_" + e] = 0
        self.pending = {e: False for e in self.own}
        self.dmasems = set()
        self.free_dsems = []
        self.seen = {e: {} for e in self.engs}
        self.grave = {}
        self.grave_ranges = []
        self.conservative = True
        with nc.sbuf_tensor("probe0", [128, 8], F32) as h0:
            a_ = int(nc.lookup_mloc(h0).addr)
        fill = (512 - a_ % 512) % 512
        if fill:
            self.root.enter_context(nc.sbuf_tensor("fill0", [128, fill // 4], F32))
        self.nid = 0
        self.ps_tiles = []
        self.ps_i = 0
        self.wb_tiles = []
        self.wb_i = 0

    def sb(self, name, shape, persist=False, dt=F32):
        self.nid += 1
        nm = "%s_%d" % (name, self.nid)
        st = self.root if persist else self.stacks[-1]
        shape = list(shape)
        nfree = 1
        for d_ in shape[1:]:
            nfree *= d_
        esz = 2 if dt == BF16 else 4
        per = 512 // esz
        npad = ((nfree + per - 1) // per) * per
        h = st.enter_context(self.nc.sbuf_tensor(nm, [shape[0], npad], dt))
        ml = self.nc.lookup_mloc(h)
        a0 = int(ml.addr)
        a1 = a0 + int(ml.dims[1])
        inh = {}
        keep = []
        for (g0, g1, evs) in self.grave_ranges:
            if g0 < a1 and a0 < g1:
                for sn, v in evs.items():
                    if inh.get(sn, 0) < v:
                        inh[sn] = v
                if a0 <= g0 and g1 <= a1:
                    continue
            keep.append((g0, g1, evs))
        self.grave_ranges = keep
        if self.conservative:
            for sn, v in self.grave.items():
                if inh.get(sn, 0) < v:
                    inh[sn] = v
        ap = h[:, 0:nfree]
        if len(shape) == 3:
            ap = ap.rearrange("p (a b) -> p a b", a=shape[1])
        elif len(shape) == 4:
            ap = ap.rearrange("p (a b c) -> p a b c", a=shape[1], b=shape[2])
        t = T(nm, ap, inh)
        t.a0, t.a1 = a0, a1
        self.min_rem = min(getattr(self, "min_rem", 1 << 30), self.nc.sbuf_bytes_remaining)
        if not persist:
            self.scope_tiles[-1].append(t)
        return t

    def psum(self, name, shape):
        h = self.root.enter_context(self.nc.psum_tensor(name, list(shape), F32))
        return T(name, h[:], {})

    def scope(self):
        kb = self

        class _S:
            def __enter__(s):
                kb.stacks.append(ExitStack())
                kb.scope_tiles.append([])

            def __exit__(s, *a):
                kb._free_tiles(kb.scope_tiles.pop())
                kb.stacks.pop().close()
                return False

        return _S()

    def push(self):
        self.stacks.append(ExitStack())
        self.scope_tiles.append([])

    def _free_tiles(self, tiles):
        for t in tiles:
            evs = dict(t.rd)
            if t.lw is not None and evs.get(t.lw[0], 0) < t.lw[1]:
                evs[t.lw[0]] = t.lw[1]
            if t.dsem is not None:
                if evs.get(t.dsem, 0) < self.cnt[t.dsem]:
                    evs[t.dsem] = self.cnt[t.dsem]
                self.free_dsems.append(t.dsem)
                t.dsem = None
            if evs:
                self.grave_ranges.append((t.a0, t.a1, evs))
                for sn, v in evs.items():
                    if self.grave.get(sn, 0) < v:
                        self.grave[sn] = v

    def pop(self):
        self._free_tiles(self.scope_tiles.pop())
        self.stacks.pop().close()

    def _g(self, ev):
        s, v = ev
        if self.grave.get(s, 0) < v:
            self.grave[s] = v

    def ps(self):
        t = self.ps_tiles[self.ps_i % len(self.ps_tiles)]
        self.ps_i += 1
        return t

    def wbuf(self):
        t = self.wb_tiles[self.wb_i % len(self.wb_tiles)]
        self.wb_i += 1
        return t

    def op(self, eng, fn, reads=(), writes=(), sig=True, dma_tile=None):
        e = self.engs[eng]
        is_load = dma_tile is not None and (dma_tile in writes) and dma_tile.lw is not None and dma_tile.lw[0] == dma_tile.dsem and not dma_tile.rd
        deps = {}

        def add(ev):
            if ev is None:
                return
            s, v = ev
            if deps.get(s, 0) < v:
                deps[s] = v

        for t in reads:
            add(t.lw)
        for t in writes:
            add(t.lw)
            for s, v in t.rd.items():
                add((s, v))
        own = self.own.get(eng) if dma_tile is None else None
        seen = self.seen[eng]
        for s, v in deps.items():
            if eng == "pe" and s == own:
                continue
            if dma_tile is not None and s == dma_tile.dsem and is_load:
                continue
            if s in self.dmasems:
                v = self.cnt[s]
            if seen.get(s, 0) >= v:
                continue
            e.wait_ge(self.semh[s], v)
            seen[s] = v
        if dma_tile is not None and dma_tile.dsem is None:
            if self.free_dsems:
                nm = self.free_dsems.pop(0)
                if seen.get(nm, 0) < self.cnt[nm]:
                    e.wait_ge(self.semh[nm], self.cnt[nm])
                    seen[nm] = self.cnt[nm]
            else:
                nm = "d_%d" % len(self.dmasems)
                self.semh[nm] = self.root.enter_context(self.nc.semaphore(nm))
                self.cnt[nm] = 0
                self.dmasems.add(nm)
            dma_tile.dsem = nm
        inst = fn(e)
        if dma_tile is not None:
            s = dma_tile.dsem
            inst.then_inc(self.semh[s], 16)
            self.cnt[s] += 16
            ev = (s, self.cnt[s])
        else:
            if sig:
                inst.then_inc(self.semh[own], 1)
                self.cnt[own] += 1
                ev = (own, self.cnt[own])
                self.pending[eng] = False
            else:
                ev = (own, self.cnt[own] + 1)
                self.pending[eng] = True
        for t in writes:
            t.lw = ev
            t.rd = {}
        for t in reads:
            if t.rd.get(ev[0], 0) < ev[1]:
                t.rd[ev[0]] = ev[1]
        return inst

    def finish(self):
        assert not any(self.pending.values())
        sp = self.engs["sp"]
        for s in sorted(self.dmasems):
            if self.cnt[s] > self.seen["sp"].get(s, 0):
                sp.wait_ge(self.semh[s], self.cnt[s])
        for e in ("pe", "act", "dve", "pool"):
            s = self.own[e]
            if self.cnt[s] > 0:
                sp.wait_ge(self.semh[s], self.cnt[s])
        self.root.close()

    def mm(self, out, lhsT, rhs, start=True, stop=True, sig=None):
        self.op("pe", lambda e: e.matmul(out.ap, lhsT=lhsT.ap, rhs=rhs.ap, start=start, stop=stop),
                reads=[lhsT.t, rhs.t], writes=[out.t], sig=(stop if sig is None else sig))

    def tr(self, out, in_, ident):
        if isinstance(ident, T):
            ident = ident.v
        r = in_.ap.shape[0]
        self.op("pe", lambda e: e.transpose(out=out.ap, in_=in_.ap, identity=ident.ap[0:r, 0:r]),
                reads=[in_.t, ident.t], writes=[out.t])

    def act(self, out, in_, func, bias=None, scale=None, eng="act"):
        kw = {}
        rd = [in_.t]
        if bias is not None:
            if isinstance(bias, V):
                kw["bias"] = bias.ap
                rd.append(bias.t)
            else:
                kw["bias"] = bias
        if scale is not None:
            if isinstance(scale, V):
                kw["scale"] = scale.ap
                rd.append(scale.t)
            else:
                kw["scale"] = scale
        self.op("act", lambda e: e.activation(out=out.ap, in_=in_.ap, func=func, **kw), reads=rd, writes=[out.t])

    def ts(self, out, in0, s1, op0, s2=None, op1=None, eng="dve"):
        rd = [in0.t]
        a1 = s1
        a2 = s2
        if isinstance(s1, V):
            rd.append(s1.t)
            a1 = s1.ap
        if isinstance(s2, V):
            rd.append(s2.t)
            a2 = s2.ap
        if op1 is None:
            self.op(eng, lambda e: e.tensor_scalar(out=out.ap, in0=in0.ap, scalar1=a1, scalar2=None, op0=op0),
                    reads=rd, writes=[out.t])
        else:
            self.op(eng, lambda e: e.tensor_scalar(out=out.ap, in0=in0.ap, scalar1=a1, scalar2=a2, op0=op0, op1=op1),
                    reads=rd, writes=[out.t])

    def tt(self, out, in0, in1, op, eng="dve"):
        self.op(eng, lambda e: e.tensor_tensor(out=out.ap, in0=in0.ap, in1=in1.ap, op=op),
                reads=[in0.t, in1.t], writes=[out.t])

    def stt(self, out, in0, scalar, in1, op0, op1, eng="dve"):
        rd = [in0.t, in1.t]
        a = scalar
        if isinstance(scalar, V):
            rd.append(scalar.t)
            a = scalar.ap
        self.op(eng, lambda e: e.scalar_tensor_tensor(out=out.ap, in0=in0.ap, scalar=a, in1=in1.ap, op0=op0, op1=op1),
                reads=rd, writes=[out.t])

    def cp(self, out, in_, eng="dve"):
        if eng == "act":
            self.op("act", lambda e: e.copy(out=out.ap, in_=in_.ap), reads=[in_.t], writes=[out.t])
        else:
            self.op(eng, lambda e: e.tensor_copy(out=out.ap, in_=in_.ap), reads=[in_.t], writes=[out.t])

    def memset(self, out, val, eng="dve"):
        self.op(eng, lambda e: e.memset(out.ap, val), writes=[out.t])

    def red(self, out, in_, op, eng="dve"):
        self.op(eng, lambda e: e.tensor_reduce(out=out.ap, in_=in_.ap, axis=AX.X, op=op), reads=[in_.t], writes=[out.t])

    def scan(self, out, in_, op0=ALU.add):
        self.op("dve", lambda e: e.tensor_tensor_scan(out=out.ap, data0=in_.ap, data1=in_.ap, initial=0.0 if op0 == ALU.add else -3.0e38,
                                                      op0=op0, op1=ALU.bypass), reads=[in_.t], writes=[out.t])

    def dma_in(self, out, in_ap, q="sp", slow=False):
        kw = {"allow_slow_non_contiguous": True} if slow else {}
        self.op(q, lambda e: e.dma_start(out=out.ap, in_=in_ap, **kw), writes=[out.t], dma_tile=out.t)

    def dma_out(self, out_ap, in_, q="sp", slow=False):
        kw = {"allow_slow_non_contiguous": True} if slow else {}
        self.op(q, lambda e: e.dma_start(out=out_ap, in_=in_.ap, **kw), reads=[in_.t], dma_tile=in_.t)


def build(TT=(384, 384, 384, 384, 384, 128), taps=()):
    nc = bass.Bass("TRN2", target_bir_lowering=False)
    k = KB(nc)
    taps = set(taps)
    tapouts = {}

    def din(name, shape):
        return nc.dram_tensor(name, list(shape), F32, kind="ExternalInput").ap()

    def dout(name, shape):
        return nc.dram_tensor(name, list(shape), F32, kind="ExternalOutput").ap()

    xp = din("xp", [2048, D])
    xs = din("xs", [64, D])
    meta = din("meta", [16, D])
    sC = din("sC", [DEPTH, NS * 4, 128, 128])
    sn = din("sn", [DEPTH, NS * 4, 128])
    sm = din("sm", [DEPTH, NS, 4])
    sS = din("sS", [DEPTH, NS * 4, 128, 128])
    sgc = din("sgc", [DEPTH, NS * 3, 1536])
    sfc = din("sfc", [DEPTH, NS * 2, 2 * DFF])
    ln_emb_g = din("ln_emb_g", [1, D])
    ln_emb_b = din("ln_emb_b", [1, D])
    w_in = din("w_in", [DEPTH, D, NIN])
    gate_bias = din("gate_bias", [DEPTH, 8, 1])
    mnorm_w = din("mnorm_w", [DEPTH, 1, 512])
    gconv_w = din("gconv_w", [DEPTH, 4, 1536])
    A_log = din("A_log", [DEPTH, 4, 1])
    dt_bias = din("dt_bias", [DEPTH, 4, 1])
    gnorm_w = din("gnorm_w", [DEPTH, 1, 128])
    w_pa = din("w_pa", [DEPTH, 512, D])
    w_pb = din("w_pb", [DEPTH, 512, D])
    w_out = din("w_out", [DEPTH, D, D])
    ln1_g = din("ln1_g", [DEPTH, 1, D])
    ln1_b = din("ln1_b", [DEPTH, 1, D])
    w_up = din("w_up", [DEPTH, D, 2 * DFF])
    fconv_w = din("fconv_w", [DEPTH, 3, 2 * DFF])
    w_down = din("w_down", [DEPTH, DFF, D])
    ln2_g = din("ln2_g", [DEPTH, 1, D])
    ln2_b = din("ln2_b", [DEPTH, 1, D])

    yp = dout("yp", [2048, D])
    ys = dout("ys", [64, D])
    pC = dout("pC", [DEPTH, 4, 128, 128])
    pn = dout("pn", [DEPTH, 4, 128])
    pm = dout("pm", [DEPTH, 4, 1])
    pS = dout("pS", [DEPTH, 4, 128, 128])
    pgc = dout("pgc", [DEPTH, 3, 1536])
    pfc = dout("pfc", [DEPTH, 2, 2 * DFF])
    oC = dout("oC", [DEPTH, NS * 4, 128, 128])
    on = dout("on", [DEPTH, NS * 4, 128])
    om = dout("om", [DEPTH, NS, 4])
    oS = dout("oS", [DEPTH, NS * 4, 128, 128])
    ogc = dout("ogc", [DEPTH, NS * 3, 1536])
    ofc = dout("ofc", [DEPTH, NS * 2, 2 * DFF])

    def tap(name, view):
        if name not in taps:
            return
        shp = list(view.ap.shape)
        o = nc.dram_tensor("tap_" + name, shp, view.ap.dtype, kind="ExternalOutput").ap()
        tapouts[name] = shp
        k.dma_out(o, view, q="sp", slow=True)

    k.ps_tiles = [k.psum("ps%d" % i, [128, 512]) for i in range(6)]
    psd = k.psum("psd", [128, 1024])
    k.wb_tiles = [k.sb("wb%d" % i, [128, 8, 512], persist=True, dt=BF16) for i in range(4)]

    ident = k.sb("ident", [128, 128], persist=True)
    k.memset(ident.v, 0.0, eng="pool")
    k.op("pool", lambda e: e.affine_select(out=ident.ap, in_=ident.ap, pattern=[[-1, 128]], compare_op=ALU.not_equal,
                                           fill=1.0, base=0, channel_multiplier=1), reads=[ident], writes=[ident])
    aident = k.sb("aident", [128, 128], persist=True)
    k.ts(aident.v, ident.v, ALPHA, ALU.mult)
    ones = k.sb("ones", [128, 128], persist=True)
    k.memset(ones.v, 1.0)
    cst = k.sb("cst", [128, 8], persist=True)
    k.memset(cst[:, 4:5], LN_EPS)
    k.memset(cst[:, 0:1], 1.0)
    k.memset(cst[:, 1:2], NORM_EPS)
    k.memset(cst[:, 2:3], math.log(128.0 ** -0.5))
    k.memset(cst[:, 3:4], 0.0)

    def aff(t, pattern, op, fill, base, cm):
        k.op("pool", lambda e: e.affine_select(out=t.ap, in_=t.ap, pattern=pattern, compare_op=op, fill=fill,
                                               base=base, channel_multiplier=cm), reads=[t], writes=[t])

    m01T = k.sb("m01T", [128, 128], persist=True)
    k.memset(m01T.v, 1.0, eng="pool")
    aff(m01T, [[1, 128]], ALU.is_ge, 0.0, 0, -1)
    mn_st = k.sb("mn_st", [128, 128], persist=True)
    k.memset(mn_st.v, 0.0, eng="pool")
    aff(mn_st, [[-1, 128]], ALU.is_gt, NEG, 0, 1)
    mn_stT = k.sb("mn_stT", [128, 128], persist=True)
    k.memset(mn_stT.v, 0.0, eng="pool")
    aff(mn_stT, [[1, 128]], ALU.is_gt, NEG, 0, -1)
    mn_inT = k.sb("mn_inT", [128, 128], persist=True)
    k.memset(mn_inT.v, 0.0, eng="pool")
    aff(mn_inT, [[1, 128]], ALU.is_ge, NEG, 0, -1)
    indT = k.sb("indT", [16, 64], persist=True)
    k.memset(indT.v, 1.0, eng="pool")
    aff(indT, [[1, 64]], ALU.is_ge, 0.0, 0, -4)
    aff(indT, [[-1, 64]], ALU.is_ge, 0.0, 3, 4)
    ind = k.sb("ind", [64, 16], persist=True)
    k.memset(ind.v, 1.0, eng="pool")
    aff(ind, [[-4, 16]], ALU.is_ge, 0.0, 0, 1)
    aff(ind, [[4, 16]], ALU.is_ge, 0.0, 3, -1)
    same = k.sb("same", [64, 64], persist=True)
    p0 = k.ps()
    k.mm(p0[0:64, 0:64], indT.v, indT.v)
    k.cp(same.v, p0[0:64, 0:64])
    sneg = k.sb("sneg", [64, 64], persist=True)
    k.ts(sneg.v, same.v, -NEG, ALU.mult, NEG, ALU.add)
    b01T = k.sb("b01T", [64, 64], persist=True)
    k.tt(b01T.v, m01T[0:64, 0:64], same.v, ALU.mult)
    bn_st = k.sb("bn_st", [64, 64], persist=True)
    k.tt(bn_st.v, mn_st[0:64, 0:64], sneg.v, ALU.min)
    bn_stT = k.sb("bn_stT", [64, 64], persist=True)
    k.tt(bn_stT.v, mn_stT[0:64, 0:64], sneg.v, ALU.min)
    bn_inT = k.sb("bn_inT", [64, 64], persist=True)
    k.tt(bn_inT.v, mn_inT[0:64, 0:64], sneg.v, ALU.min)
    MASKS = {"c": (m01T, mn_st, mn_stT, mn_inT), "b": (b01T, bn_st, bn_stT, bn_inT)}
    oh = k.sb("oh", [4, 4, 128], persist=True)
    k.memset(oh.v, 0.0, eng="pool")
    aff(oh, [[-1, 4], [0, 128]], ALU.not_equal, 1.0, 0, 1)

    def load_fm(name, src2d, R, C):
        dst = k.sb(name, [128, C, R], persist=True)
        with k.scope():
            tmp = k.sb("ldfm", [R, C * 128])
            k.dma_in(tmp.v, src2d)
            c0 = 0
            while c0 < C:
                n = min(C - c0, 512 // R)
                p = k.ps()
                for c in range(n):
                    k.tr(p[:, c * R:(c + 1) * R], tmp[:, (c0 + c) * 128:(c0 + c + 1) * 128], ident)
                k.cp(dst[:, c0:c0 + n, :], p[:, 0:n * R].re("p (c r) -> p c r", r=R))
                c0 += n
        return dst

    def load_bc(name, src_row, n):
        dst = k.sb(name, [128, n], persist=True)
        k.dma_in(dst.v, src_row.to_broadcast([128, n]), slow=True)
        return dst

    g_emb = load_fm("g_emb", ln_emb_g, 1, 8)
    b_emb = load_fm("b_emb", ln_emb_b, 1, 8)
    LP = []
    for l in range(DEPTH):
        P = {}
        P["g1"] = load_fm("g1", ln1_g[l], 1, 8)
        P["b1"] = load_fm("b1", ln1_b[l], 1, 8)
        P["g2"] = load_fm("g2", ln2_g[l], 1, 8)
        P["b2"] = load_fm("b2", ln2_b[l], 1, 8)
        P["cwg"] = load_fm("cwg", gconv_w[l], 4, 12)
        P["cwf"] = load_fm("cwf", fconv_w[l], 3, 44)
        P["anw"] = load_bc("anw", mnorm_w[l], 512)
        P["gnw"] = load_bc("gnw", gnorm_w[l], 128)
        gbi = k.sb("gbi", [4, 1], persist=True)
        gbf = k.sb("gbf", [4, 1], persist=True)
        gal = k.sb("gal", [4, 1], persist=True)
        gdt = k.sb("gdt", [4, 1], persist=True)
        k.dma_in(gbi.v, gate_bias[l][0:4, :], slow=True)
        k.dma_in(gbf.v, gate_bias[l][4:8, :], slow=True)
        k.dma_in(gal.v, A_log[l], slow=True)
        k.dma_in(gdt.v, dt_bias[l], slow=True)
        k.ts(gbf.v, gbf.v, -1.0, ALU.mult)
        k.act(gal.v, gal.v, AF.Exp)
        k.ts(gal.v, gal.v, -1.0, ALU.mult)
        P["gb"] = (gbi, gbf, gal, gdt)
        Um = k.sb("Um", [128, 4, 129], persist=True)
        k.memset(Um.v, 0.0)
        mcur = k.sb("mcur", [4, 1], persist=True)
        k.memset(mcur.v, 0.0)
        S = k.sb("S", [128, 4, 128], persist=True)
        k.memset(S.v, 0.0)
        cg = k.sb("cg", [128, 12, 3], persist=True)
        k.memset(cg.v, 0.0)
        cf = k.sb("cf", [128, 44, 2], persist=True)
        k.memset(cf.v, 0.0)
        P.update(Um=Um, mcur=mcur, S=S, cg=cg, cf=cf)
        LP.append(P)
    g2bc = load_bc("g2bc", ln2_g[DEPTH - 1], D)
    b2bc = load_bc("b2bc", ln2_b[DEPTH - 1], D)

    tiles = [dict(TT=80, first=True, chunks=[dict(c0=0, c=16, kind="meta"), dict(c0=16, c=64, kind="samp")],
                  groups=[(0, 16)] + [(16 + 4 * i, 4) for i in range(16)], tgs=[(0, 80)], tok0=0,
                  pgroups=[(0, 16, [(0, 0, 16)]), (16, 64, [(1, 0, 64)])])]
    sizes = TT if isinstance(TT, (list, tuple)) else [TT] * (2048 // TT)
    assert sum(sizes) == 2048
    t0_ = 0
    for sz in sizes:
        tiles.append(dict(TT=sz, first=False, chunks=[dict(c0=128 * j, c=128, kind="p", tok0=t0_ + 128 * j) for j in range(sz // 128)],
                          groups=[(128 * j, 128) for j in range(sz // 128)], tgs=[(128 * g, 128) for g in range(sz // 128)], tok0=t0_,
                          pgroups=[(128 * g, 128, [(g, 0, 128)]) for g in range(sz // 128)]))
        t0_ += sz

    scratch = {}
    cur = {"ti": 0, "n": 0}

    def wload_parts(parts, ncols):
        wb = k.wbuf()
        ktot = sum(kc for _, kc in parts)
        key = cur["n"]
        cur["n"] += 1
        if cur["ti"] == 0:
            k0 = 0
            for (src, kc) in parts:
                k.dma_in(wb[:, k0:k0 + kc, 0:ncols], src.rearrange("(k p) n -> p k n", p=128), q="pool")
                k0 += kc
            scr = nc.dram_tensor("wscr_%d" % key, [128, ktot, ncols], BF16).ap()
            st = T("scr%d" % key, None, {})
            scratch[key] = (scr, st)
            k.op("sp", lambda e: e.dma_start(out=scr, in_=wb.ap[:, 0:ktot, 0:ncols]), reads=[wb], writes=[st], dma_tile=wb)
        else:
            scr, st = scratch[key]
            k.op("sp", lambda e: e.dma_start(out=wb.ap[:, 0:ktot, 0:ncols], in_=scr), reads=[st], writes=[wb], dma_tile=wb)
        return wb

    def wload(src2d, kc, ncols):
        return wload_parts([(src2d, kc)], ncols)

    def proj_fm(wb, cb, xT, n, nk=8, k0=0):
        p = k.ps()
        for kk in range(nk):
            k.mm(p[:, 0:n], wb[:, k0 + kk, cb:cb + 128], xT[:, kk, 0:n], start=(kk == 0), stop=(kk == nk - 1))
        return p

    def proj_tm(wb, ncols, xT, c0, c):
        p = k.ps()
        for kk in range(8):
            k.mm(p[0:c, 0:ncols], xT[:, kk, c0:c0 + c], wb[:, kk, 0:ncols], start=(kk == 0), stop=(kk == 7))
        return p

    def rsqrt_small(out, in_, eps):
        n = in_.ap.shape[0]
        col = 1 if eps == NORM_EPS else 4
        k.act(out, in_, AF.Ln, bias=cst[0:n, col:col + 1])
        k.act(out, out, AF.Exp, scale=-0.5)

    def ln_chunk(src, c, g_fm, b_fm, dstT, c0, final_dst=None, dstB=None):
        with k.scope():
            st = k.sb("lnst", [64, 12])
            mv = k.sb("lnmv", [64, 2])
            rs = k.sb("lnrs", [64, 1])
            xn = k.sb("lnxn", [64, 1024])
            k.op("dve", lambda e: e.bn_stats(out=st.ap[0:c, 0:6], in_=src.ap[:, 0:512]), reads=[src.t], writes=[st])
            k.op("dve", lambda e: e.bn_stats(out=st.ap[0:c, 6:12], in_=src.ap[:, 512:1024]), reads=[src.t], writes=[st])
            k.op("dve", lambda e: e.bn_aggr(out=mv.ap[0:c, :], in_=st.ap[0:c, :]), reads=[st], writes=[mv])
            rsqrt_small(rs[0:c, :], mv[0:c, 1:2], LN_EPS)
            k.ts(xn[0:c, :], src, mv[0:c, 0:1], ALU.subtract, rs[0:c, 0:1], ALU.mult)
            if final_dst is not None:
                k.tt(xn[0:c, :], xn[0:c, :], g2bc[0:c, :], ALU.mult)
                k.tt(xn[0:c, :], xn[0:c, :], b2bc[0:c, :], ALU.add)
                k.dma_out(final_dst, xn[0:c, :])
            else:
                p = k.ps()
                for cc in range(8):
                    k.tr(p[:, cc * c:(cc + 1) * c], xn[0:c, cc * 128:(cc + 1) * 128], ident)
                for cc in range(8):
                    if cc % 2 == 0:
                        k.act(dstT[:, cc, c0:c0 + c], p[:, cc * c:(cc + 1) * c], AF.Identity,
                              bias=b_fm[:, cc, 0:1], scale=g_fm[:, cc, 0:1])
                    else:
                        k.ts(dstT[:, cc, c0:c0 + c], p[:, cc * c:(cc + 1) * c], g_fm[:, cc, 0:1], ALU.mult,
                             b_fm[:, cc, 0:1], ALU.add)
                k.cp(dstB[:, :, c0:c0 + c], dstT[:, :, c0:c0 + c])

    def fm_to_rows(src3, R, C, dram2d):
        with k.scope():
            ob = k.sb("f2r", [R, C * 128])
            c0 = 0
            while c0 < C:
                n = min(4, C - c0)
                p = k.ps()
                for c in range(n):
                    k.tr(p[0:R, c * 128:(c + 1) * 128], src3[:, c0 + c, :], ident)
                k.cp(ob[:, c0 * 128:(c0 + n) * 128], p[0:R, 0:n * 128])
                c0 += n
            k.dma_out(dram2d, ob.v)

    def bcast_rows(row4, n, name):
        R = k.sb("bcR", [4, n, 4])
        k.tt(R.v, row4.unsq(2).bc([4, n, 4]), ident[0:4, 0:4].unsq(1).bc([4, n, 4]), ALU.mult)
        p = k.ps()
        k.mm(p[:, 0:n * 4], ones[0:4, :], R.v.re("p a b -> p (a b)"))
        dst = k.sb(name, [128, n, 4])
        k.cp(dst.v, p[:, 0:n * 4].re("p (a b) -> p a b", b=4))
        return dst

    class RPool:
        def __init__(self, name, shape, n, dt=F32):
            self.tiles = [k.sb(name, shape, dt=dt) for _ in range(n)]
            self.i = 0

        def get(self):
            t = self.tiles[self.i % len(self.tiles)]
            self.i += 1
            return t

    def interleave(gens):
        gens = list(gens)
        while gens:
            nxt = []
            for g in gens:
                try:
                    next(g)
                    nxt.append(g)
                except StopIteration:
                    pass
            gens = nxt

    def rows_to_tm(rows, c0, c, name):
        p = k.ps()
        for i, r in enumerate(rows):
            k.tr(p[0:c, 4 * i:4 * i + 4], r[:, c0:c0 + c], ident)
        dst = k.sb(name, [128, 4 * len(rows)])
        k.cp(dst[0:c, :], p[0:c, 0:4 * len(rows)])
        return dst

    def mlstm_phase(l, tl, xT, hAT):
        P = LP[l]
        TTc = tl["TT"]
        chunks = tl["chunks"]
        CP = max(ch["c"] for ch in chunks)
        Um, mcur = P["Um"], P["mcur"]
        gbi, gbf, gal, gdt = P["gb"]
        qT = k.sb("qT", [128, 4, TTc])
        kT = k.sb("kT", [128, 4, TTc])
        wb = wload(w_in[l][:, A_Q:A_Q + 512], 8, 512)
        for h in range(4):
            p = proj_fm(wb, h * 128, xT, TTc)
            k.act(qT[:, h, :], p[:, 0:TTc], AF.Identity, scale=128.0 ** -0.5)
        wb = wload(w_in[l][:, A_K:A_K + 512], 8, 512)
        for h in range(4):
            p = proj_fm(wb, h * 128, xT, TTc)
            k.cp(kT[:, h, :], p[:, 0:TTc])
        kc = [k.sb("kc", [CP, 4, 128]) for _ in chunks]
        va = [k.sb("va", [CP, 4, 129]) for _ in chunks]
        ow = [k.sb("ow", [CP, 512]) for _ in chunks]
        for (g0, gn, mem) in tl["pgroups"]:
            p = proj_tm(wb, 512, xT, g0, gn)
            for (ci, r0, c) in mem:
                k.cp(kc[ci][0:c], p[r0:r0 + c, :].re("p (h d) -> p h d", h=4), eng="act")
        wb = wload(w_in[l][:, A_V:A_V + 512], 8, 512)
        for (g0, gn, mem) in tl["pgroups"]:
            p = proj_tm(wb, 512, xT, g0, gn)
            for (ci, r0, c) in mem:
                k.cp(va[ci][0:c, :, 0:128], p[r0:r0 + c, :].re("p (h d) -> p h d", h=4))
                k.memset(va[ci][0:c, :, 128:129], 1.0)
        wb = wload(w_in[l][:, A_O:A_O + 512], 8, 512)
        for (g0, gn, mem) in tl["pgroups"]:
            p = proj_tm(wb, 512, xT, g0, gn)
            for (ci, r0, c) in mem:
                k.act(ow[ci][0:c, :], p[r0:r0 + c, :], AF.Sigmoid)
                k.tt(ow[ci][0:c, :], ow[ci][0:c, :], P["anw"][0:c, :], ALU.mult)
        wgt = wload(w_in[l][:, A_I - 248:A_I + 8], 8, 256)
        pi = k.ps()
        for kk in range(8):
            k.mm(pi[0:4, 0:TTc], wgt[:, kk, 248:252], xT[:, kk, 0:TTc], start=(kk == 0), stop=(kk == 7))
        pf = k.ps()
        for kk in range(8):
            k.mm(pf[0:4, 0:TTc], wgt[:, kk, 252:256], xT[:, kk, 0:TTc], start=(kk == 0), stop=(kk == 7))
        igr = k.sb("igr", [4, TTc])
        lf = k.sb("lf", [4, TTc])
        b = k.sb("b", [4, TTc])
        a = k.sb("a", [4, TTc])
        Er = k.sb("Er", [4, TTc])
        Th = k.sb("Th", [4, TTc])
        k.act(igr.v, pi[0:4, 0:TTc], AF.Identity, bias=gbi[:, 0:1])
        k.act(lf.v, pf[0:4, 0:TTc], AF.Exp, bias=gbf[:, 0:1], scale=-1.0)
        k.act(lf.v, lf.v, AF.Ln, bias=cst[0:4, 0:1])
        k.ts(lf.v, lf.v, -1.0, ALU.mult)
        for (g0, gl) in tl["groups"]:
            k.scan(b[:, g0:g0 + gl], lf[:, g0:g0 + gl])
        k.tt(a.v, igr.v, b.v, ALU.subtract)
        nb = k.sb("nb", [4, TTc])
        k.ts(nb.v, b.v, -1.0, ALU.mult)
        nch = len(chunks)
        rr = k.sb("rr", [4, nch + 16])
        nrr = k.sb("nrr", [4, nch + 16])
        mprev = k.sb("mprev", [4, nch + 16])
        am = k.sb("am", [4, nch + 16])
        wrow = k.sb("wrow", [4, nch + 16])
        for j, ch in enumerate(chunks):
            c0, c = ch["c0"], ch["c"]
            if ch["kind"] == "samp":
                av = a[:, c0:c0 + 64].re("p (s j) -> p s j", j=4)
                bv = b[:, c0:c0 + 64].re("p (s j) -> p s j", j=4)
                k.red(am[:, j:j + 16], av, ALU.max)
                m0t = k.sb("m0t", [16, 4])
                k.dma_in(m0t.v, sm[l])
                pm0 = k.ps()
                k.tr(pm0[0:4, 0:16], m0t.v, ident)
                k.cp(mprev[:, j:j + 16], pm0[0:4, 0:16])
                k.tt(rr[:, j:j + 16], mprev[:, j:j + 16], am[:, j:j + 16], ALU.max)
                mnew = k.sb("mnew", [4, 16])
                k.tt(mnew.v, rr[:, j:j + 16], bv[:, :, 3], ALU.add)
                mex = k.sb("mex", [4, 16, 4])
                k.tt(mex.v, mnew.v.unsq(2).bc([4, 16, 4]), ident[0:4, 0:4].unsq(1).bc([4, 16, 4]), ALU.mult)
                pmo = k.ps()
                k.mm(pmo[0:1, 0:64], ones[0:4, 0:1], mex.v.re("p a b -> p (a b)"))
                mrow = k.sb("mrow", [1, 64])
                k.cp(mrow.v, pmo[0:1, 0:64])
                k.dma_out(om[l].rearrange("s h -> (s h)").rearrange("(o n) -> o n", o=1), mrow.v)
                k.ts(nrr[:, j:j + 16], rr[:, j:j + 16], -1.0, ALU.mult)
                k.tt(Er[:, c0:c0 + 64].re("p (s j) -> p s j", j=4), av, rr[:, j:j + 16].unsq(2).bc([4, 16, 4]), ALU.subtract)
                k.tt(Th[:, c0:c0 + 64].re("p (s j) -> p s j", j=4), nb[:, c0:c0 + 64].re("p (s j) -> p s j", j=4),
                     rr[:, j:j + 16].unsq(2).bc([4, 16, 4]), ALU.subtract)
                k.act(Er[:, c0:c0 + 64], Er[:, c0:c0 + 64], AF.Exp)
                k.act(Th[:, c0:c0 + 64], Th[:, c0:c0 + 64], AF.Exp)
                k.tt(wrow[:, j:j + 16], mprev[:, j:j + 16], rr[:, j:j + 16], ALU.subtract)
            else:
                k.red(am[:, j:j + 1], a[:, c0:c0 + c], ALU.max)
                k.cp(mprev[:, j:j + 1], mcur.v)
                k.tt(rr[:, j:j + 1], mcur.v, am[:, j:j + 1], ALU.max)
                k.tt(mcur.v, rr[:, j:j + 1], b[:, c0 + c - 1:c0 + c], ALU.add)
                k.ts(nrr[:, j:j + 1], rr[:, j:j + 1], -1.0, ALU.mult)
                k.act(Er[:, c0:c0 + c], a[:, c0:c0 + c], AF.Exp, bias=nrr[:, j:j + 1])
                k.act(Th[:, c0:c0 + c], nb[:, c0:c0 + c], AF.Exp, bias=nrr[:, j:j + 1])
                k.tt(wrow[:, j:j + 1], mprev[:, j:j + 1], rr[:, j:j + 1], ALU.subtract)
        nw = nch + 15 if chunks[-1]["kind"] == "samp" else nch
        k.act(wrow[:, 0:nw], wrow[:, 0:nw], AF.Exp)
        wbc = bcast_rows(wrow[:, 0:nw], nw, "wbc")
        tap("Er%d" % l, Er.v)
        tap("Th%d" % l, Th.v)
        for j, ch in enumerate(chunks):
            c0, c, kind = ch["c0"], ch["c"], ch["kind"]
            m01 = MASKS["b" if kind == "samp" else "c"][0]
            with k.scope():
                sc = rows_to_tm([Er.v, Th.v], c0, c, "scm")
                vE = k.sb("vE", [CP, 4, 129])
                k.tt(vE[0:c], va[j][0:c], sc[0:c, 0:4].unsq(2).bc([c, 4, 129]), ALU.mult)
                if kind != "samp":
                    offs = [(h // 2) * 512 + (h % 2) * 129 for h in range(4)]
                    psts = []
                    for h in range(4):
                        pst = k.ps()
                        k.mm(pst[0:c, 0:c], kT[:, h, c0:c0 + c], qT[:, h, c0:c0 + c])
                        psts.append(pst)
                    STl, Chl = [], []
                    for h in range(4):
                        STs = k.sb("STs", [CP, CP])
                        k.stt(STs[0:c, 0:c], psts[h][0:c, 0:c], sc[0:c, h:h + 1], m01[0:c, 0:c], ALU.mult, ALU.mult)
                        STl.append(STs)
                        Ch = k.sb("Ch", [128, 129])
                        k.ts(Ch.v, Um[:, h, :], wbc[:, j, h:h + 1], ALU.mult)
                        Chl.append(Ch)
                    for h in range(4):
                        k.mm(psd[0:c, offs[h]:offs[h] + 129], STl[h][0:c, 0:c], va[j][0:c, h, :], start=True, stop=False)
                        k.mm(psd[0:c, offs[h]:offs[h] + 129], qT[:, h, c0:c0 + c], Chl[h].v, start=False, stop=True)
                    pps = []
                    for h in range(4):
                        pp = k.ps()
                        k.mm(pp[:, 0:129], kc[j][0:c, h, :], vE[0:c, h, :])
                        pps.append(pp)
                    for h in range(4):
                        k.tt(Um[:, h, :], Chl[h].v, pps[h][:, 0:129], ALU.add)
                for h in (range(4) if kind == "samp" else []):
                    k.push()
                    off = (h // 2) * 512 + (h % 2) * 129
                    pst = k.ps()
                    k.mm(pst[0:c, 0:c], kT[:, h, c0:c0 + c], qT[:, h, c0:c0 + c])
                    STs = k.sb("STs", [CP, CP])
                    k.stt(STs[0:c, 0:c], pst[0:c, 0:c], sc[0:c, h:h + 1], m01[0:c, 0:c], ALU.mult, ALU.mult)
                    if kind != "samp":
                        pass
                    else:
                        qTm = k.sb("qTm", [128, 16, 64])
                        k.memset(qTm.v, 0.0)
                        for i in range(16):
                            k.cp(qTm[:, i, 4 * i:4 * i + 4], qT[:, h, c0 + 4 * i:c0 + 4 * i + 4])
                        k.mm(psd[0:c, off:off + 129], STs[0:c, 0:c], va[j][0:c, h, :], start=True, stop=False)
                        for g in range(4):
                            with k.scope():
                                Cg = k.sb("Cg", [128, 4, 129])
                                Cst = k.sb("Cst", [128, 4, 128])
                                k.dma_in(Cst.v, sC[l].rearrange("(s h) d v -> h d s v", h=4)[h][:, 4 * g:4 * g + 4, :])
                                k.cp(Cg[:, :, 0:128], Cst.v, eng="act")
                                nrow = k.sb("nrow", [4, 128])
                                k.dma_in(nrow.v, sn[l].rearrange("(s h) d -> h s d", h=4)[h][4 * g:4 * g + 4, :])
                                pn_ = k.ps()
                                k.tr(pn_[:, 0:4], nrow.v, ident)
                                k.cp(Cg[:, :, 128], pn_[:, 0:4])
                                wv = wbc[:, j + 4 * g:j + 4 * g + 4, h:h + 1]
                                k.tt(Cg.v, Cg.v, wv.bc([128, 4, 129]), ALU.mult)
                                for ii in range(4):
                                    i = 4 * g + ii
                                    k.mm(psd[0:c, off:off + 129], qTm[:, i, :], Cg[:, ii, :], start=False,
                                         stop=(i == 15))
                                vEx = k.sb("vEx", [64, 4, 129])
                                k.tt(vEx.v, vE[0:64, h, :].unsq(1).bc([64, 4, 129]),
                                     ind[:, 4 * g:4 * g + 4].unsq(2).bc([64, 4, 129]), ALU.mult)
                                vf = vEx.v.re("p a b -> p (a b)")
                                pu = k.ps()
                                pu2 = k.ps()
                                k.mm(pu[:, 0:512], kc[j][0:64, h, :], vf[:, 0:512])
                                k.mm(pu2[:, 0:4], kc[j][0:64, h, :], vf[:, 512:516])
                                Cf = Cg.v.re("p a b -> p (a b)")
                                k.tt(Cf[:, 0:512], Cf[:, 0:512], pu[:, 0:512], ALU.add)
                                k.tt(Cf[:, 512:516], Cf[:, 512:516], pu2[:, 0:4], ALU.add)
                                k.dma_out(oC[l].rearrange("(s h) d v -> h d s v", h=4)[h][:, 4 * g:4 * g + 4, :], Cg[:, :, 0:128])
                                pn2 = k.ps()
                                k.tr(pn2[0:4, 0:128], Cg[:, :, 128], ident)
                                nout = k.sb("nout", [4, 128])
                                k.cp(nout.v, pn2[0:4, 0:128])
                                k.dma_out(on[l].rearrange("(s h) d -> h s d", h=4)[h][4 * g:4 * g + 4, :], nout.v)
                    k.pop()
                numv = psd[0:c, :].re("p (a r) -> p a r", a=2)[:, :, 0:258].re("p a (h q) -> p a h q", h=2)
                den = k.sb("den", [CP, 2, 2])
                k.cp(den[0:c], numv[:, :, :, 128])
                dn = k.sb("dn", [CP, 4])
                denf = den[0:c].re("p a b -> p (a b)")
                k.ts(dn[0:c], denf, -1.0, ALU.mult)
                k.tt(dn[0:c], dn[0:c], denf, ALU.max)
                k.tt(dn[0:c], dn[0:c], sc[0:c, 4:8], ALU.max)
                rden = k.sb("rden", [CP, 4])
                k.op("dve", lambda e: e.reciprocal(out=rden.ap[0:c], in_=dn.ap[0:c]), reads=[dn], writes=[rden])
                st = k.sb("hst", [CP, 4, 6])
                mv = k.sb("hmv", [CP, 4, 2])
                for h in range(4):
                    off = (h // 2) * 512 + (h % 2) * 129
                    k.op("dve", lambda e, h=h, off=off: e.bn_stats(out=st.ap[0:c, h, :], in_=psd.ap[0:c, off:off + 128]),
                         reads=[psd], writes=[st])
                for h in range(4):
                    k.op("dve", lambda e, h=h: e.bn_aggr(out=mv.ap[0:c, h, :], in_=st.ap[0:c, h, :]), reads=[st], writes=[mv])
                t1 = k.sb("t1", [CP, 4])
                k.tt(t1[0:c], rden[0:c], rden[0:c], ALU.mult)
                k.tt(t1[0:c], t1[0:c], mv[0:c, :, 1], ALU.mult)
                rsq = k.sb("rsq", [CP, 4])
                rsqrt_small(rsq[0:c], t1[0:c], NORM_EPS)
                k.tt(rsq[0:c], rsq[0:c], rden[0:c], ALU.mult)
                hA = k.sb("hA", [CP, 512])
                for h in range(4):
                    off = (h // 2) * 512 + (h % 2) * 129
                    k.ts(hA[0:c, h * 128:(h + 1) * 128], psd[0:c, off:off + 128], mv[0:c, h, 0:1], ALU.subtract,
                         rsq[0:c, h:h + 1], ALU.mult)
                k.tt(hA[0:c], hA[0:c], ow[j][0:c], ALU.mult)
                pt = k.ps()
                for h in range(4):
                    k.tr(pt[:, h * c:(h + 1) * c], hA[0:c, h * 128:(h + 1) * 128], ident)
                k.cp(hAT[:, :, c0:c0 + c], pt[:, 0:4 * c].re("p (h t) -> p h t", h=4), eng="act")

    def gdn_phase(l, tl, xT, hBT):
        P = LP[l]
        TTc = tl["TT"]
        chunks = tl["chunks"]
        CP = max(ch["c"] for ch in chunks)
        S, cg, cwg = P["S"], P["cg"], P["cwg"]
        gbi, gbf, gal, gdt = P["gb"]
        qkvT = k.sb("qkvT", [128, 12, TTc])
        k.push()
        pools = None if tl["first"] else (RPool("cext", [128, TTc + 3], 3), RPool("cacc", [128, TTc], 3))
        sqp = RPool("csq", [128, TTc], 2)
        for blk in range(3):
            wb = wload(w_in[l][:, B_QKV + 512 * blk:B_QKV + 512 * (blk + 1)], 8, 512)
            for q4 in range(4):
                cc = blk * 4 + q4
                p = proj_fm(wb, q4 * 128, xT, TTc)
                with k.scope():
                    acc = conv_fm(p, cc, 4, cg, cwg, sgc[l], ogc[l], tl, "c", pools)
                    k.act(qkvT[:, cc, :], acc.v, AF.Silu)
                    if cc < 8:
                        sq = sqp.get()
                        k.tt(sq.v, qkvT[:, cc, :], qkvT[:, cc, :], ALU.mult, eng=("dve" if tl["first"] else "pool"))
                        pq = k.ps()
                        k.mm(pq[:, 0:TTc], ones.v, sq.v)
                        k.act(sq.v, pq[:, 0:TTc], AF.Ln, bias=cst[:, 1:2])
                        k.act(sq.v, sq.v, AF.Exp, scale=-0.5, bias=(cst[:, 2:3] if cc < 4 else cst[:, 3:4]))
                        k.tt(qkvT[:, cc, :], qkvT[:, cc, :], sq.v, ALU.mult)
        k.pop()
        wb = wload(w_in[l][:, B_Z:B_Z + 512], 8, 512)
        wz = [k.sb("wz", [CP, 4, 128]) for _ in chunks]
        for (g0, gn, mem) in tl["pgroups"]:
            p = proj_tm(wb, 512, xT, g0, gn)
            for (ci, r0, c) in mem:
                k.act(wz[ci][0:c].re("p h d -> p (h d)"), p[r0:r0 + c, :], AF.Silu)
                k.tt(wz[ci][0:c], wz[ci][0:c], P["gnw"][0:c, :].unsq(1).bc([c, 4, 128]), ALU.mult)
        wgt = wload(w_in[l][:, B_BETA - 248:B_BETA + 8], 8, 256)
        pbt = k.ps()
        for kk in range(8):
            k.mm(pbt[0:4, 0:TTc], wgt[:, kk, 248:252], xT[:, kk, 0:TTc], start=(kk == 0), stop=(kk == 7))
        pa_ = k.ps()
        for kk in range(8):
            k.mm(pa_[0:4, 0:TTc], wgt[:, kk, 252:256], xT[:, kk, 0:TTc], start=(kk == 0), stop=(kk == 7))
        spb = k.sb("spb", [4, TTc])
        k.act(spb.v, pbt[0:4, 0:TTc], AF.Exp, scale=-1.0)
        k.act(spb.v, spb.v, AF.Ln, bias=cst[0:4, 0:1])
        g = k.sb("g", [4, TTc])
        k.act(g.v, pa_[0:4, 0:TTc], AF.Exp, bias=gdt[:, 0:1])
        k.act(g.v, g.v, AF.Ln, bias=cst[0:4, 0:1])
        k.ts(g.v, g.v, gal[:, 0:1], ALU.mult)
        G = k.sb("G", [4, TTc])
        for (g0, gl) in tl["groups"]:
            k.scan(G[:, g0:g0 + gl], g[:, g0:g0 + gl])
        Gb = k.sb("Gb", [4, TTc])
        k.tt(Gb.v, G.v, spb.v, ALU.subtract)
        nG = k.sb("nG", [4, TTc])
        k.ts(nG.v, G.v, -1.0, ALU.mult)
        r_beta = k.sb("r_beta", [4, TTc])
        k.act(r_beta.v, spb.v, AF.Exp, scale=-1.0)
        r_bg = k.sb("r_bg", [4, TTc])
        k.act(r_bg.v, Gb.v, AF.Exp)
        r_eg = k.sb("r_eg", [4, TTc])
        k.act(r_eg.v, G.v, AF.Exp)
        r_kd = k.sb("r_kd", [4, TTc])
        nch = len(chunks)
        ge = k.sb("ge", [4, nch + 16])
        for j, ch in enumerate(chunks):
            c0, c = ch["c0"], ch["c"]
            if ch["kind"] == "samp":
                Gv = G[:, c0:c0 + 64].re("p (s j) -> p s j", j=4)
                k.cp(ge[:, j:j + 16], Gv[:, :, 3])
                k.tt(r_kd[:, c0:c0 + 64].re("p (s j) -> p s j", j=4), nG[:, c0:c0 + 64].re("p (s j) -> p s j", j=4),
                     ge[:, j:j + 16].unsq(2).bc([4, 16, 4]), ALU.add)
            else:
                k.cp(ge[:, j:j + 1], G[:, c0 + c - 1:c0 + c])
                k.ts(r_kd[:, c0:c0 + c], nG[:, c0:c0 + c], ge[:, j:j + 1], ALU.add)
        k.act(r_kd.v, r_kd.v, AF.Exp)
        nw = nch + 15 if chunks[-1]["kind"] == "samp" else nch
        k.act(ge[:, 0:nw], ge[:, 0:nw], AF.Exp)
        gbc = bcast_rows(ge[:, 0:nw], nw, "gbc")
        for j, ch in enumerate(chunks):
            c0, c, kind = ch["c0"], ch["c"], ch["kind"]
            _, m_st, m_stT, m_inT = MASKS["b" if kind == "samp" else "c"]
            nsq = {128: 6, 64: 5, 16: 3}[c] if kind != "samp" else 1
            with k.scope():
                sc = rows_to_tm([r_beta.v, r_bg.v, r_eg.v, r_kd.v], c0, c, "scg")
                rv = k.sb("rv", [CP, 4, 128])
                rk = k.sb("rk", [CP, 4, 128])
                kd = k.sb("kd", [CP, 4, 128])
                k.push()
                kcn = k.sb("kcn", [CP, 4, 128])
                vcn = k.sb("vcn", [CP, 4, 128])
                p = k.ps()
                for h in range(4):
                    k.tr(p[0:c, h * 128:(h + 1) * 128], qkvT[:, 4 + h, c0:c0 + c], ident)
                k.cp(kcn[0:c].re("p h d -> p (h d)"), p[0:c, :], eng="act")
                p = k.ps()
                for h in range(4):
                    k.tr(p[0:c, h * 128:(h + 1) * 128], qkvT[:, 8 + h, c0:c0 + c], ident)
                k.cp(vcn[0:c].re("p h d -> p (h d)"), p[0:c, :])
                k.tt(rv[0:c], vcn[0:c], sc[0:c, 0:4].unsq(2).bc([c, 4, 128]), ALU.mult)
                k.tt(rk[0:c], kcn[0:c], sc[0:c, 4:8].unsq(2).bc([c, 4, 128]), ALU.mult)
                k.tt(kd[0:c], kcn[0:c], sc[0:c, 12:16].unsq(2).bc([c, 4, 128]), ALU.mult)
                k.pop()
                ob = k.sb("ob", [CP, 4, 128])

                def head(h):
                    if kind == "samp":
                        k.push()
                    qh = qkvT[:, h, c0:c0 + c]
                    kh = qkvT[:, 4 + h, c0:c0 + c]
                    pe_ = k.ps()
                    k.mm(pe_[0:c, 0:c], Gb[:, c0:c0 + c], oh[:, h, 0:c], start=True, stop=False)
                    k.mm(pe_[0:c, 0:c], oh[:, h, 0:c], nG[:, c0:c0 + c], start=False, stop=False)
                    k.mm(pe_[0:c, 0:c], ident[0:c, 0:c], m_st[0:c, 0:c], start=False, stop=True)
                    k.mm(pe_[0:c, c:2 * c], oh[:, h, 0:c], Gb[:, c0:c0 + c], start=True, stop=False)
                    k.mm(pe_[0:c, c:2 * c], nG[:, c0:c0 + c], oh[:, h, 0:c], start=False, stop=False)
                    k.mm(pe_[0:c, c:2 * c], ident[0:c, 0:c], m_stT[0:c, 0:c], start=False, stop=True)
                    k.mm(pe_[0:c, 2 * c:3 * c], oh[:, h, 0:c], G[:, c0:c0 + c], start=True, stop=False)
                    k.mm(pe_[0:c, 2 * c:3 * c], nG[:, c0:c0 + c], oh[:, h, 0:c], start=False, stop=False)
                    k.mm(pe_[0:c, 2 * c:3 * c], ident[0:c, 0:c], m_inT[0:c, 0:c], start=False, stop=True)
                    Ex = k.sb("Ex", [CP, 3 * CP])
                    k.act(Ex[0:c, 0:3 * c], pe_[0:c, 0:3 * c], AF.Exp)
                    yield
                    psc = k.ps()
                    k.mm(psc[0:c, 0:c], kh, kh)
                    k.mm(psc[0:c, c:2 * c], kh, kh)
                    k.mm(psc[0:c, 2 * c:3 * c], kh, qh)
                    M3 = Ex
                    k.tt(M3[0:c, 0:3 * c], psc[0:c, 0:3 * c], Ex[0:c, 0:3 * c], ALU.mult)
                    TTs = [k.sb("TTm", [CP, CP]), k.sb("TTm", [CP, CP])]
                    P2s = [k.sb("P2", [CP, 2 * CP]), k.sb("P2", [CP, 2 * CP])]
                    TTm = TTs[0]
                    k.tt(TTm[0:c, 0:c], ident[0:c, 0:c], M3[0:c, c:2 * c], ALU.subtract)
                    yield
                    Pw = M3
                    for q in range(nsq):
                        pp = k.ps()
                        k.mm(pp[0:c, 0:c], Pw[0:c, c:2 * c], Pw[0:c, 0:c])
                        P2 = P2s[q % 2]
                        if q < nsq - 1:
                            k.mm(pp[0:c, c:2 * c], Pw[0:c, 0:c], Pw[0:c, c:2 * c])
                            k.cp(P2[0:c, 0:2 * c], pp[0:c, 0:2 * c], eng="act")
                        else:
                            k.cp(P2[0:c, 0:c], pp[0:c, 0:c], eng="act")
                        yield
                        pt_ = k.ps()
                        k.mm(pt_[0:c, 0:c], P2[0:c, 0:c], TTm[0:c, 0:c])
                        Tn = TTs[(q + 1) % 2]
                        k.tt(Tn[0:c, 0:c], TTm[0:c, 0:c], pt_[0:c, 0:c], ALU.add)
                        TTm = Tn
                        Pw = P2
                        yield
                    pw = k.ps()
                    k.mm(pw[:, 0:c], rk[0:c, h, :], TTm[0:c, 0:c])
                    WTn = k.sb("WTn", [128, CP])
                    k.ts(WTn[:, 0:c], pw[:, 0:c], -1.0, ALU.mult)
                    yield
                    pu = k.ps()
                    po = k.ps()
                    if kind != "samp":
                        k.mm(pu[0:c, 0:128], TTm[0:c, 0:c], rv[0:c, h, :], start=True, stop=False)
                        k.mm(pu[0:c, 0:128], WTn[:, 0:c], S[:, h, :], start=False, stop=True)
                        Us = P2s[1][:, 0:128]
                        k.cp(Us[0:c], pu[0:c, 0:128], eng="act")
                        k.mm(po[0:c, 0:128], qh, S[:, h, :])
                        o1 = P2s[0][:, 0:128]
                        k.act(o1[0:c], po[0:c, 0:128], AF.Identity, scale=sc[0:c, 8 + h:9 + h])
                        yield
                        po2 = k.ps()
                        k.mm(po2[0:c, 0:128], M3[0:c, 2 * c:3 * c], Us[0:c])
                        k.tt(ob[0:c, h, :], o1[0:c], po2[0:c, 0:128], ALU.add)
                        pS_ = k.ps()
                        k.mm(pS_[:, 0:128], kd[0:c, h, :], Us[0:c])
                        k.stt(S[:, h, :], S[:, h, :], gbc[:, j, h:h + 1], pS_[:, 0:128], ALU.mult, ALU.add)
                    else:
                        WTm = k.sb("WTm", [128, 16, 64])
                        qTm = k.sb("qTm", [128, 16, 64])
                        k.memset(WTm.v, 0.0)
                        k.memset(qTm.v, 0.0)
                        for i in range(16):
                            k.cp(WTm[:, i, 4 * i:4 * i + 4], WTn[:, 4 * i:4 * i + 4])
                            k.cp(qTm[:, i, 4 * i:4 * i + 4], qkvT[:, h, c0 + 4 * i:c0 + 4 * i + 4], eng="act")
                        Sg = []
                        for gq in range(4):
                            t = k.sb("Sg", [128, 4, 128])
                            k.dma_in(t.v, sS[l].rearrange("(s h) d v -> h d s v", h=4)[h][:, 4 * gq:4 * gq + 4, :])
                            Sg.append(t)
                        k.mm(pu[0:c, 0:128], TTm[0:c, 0:c], rv[0:c, h, :], start=True, stop=False)
                        for i in range(16):
                            k.mm(pu[0:c, 0:128], WTm[:, i, :], Sg[i // 4][:, i % 4, :], start=False, stop=(i == 15))
                        Us = k.sb("Us", [CP, 128])
                        k.cp(Us[0:c], pu[0:c, 0:128], eng="act")
                        for i in range(16):
                            k.mm(po[0:c, 0:128], qTm[:, i, :], Sg[i // 4][:, i % 4, :], start=(i == 0), stop=(i == 15))
                        o1 = k.sb("o1", [CP, 128])
                        k.act(o1[0:c], po[0:c, 0:128], AF.Identity, scale=sc[0:c, 8 + h:9 + h])
                        po2 = k.ps()
                        k.mm(po2[0:c, 0:128], M3[0:c, 2 * c:3 * c], Us[0:c])
                        k.tt(ob[0:c, h, :], o1[0:c], po2[0:c, 0:128], ALU.add)
                        for gq in range(4):
                            Ux = k.sb("Ux", [64, 4, 128])
                            k.tt(Ux.v, Us.v.unsq(1).bc([64, 4, 128]), ind[:, 4 * gq:4 * gq + 4].unsq(2).bc([64, 4, 128]), ALU.mult)
                            pS_ = k.ps()
                            k.mm(pS_[:, 0:512], kd[0:64, h, :], Ux.v.re("p a b -> p (a b)"))
                            k.tt(Sg[gq].v, Sg[gq].v, gbc[:, j + 4 * gq:j + 4 * gq + 4, h:h + 1].bc([128, 4, 128]), ALU.mult)
                            k.tt(Sg[gq].v, Sg[gq].v, pS_[:, 0:512].re("p (a b) -> p a b", a=4), ALU.add)
                            k.dma_out(oS[l].rearrange("(s h) d v -> h d s v", h=4)[h][:, 4 * gq:4 * gq + 4, :], Sg[gq].v)
                    if kind == "samp":
                        k.pop()

                if kind == "samp":
                    for h in range(4):
                        for _ in head(h):
                            pass
                else:
                    interleave([head(h) for h in range(4)])
                sq = rv
                k.tt(sq[0:c], ob[0:c], ob[0:c], ALU.mult)
                ss = rk[:, 0, 0:4]
                k.red(ss[0:c], sq[0:c], ALU.add)
                k.ts(ss[0:c], ss[0:c], 1.0 / 128.0, ALU.mult)
                rs = kd[:, 0, 0:4]
                rsqrt_small(rs[0:c], ss[0:c], NORM_EPS)
                k.tt(ob[0:c], ob[0:c], rs[0:c].unsq(2).bc([c, 4, 128]), ALU.mult)
                k.tt(ob[0:c], ob[0:c], wz[j][0:c], ALU.mult)
                pt = k.ps()
                for h in range(4):
                    k.tr(pt[:, h * c:(h + 1) * c], ob[0:c, h, :], ident)
                k.cp(hBT[:, :, c0:c0 + c], pt[:, 0:4 * c].re("p (h t) -> p h t", h=4), eng="act")

    def ln_rows(src, n, g_fm, b_fm, dst32, dstB, c0, final=None):
        with k.scope():
            st = k.sb("lnst", [128, 12])
            mv = k.sb("lnmv", [128, 2])
            rs = k.sb("lnrs", [128, 1])
            xn = k.sb("lnxn", [128, 1024])
            k.op("dve", lambda e: e.bn_stats(out=st.ap[0:n, 0:6], in_=src.ap[:, 0:512]), reads=[src.t], writes=[st])
            k.op("dve", lambda e: e.bn_stats(out=st.ap[0:n, 6:12], in_=src.ap[:, 512:1024]), reads=[src.t], writes=[st])
            k.op("dve", lambda e: e.bn_aggr(out=mv.ap[0:n, :], in_=st.ap[0:n, :]), reads=[st], writes=[mv])
            rsqrt_small(rs[0:n, :], mv[0:n, 1:2], LN_EPS)
            k.ts(xn[0:n, :], src, mv[0:n, 0:1], ALU.subtract, rs[0:n, 0:1], ALU.mult)
            if final is not None:
                k.tt(xn[0:n, :], xn[0:n, :], g2bc[0:n, :], ALU.mult)
                k.tt(xn[0:n, :], xn[0:n, :], b2bc[0:n, :], ALU.add)
                for (dst, r0, r1) in final:
                    k.dma_out(dst, xn[r0:r1, :])
            else:
                for half in range(2):
                    p = k.ps()
                    for q in range(4):
                        cc = half * 4 + q
                        k.tr(p[:, q * n:(q + 1) * n], xn[0:n, cc * 128:(cc + 1) * 128], ident)
                    for q in range(4):
                        cc = half * 4 + q
                        if q % 2 == 0:
                            k.act(dst32[:, cc, c0:c0 + n], p[:, q * n:(q + 1) * n], AF.Identity,
                                  bias=b_fm[:, cc, 0:1], scale=g_fm[:, cc, 0:1])
                        else:
                            k.ts(dst32[:, cc, c0:c0 + n], p[:, q * n:(q + 1) * n], g_fm[:, cc, 0:1], ALU.mult,
                                 b_fm[:, cc, 0:1], ALU.add)
                k.cp(dstB[:, :, c0:c0 + n], dst32[:, :, c0:c0 + n], eng="act")

    def merge_phase(l, tl, xT, xT32, hAT, hBT, x1T32, x1T):
        P = LP[l]
        TTc = tl["TT"]
        yT = k.sb("yT", [128, 8, TTc], dt=BF16)
        for hh in range(2):
            wpp = wload_parts([(w_pa[l][:, hh * 512:(hh + 1) * 512], 4), (w_pb[l][:, hh * 512:(hh + 1) * 512], 4)], 512)
            wga = wload(w_in[l][:, G_MERGE + hh * 512:G_MERGE + (hh + 1) * 512], 8, 512)
            ga = []
            for q4 in range(4):
                p = proj_fm(wga, q4 * 128, xT, TTc)
                t = k.sb("ga", [128, TTc])
                k.act(t.v, p[:, 0:TTc], AF.Sigmoid)
                ppa = k.ps()
                for kk in range(4):
                    k.mm(ppa[:, 0:TTc], wpp[:, kk, q4 * 128:(q4 + 1) * 128], hAT[:, kk, :], start=(kk == 0), stop=(kk == 3))
                k.tt(t.v, t.v, ppa[:, 0:TTc], ALU.mult)
                ga.append(t)
            wgb = wload(w_in[l][:, G_MERGE + D + hh * 512:G_MERGE + D + (hh + 1) * 512], 8, 512)
            for q4 in range(4):
                n = hh * 4 + q4
                p = proj_fm(wgb, q4 * 128, xT, TTc)
                t = k.sb("gbm", [128, TTc])
                k.act(t.v, p[:, 0:TTc], AF.Sigmoid)
                ppb = k.ps()
                for kk in range(4):
                    k.mm(ppb[:, 0:TTc], wpp[:, 4 + kk, q4 * 128:(q4 + 1) * 128], hBT[:, kk, :], start=(kk == 0), stop=(kk == 3))
                k.tt(t.v, t.v, ppb[:, 0:TTc], ALU.mult)
                k.tt(yT[:, n, :], t.v, ga[q4].v, ALU.add)
        tgs = tl["tgs"]
        osb = [k.sb("osb", [128, 1024]) for _ in tgs]
        for hh in range(2):
            wb = wload(w_out[l][:, hh * 512:(hh + 1) * 512], 8, 512)
            for j, (c0, n) in enumerate(tgs):
                p = k.ps()
                for n8 in range(8):
                    k.mm(p[0:n, 0:512], yT[:, n8, c0:c0 + n], wb[:, n8, :], start=(n8 == 0), stop=False)
                for q4 in range(4):
                    k.mm(p[0:n, q4 * 128:(q4 + 1) * 128], xT32[:, hh * 4 + q4, c0:c0 + n], aident.v, start=False, stop=(q4 == 3))
                k.cp(osb[j][0:n, hh * 512:(hh + 1) * 512], p[0:n, 0:512], eng="act")
        for j, (c0, n) in enumerate(tgs):
            ln_rows(osb[j][0:n, :], n, P["g1"], P["b1"], x1T32, x1T, c0)

    def conv_fm(p, cc, nt, carry, cw, s_in, s_out, tl, tag, pools=None):
        TTc = tl["TT"]
        hh = nt - 1
        acc = pools[1].get() if pools is not None else k.sb(tag + "acc", [128, TTc])
        if tl["first"]:
            ext = k.sb(tag + "ext", [128, 16 + hh])
            k.cp(ext[:, 0:hh], carry[:, cc, :])
            k.cp(ext[:, hh:16 + hh], p[:, 0:16], eng="act")
            exs = k.sb(tag + "exs", [128, 16, 4 + hh])
            hist = k.sb(tag + "hist", [16 * hh, 128])
            k.dma_in(hist.v, s_in[:, cc * 128:(cc + 1) * 128])
            ph = k.ps()
            k.tr(ph[:, 0:16 * hh], hist.v, ident)
            k.cp(exs[:, :, 0:hh], ph[:, 0:16 * hh].re("p (s j) -> p s j", j=hh))
            k.cp(exs[:, :, hh:4 + hh], p[:, 16:80].re("p (s j) -> p s j", j=4), eng="act")
            k.cp(carry[:, cc, :], ext[:, 16:16 + hh])
            pc = k.ps()
            nst = k.sb(tag + "nst", [128, 16, hh])
            k.cp(nst.v, exs[:, :, 4:4 + hh])
            k.tr(pc[0:16 * hh, 0:128], nst.v.re("p s j -> p (s j)"), ident)
            ob = k.sb(tag + "ob", [16 * hh, 128])
            k.cp(ob.v, pc[0:16 * hh, 0:128])
            k.dma_out(s_out[:, cc * 128:(cc + 1) * 128], ob.v)
            av = acc[:, 0:16]
            k.ts(av, ext[:, 0:16], cw[:, cc, 0:1], ALU.mult)
            for jj in range(1, nt):
                k.stt(av, ext[:, jj:jj + 16], cw[:, cc, jj:jj + 1], av, ALU.mult, ALU.add)
            av = acc[:, 16:80].re("p (s j) -> p s j", j=4)
            k.ts(av, exs[:, :, 0:4], cw[:, cc, 0:1], ALU.mult)
            for jj in range(1, nt):
                k.stt(av, exs[:, :, jj:jj + 4], cw[:, cc, jj:jj + 1], av, ALU.mult, ALU.add)
        else:
            ext = pools[0].get()
            k.cp(ext[:, 0:hh], carry[:, cc, :], eng="pool")
            k.cp(ext[:, hh:TTc + hh], p[:, 0:TTc], eng="act")
            k.act(acc.v, p[:, 0:TTc], AF.Identity, scale=cw[:, cc, hh:hh + 1])
            k.cp(carry[:, cc, :], ext[:, TTc:TTc + hh], eng="pool")
            for jj in range(0, hh):
                k.stt(acc.v, ext[:, jj:jj + TTc], cw[:, cc, jj:jj + 1], acc.v, ALU.mult, ALU.add)
        return acc

    def ffn_phase(l, tl, x1T, x1T32, x2T32, x2T, last):
        P = LP[l]
        TTc = tl["TT"]
        cf, cwf = P["cf"], P["cwf"]
        hT = k.sb("hT", [128, 22, TTc], dt=BF16)
        k.push()
        pools = None if tl["first"] else (RPool("fext", [128, TTc + 2], 4), RPool("facc", [128, TTc], 4))
        for blk in range(6):
            ncols = 512 if blk < 5 else 256
            wb1 = wload(w_up[l][:, blk * 512:blk * 512 + ncols], 8, ncols)
            wb2 = wload(w_up[l][:, DFF + blk * 512:DFF + blk * 512 + ncols], 8, ncols)
            for q4 in range(ncols // 128):
                f = blk * 4 + q4
                with k.scope():
                    p1 = proj_fm(wb1, q4 * 128, x1T, TTc)
                    a1 = conv_fm(p1, f, 3, cf, cwf, sfc[l], ofc[l], tl, "f", pools)
                    k.act(a1.v, a1.v, AF.Silu)
                    p2 = proj_fm(wb2, q4 * 128, x1T, TTc)
                    a2 = conv_fm(p2, 22 + f, 3, cf, cwf, sfc[l], ofc[l], tl, "f", pools)
                    k.tt(hT[:, f, :], a1.v, a2.v, ALU.mult, eng=("dve" if tl["first"] else "pool"))
        k.pop()
        tgs = tl["tgs"]
        osb = [k.sb("osb2", [128, 1024]) for _ in tgs]
        kgs = [(0, 8), (8, 8), (16, 6)]
        for hh in range(2):
            pts = [k.ps() for _ in tgs]
            for (k0, nk) in kgs:
                wb = wload(w_down[l][k0 * 128:(k0 + nk) * 128, hh * 512:(hh + 1) * 512], nk, 512)
                for j, (c0, n) in enumerate(tgs):
                    for kk in range(nk):
                        k.mm(pts[j][0:n, 0:512], hT[:, k0 + kk, c0:c0 + n], wb[:, kk, :], start=(k0 + kk == 0), stop=False,
                             sig=(kk == nk - 1))
            for j, (c0, n) in enumerate(tgs):
                for q4 in range(4):
                    k.mm(pts[j][0:n, q4 * 128:(q4 + 1) * 128], x1T32[:, hh * 4 + q4, c0:c0 + n], aident.v, start=False, stop=(q4 == 3))
                k.cp(osb[j][0:n, hh * 512:(hh + 1) * 512], pts[j][0:n, 0:512], eng="act")
        for j, (c0, n) in enumerate(tgs):
            fin = None
            if last:
                if tl["first"]:
                    fin = [(ys[:, :], 16, 80)]
                else:
                    tok0 = tl["tok0"] + c0
                    fin = [(yp[tok0:tok0 + n, :], 0, n)]
            ln_rows(osb[j][0:n, :], n, P["g2"], P["b2"], x2T32, x2T, c0, final=fin)

    for ti, tl in enumerate(tiles):
        TTc = tl["TT"]
        cur["ti"] = ti
        cur["n"] = 0
        k.conservative = False
        with k.scope():
            xT32 = k.sb("xT32", [128, 8, TTc])
            xT = k.sb("xT", [128, 8, TTc], dt=BF16)
            for (c0, n) in tl["tgs"]:
                with k.scope():
                    xin = k.sb("xin", [128, 1024])
                    if tl["first"]:
                        k.dma_in(xin[0:16, :], meta[:, :])
                        k.dma_in(xin[16:80, :], xs[:, :])
                    else:
                        k.dma_in(xin[0:n, :], xp[tl["tok0"] + c0:tl["tok0"] + c0 + n, :])
                    ln_rows(xin[0:n, :], n, g_emb, b_emb, xT32, xT, c0)
            if ti == 0:
                tap("xT0", xT32.v)
                tap("xTb0", xT.v)
            xpairs = [(xT32, xT), (k.sb("xB32", [128, 8, TTc]), k.sb("xB", [128, 8, TTc], dt=BF16)),
                      (k.sb("xC32", [128, 8, TTc]), k.sb("xC", [128, 8, TTc], dt=BF16))]
            for l in range(DEPTH):
                (x1T32, x1T) = xpairs[1]
                (x2T32, x2T) = xpairs[2] if l % 2 == 0 else xpairs[0]
                with k.scope():
                    hAT = k.sb("hAT", [128, 4, TTc], dt=BF16)
                    hBT = k.sb("hBT", [128, 4, TTc], dt=BF16)
                    with k.scope():
                        mlstm_phase(l, tl, xT, hAT)
                    if ti == 0:
                        tap("hAT%d" % l, hAT.v)
                    with k.scope():
                        gdn_phase(l, tl, xT, hBT)
                    if ti == 0:
                        tap("hBT%d" % l, hBT.v)
                    with k.scope():
                        merge_phase(l, tl, xT, xT32, hAT, hBT, x1T32, x1T)
                if ti == 0:
                    tap("x1T%d" % l, x1T32.v)
                with k.scope():
                    ffn_phase(l, tl, x1T, x1T32, x2T32, x2T, last=(l == DEPTH - 1))
                if ti == 0 and l == 0:
                    tap("x2T%d" % l, x2T32.v)
                xT, xT32 = x2T, x2T32
    for l in range(DEPTH):
        P = LP[l]
        k.dma_out(pC[l].rearrange("h d v -> d h v"), P["Um"][:, :, 0:128])
        with k.scope():
            pt = k.ps()
            k.tr(pt[0:4, 0:128], P["Um"][:, :, 128], ident)
            nr = k.sb("pnr", [4, 128])
            k.cp(nr.v, pt[0:4, 0:128])
            k.dma_out(pn[l], nr.v)
        with k.scope():
            ptm = k.ps()
            k.tr(ptm[0:1, 0:4], P["mcur"].v, ident)
            mr = k.sb("pmr", [1, 4])
            k.cp(mr.v, ptm[0:1, 0:4])
            k.dma_out(pm[l].rearrange("h o -> o h"), mr.v)
        k.dma_out(pS[l].rearrange("h d v -> d h v"), P["S"].v)
        fm_to_rows(P["cg"].v, 3, 12, pgc[l])
        fm_to_rows(P["cf"].v, 2, 44, pfc[l])
    k.finish()
    return nc, tapouts


_CACHE = {}


def _prep_inputs(inp, c):
    f = lambda a: np.ascontiguousarray(np.asarray(a, dtype=np.float32))
    s0, s1 = NS * c, NS * (c + 1)
    m = {
        "xp": f(inp["x_prompt"][c]),
        "xs": f(inp["x_sample"][s0:s1]).reshape(64, D),
        "meta": f(inp["meta_tokens"]),
        "sC": f(inp["state_mlstm_C"][:, s0:s1]).reshape(DEPTH, NS * 4, 128, 128),
        "sn": f(inp["state_mlstm_n"][:, s0:s1]).reshape(DEPTH, NS * 4, 128),
        "sm": f(inp["state_mlstm_m"][:, s0:s1]),
        "sS": f(inp["state_gdn_S"][:, s0:s1]).reshape(DEPTH, NS * 4, 128, 128),
        "sgc": f(inp["state_gdn_conv"][:, s0:s1]).reshape(DEPTH, NS * 3, 1536),
        "sfc": f(inp["state_ffn_conv"][:, s0:s1]).reshape(DEPTH, NS * 2, 2 * DFF),
        "ln_emb_g": f(inp["ln_emb_g"]).reshape(1, D),
        "ln_emb_b": f(inp["ln_emb_b"]).reshape(1, D),
        "w_in": f(inp["w_in"]),
        "gate_bias": f(inp["mlstm_gate_bias"]).reshape(DEPTH, 8, 1),
        "mnorm_w": f(inp["mlstm_norm_w"]).reshape(DEPTH, 1, 512),
        "gconv_w": f(inp["gdn_conv_w"]),
        "A_log": f(inp["gdn_A_log"]).reshape(DEPTH, 4, 1),
        "dt_bias": f(inp["gdn_dt_bias"]).reshape(DEPTH, 4, 1),
        "gnorm_w": f(inp["gdn_norm_w"]).reshape(DEPTH, 1, 128),
        "w_pa": f(inp["w_branch_a"]),
        "w_pb": f(inp["w_branch_b"]),
        "w_out": f(inp["w_out"]),
        "ln1_g": f(inp["ln1_g"]).reshape(DEPTH, 1, D),
        "ln1_b": f(inp["ln1_b"]).reshape(DEPTH, 1, D),
        "w_up": f(inp["w_up"]),
        "fconv_w": f(inp["ffn_conv_w"]),
        "w_down": f(inp["w_down"]),
        "ln2_g": f(inp["ln2_g"]).reshape(DEPTH, 1, D),
        "ln2_b": f(inp["ln2_b"]).reshape(DEPTH, 1, D),
    }
    return m


def kernel(**inp):
    if "nc" not in _CACHE:
        _CACHE["nc"] = build()[0]
    nc = _CACHE["nc"]
    in_maps = [_prep_inputs(inp, c) for c in range(8)]
    res = run_bass_kernel_spmd(nc, in_maps, core_ids=list(range(8)))
    R = res.results
    st = lambda name: np.stack([np.asarray(R[c][name]) for c in range(8)], axis=1)
    cat = lambda name, shp: np.concatenate([np.asarray(R[c][name]).reshape(shp) for c in range(8)], axis=1)
    y_prompt = np.stack([np.asarray(R[c]["yp"]) for c in range(8)], axis=0)
    y_sample = np.concatenate([np.asarray(R[c]["ys"]).reshape(NS, 4, D) for c in range(8)], axis=0)
    outs = (
        y_prompt, y_sample,
        st("pC"), st("pn"), st("pm").reshape(DEPTH, 8, 4), st("pS"), st("pgc"), st("pfc"),
        cat("oC", (DEPTH, NS, 4, 128, 128)), cat("on", (DEPTH, NS, 4, 128)), cat("om", (DEPTH, NS, 4)),
        cat("oS", (DEPTH, NS, 4, 128, 128)), cat("ogc", (DEPTH, NS, 3, 1536)), cat("ofc", (DEPTH, NS, 2, 2 * DFF)),
    )
    return tuple(np.ascontiguousarray(o, dtype=np.float32) for o in outs)
```

```python
import math
from contextlib import ExitStack
import numpy as np
import concourse.bass as bass
import concourse.mybir as mybir
from concourse.bass_utils import run_bass_kernel_spmd

F32 = mybir.dt.float32
BF16 = mybir.dt.bfloat16
AF = mybir.ActivationFunctionType
ALU = mybir.AluOpType
AX = mybir.AxisListType

NEG = -30000.0
LN_EPS = 1e-5
NORM_EPS = 1e-6
ALPHA = 4.0 ** 0.25
DEPTH = 2
D = 1024
KC = 8
NIN = 6160
A_Q, A_K, A_V, A_O, A_I = 0, 512, 1024, 1536, 2048
B_QKV, B_Z, B_BETA, B_A, G_MERGE = 2056, 3592, 4104, 4108, 4112
DFF = 2816
NS = 16


class V:
    __slots__ = ("t", "ap")

    def __init__(self, t, ap):
        self.t = t
        self.ap = ap

    def __getitem__(self, idx):
        return V(self.t, self.ap[idx])

    def re(self, pat, **kw):
        return V(self.t, self.ap.rearrange(pat, **kw))

    def bc(self, shape):
        return V(self.t, self.ap.to_broadcast(list(shape)))

    def unsq(self, axis):
        return V(self.t, self.ap.unsqueeze(axis))


class T:
    __slots__ = ("ap", "lw", "rd", "dsem", "name", "a0", "a1")

    def __init__(self, name, ap, rd):
        self.name = name
        self.ap = ap
        self.lw = None
        self.rd = dict(rd)
        self.dsem = None

    def __getitem__(self, idx):
        return V(self, self.ap[idx])

    @property
    def v(self):
        return V(self, self.ap)


class KB:
    def __init__(self, nc):
        self.nc = nc
        self.root = ExitStack()
        self.stacks = [self.root]
        self.scope_tiles = [[]]
        self.engs = {"pe": nc.tensor, "act": nc.scalar, "dve": nc.vector, "pool": nc.gpsimd, "sp": nc.sync}
        self.semh = {}
        self.own = {}
        self.cnt = {}
        for e in ("pe", "act", "dve", "pool"):
            s = self.root.enter_context(nc.semaphore("s_" + e))
            self.semh["s_" + e] = s
            self.own[e] = "s_" + e
            self.cnt["s_" + e] = 0
        self.pending = {e: False for e in self.own}
        self.dmasems = set()
        self.free_dsems = []
        self.seen = {e: {} for e in self.engs}
        self.grave = {}
        self.grave_ranges = []
        self.conservative = True
        with nc.sbuf_tensor("probe0", [128, 8], F32) as h0:
            a_ = int(nc.lookup_mloc(h0).addr)
        fill = (512 - a_ % 512) % 512
        if fill:
            self.root.enter_context(nc.sbuf_tensor("fill0", [128, fill // 4], F32))
        self.nid = 0
        self.ps_tiles = []
        self.ps_i = 0
        self.wb_tiles = []
        self.wb_i = 0

    def sb(self, name, shape, persist=False, dt=F32):
        self.nid += 1
        nm = "%s_%d" % (name, self.nid)
        st = self.root if persist else self.stacks[-1]
        shape = list(shape)
        nfree = 1
        for d_ in shape[1:]:
            nfree *= d_
        esz = 2 if dt == BF16 else 4
        per = 512 // esz
        npad = ((nfree + per - 1) // per) * per
        h = st.enter_context(self.nc.sbuf_tensor(nm, [shape[0], npad], dt))
        ml = self.nc.lookup_mloc(h)
        a0 = int(ml.addr)
        a1 = a0 + int(ml.dims[1])
        inh = {}
        keep = []
        for (g0, g1, evs) in self.grave_ranges:
            if g0 < a1 and a0 < g1:
                for sn, v in evs.items():
                    if inh.get(sn, 0) < v:
                        inh[sn] = v
                if a0 <= g0 and g1 <= a1:
                    continue
            keep.append((g0, g1, evs))
        self.grave_ranges = keep
        if self.conservative:
            for sn, v in self.grave.items():
                if inh.get(sn, 0) < v:
                    inh[sn] = v
        ap = h[:, 0:nfree]
        if len(shape) == 3:
            ap = ap.rearrange("p (a b) -> p a b", a=shape[1])
        elif len(shape) == 4:
            ap = ap.rearrange("p (a b c) -> p a b c", a=shape[1], b=shape[2])
        t = T(nm, ap, inh)
        t.a0, t.a1 = a0, a1
        self.min_rem = min(getattr(self, "min_rem", 1 << 30), self.nc.sbuf_bytes_remaining)
        if not persist:
            self.scope_tiles[-1].append(t)
        return t

    def psum(self, name, shape):
        h = self.root.enter_context(self.nc.psum_tensor(name, list(shape), F32))
        return T(name, h[:], {})

    def scope(self):
        kb = self

        class _S:
            def __enter__(s):
                kb.stacks.append(ExitStack())
                kb.scope_tiles.append([])

            def __exit__(s, *a):
                kb._free_tiles(kb.scope_tiles.pop())
                kb.stacks.pop().close()
                return False

        return _S()

    def push(self):
        self.stacks.append(ExitStack())
        self.scope_tiles.append([])

    def _free_tiles(self, tiles):
        for t in tiles:
            evs = dict(t.rd)
            if t.lw is not None and evs.get(t.lw[0], 0) < t.lw[1]:
                evs[t.lw[0]] = t.lw[1]
            if t.dsem is not None:
                if evs.get(t.dsem, 0) < self.cnt[t.dsem]:
                    evs[t.dsem] = self.cnt[t.dsem]
                self.free_dsems.append(t.dsem)
                t.dsem = None
            if evs:
                self.grave_ranges.append((t.a0, t.a1, evs))
                for sn, v in evs.items():
                    if self.grave.get(sn, 0) < v:
                        self.grave[sn] = v

    def pop(self):
        self._free_tiles(self.scope_tiles.pop())
        self.stacks.pop().close()

    def _g(self, ev):
        s, v = ev
        if self.grave.get(s, 0) < v:
            self.grave[s] = v

    def ps(self):
        t = self.ps_tiles[self.ps_i % len(self.ps_tiles)]
        self.ps_i += 1
        return t

    def wbuf(self):
        t = self.wb_tiles[self.wb_i % len(self.wb_tiles)]
        self.wb_i += 1
        return t

    def op(self, eng, fn, reads=(), writes=(), sig=True, dma_tile=None):
        e = self.engs[eng]
        is_load = dma_tile is not None and (dma_tile in writes) and dma_tile.lw is not None and dma_tile.lw[0] == dma_tile.dsem and not dma_tile.rd
        deps = {}

        def add(ev):
            if ev is None:
                return
            s, v = ev
            if deps.get(s, 0) < v:
                deps[s] = v

        for t in reads:
            add(t.lw)
        for t in writes:
            add(t.lw)
            for s, v in t.rd.items():
                add((s, v))
        own = self.own.get(eng) if dma_tile is None else None
        seen = self.seen[eng]
        for s, v in deps.items():
            if eng == "pe" and s == own:
                continue
            if dma_tile is not None and s == dma_tile.dsem and is_load:
                continue
            if s in self.dmasems:
                v = self.cnt[s]
            if seen.get(s, 0) >= v:
                continue
            e.wait_ge(self.semh[s], v)
            seen[s] = v
        if dma_tile is not None and dma_tile.dsem is None:
            if self.free_dsems:
                nm = self.free_dsems.pop(0)
                if seen.get(nm, 0) < self.cnt[nm]:
                    e.wait_ge(self.semh[nm], self.cnt[nm])
                    seen[nm] = self.cnt[nm]
            else:
                nm = "d_%d" % len(self.dmasems)
                self.semh[nm] = self.root.enter_context(self.nc.semaphore(nm))
                self.cnt[nm] = 0
                self.dmasems.add(nm)
            dma_tile.dsem = nm
        inst = fn(e)
        if dma_tile is not None:
            s = dma_tile.dsem
            inst.then_inc(self.semh[s], 16)
            self.cnt[s] += 16
            ev = (s, self.cnt[s])
        else:
            if sig:
                inst.then_inc(self.semh[own], 1)
                self.cnt[own] += 1
                ev = (own, self.cnt[own])
                self.pending[eng] = False
            else:
                ev = (own, self.cnt[own] + 1)
                self.pending[eng] = True
        for t in writes:
            t.lw = ev
            t.rd = {}
        for t in reads:
            if t.rd.get(ev[0], 0) < ev[1]:
                t.rd[ev[0]] = ev[1]
        return inst

    def finish(self):
        assert not any(self.pending.values())
        sp = self.engs["sp"]
        for s in sorted(self.dmasems):
            if self.cnt[s] > self.seen["sp"].get(s, 0):
                sp.wait_ge(self.semh[s], self.cnt[s])
        for e in ("pe", "act", "dve", "pool"):
            s = self.own[e]
            if self.cnt[s] > 0:
                sp.wait_ge(self.semh[s], self.cnt[s])
        self.root.close()

    def mm(self, out, lhsT, rhs, start=True, stop=True, sig=None):
        self.op("pe", lambda e: e.matmul(out.ap, lhsT=lhsT.ap, rhs=rhs.ap, start=start, stop=stop),
                reads=[lhsT.t, rhs.t], writes=[out.t], sig=(stop if sig is None else sig))

    def tr(self, out, in_, ident):
        if isinstance(ident, T):
            ident = ident.v
        r = in_.ap.shape[0]
        self.op("pe", lambda e: e.transpose(out=out.ap, in_=in_.ap, identity=ident.ap[0:r, 0:r]),
                reads=[in_.t, ident.t], writes=[out.t])

    def act(self, out, in_, func, bias=None, scale=None, eng="act"):
        kw = {}
        rd = [in_.t]
        if bias is not None:
            if isinstance(bias, V):
                kw["bias"] = bias.ap
                rd.append(bias.t)
            else:
                kw["bias"] = bias
        if scale is not None:
            if isinstance(scale, V):
                kw["scale"] = scale.ap
                rd.append(scale.t)
            else:
                kw["scale"] = scale
        self.op("act", lambda e: e.activation(out=out.ap, in_=in_.ap, func=func, **kw), reads=rd, writes=[out.t])

    def ts(self, out, in0, s1, op0, s2=None, op1=None, eng="dve"):
        rd = [in0.t]
        a1 = s1
        a2 = s2
        if isinstance(s1, V):
            rd.append(s1.t)
            a1 = s1.ap
        if isinstance(s2, V):
            rd.append(s2.t)
            a2 = s2.ap
        if op1 is None:
            self.op(eng, lambda e: e.tensor_scalar(out=out.ap, in0=in0.ap, scalar1=a1, scalar2=None, op0=op0),
                    reads=rd, writes=[out.t])
        else:
            self.op(eng, lambda e: e.tensor_scalar(out=out.ap, in0=in0.ap, scalar1=a1, scalar2=a2, op0=op0, op1=op1),
                    reads=rd, writes=[out.t])

    def tt(self, out, in0, in1, op, eng="dve"):
        self.op(eng, lambda e: e.tensor_tensor(out=out.ap, in0=in0.ap, in1=in1.ap, op=op),
                reads=[in0.t, in1.t], writes=[out.t])

    def stt(self, out, in0, scalar, in1, op0, op1, eng="dve"):
        rd = [in0.t, in1.t]
        a = scalar
        if isinstance(scalar, V):
            rd.append(scalar.t)
            a = scalar.ap
        self.op(eng, lambda e: e.scalar_tensor_tensor(out=out.ap, in0=in0.ap, scalar=a, in1=in1.ap, op0=op0, op1=op1),
                reads=rd, writes=[out.t])

    def cp(self, out, in_, eng="dve"):
        if eng == "act":
            self.op("act", lambda e: e.copy(out=out.ap, in_=in_.ap), reads=[in_.t], writes=[out.t])
        else:
            self.op(eng, lambda e: e.tensor_copy(out=out.ap, in_=in_.ap), reads=[in_.t], writes=[out.t])

    def memset(self, out, val, eng="dve"):
        self.op(eng, lambda e: e.memset(out.ap, val), writes=[out.t])

    def red(self, out, in_, op, eng="dve"):
        self.op(eng, lambda e: e.tensor_reduce(out=out.ap, in_=in_.ap, axis=AX.X, op=op), reads=[in_.t], writes=[out.t])

    def scan(self, out, in_, op0=ALU.add):
        self.op("dve", lambda e: e.tensor_tensor_scan(out=out.ap, data0=in_.ap, data1=in_.ap, initial=0.0 if op0 == ALU.add else -3.0e38,
                                                      op0=op0, op1=ALU.bypass), reads=[in_.t], writes=[out.t])

    def dma_in(self, out, in_ap, q="sp", slow=False):
        kw = {"allow_slow_non_contiguous": True} if slow else {}
        self.op(q, lambda e: e.dma_start(out=out.ap, in_=in_ap, **kw), writes=[out.t], dma_tile=out.t)

    def dma_out(self, out_ap, in_, q="sp", slow=False):
        kw = {"allow_slow_non_contiguous": True} if slow else {}
        self.op(q, lambda e: e.dma_start(out=out_ap, in_=in_.ap, **kw), reads=[in_.t], dma_tile=in_.t)


def build(TT=(384, 384, 384, 384, 384), taps=()):
    nc = bass.Bass("TRN2", target_bir_lowering=False)
    k = KB(nc)
    taps = set(taps)
    tapouts = {}

    def din(name, shape):
        return nc.dram_tensor(name, list(shape), F32, kind="ExternalInput").ap()

    def dout(name, shape):
        return nc.dram_tensor(name, list(shape), F32, kind="ExternalOutput").ap()

    xp = din("xp", [2048, D])
    xs = din("xs", [64, D])
    meta = din("meta", [16, D])
    sC = din("sC", [DEPTH, NS * 4, 128, 128])
    sn = din("sn", [DEPTH, NS * 4, 128])
    sm = din("sm", [DEPTH, NS, 4])
    sS = din("sS", [DEPTH, NS * 4, 128, 128])
    sgc = din("sgc", [DEPTH, NS * 3, 1536])
    sfc = din("sfc", [DEPTH, NS * 2, 2 * DFF])
    ln_emb_g = din("ln_emb_g", [1, D])
    ln_emb_b = din("ln_emb_b", [1, D])
    w_in = din("w_in", [DEPTH, D, NIN])
    gate_bias = din("gate_bias", [DEPTH, 8, 1])
    mnorm_w = din("mnorm_w", [DEPTH, 1, 512])
    gconv_w = din("gconv_w", [DEPTH, 4, 1536])
    A_log = din("A_log", [DEPTH, 4, 1])
    dt_bias = din("dt_bias", [DEPTH, 4, 1])
    gnorm_w = din("gnorm_w", [DEPTH, 1, 128])
    w_pa = din("w_pa", [DEPTH, 512, D])
    w_pb = din("w_pb", [DEPTH, 512, D])
    w_out = din("w_out", [DEPTH, D, D])
    ln1_g = din("ln1_g", [DEPTH, 1, D])
    ln1_b = din("ln1_b", [DEPTH, 1, D])
    w_up = din("w_up", [DEPTH, D, 2 * DFF])
    fconv_w = din("fconv_w", [DEPTH, 3, 2 * DFF])
    w_down = din("w_down", [DEPTH, DFF, D])
    ln2_g = din("ln2_g", [DEPTH, 1, D])
    ln2_b = din("ln2_b", [DEPTH, 1, D])

    yp = dout("yp", [2048, D])
    ys = dout("ys", [64, D])
    pC = dout("pC", [DEPTH, 4, 128, 128])
    pn = dout("pn", [DEPTH, 4, 128])
    pm = dout("pm", [DEPTH, 4, 1])
    pS = dout("pS", [DEPTH, 4, 128, 128])
    pgc = dout("pgc", [DEPTH, 3, 1536])
    pfc = dout("pfc", [DEPTH, 2, 2 * DFF])
    oC = dout("oC", [DEPTH, NS * 4, 128, 128])
    on = dout("on", [DEPTH, NS * 4, 128])
    om = dout("om", [DEPTH, NS, 4])
    oS = dout("oS", [DEPTH, NS * 4, 128, 128])
    ogc = dout("ogc", [DEPTH, NS * 3, 1536])
    ofc = dout("ofc", [DEPTH, NS * 2, 2 * DFF])

    def tap(name, view):
        if name not in taps:
            return
        shp = list(view.ap.shape)
        o = nc.dram_tensor("tap_" + name, shp, view.ap.dtype, kind="ExternalOutput").ap()
        tapouts[name] = shp
        k.dma_out(o, view, q="sp", slow=True)

    k.ps_tiles = [k.psum("ps%d" % i, [128, 512]) for i in range(6)]
    psd = k.psum("psd", [128, 1024])
    k.wb_tiles = [k.sb("wb%d" % i, [128, 8, 512], persist=True, dt=BF16) for i in range(4)]

    ident = k.sb("ident", [128, 128], persist=True)
    k.memset(ident.v, 0.0, eng="pool")
    k.op("pool", lambda e: e.affine_select(out=ident.ap, in_=ident.ap, pattern=[[-1, 128]], compare_op=ALU.not_equal,
                                           fill=1.0, base=0, channel_multiplier=1), reads=[ident], writes=[ident])
    aident = k.sb("aident", [128, 128], persist=True)
    k.ts(aident.v, ident.v, ALPHA, ALU.mult)
    ones = k.sb("ones", [128, 128], persist=True)
    k.memset(ones.v, 1.0)
    cst = k.sb("cst", [128, 8], persist=True)
    k.memset(cst[:, 4:5], LN_EPS)
    k.memset(cst[:, 0:1], 1.0)
    k.memset(cst[:, 1:2], NORM_EPS)
    k.memset(cst[:, 2:3], math.log(128.0 ** -0.5))
    k.memset(cst[:, 3:4], 0.0)

    def aff(t, pattern, op, fill, base, cm):
        k.op("pool", lambda e: e.affine_select(out=t.ap, in_=t.ap, pattern=pattern, compare_op=op, fill=fill,
                                               base=base, channel_multiplier=cm), reads=[t], writes=[t])

    m01T = k.sb("m01T", [128, 128], persist=True)
    k.memset(m01T.v, 1.0, eng="pool")
    aff(m01T, [[1, 128]], ALU.is_ge, 0.0, 0, -1)
    mn_st = k.sb("mn_st", [128, 128], persist=True)
    k.memset(mn_st.v, 0.0, eng="pool")
    aff(mn_st, [[-1, 128]], ALU.is_gt, NEG, 0, 1)
    mn_stT = k.sb("mn_stT", [128, 128], persist=True)
    k.memset(mn_stT.v, 0.0, eng="pool")
    aff(mn_stT, [[1, 128]], ALU.is_gt, NEG, 0, -1)
    mn_inT = k.sb("mn_inT", [128, 128], persist=True)
    k.memset(mn_inT.v, 0.0, eng="pool")
    aff(mn_inT, [[1, 128]], ALU.is_ge, NEG, 0, -1)
    indT = k.sb("indT", [16, 64], persist=True)
    k.memset(indT.v, 1.0, eng="pool")
    aff(indT, [[1, 64]], ALU.is_ge, 0.0, 0, -4)
    aff(indT, [[-1, 64]], ALU.is_ge, 0.0, 3, 4)
    ind = k.sb("ind", [64, 16], persist=True)
    k.memset(ind.v, 1.0, eng="pool")
    aff(ind, [[-4, 16]], ALU.is_ge, 0.0, 0, 1)
    aff(ind, [[4, 16]], ALU.is_ge, 0.0, 3, -1)
    same = k.sb("same", [64, 64], persist=True)
    p0 = k.ps()
    k.mm(p0[0:64, 0:64], indT.v, indT.v)
    k.cp(same.v, p0[0:64, 0:64])
    sneg = k.sb("sneg", [64, 64], persist=True)
    k.ts(sneg.v, same.v, -NEG, ALU.mult, NEG, ALU.add)
    b01T = k.sb("b01T", [64, 64], persist=True)
    k.tt(b01T.v, m01T[0:64, 0:64], same.v, ALU.mult)
    bn_st = k.sb("bn_st", [64, 64], persist=True)
    k.tt(bn_st.v, mn_st[0:64, 0:64], sneg.v, ALU.min)
    bn_stT = k.sb("bn_stT", [64, 64], persist=True)
    k.tt(bn_stT.v, mn_stT[0:64, 0:64], sneg.v, ALU.min)
    bn_inT = k.sb("bn_inT", [64, 64], persist=True)
    k.tt(bn_inT.v, mn_inT[0:64, 0:64], sneg.v, ALU.min)
    MASKS = {"c": (m01T, mn_st, mn_stT, mn_inT), "b": (b01T, bn_st, bn_stT, bn_inT)}
    oh = k.sb("oh", [4, 4, 128], persist=True)
    k.memset(oh.v, 0.0, eng="pool")
    aff(oh, [[-1, 4], [0, 128]], ALU.not_equal, 1.0, 0, 1)

    def load_fm(name, src2d, R, C):
        dst = k.sb(name, [128, C, R], persist=True)
        with k.scope():
            tmp = k.sb("ldfm", [R, C * 128])
            k.dma_in(tmp.v, src2d)
            c0 = 0
            while c0 < C:
                n = min(C - c0, 512 // R)
                p = k.ps()
                for c in range(n):
                    k.tr(p[:, c * R:(c + 1) * R], tmp[:, (c0 + c) * 128:(c0 + c + 1) * 128], ident)
                k.cp(dst[:, c0:c0 + n, :], p[:, 0:n * R].re("p (c r) -> p c r", r=R))
                c0 += n
        return dst

    def load_bc(name, src_row, n):
        dst = k.sb(name, [128, n], persist=True)
        k.dma_in(dst.v, src_row.to_broadcast([128, n]), slow=True)
        return dst

    g_emb = load_fm("g_emb", ln_emb_g, 1, 8)
    b_emb = load_fm("b_emb", ln_emb_b, 1, 8)
    LP = []
    for l in range(DEPTH):
        P = {}
        P["g1"] = load_fm("g1", ln1_g[l], 1, 8)
        P["b1"] = load_fm("b1", ln1_b[l], 1, 8)
        P["g2"] = load_fm("g2", ln2_g[l], 1, 8)
        P["b2"] = load_fm("b2", ln2_b[l], 1, 8)
        P["cwg"] = load_fm("cwg", gconv_w[l], 4, 12)
        P["cwf"] = load_fm("cwf", fconv_w[l], 3, 44)
        P["anw"] = load_bc("anw", mnorm_w[l], 512)
        P["gnw"] = load_bc("gnw", gnorm_w[l], 128)
        gbi = k.sb("gbi", [4, 1], persist=True)
        gbf = k.sb("gbf", [4, 1], persist=True)
        gal = k.sb("gal", [4, 1], persist=True)
        gdt = k.sb("gdt", [4, 1], persist=True)
        k.dma_in(gbi.v, gate_bias[l][0:4, :], slow=True)
        k.dma_in(gbf.v, gate_bias[l][4:8, :], slow=True)
        k.dma_in(gal.v, A_log[l], slow=True)
        k.dma_in(gdt.v, dt_bias[l], slow=True)
        k.ts(gbf.v, gbf.v, -1.0, ALU.mult)
        k.act(gal.v, gal.v, AF.Exp)
        k.ts(gal.v, gal.v, -1.0, ALU.mult)
        P["gb"] = (gbi, gbf, gal, gdt)
        Um = k.sb("Um", [128, 4, 129], persist=True)
        k.memset(Um.v, 0.0)
        mcur = k.sb("mcur", [4, 1], persist=True)
        k.memset(mcur.v, 0.0)
        S = k.sb("S", [128, 4, 128], persist=True)
        k.memset(S.v, 0.0)
        cg = k.sb("cg", [128, 12, 3], persist=True)
        k.memset(cg.v, 0.0)
        cf = k.sb("cf", [128, 44, 2], persist=True)
        k.memset(cf.v, 0.0)
        P.update(Um=Um, mcur=mcur, S=S, cg=cg, cf=cf)
        LP.append(P)
    g2bc = load_bc("g2bc", ln2_g[DEPTH - 1], D)
    b2bc = load_bc("b2bc", ln2_b[DEPTH - 1], D)

    NP0 = 128
    tiles = [dict(TT=80 + NP0, first=True, poff=80, tok0=0,
                  chunks=[dict(c0=0, c=16, kind="meta"), dict(c0=80, c=NP0, kind="p", tok0=0), dict(c0=16, c=64, kind="samp")],
                  groups=[(0, 16)] + [(16 + 4 * i, 4) for i in range(16)] + [(80, NP0)], tgs=[(0, 80), (80, NP0)],
                  pgroups=[(0, 16, [(0, 0, 16)]), (16, 64, [(2, 0, 64)]), (80, NP0, [(1, 0, NP0)])])]
    sizes = TT if isinstance(TT, (list, tuple)) else [TT] * (2048 // TT)
    assert sum(sizes) + NP0 == 2048
    t0_ = NP0
    for sz in sizes:
        tiles.append(dict(TT=sz, first=False, poff=0, chunks=[dict(c0=128 * j, c=128, kind="p", tok0=t0_ + 128 * j) for j in range(sz // 128)],
                          groups=[(128 * j, 128) for j in range(sz // 128)], tgs=[(128 * g, 128) for g in range(sz // 128)], tok0=t0_,
                          pgroups=[(128 * g, 128, [(g, 0, 128)]) for g in range(sz // 128)]))
        t0_ += sz

    scratch = {}
    cur = {"ti": 0, "n": 0}

    def wload_parts(parts, ncols):
        wb = k.wbuf()
        ktot = sum(kc for _, kc in parts)
        key = cur["n"]
        cur["n"] += 1
        if cur["ti"] == 0:
            k0 = 0
            for (src, kc) in parts:
                k.dma_in(wb[:, k0:k0 + kc, 0:ncols], src.rearrange("(k p) n -> p k n", p=128), q="pool")
                k0 += kc
            scr = nc.dram_tensor("wscr_%d" % key, [128, ktot, ncols], BF16).ap()
            st = T("scr%d" % key, None, {})
            scratch[key] = (scr, st)
            k.op("sp", lambda e: e.dma_start(out=scr, in_=wb.ap[:, 0:ktot, 0:ncols]), reads=[wb], writes=[st], dma_tile=wb)
        else:
            scr, st = scratch[key]
            k.op("sp", lambda e: e.dma_start(out=wb.ap[:, 0:ktot, 0:ncols], in_=scr), reads=[st], writes=[wb], dma_tile=wb)
        return wb

    def wload(src2d, kc, ncols):
        return wload_parts([(src2d, kc)], ncols)

    def proj_fm(wb, cb, xT, n, nk=8, k0=0):
        p = k.ps()
        for kk in range(nk):
            k.mm(p[:, 0:n], wb[:, k0 + kk, cb:cb + 128], xT[:, kk, 0:n], start=(kk == 0), stop=(kk == nk - 1))
        return p

    def proj_tm(wb, ncols, xT, c0, c):
        p = k.ps()
        for kk in range(8):
            k.mm(p[0:c, 0:ncols], xT[:, kk, c0:c0 + c], wb[:, kk, 0:ncols], start=(kk == 0), stop=(kk == 7))
        return p

    def rsqrt_small(out, in_, eps):
        n = in_.ap.shape[0]
        col = 1 if eps == NORM_EPS else 4
        k.act(out, in_, AF.Ln, bias=cst[0:n, col:col + 1])
        k.act(out, out, AF.Exp, scale=-0.5)

    def ln_chunk(src, c, g_fm, b_fm, dstT, c0, final_dst=None, dstB=None):
        with k.scope():
            st = k.sb("lnst", [64, 12])
            mv = k.sb("lnmv", [64, 2])
            rs = k.sb("lnrs", [64, 1])
            xn = k.sb("lnxn", [64, 1024])
            k.op("dve", lambda e: e.bn_stats(out=st.ap[0:c, 0:6], in_=src.ap[:, 0:512]), reads=[src.t], writes=[st])
            k.op("dve", lambda e: e.bn_stats(out=st.ap[0:c, 6:12], in_=src.ap[:, 512:1024]), reads=[src.t], writes=[st])
            k.op("dve", lambda e: e.bn_aggr(out=mv.ap[0:c, :], in_=st.ap[0:c, :]), reads=[st], writes=[mv])
            rsqrt_small(rs[0:c, :], mv[0:c, 1:2], LN_EPS)
            k.ts(xn[0:c, :], src, mv[0:c, 0:1], ALU.subtract, rs[0:c, 0:1], ALU.mult)
            if final_dst is not None:
                k.tt(xn[0:c, :], xn[0:c, :], g2bc[0:c, :], ALU.mult)
                k.tt(xn[0:c, :], xn[0:c, :], b2bc[0:c, :], ALU.add)
                k.dma_out(final_dst, xn[0:c, :])
            else:
                p = k.ps()
                for cc in range(8):
                    k.tr(p[:, cc * c:(cc + 1) * c], xn[0:c, cc * 128:(cc + 1) * 128], ident)
                for cc in range(8):
                    if cc % 2 == 0:
                        k.act(dstT[:, cc, c0:c0 + c], p[:, cc * c:(cc + 1) * c], AF.Identity,
                              bias=b_fm[:, cc, 0:1], scale=g_fm[:, cc, 0:1])
                    else:
                        k.ts(dstT[:, cc, c0:c0 + c], p[:, cc * c:(cc + 1) * c], g_fm[:, cc, 0:1], ALU.mult,
                             b_fm[:, cc, 0:1], ALU.add)
                k.cp(dstB[:, :, c0:c0 + c], dstT[:, :, c0:c0 + c])

    def fm_to_rows(src3, R, C, dram2d):
        with k.scope():
            ob = k.sb("f2r", [R, C * 128])
            c0 = 0
            while c0 < C:
                n = min(4, C - c0)
                p = k.ps()
                for c in range(n):
                    k.tr(p[0:R, c * 128:(c + 1) * 128], src3[:, c0 + c, :], ident)
                k.cp(ob[:, c0 * 128:(c0 + n) * 128], p[0:R, 0:n * 128])
                c0 += n
            k.dma_out(dram2d, ob.v)

    def bcast_rows(row4, n, name):
        R = k.sb("bcR", [4, n, 4])
        k.tt(R.v, row4.unsq(2).bc([4, n, 4]), ident[0:4, 0:4].unsq(1).bc([4, n, 4]), ALU.mult)
        p = k.ps()
        k.mm(p[:, 0:n * 4], ones[0:4, :], R.v.re("p a b -> p (a b)"))
        dst = k.sb(name, [128, n, 4])
        k.cp(dst.v, p[:, 0:n * 4].re("p (a b) -> p a b", b=4))
        return dst

    class RPool:
        def __init__(self, name, shape, n, dt=F32):
            self.tiles = [k.sb(name, shape, dt=dt) for _ in range(n)]
            self.i = 0

        def get(self):
            t = self.tiles[self.i % len(self.tiles)]
            self.i += 1
            return t

    def interleave(gens):
        gens = list(gens)
        while gens:
            nxt = []
            for g in gens:
                try:
                    next(g)
                    nxt.append(g)
                except StopIteration:
                    pass
            gens = nxt

    def rows_to_tm(rows, c0, c, name):
        p = k.ps()
        for i, r in enumerate(rows):
            k.tr(p[0:c, 4 * i:4 * i + 4], r[:, c0:c0 + c], ident)
        dst = k.sb(name, [128, 4 * len(rows)])
        k.cp(dst[0:c, :], p[0:c, 0:4 * len(rows)])
        return dst

    def mlstm_phase(l, tl, xT, hAT):
        P = LP[l]
        TTc = tl["TT"]
        chunks = tl["chunks"]
        CP = max(ch["c"] for ch in chunks)
        Um, mcur = P["Um"], P["mcur"]
        gbi, gbf, gal, gdt = P["gb"]
        qT = k.sb("qT", [128, 4, TTc])
        kT = k.sb("kT", [128, 4, TTc])
        wb = wload(w_in[l][:, A_Q:A_Q + 512], 8, 512)
        for h in range(4):
            p = proj_fm(wb, h * 128, xT, TTc)
            k.act(qT[:, h, :], p[:, 0:TTc], AF.Identity, scale=128.0 ** -0.5)
        wb = wload(w_in[l][:, A_K:A_K + 512], 8, 512)
        for h in range(4):
            p = proj_fm(wb, h * 128, xT, TTc)
            k.cp(kT[:, h, :], p[:, 0:TTc])
        kc = [k.sb("kc", [CP, 4, 128]) for _ in chunks]
        va = [k.sb("va", [CP, 4, 129]) for _ in chunks]
        ow = [k.sb("ow", [CP, 512]) for _ in chunks]
        for (g0, gn, mem) in tl["pgroups"]:
            p = proj_tm(wb, 512, xT, g0, gn)
            for (ci, r0, c) in mem:
                k.cp(kc[ci][0:c], p[r0:r0 + c, :].re("p (h d) -> p h d", h=4), eng="act")
        wb = wload(w_in[l][:, A_V:A_V + 512], 8, 512)
        for (g0, gn, mem) in tl["pgroups"]:
            p = proj_tm(wb, 512, xT, g0, gn)
            for (ci, r0, c) in mem:
                k.cp(va[ci][0:c, :, 0:128], p[r0:r0 + c, :].re("p (h d) -> p h d", h=4))
                k.memset(va[ci][0:c, :, 128:129], 1.0)
        wb = wload(w_in[l][:, A_O:A_O + 512], 8, 512)
        for (g0, gn, mem) in tl["pgroups"]:
            p = proj_tm(wb, 512, xT, g0, gn)
            for (ci, r0, c) in mem:
                k.act(ow[ci][0:c, :], p[r0:r0 + c, :], AF.Sigmoid)
                k.tt(ow[ci][0:c, :], ow[ci][0:c, :], P["anw"][0:c, :], ALU.mult)
        wgt = wload(w_in[l][:, A_I - 248:A_I + 8], 8, 256)
        pi = k.ps()
        for kk in range(8):
            k.mm(pi[0:4, 0:TTc], wgt[:, kk, 248:252], xT[:, kk, 0:TTc], start=(kk == 0), stop=(kk == 7))
        pf = k.ps()
        for kk in range(8):
            k.mm(pf[0:4, 0:TTc], wgt[:, kk, 252:256], xT[:, kk, 0:TTc], start=(kk == 0), stop=(kk == 7))
        igr = k.sb("igr", [4, TTc])
        lf = k.sb("lf", [4, TTc])
        b = k.sb("b", [4, TTc])
        a = k.sb("a", [4, TTc])
        Er = k.sb("Er", [4, TTc])
        Th = k.sb("Th", [4, TTc])
        k.act(igr.v, pi[0:4, 0:TTc], AF.Identity, bias=gbi[:, 0:1])
        k.act(lf.v, pf[0:4, 0:TTc], AF.Exp, bias=gbf[:, 0:1], scale=-1.0)
        k.act(lf.v, lf.v, AF.Ln, bias=cst[0:4, 0:1])
        k.ts(lf.v, lf.v, -1.0, ALU.mult)
        for (g0, gl) in tl["groups"]:
            k.scan(b[:, g0:g0 + gl], lf[:, g0:g0 + gl])
        k.tt(a.v, igr.v, b.v, ALU.subtract)
        nb = k.sb("nb", [4, TTc])
        k.ts(nb.v, b.v, -1.0, ALU.mult)
        nch = len(chunks)
        rr = k.sb("rr", [4, nch + 16])
        nrr = k.sb("nrr", [4, nch + 16])
        mprev = k.sb("mprev", [4, nch + 16])
        am = k.sb("am", [4, nch + 16])
        wrow = k.sb("wrow", [4, nch + 16])
        for j, ch in enumerate(chunks):
            c0, c = ch["c0"], ch["c"]
            if ch["kind"] == "samp":
                av = a[:, c0:c0 + 64].re("p (s j) -> p s j", j=4)
                bv = b[:, c0:c0 + 64].re("p (s j) -> p s j", j=4)
                k.red(am[:, j:j + 16], av, ALU.max)
                m0t = k.sb("m0t", [16, 4])
                k.dma_in(m0t.v, sm[l])
                pm0 = k.ps()
                k.tr(pm0[0:4, 0:16], m0t.v, ident)
                k.cp(mprev[:, j:j + 16], pm0[0:4, 0:16])
                k.tt(rr[:, j:j + 16], mprev[:, j:j + 16], am[:, j:j + 16], ALU.max)
                mnew = k.sb("mnew", [4, 16])
                k.tt(mnew.v, rr[:, j:j + 16], bv[:, :, 3], ALU.add)
                mex = k.sb("mex", [4, 16, 4])
                k.tt(mex.v, mnew.v.unsq(2).bc([4, 16, 4]), ident[0:4, 0:4].unsq(1).bc([4, 16, 4]), ALU.mult)
                pmo = k.ps()
                k.mm(pmo[0:1, 0:64], ones[0:4, 0:1], mex.v.re("p a b -> p (a b)"))
                mrow = k.sb("mrow", [1, 64])
                k.cp(mrow.v, pmo[0:1, 0:64])
                k.dma_out(om[l].rearrange("s h -> (s h)").rearrange("(o n) -> o n", o=1), mrow.v)
                k.ts(nrr[:, j:j + 16], rr[:, j:j + 16], -1.0, ALU.mult)
                k.tt(Er[:, c0:c0 + 64].re("p (s j) -> p s j", j=4), av, rr[:, j:j + 16].unsq(2).bc([4, 16, 4]), ALU.subtract)
                k.tt(Th[:, c0:c0 + 64].re("p (s j) -> p s j", j=4), nb[:, c0:c0 + 64].re("p (s j) -> p s j", j=4),
                     rr[:, j:j + 16].unsq(2).bc([4, 16, 4]), ALU.subtract)
                k.act(Er[:, c0:c0 + 64], Er[:, c0:c0 + 64], AF.Exp)
                k.act(Th[:, c0:c0 + 64], Th[:, c0:c0 + 64], AF.Exp)
                k.tt(wrow[:, j:j + 16], mprev[:, j:j + 16], rr[:, j:j + 16], ALU.subtract)
            else:
                k.red(am[:, j:j + 1], a[:, c0:c0 + c], ALU.max)
                k.cp(mprev[:, j:j + 1], mcur.v)
                k.tt(rr[:, j:j + 1], mcur.v, am[:, j:j + 1], ALU.max)
                k.tt(mcur.v, rr[:, j:j + 1], b[:, c0 + c - 1:c0 + c], ALU.add)
                k.ts(nrr[:, j:j + 1], rr[:, j:j + 1], -1.0, ALU.mult)
                k.act(Er[:, c0:c0 + c], a[:, c0:c0 + c], AF.Exp, bias=nrr[:, j:j + 1])
                k.act(Th[:, c0:c0 + c], nb[:, c0:c0 + c], AF.Exp, bias=nrr[:, j:j + 1])
                k.tt(wrow[:, j:j + 1], mprev[:, j:j + 1], rr[:, j:j + 1], ALU.subtract)
        nw = nch + 15 if chunks[-1]["kind"] == "samp" else nch
        k.act(wrow[:, 0:nw], wrow[:, 0:nw], AF.Exp)
        wbc = bcast_rows(wrow[:, 0:nw], nw, "wbc")
        tap("Er%d" % l, Er.v)
        tap("Th%d" % l, Th.v)
        for j, ch in enumerate(chunks):
            c0, c, kind = ch["c0"], ch["c"], ch["kind"]
            m01 = MASKS["b" if kind == "samp" else "c"][0]
            with k.scope():
                sc = rows_to_tm([Er.v, Th.v], c0, c, "scm")
                vE = k.sb("vE", [CP, 4, 129])
                k.tt(vE[0:c], va[j][0:c], sc[0:c, 0:4].unsq(2).bc([c, 4, 129]), ALU.mult)
                if kind != "samp":
                    offs = [(h // 2) * 512 + (h % 2) * 129 for h in range(4)]
                    psts = []
                    for h in range(4):
                        pst = k.ps()
                        k.mm(pst[0:c, 0:c], kT[:, h, c0:c0 + c], qT[:, h, c0:c0 + c])
                        psts.append(pst)
                    STl, Chl = [], []
                    for h in range(4):
                        STs = k.sb("STs", [CP, CP])
                        k.stt(STs[0:c, 0:c], psts[h][0:c, 0:c], sc[0:c, h:h + 1], m01[0:c, 0:c], ALU.mult, ALU.mult)
                        STl.append(STs)
                        Ch = k.sb("Ch", [128, 129])
                        k.ts(Ch.v, Um[:, h, :], wbc[:, j, h:h + 1], ALU.mult)
                        Chl.append(Ch)
                    for h in range(4):
                        k.mm(psd[0:c, offs[h]:offs[h] + 129], STl[h][0:c, 0:c], va[j][0:c, h, :], start=True, stop=False)
                        k.mm(psd[0:c, offs[h]:offs[h] + 129], qT[:, h, c0:c0 + c], Chl[h].v, start=False, stop=True)
                    pps = []
                    for h in range(4):
                        pp = k.ps()
                        k.mm(pp[:, 0:129], kc[j][0:c, h, :], vE[0:c, h, :])
                        pps.append(pp)
                    for h in range(4):
                        k.tt(Um[:, h, :], Chl[h].v, pps[h][:, 0:129], ALU.add)
                for h in (range(4) if kind == "samp" else []):
                    k.push()
                    off = (h // 2) * 512 + (h % 2) * 129
                    pst = k.ps()
                    k.mm(pst[0:c, 0:c], kT[:, h, c0:c0 + c], qT[:, h, c0:c0 + c])
                    STs = k.sb("STs", [CP, CP])
                    k.stt(STs[0:c, 0:c], pst[0:c, 0:c], sc[0:c, h:h + 1], m01[0:c, 0:c], ALU.mult, ALU.mult)
                    if kind != "samp":
                        pass
                    else:
                        qTm = k.sb("qTm", [128, 16, 64])
                        k.memset(qTm.v, 0.0)
                        for i in range(16):
                            k.cp(qTm[:, i, 4 * i:4 * i + 4], qT[:, h, c0 + 4 * i:c0 + 4 * i + 4])
                        k.mm(psd[0:c, off:off + 129], STs[0:c, 0:c], va[j][0:c, h, :], start=True, stop=False)
                        for g in range(4):
                            with k.scope():
                                Cg = k.sb("Cg", [128, 4, 129])
                                Cst = k.sb("Cst", [128, 4, 128])
                                k.dma_in(Cst.v, sC[l].rearrange("(s h) d v -> h d s v", h=4)[h][:, 4 * g:4 * g + 4, :])
                                k.cp(Cg[:, :, 0:128], Cst.v, eng="act")
                                nrow = k.sb("nrow", [4, 128])
                                k.dma_in(nrow.v, sn[l].rearrange("(s h) d -> h s d", h=4)[h][4 * g:4 * g + 4, :])
                                pn_ = k.ps()
                                k.tr(pn_[:, 0:4], nrow.v, ident)
                                k.cp(Cg[:, :, 128], pn_[:, 0:4])
                                wv = wbc[:, j + 4 * g:j + 4 * g + 4, h:h + 1]
                                k.tt(Cg.v, Cg.v, wv.bc([128, 4, 129]), ALU.mult)
                                for ii in range(4):
                                    i = 4 * g + ii
                                    k.mm(psd[0:c, off:off + 129], qTm[:, i, :], Cg[:, ii, :], start=False,
                                         stop=(i == 15))
                                vEx = k.sb("vEx", [64, 4, 129])
                                k.tt(vEx.v, vE[0:64, h, :].unsq(1).bc([64, 4, 129]),
                                     ind[:, 4 * g:4 * g + 4].unsq(2).bc([64, 4, 129]), ALU.mult)
                                vf = vEx.v.re("p a b -> p (a b)")
                                pu = k.ps()
                                pu2 = k.ps()
                                k.mm(pu[:, 0:512], kc[j][0:64, h, :], vf[:, 0:512])
                                k.mm(pu2[:, 0:4], kc[j][0:64, h, :], vf[:, 512:516])
                                Cf = Cg.v.re("p a b -> p (a b)")
                                k.tt(Cf[:, 0:512], Cf[:, 0:512], pu[:, 0:512], ALU.add)
                                k.tt(Cf[:, 512:516], Cf[:, 512:516], pu2[:, 0:4], ALU.add)
                                k.dma_out(oC[l].rearrange("(s h) d v -> h d s v", h=4)[h][:, 4 * g:4 * g + 4, :], Cg[:, :, 0:128])
                                pn2 = k.ps()
                                k.tr(pn2[0:4, 0:128], Cg[:, :, 128], ident)
                                nout = k.sb("nout", [4, 128])
                                k.cp(nout.v, pn2[0:4, 0:128])
                                k.dma_out(on[l].rearrange("(s h) d -> h s d", h=4)[h][4 * g:4 * g + 4, :], nout.v)
                    k.pop()
                numv = psd[0:c, :].re("p (a r) -> p a r", a=2)[:, :, 0:258].re("p a (h q) -> p a h q", h=2)
                den = k.sb("den", [CP, 2, 2])
                k.cp(den[0:c], numv[:, :, :, 128])
                dn = k.sb("dn", [CP, 4])
                denf = den[0:c].re("p a b -> p (a b)")
                k.ts(dn[0:c], denf, -1.0, ALU.mult)
                k.tt(dn[0:c], dn[0:c], denf, ALU.max)
                k.tt(dn[0:c], dn[0:c], sc[0:c, 4:8], ALU.max)
                rden = k.sb("rden", [CP, 4])
                k.op("dve", lambda e: e.reciprocal(out=rden.ap[0:c], in_=dn.ap[0:c]), reads=[dn], writes=[rden])
                st = k.sb("hst", [CP, 4, 6])
                mv = k.sb("hmv", [CP, 4, 2])
                for h in range(4):
                    off = (h // 2) * 512 + (h % 2) * 129
                    k.op("dve", lambda e, h=h, off=off: e.bn_stats(out=st.ap[0:c, h, :], in_=psd.ap[0:c, off:off + 128]),
                         reads=[psd], writes=[st])
                for h in range(4):
                    k.op("dve", lambda e, h=h: e.bn_aggr(out=mv.ap[0:c, h, :], in_=st.ap[0:c, h, :]), reads=[st], writes=[mv])
                t1 = k.sb("t1", [CP, 4])
                k.tt(t1[0:c], rden[0:c], rden[0:c], ALU.mult)
                k.tt(t1[0:c], t1[0:c], mv[0:c, :, 1], ALU.mult)
                rsq = k.sb("rsq", [CP, 4])
                rsqrt_small(rsq[0:c], t1[0:c], NORM_EPS)
                k.tt(rsq[0:c], rsq[0:c], rden[0:c], ALU.mult)
                hA = k.sb("hA", [CP, 512])
                for h in range(4):
                    off = (h // 2) * 512 + (h % 2) * 129
                    k.ts(hA[0:c, h * 128:(h + 1) * 128], psd[0:c, off:off + 128], mv[0:c, h, 0:1], ALU.subtract,
                         rsq[0:c, h:h + 1], ALU.mult)
                k.tt(hA[0:c], hA[0:c], ow[j][0:c], ALU.mult)
                pt = k.ps()
                for h in range(4):
                    k.tr(pt[:, h * c:(h + 1) * c], hA[0:c, h * 128:(h + 1) * 128], ident)
                k.cp(hAT[:, :, c0:c0 + c], pt[:, 0:4 * c].re("p (h t) -> p h t", h=4), eng="act")

    def gdn_phase(l, tl, xT, hBT):
        P = LP[l]
        TTc = tl["TT"]
        chunks = tl["chunks"]
        CP = max(ch["c"] for ch in chunks)
        S, cg, cwg = P["S"], P["cg"], P["cwg"]
        gbi, gbf, gal, gdt = P["gb"]
        qkvT = k.sb("qkvT", [128, 12, TTc])
        k.push()
        pools = None if tl["first"] else (RPool("cext", [128, TTc + 3], 3), RPool("cacc", [128, TTc], 3))
        sqp = RPool("csq", [128, TTc], 2)
        for blk in range(3):
            wb = wload(w_in[l][:, B_QKV + 512 * blk:B_QKV + 512 * (blk + 1)], 8, 512)
            for q4 in range(4):
                cc = blk * 4 + q4
                p = proj_fm(wb, q4 * 128, xT, TTc)
                with k.scope():
                    acc = conv_fm(p, cc, 4, cg, cwg, sgc[l], ogc[l], tl, "c", pools)
                    k.act(qkvT[:, cc, :], acc.v, AF.Silu)
                    if cc < 8:
                        sq = sqp.get()
                        k.tt(sq.v, qkvT[:, cc, :], qkvT[:, cc, :], ALU.mult, eng=("dve" if tl["first"] else "pool"))
                        pq = k.ps()
                        k.mm(pq[:, 0:TTc], ones.v, sq.v)
                        k.act(sq.v, pq[:, 0:TTc], AF.Ln, bias=cst[:, 1:2])
                        k.act(sq.v, sq.v, AF.Exp, scale=-0.5, bias=(cst[:, 2:3] if cc < 4 else cst[:, 3:4]))
                        k.tt(qkvT[:, cc, :], qkvT[:, cc, :], sq.v, ALU.mult)
        k.pop()
        wb = wload(w_in[l][:, B_Z:B_Z + 512], 8, 512)
        wz = [k.sb("wz", [CP, 4, 128]) for _ in chunks]
        for (g0, gn, mem) in tl["pgroups"]:
            p = proj_tm(wb, 512, xT, g0, gn)
            for (ci, r0, c) in mem:
                k.act(wz[ci][0:c].re("p h d -> p (h d)"), p[r0:r0 + c, :], AF.Silu)
                k.tt(wz[ci][0:c], wz[ci][0:c], P["gnw"][0:c, :].unsq(1).bc([c, 4, 128]), ALU.mult)
        wgt = wload(w_in[l][:, B_BETA - 248:B_BETA + 8], 8, 256)
        pbt = k.ps()
        for kk in range(8):
            k.mm(pbt[0:4, 0:TTc], wgt[:, kk, 248:252], xT[:, kk, 0:TTc], start=(kk == 0), stop=(kk == 7))
        pa_ = k.ps()
        for kk in range(8):
            k.mm(pa_[0:4, 0:TTc], wgt[:, kk, 252:256], xT[:, kk, 0:TTc], start=(kk == 0), stop=(kk == 7))
        spb = k.sb("spb", [4, TTc])
        k.act(spb.v, pbt[0:4, 0:TTc], AF.Exp, scale=-1.0)
        k.act(spb.v, spb.v, AF.Ln, bias=cst[0:4, 0:1])
        g = k.sb("g", [4, TTc])
        k.act(g.v, pa_[0:4, 0:TTc], AF.Exp, bias=gdt[:, 0:1])
        k.act(g.v, g.v, AF.Ln, bias=cst[0:4, 0:1])
        k.ts(g.v, g.v, gal[:, 0:1], ALU.mult)
        G = k.sb("G", [4, TTc])
        for (g0, gl) in tl["groups"]:
            k.scan(G[:, g0:g0 + gl], g[:, g0:g0 + gl])
        Gb = k.sb("Gb", [4, TTc])
        k.tt(Gb.v, G.v, spb.v, ALU.subtract)
        nG = k.sb("nG", [4, TTc])
        k.ts(nG.v, G.v, -1.0, ALU.mult)
        r_beta = k.sb("r_beta", [4, TTc])
        k.act(r_beta.v, spb.v, AF.Exp, scale=-1.0)
        r_bg = k.sb("r_bg", [4, TTc])
        k.act(r_bg.v, Gb.v, AF.Exp)
        r_eg = k.sb("r_eg", [4, TTc])
        k.act(r_eg.v, G.v, AF.Exp)
        r_kd = k.sb("r_kd", [4, TTc])
        nch = len(chunks)
        ge = k.sb("ge", [4, nch + 16])
        for j, ch in enumerate(chunks):
            c0, c = ch["c0"], ch["c"]
            if ch["kind"] == "samp":
                Gv = G[:, c0:c0 + 64].re("p (s j) -> p s j", j=4)
                k.cp(ge[:, j:j + 16], Gv[:, :, 3])
                k.tt(r_kd[:, c0:c0 + 64].re("p (s j) -> p s j", j=4), nG[:, c0:c0 + 64].re("p (s j) -> p s j", j=4),
                     ge[:, j:j + 16].unsq(2).bc([4, 16, 4]), ALU.add)
            else:
                k.cp(ge[:, j:j + 1], G[:, c0 + c - 1:c0 + c])
                k.ts(r_kd[:, c0:c0 + c], nG[:, c0:c0 + c], ge[:, j:j + 1], ALU.add)
        k.act(r_kd.v, r_kd.v, AF.Exp)
        nw = nch + 15 if chunks[-1]["kind"] == "samp" else nch
        k.act(ge[:, 0:nw], ge[:, 0:nw], AF.Exp)
        gbc = bcast_rows(ge[:, 0:nw], nw, "gbc")
        for j, ch in enumerate(chunks):
            c0, c, kind = ch["c0"], ch["c"], ch["kind"]
            _, m_st, m_stT, m_inT = MASKS["b" if kind == "samp" else "c"]
            nsq = {128: 6, 64: 5, 16: 3}[c] if kind != "samp" else 1
            with k.scope():
                sc = rows_to_tm([r_beta.v, r_bg.v, r_eg.v, r_kd.v], c0, c, "scg")
                rv = k.sb("rv", [CP, 4, 128])
                rk = k.sb("rk", [CP, 4, 128])
                kd = k.sb("kd", [CP, 4, 128])
                k.push()
                kcn = k.sb("kcn", [CP, 4, 128])
                vcn = k.sb("vcn", [CP, 4, 128])
                p = k.ps()
                for h in range(4):
                    k.tr(p[0:c, h * 128:(h + 1) * 128], qkvT[:, 4 + h, c0:c0 + c], ident)
                k.cp(kcn[0:c].re("p h d -> p (h d)"), p[0:c, :], eng="act")
                p = k.ps()
                for h in range(4):
                    k.tr(p[0:c, h * 128:(h + 1) * 128], qkvT[:, 8 + h, c0:c0 + c], ident)
                k.cp(vcn[0:c].re("p h d -> p (h d)"), p[0:c, :])
                k.tt(rv[0:c], vcn[0:c], sc[0:c, 0:4].unsq(2).bc([c, 4, 128]), ALU.mult)
                k.tt(rk[0:c], kcn[0:c], sc[0:c, 4:8].unsq(2).bc([c, 4, 128]), ALU.mult)
                k.tt(kd[0:c], kcn[0:c], sc[0:c, 12:16].unsq(2).bc([c, 4, 128]), ALU.mult)
                k.pop()
                ob = k.sb("ob", [CP, 4, 128])

                def head(h):
                    if kind == "samp":
                        k.push()
                    qh = qkvT[:, h, c0:c0 + c]
                    kh = qkvT[:, 4 + h, c0:c0 + c]
                    pe_ = k.ps()
                    k.mm(pe_[0:c, 0:c], Gb[:, c0:c0 + c], oh[:, h, 0:c], start=True, stop=False)
                    k.mm(pe_[0:c, 0:c], oh[:, h, 0:c], nG[:, c0:c0 + c], start=False, stop=False)
                    k.mm(pe_[0:c, 0:c], ident[0:c, 0:c], m_st[0:c, 0:c], start=False, stop=True)
                    k.mm(pe_[0:c, c:2 * c], oh[:, h, 0:c], Gb[:, c0:c0 + c], start=True, stop=False)
                    k.mm(pe_[0:c, c:2 * c], nG[:, c0:c0 + c], oh[:, h, 0:c], start=False, stop=False)
                    k.mm(pe_[0:c, c:2 * c], ident[0:c, 0:c], m_stT[0:c, 0:c], start=False, stop=True)
                    k.mm(pe_[0:c, 2 * c:3 * c], oh[:, h, 0:c], G[:, c0:c0 + c], start=True, stop=False)
                    k.mm(pe_[0:c, 2 * c:3 * c], nG[:, c0:c0 + c], oh[:, h, 0:c], start=False, stop=False)
                    k.mm(pe_[0:c, 2 * c:3 * c], ident[0:c, 0:c], m_inT[0:c, 0:c], start=False, stop=True)
                    Ex = k.sb("Ex", [CP, 3 * CP])
                    k.act(Ex[0:c, 0:3 * c], pe_[0:c, 0:3 * c], AF.Exp)
                    yield
                    psc = k.ps()
                    k.mm(psc[0:c, 0:c], kh, kh)
                    k.mm(psc[0:c, c:2 * c], kh, kh)
                    k.mm(psc[0:c, 2 * c:3 * c], kh, qh)
                    M3 = Ex
                    k.tt(M3[0:c, 0:3 * c], psc[0:c, 0:3 * c], Ex[0:c, 0:3 * c], ALU.mult)
                    TTs = [k.sb("TTm", [CP, CP]), k.sb("TTm", [CP, CP])]
                    P2s = [k.sb("P2", [CP, 2 * CP]), k.sb("P2", [CP, 2 * CP])]
                    TTm = TTs[0]
                    k.tt(TTm[0:c, 0:c], ident[0:c, 0:c], M3[0:c, c:2 * c], ALU.subtract)
                    yield
                    Pw = M3
                    for q in range(nsq):
                        pp = k.ps()
                        k.mm(pp[0:c, 0:c], Pw[0:c, c:2 * c], Pw[0:c, 0:c])
                        P2 = P2s[q % 2]
                        if q < nsq - 1:
                            k.mm(pp[0:c, c:2 * c], Pw[0:c, 0:c], Pw[0:c, c:2 * c])
                            k.cp(P2[0:c, 0:2 * c], pp[0:c, 0:2 * c], eng="act")
                        else:
                            k.cp(P2[0:c, 0:c], pp[0:c, 0:c], eng="act")
                        yield
                        pt_ = k.ps()
                        k.mm(pt_[0:c, 0:c], P2[0:c, 0:c], TTm[0:c, 0:c])
                        Tn = TTs[(q + 1) % 2]
                        k.tt(Tn[0:c, 0:c], TTm[0:c, 0:c], pt_[0:c, 0:c], ALU.add)
                        TTm = Tn
                        Pw = P2
                        yield
                    pw = k.ps()
                    k.mm(pw[:, 0:c], rk[0:c, h, :], TTm[0:c, 0:c])
                    WTn = k.sb("WTn", [128, CP])
                    k.ts(WTn[:, 0:c], pw[:, 0:c], -1.0, ALU.mult)
                    yield
                    pu = k.ps()
                    po = k.ps()
                    if kind != "samp":
                        k.mm(pu[0:c, 0:128], TTm[0:c, 0:c], rv[0:c, h, :], start=True, stop=False)
                        k.mm(pu[0:c, 0:128], WTn[:, 0:c], S[:, h, :], start=False, stop=True)
                        Us = P2s[1][:, 0:128]
                        k.cp(Us[0:c], pu[0:c, 0:128], eng="act")
                        k.mm(po[0:c, 0:128], qh, S[:, h, :])
                        o1 = P2s[0][:, 0:128]
                        k.act(o1[0:c], po[0:c, 0:128], AF.Identity, scale=sc[0:c, 8 + h:9 + h])
                        yield
                        po2 = k.ps()
                        k.mm(po2[0:c, 0:128], M3[0:c, 2 * c:3 * c], Us[0:c])
                        k.tt(ob[0:c, h, :], o1[0:c], po2[0:c, 0:128], ALU.add)
                        pS_ = k.ps()
                        k.mm(pS_[:, 0:128], kd[0:c, h, :], Us[0:c])
                        k.stt(S[:, h, :], S[:, h, :], gbc[:, j, h:h + 1], pS_[:, 0:128], ALU.mult, ALU.add)
                    else:
                        WTm = k.sb("WTm", [128, 16, 64])
                        qTm = k.sb("qTm", [128, 16, 64])
                        k.memset(WTm.v, 0.0)
                        k.memset(qTm.v, 0.0)
                        for i in range(16):
                            k.cp(WTm[:, i, 4 * i:4 * i + 4], WTn[:, 4 * i:4 * i + 4])
                            k.cp(qTm[:, i, 4 * i:4 * i + 4], qkvT[:, h, c0 + 4 * i:c0 + 4 * i + 4], eng="act")
                        Sg = []
                        for gq in range(4):
                            t = k.sb("Sg", [128, 4, 128])
                            k.dma_in(t.v, sS[l].rearrange("(s h) d v -> h d s v", h=4)[h][:, 4 * gq:4 * gq + 4, :])
                            Sg.append(t)
                        k.mm(pu[0:c, 0:128], TTm[0:c, 0:c], rv[0:c, h, :], start=True, stop=False)
                        for i in range(16):
                            k.mm(pu[0:c, 0:128], WTm[:, i, :], Sg[i // 4][:, i % 4, :], start=False, stop=(i == 15))
                        Us = k.sb("Us", [CP, 128])
                        k.cp(Us[0:c], pu[0:c, 0:128], eng="act")
                        for i in range(16):
                            k.mm(po[0:c, 0:128], qTm[:, i, :], Sg[i // 4][:, i % 4, :], start=(i == 0), stop=(i == 15))
                        o1 = k.sb("o1", [CP, 128])
                        k.act(o1[0:c], po[0:c, 0:128], AF.Identity, scale=sc[0:c, 8 + h:9 + h])
                        po2 = k.ps()
                        k.mm(po2[0:c, 0:128], M3[0:c, 2 * c:3 * c], Us[0:c])
                        k.tt(ob[0:c, h, :], o1[0:c], po2[0:c, 0:128], ALU.add)
                        for gq in range(4):
                            Ux = k.sb("Ux", [64, 4, 128])
                            k.tt(Ux.v, Us[0:64].unsq(1).bc([64, 4, 128]), ind[:, 4 * gq:4 * gq + 4].unsq(2).bc([64, 4, 128]), ALU.mult)
                            pS_ = k.ps()
                            k.mm(pS_[:, 0:512], kd[0:64, h, :], Ux.v.re("p a b -> p (a b)"))
                            k.tt(Sg[gq].v, Sg[gq].v, gbc[:, j + 4 * gq:j + 4 * gq + 4, h:h + 1].bc([128, 4, 128]), ALU.mult)
                            k.tt(Sg[gq].v, Sg[gq].v, pS_[:, 0:512].re("p (a b) -> p a b", a=4), ALU.add)
                            k.dma_out(oS[l].rearrange("(s h) d v -> h d s v", h=4)[h][:, 4 * gq:4 * gq + 4, :], Sg[gq].v)
                    if kind == "samp":
                        k.pop()

                if kind == "samp":
                    for h in range(4):
                        for _ in head(h):
                            pass
                else:
                    interleave([head(h) for h in range(4)])
                sq = rv
                k.tt(sq[0:c], ob[0:c], ob[0:c], ALU.mult)
                ss = rk[:, 0, 0:4]
                k.red(ss[0:c], sq[0:c], ALU.add)
                k.ts(ss[0:c], ss[0:c], 1.0 / 128.0, ALU.mult)
                rs = kd[:, 0, 0:4]
                rsqrt_small(rs[0:c], ss[0:c], NORM_EPS)
                k.tt(ob[0:c], ob[0:c], rs[0:c].unsq(2).bc([c, 4, 128]), ALU.mult)
                k.tt(ob[0:c], ob[0:c], wz[j][0:c], ALU.mult)
                pt = k.ps()
                for h in range(4):
                    k.tr(pt[:, h * c:(h + 1) * c], ob[0:c, h, :], ident)
                k.cp(hBT[:, :, c0:c0 + c], pt[:, 0:4 * c].re("p (h t) -> p h t", h=4), eng="act")

    def ln_rows(src, n, g_fm, b_fm, dst32, dstB, c0, final=None):
        with k.scope():
            st = k.sb("lnst", [128, 12])
            mv = k.sb("lnmv", [128, 2])
            rs = k.sb("lnrs", [128, 1])
            xn = k.sb("lnxn", [128, 1024])
            k.op("dve", lambda e: e.bn_stats(out=st.ap[0:n, 0:6], in_=src.ap[:, 0:512]), reads=[src.t], writes=[st])
            k.op("dve", lambda e: e.bn_stats(out=st.ap[0:n, 6:12], in_=src.ap[:, 512:1024]), reads=[src.t], writes=[st])
            k.op("dve", lambda e: e.bn_aggr(out=mv.ap[0:n, :], in_=st.ap[0:n, :]), reads=[st], writes=[mv])
            rsqrt_small(rs[0:n, :], mv[0:n, 1:2], LN_EPS)
            k.ts(xn[0:n, :], src, mv[0:n, 0:1], ALU.subtract, rs[0:n, 0:1], ALU.mult)
            if final is not None:
                k.tt(xn[0:n, :], xn[0:n, :], g2bc[0:n, :], ALU.mult)
                k.tt(xn[0:n, :], xn[0:n, :], b2bc[0:n, :], ALU.add)
                for (dst, r0, r1) in final:
                    k.dma_out(dst, xn[r0:r1, :])
            else:
                for half in range(2):
                    p = k.ps()
                    for q in range(4):
                        cc = half * 4 + q
                        k.tr(p[:, q * n:(q + 1) * n], xn[0:n, cc * 128:(cc + 1) * 128], ident)
                    for q in range(4):
                        cc = half * 4 + q
                        if q % 2 == 0:
                            k.act(dst32[:, cc, c0:c0 + n], p[:, q * n:(q + 1) * n], AF.Identity,
                                  bias=b_fm[:, cc, 0:1], scale=g_fm[:, cc, 0:1])
                        else:
                            k.ts(dst32[:, cc, c0:c0 + n], p[:, q * n:(q + 1) * n], g_fm[:, cc, 0:1], ALU.mult,
                                 b_fm[:, cc, 0:1], ALU.add)
                k.cp(dstB[:, :, c0:c0 + n], dst32[:, :, c0:c0 + n], eng="act")

    def merge_phase(l, tl, xT, xT32, hAT, hBT, x1T32, x1T):
        P = LP[l]
        TTc = tl["TT"]
        yT = k.sb("yT", [128, 8, TTc], dt=BF16)
        for hh in range(2):
            wpp = wload_parts([(w_pa[l][:, hh * 512:(hh + 1) * 512], 4), (w_pb[l][:, hh * 512:(hh + 1) * 512], 4)], 512)
            wga = wload(w_in[l][:, G_MERGE + hh * 512:G_MERGE + (hh + 1) * 512], 8, 512)
            ga = []
            for q4 in range(4):
                p = proj_fm(wga, q4 * 128, xT, TTc)
                t = k.sb("ga", [128, TTc])
                k.act(t.v, p[:, 0:TTc], AF.Sigmoid)
                ppa = k.ps()
                for kk in range(4):
                    k.mm(ppa[:, 0:TTc], wpp[:, kk, q4 * 128:(q4 + 1) * 128], hAT[:, kk, :], start=(kk == 0), stop=(kk == 3))
                k.tt(t.v, t.v, ppa[:, 0:TTc], ALU.mult)
                ga.append(t)
            wgb = wload(w_in[l][:, G_MERGE + D + hh * 512:G_MERGE + D + (hh + 1) * 512], 8, 512)
            for q4 in range(4):
                n = hh * 4 + q4
                p = proj_fm(wgb, q4 * 128, xT, TTc)
                t = k.sb("gbm", [128, TTc])
                k.act(t.v, p[:, 0:TTc], AF.Sigmoid)
                ppb = k.ps()
                for kk in range(4):
                    k.mm(ppb[:, 0:TTc], wpp[:, 4 + kk, q4 * 128:(q4 + 1) * 128], hBT[:, kk, :], start=(kk == 0), stop=(kk == 3))
                k.tt(t.v, t.v, ppb[:, 0:TTc], ALU.mult)
                k.tt(yT[:, n, :], t.v, ga[q4].v, ALU.add)
        tgs = tl["tgs"]
        osb = [k.sb("osb", [128, 1024]) for _ in tgs]
        for hh in range(2):
            wb = wload(w_out[l][:, hh * 512:(hh + 1) * 512], 8, 512)
            for j, (c0, n) in enumerate(tgs):
                p = k.ps()
                for n8 in range(8):
                    k.mm(p[0:n, 0:512], yT[:, n8, c0:c0 + n], wb[:, n8, :], start=(n8 == 0), stop=False)
                for q4 in range(4):
                    k.mm(p[0:n, q4 * 128:(q4 + 1) * 128], xT32[:, hh * 4 + q4, c0:c0 + n], aident.v, start=False, stop=(q4 == 3))
                k.cp(osb[j][0:n, hh * 512:(hh + 1) * 512], p[0:n, 0:512], eng="act")
        for j, (c0, n) in enumerate(tgs):
            ln_rows(osb[j][0:n, :], n, P["g1"], P["b1"], x1T32, x1T, c0)

    def conv_fm(p, cc, nt, carry, cw, s_in, s_out, tl, tag, pools=None):
        TTc = tl["TT"]
        hh = nt - 1
        acc = pools[1].get() if pools is not None else k.sb(tag + "acc", [128, TTc])
        if tl["first"]:
            ext = k.sb(tag + "ext", [128, 16 + hh])
            k.cp(ext[:, 0:hh], carry[:, cc, :])
            k.cp(ext[:, hh:16 + hh], p[:, 0:16], eng="act")
            exs = k.sb(tag + "exs", [128, 16, 4 + hh])
            hist = k.sb(tag + "hist", [16 * hh, 128])
            k.dma_in(hist.v, s_in[:, cc * 128:(cc + 1) * 128])
            ph = k.ps()
            k.tr(ph[:, 0:16 * hh], hist.v, ident)
            k.cp(exs[:, :, 0:hh], ph[:, 0:16 * hh].re("p (s j) -> p s j", j=hh))
            k.cp(exs[:, :, hh:4 + hh], p[:, 16:80].re("p (s j) -> p s j", j=4), eng="act")
            k.cp(carry[:, cc, :], ext[:, 16:16 + hh])
            pc = k.ps()
            nst = k.sb(tag + "nst", [128, 16, hh])
            k.cp(nst.v, exs[:, :, 4:4 + hh])
            k.tr(pc[0:16 * hh, 0:128], nst.v.re("p s j -> p (s j)"), ident)
            ob = k.sb(tag + "ob", [16 * hh, 128])
            k.cp(ob.v, pc[0:16 * hh, 0:128])
            k.dma_out(s_out[:, cc * 128:(cc + 1) * 128], ob.v)
            av = acc[:, 0:16]
            k.ts(av, ext[:, 0:16], cw[:, cc, 0:1], ALU.mult)
            for jj in range(1, nt):
                k.stt(av, ext[:, jj:jj + 16], cw[:, cc, jj:jj + 1], av, ALU.mult, ALU.add)
            av = acc[:, 16:80].re("p (s j) -> p s j", j=4)
            k.ts(av, exs[:, :, 0:4], cw[:, cc, 0:1], ALU.mult)
            for jj in range(1, nt):
                k.stt(av, exs[:, :, jj:jj + 4], cw[:, cc, jj:jj + 1], av, ALU.mult, ALU.add)
            if TTc > 80:
                n2 = TTc - 80
                ext2 = k.sb(tag + "ext2", [128, n2 + hh])
                k.cp(ext2[:, 0:hh], carry[:, cc, :])
                k.cp(ext2[:, hh:n2 + hh], p[:, 80:TTc], eng="act")
                k.cp(carry[:, cc, :], ext2[:, n2:n2 + hh])
                av = acc[:, 80:TTc]
                k.ts(av, ext2[:, 0:n2], cw[:, cc, 0:1], ALU.mult)
                for jj in range(1, nt):
                    k.stt(av, ext2[:, jj:jj + n2], cw[:, cc, jj:jj + 1], av, ALU.mult, ALU.add)
        else:
            ext = pools[0].get()
            k.cp(ext[:, 0:hh], carry[:, cc, :], eng="pool")
            k.cp(ext[:, hh:TTc + hh], p[:, 0:TTc], eng="act")
            k.act(acc.v, p[:, 0:TTc], AF.Identity, scale=cw[:, cc, hh:hh + 1])
            k.cp(carry[:, cc, :], ext[:, TTc:TTc + hh], eng="pool")
            for jj in range(0, hh):
                k.stt(acc.v, ext[:, jj:jj + TTc], cw[:, cc, jj:jj + 1], acc.v, ALU.mult, ALU.add)
        return acc

    def ffn_phase(l, tl, x1T, x1T32, x2T32, x2T, last):
        P = LP[l]
        TTc = tl["TT"]
        cf, cwf = P["cf"], P["cwf"]
        hT = k.sb("hT", [128, 22, TTc], dt=BF16)
        k.push()
        pools = None if tl["first"] else (RPool("fext", [128, TTc + 2], 4), RPool("facc", [128, TTc], 4))
        for blk in range(6):
            ncols = 512 if blk < 5 else 256
            wb1 = wload(w_up[l][:, blk * 512:blk * 512 + ncols], 8, ncols)
            wb2 = wload(w_up[l][:, DFF + blk * 512:DFF + blk * 512 + ncols], 8, ncols)
            for q4 in range(ncols // 128):
                f = blk * 4 + q4
                with k.scope():
                    p1 = proj_fm(wb1, q4 * 128, x1T, TTc)
                    a1 = conv_fm(p1, f, 3, cf, cwf, sfc[l], ofc[l], tl, "f", pools)
                    k.act(a1.v, a1.v, AF.Silu)
                    p2 = proj_fm(wb2, q4 * 128, x1T, TTc)
                    a2 = conv_fm(p2, 22 + f, 3, cf, cwf, sfc[l], ofc[l], tl, "f", pools)
                    k.tt(hT[:, f, :], a1.v, a2.v, ALU.mult, eng=("dve" if tl["first"] else "pool"))
        k.pop()
        tgs = tl["tgs"]
        osb = [k.sb("osb2", [128, 1024]) for _ in tgs]
        kgs = [(0, 8), (8, 8), (16, 6)]
        for hh in range(2):
            pts = [k.ps() for _ in tgs]
            for (k0, nk) in kgs:
                wb = wload(w_down[l][k0 * 128:(k0 + nk) * 128, hh * 512:(hh + 1) * 512], nk, 512)
                for j, (c0, n) in enumerate(tgs):
                    for kk in range(nk):
                        k.mm(pts[j][0:n, 0:512], hT[:, k0 + kk, c0:c0 + n], wb[:, kk, :], start=(k0 + kk == 0), stop=False,
                             sig=(kk == nk - 1))
            for j, (c0, n) in enumerate(tgs):
                for q4 in range(4):
                    k.mm(pts[j][0:n, q4 * 128:(q4 + 1) * 128], x1T32[:, hh * 4 + q4, c0:c0 + n], aident.v, start=False, stop=(q4 == 3))
                k.cp(osb[j][0:n, hh * 512:(hh + 1) * 512], pts[j][0:n, 0:512], eng="act")
        for j, (c0, n) in enumerate(tgs):
            fin = None
            if last:
                if tl["first"] and c0 == 0:
                    fin = [(ys[:, :], 16, 80)]
                else:
                    tok0 = tl["tok0"] + c0 - tl["poff"]
                    fin = [(yp[tok0:tok0 + n, :], 0, n)]
            ln_rows(osb[j][0:n, :], n, P["g2"], P["b2"], x2T32, x2T, c0, final=fin)

    for ti, tl in enumerate(tiles):
        TTc = tl["TT"]
        cur["ti"] = ti
        cur["n"] = 0
        k.conservative = False
        with k.scope():
            xT32 = k.sb("xT32", [128, 8, TTc])
            xT = k.sb("xT", [128, 8, TTc], dt=BF16)
            for (c0, n) in tl["tgs"]:
                with k.scope():
                    xin = k.sb("xin", [128, 1024])
                    if tl["first"] and c0 == 0:
                        k.dma_in(xin[0:16, :], meta[:, :])
                        k.dma_in(xin[16:80, :], xs[:, :])
                    else:
                        tk = tl["tok0"] + c0 - tl["poff"]
                        k.dma_in(xin[0:n, :], xp[tk:tk + n, :])
                    ln_rows(xin[0:n, :], n, g_emb, b_emb, xT32, xT, c0)
            if ti == 0:
                tap("xT0", xT32.v)
                tap("xTb0", xT.v)
            xpairs = [(xT32, xT), (k.sb("xB32", [128, 8, TTc]), k.sb("xB", [128, 8, TTc], dt=BF16)),
                      (k.sb("xC32", [128, 8, TTc]), k.sb("xC", [128, 8, TTc], dt=BF16))]
            for l in range(DEPTH):
                (x1T32, x1T) = xpairs[1]
                (x2T32, x2T) = xpairs[2] if l % 2 == 0 else xpairs[0]
                with k.scope():
                    hAT = k.sb("hAT", [128, 4, TTc], dt=BF16)
                    hBT = k.sb("hBT", [128, 4, TTc], dt=BF16)
                    with k.scope():
                        mlstm_phase(l, tl, xT, hAT)
                    if ti == 0:
                        tap("hAT%d" % l, hAT.v)
                    with k.scope():
                        gdn_phase(l, tl, xT, hBT)
                    if ti == 0:
                        tap("hBT%d" % l, hBT.v)
                    with k.scope():
                        merge_phase(l, tl, xT, xT32, hAT, hBT, x1T32, x1T)
                if ti == 0:
                    tap("x1T%d" % l, x1T32.v)
                with k.scope():
                    ffn_phase(l, tl, x1T, x1T32, x2T32, x2T, last=(l == DEPTH - 1))
                if ti == 0 and l == 0:
                    tap("x2T%d" % l, x2T32.v)
                xT, xT32 = x2T, x2T32
    for l in range(DEPTH):
        P = LP[l]
        k.dma_out(pC[l].rearrange("h d v -> d h v"), P["Um"][:, :, 0:128])
        with k.scope():
            pt = k.ps()
            k.tr(pt[0:4, 0:128], P["Um"][:, :, 128], ident)
            nr = k.sb("pnr", [4, 128])
            k.cp(nr.v, pt[0:4, 0:128])
            k.dma_out(pn[l], nr.v)
        with k.scope():
            ptm = k.ps()
            k.tr(ptm[0:1, 0:4], P["mcur"].v, ident)
            mr = k.sb("pmr", [1, 4])
            k.cp(mr.v, ptm[0:1, 0:4])
            k.dma_out(pm[l].rearrange("h o -> o h"), mr.v)
        k.dma_out(pS[l].rearrange("h d v -> d h v"), P["S"].v)
        fm_to_rows(P["cg"].v, 3, 12, pgc[l])
        fm_to_rows(P["cf"].v, 2, 44, pfc[l])
    k.finish()
    return nc, tapouts


_CACHE = {}


def _prep_inputs(inp, c):
    f = lambda a: np.ascontiguousarray(np.asarray(a, dtype=np.float32))
    s0, s1 = NS * c, NS * (c + 1)
    m = {
        "xp": f(inp["x_prompt"][c]),
        "xs": f(inp["x_sample"][s0:s1]).reshape(64, D),
        "meta": f(inp["meta_tokens"]),
        "sC": f(inp["state_mlstm_C"][:, s0:s1]).reshape(DEPTH, NS * 4, 128, 128),
        "sn": f(inp["state_mlstm_n"][:, s0:s1]).reshape(DEPTH, NS * 4, 128),
        "sm": f(inp["state_mlstm_m"][:, s0:s1]),
        "sS": f(inp["state_gdn_S"][:, s0:s1]).reshape(DEPTH, NS * 4, 128, 128),
        "sgc": f(inp["state_gdn_conv"][:, s0:s1]).reshape(DEPTH, NS * 3, 1536),
        "sfc": f(inp["state_ffn_conv"][:, s0:s1]).reshape(DEPTH, NS * 2, 2 * DFF),
        "ln_emb_g": f(inp["ln_emb_g"]).reshape(1, D),
        "ln_emb_b": f(inp["ln_emb_b"]).reshape(1, D),
        "w_in": f(inp["w_in"]),
        "gate_bias": f(inp["mlstm_gate_bias"]).reshape(DEPTH, 8, 1),
        "mnorm_w": f(inp["mlstm_norm_w"]).reshape(DEPTH, 1, 512),
        "gconv_w": f(inp["gdn_conv_w"]),
        "A_log": f(inp["gdn_A_log"]).reshape(DEPTH, 4, 1),
        "dt_bias": f(inp["gdn_dt_bias"]).reshape(DEPTH, 4, 1),
        "gnorm_w": f(inp["gdn_norm_w"]).reshape(DEPTH, 1, 128),
        "w_pa": f(inp["w_branch_a"]),
        "w_pb": f(inp["w_branch_b"]),
        "w_out": f(inp["w_out"]),
        "ln1_g": f(inp["ln1_g"]).reshape(DEPTH, 1, D),
        "ln1_b": f(inp["ln1_b"]).reshape(DEPTH, 1, D),
        "w_up": f(inp["w_up"]),
        "fconv_w": f(inp["ffn_conv_w"]),
        "w_down": f(inp["w_down"]),
        "ln2_g": f(inp["ln2_g"]).reshape(DEPTH, 1, D),
        "ln2_b": f(inp["ln2_b"]).reshape(DEPTH, 1, D),
    }
    return m


def kernel(**inp):
    if "nc" not in _CACHE:
        _CACHE["nc"] = build()[0]
    nc = _CACHE["nc"]
    in_maps = [_prep_inputs(inp, c) for c in range(8)]
    res = run_bass_kernel_spmd(nc, in_maps, core_ids=list(range(8)))
    R = res.results
    st = lambda name: np.stack([np.asarray(R[c][name]) for c in range(8)], axis=1)
    cat = lambda name, shp: np.concatenate([np.asarray(R[c][name]).reshape(shp) for c in range(8)], axis=1)
    y_prompt = np.stack([np.asarray(R[c]["yp"]) for c in range(8)], axis=0)
    y_sample = np.concatenate([np.asarray(R[c]["ys"]).reshape(NS, 4, D) for c in range(8)], axis=0)
    outs = (
        y_prompt, y_sample,
        st("pC"), st("pn"), st("pm").reshape(DEPTH, 8, 4), st("pS"), st("pgc"), st("pfc"),
        cat("oC", (DEPTH, NS, 4, 128, 128)), cat("on", (DEPTH, NS, 4, 128)), cat("om", (DEPTH, NS, 4)),
        cat("oS", (DEPTH, NS, 4, 128, 128)), cat("ogc", (DEPTH, NS, 3, 1536)), cat("ofc", (DEPTH, NS, 2, 2 * DFF)),
    )
    return tuple(np.ascontiguousarray(o, dtype=np.float32) for o in outs)
```

```python
import math
from contextlib import ExitStack
import numpy as np
import concourse.bass as bass
import concourse.mybir as mybir
from concourse.bass_utils import run_bass_kernel_spmd

F32 = mybir.dt.float32
BF16 = mybir.dt.bfloat16
AF = mybir.ActivationFunctionType
ALU = mybir.AluOpType
AX = mybir.AxisListType

NEG = -30000.0
LN_EPS = 1e-5
NORM_EPS = 1e-6
ALPHA = 4.0 ** 0.25
DEPTH = 2
D = 1024
KC = 8
NIN = 6160
A_Q, A_K, A_V, A_O, A_I = 0, 512, 1024, 1536, 2048
B_QKV, B_Z, B_BETA, B_A, G_MERGE = 2056, 3592, 4104, 4108, 4112
DFF = 2816
NS = 16


class V:
    __slots__ = ("t", "ap")

    def __init__(self, t, ap):
        self.t = t
        self.ap = ap

    def __getitem__(self, idx):
        return V(self.t, self.ap[idx])

    def re(self, pat, **kw):
        return V(self.t, self.ap.rearrange(pat, **kw))

    def bc(self, shape):
        return V(self.t, self.ap.to_broadcast(list(shape)))

    def unsq(self, axis):
        return V(self.t, self.ap.unsqueeze(axis))


class T:
    __slots__ = ("ap", "lw", "rd", "dsem", "name", "a0", "a1")

    def __init__(self, name, ap, rd):
        self.name = name
        self.ap = ap
        self.lw = None
        self.rd = dict(rd)
        self.dsem = None

    def __getitem__(self, idx):
        return V(self, self.ap[idx])

    @property
    def v(self):
        return V(self, self.ap)


class KB:
    def __init__(self, nc):
        self.nc = nc
        self.root = ExitStack()
        self.stacks = [self.root]
        self.scope_tiles = [[]]
        self.engs = {"pe": nc.tensor, "act": nc.scalar, "dve": nc.vector, "pool": nc.gpsimd, "sp": nc.sync}
        self.semh = {}
        self.own = {}
        self.cnt = {}
        for e in ("pe", "act", "dve", "pool"):
            s = self.root.enter_context(nc.semaphore("s_" + e))
            self.semh["s_" + e] = s
            self.own[e] = "s_" + e
            self.cnt["s_" + e] = 0
        self.pending = {e: False for e in self.own}
        self.dmasems = set()
        self.free_dsems = []
        self.seen = {e: {} for e in self.engs}
        self.grave = {}
        self.grave_ranges = []
        self.conservative = True
        with nc.sbuf_tensor("probe0", [128, 8], F32) as h0:
            a_ = int(nc.lookup_mloc(h0).addr)
        fill = (512 - a_ % 512) % 512
        if fill:
            self.root.enter_context(nc.sbuf_tensor("fill0", [128, fill // 4], F32))
        self.nid = 0
        self.ps_tiles = []
        self.ps_i = 0
        self.wb_tiles = []
        self.wb_i = 0

    def sb(self, name, shape, persist=False, dt=F32):
        self.nid += 1
        nm = "%s_%d" % (name, self.nid)
        st = self.root if persist else self.stacks[-1]
        shape = list(shape)
        nfree = 1
        for d_ in shape[1:]:
            nfree *= d_
        esz = 2 if dt == BF16 else 4
        per = 512 // esz
        npad = ((nfree + per - 1) // per) * per
        h = st.enter_context(self.nc.sbuf_tensor(nm, [shape[0], npad], dt))
        ml = self.nc.lookup_mloc(h)
        a0 = int(ml.addr)
        a1 = a0 + int(ml.dims[1])
        inh = {}
        keep = []
        for (g0, g1, evs) in self.grave_ranges:
            if g0 < a1 and a0 < g1:
                for sn, v in evs.items():
                    if inh.get(sn, 0) < v:
                        inh[sn] = v
                if a0 <= g0 and g1 <= a1:
                    continue
            keep.append((g0, g1, evs))
        self.grave_ranges = keep
        if self.conservative:
            for sn, v in self.grave.items():
                if inh.get(sn, 0) < v:
                    inh[sn] = v
        ap = h[:, 0:nfree]
        if len(shape) == 3:
            ap = ap.rearrange("p (a b) -> p a b", a=shape[1])
        elif len(shape) == 4:
            ap = ap.rearrange("p (a b c) -> p a b c", a=shape[1], b=shape[2])
        t = T(nm, ap, inh)
        t.a0, t.a1 = a0, a1
        self.min_rem = min(getattr(self, "min_rem", 1 << 30), self.nc.sbuf_bytes_remaining)
        if not persist:
            self.scope_tiles[-1].append(t)
        return t

    def psum(self, name, shape):
        h = self.root.enter_context(self.nc.psum_tensor(name, list(shape), F32))
        return T(name, h[:], {})

    def scope(self):
        kb = self

        class _S:
            def __enter__(s):
                kb.stacks.append(ExitStack())
                kb.scope_tiles.append([])

            def __exit__(s, *a):
                kb._free_tiles(kb.scope_tiles.pop())
                kb.stacks.pop().close()
                return False

        return _S()

    def push(self):
        self.stacks.append(ExitStack())
        self.scope_tiles.append([])

    def _free_tiles(self, tiles):
        for t in tiles:
            evs = dict(t.rd)
            if t.lw is not None and evs.get(t.lw[0], 0) < t.lw[1]:
                evs[t.lw[0]] = t.lw[1]
            if t.dsem is not None:
                if evs.get(t.dsem, 0) < self.cnt[t.dsem]:
                    evs[t.dsem] = self.cnt[t.dsem]
                self.free_dsems.append(t.dsem)
                t.dsem = None
            if evs:
                self.grave_ranges.append((t.a0, t.a1, evs))
                for sn, v in evs.items():
                    if self.grave.get(sn, 0) < v:
                        self.grave[sn] = v

    def pop(self):
        self._free_tiles(self.scope_tiles.pop())
        self.stacks.pop().close()

    def _g(self, ev):
        s, v = ev
        if self.grave.get(s, 0) < v:
            self.grave[s] = v

    def ps(self):
        t = self.ps_tiles[self.ps_i % len(self.ps_tiles)]
        self.ps_i += 1
        return t

    def wbuf(self):
        t = self.wb_tiles[self.wb_i % len(self.wb_tiles)]
        self.wb_i += 1
        return t

    def op(self, eng, fn, reads=(), writes=(), sig=True, dma_tile=None):
        e = self.engs[eng]
        is_load = dma_tile is not None and (dma_tile in writes) and dma_tile.lw is not None and dma_tile.lw[0] == dma_tile.dsem and not dma_tile.rd
        deps = {}

        def add(ev):
            if ev is None:
                return
            s, v = ev
            if deps.get(s, 0) < v:
                deps[s] = v

        for t in reads:
            add(t.lw)
        for t in writes:
            add(t.lw)
            for s, v in t.rd.items():
                add((s, v))
        own = self.own.get(eng) if dma_tile is None else None
        seen = self.seen[eng]
        for s, v in deps.items():
            if eng == "pe" and s == own:
                continue
            if dma_tile is not None and s == dma_tile.dsem and is_load:
                continue
            if s in self.dmasems:
                v = self.cnt[s]
            if seen.get(s, 0) >= v:
                continue
            e.wait_ge(self.semh[s], v)
            seen[s] = v
        if dma_tile is not None and dma_tile.dsem is None:
            if self.free_dsems:
                nm = self.free_dsems.pop(0)
                if seen.get(nm, 0) < self.cnt[nm]:
                    e.wait_ge(self.semh[nm], self.cnt[nm])
                    seen[nm] = self.cnt[nm]
            else:
                nm = "d_%d" % len(self.dmasems)
                self.semh[nm] = self.root.enter_context(self.nc.semaphore(nm))
                self.cnt[nm] = 0
                self.dmasems.add(nm)
            dma_tile.dsem = nm
        inst = fn(e)
        if dma_tile is not None:
            s = dma_tile.dsem
            inst.then_inc(self.semh[s], 16)
            self.cnt[s] += 16
            ev = (s, self.cnt[s])
        else:
            if sig:
                inst.then_inc(self.semh[own], 1)
                self.cnt[own] += 1
                ev = (own, self.cnt[own])
                self.pending[eng] = False
            else:
                ev = (own, self.cnt[own] + 1)
                self.pending[eng] = True
        for t in writes:
            t.lw = ev
            t.rd = {}
        for t in reads:
            if t.rd.get(ev[0], 0) < ev[1]:
                t.rd[ev[0]] = ev[1]
        return inst

    def finish(self):
        assert not any(self.pending.values())
        sp = self.engs["sp"]
        for s in sorted(self.dmasems):
            if self.cnt[s] > self.seen["sp"].get(s, 0):
                sp.wait_ge(self.semh[s], self.cnt[s])
        for e in ("pe", "act", "dve", "pool"):
            s = self.own[e]
            if self.cnt[s] > 0:
                sp.wait_ge(self.semh[s], self.cnt[s])
        self.root.close()

    def mm(self, out, lhsT, rhs, start=True, stop=True, sig=None):
        self.op("pe", lambda e: e.matmul(out.ap, lhsT=lhsT.ap, rhs=rhs.ap, start=start, stop=stop),
                reads=[lhsT.t, rhs.t], writes=[out.t], sig=(stop if sig is None else sig))

    def tr(self, out, in_, ident):
        if isinstance(ident, T):
            ident = ident.v
        r = in_.ap.shape[0]
        self.op("pe", lambda e: e.transpose(out=out.ap, in_=in_.ap, identity=ident.ap[0:r, 0:r]),
                reads=[in_.t, ident.t], writes=[out.t])

    def act(self, out, in_, func, bias=None, scale=None, eng="act"):
        kw = {}
        rd = [in_.t]
        if bias is not None:
            if isinstance(bias, V):
                kw["bias"] = bias.ap
                rd.append(bias.t)
            else:
                kw["bias"] = bias
        if scale is not None:
            if isinstance(scale, V):
                kw["scale"] = scale.ap
                rd.append(scale.t)
            else:
                kw["scale"] = scale
        self.op("act", lambda e: e.activation(out=out.ap, in_=in_.ap, func=func, **kw), reads=rd, writes=[out.t])

    def ts(self, out, in0, s1, op0, s2=None, op1=None, eng="dve"):
        rd = [in0.t]
        a1 = s1
        a2 = s2
        if isinstance(s1, V):
            rd.append(s1.t)
            a1 = s1.ap
        if isinstance(s2, V):
            rd.append(s2.t)
            a2 = s2.ap
        if op1 is None:
            self.op(eng, lambda e: e.tensor_scalar(out=out.ap, in0=in0.ap, scalar1=a1, scalar2=None, op0=op0),
                    reads=rd, writes=[out.t])
        else:
            self.op(eng, lambda e: e.tensor_scalar(out=out.ap, in0=in0.ap, scalar1=a1, scalar2=a2, op0=op0, op1=op1),
                    reads=rd, writes=[out.t])

    def tt(self, out, in0, in1, op, eng="dve"):
        self.op(eng, lambda e: e.tensor_tensor(out=out.ap, in0=in0.ap, in1=in1.ap, op=op),
                reads=[in0.t, in1.t], writes=[out.t])

    def stt(self, out, in0, scalar, in1, op0, op1, eng="dve"):
        rd = [in0.t, in1.t]
        a = scalar
        if isinstance(scalar, V):
            rd.append(scalar.t)
            a = scalar.ap
        self.op(eng, lambda e: e.scalar_tensor_tensor(out=out.ap, in0=in0.ap, scalar=a, in1=in1.ap, op0=op0, op1=op1),
                reads=rd, writes=[out.t])

    def cp(self, out, in_, eng="dve"):
        if eng == "act":
            self.op("act", lambda e: e.copy(out=out.ap, in_=in_.ap), reads=[in_.t], writes=[out.t])
        else:
            self.op(eng, lambda e: e.tensor_copy(out=out.ap, in_=in_.ap), reads=[in_.t], writes=[out.t])

    def memset(self, out, val, eng="dve"):
        self.op(eng, lambda e: e.memset(out.ap, val), writes=[out.t])

    def red(self, out, in_, op, eng="dve"):
        self.op(eng, lambda e: e.tensor_reduce(out=out.ap, in_=in_.ap, axis=AX.X, op=op), reads=[in_.t], writes=[out.t])

    def scan(self, out, in_, op0=ALU.add):
        self.op("dve", lambda e: e.tensor_tensor_scan(out=out.ap, data0=in_.ap, data1=in_.ap, initial=0.0 if op0 == ALU.add else -3.0e38,
                                                      op0=op0, op1=ALU.bypass), reads=[in_.t], writes=[out.t])

    def dma_in(self, out, in_ap, q="sp", slow=False):
        kw = {"allow_slow_non_contiguous": True} if slow else {}
        self.op(q, lambda e: e.dma_start(out=out.ap, in_=in_ap, **kw), writes=[out.t], dma_tile=out.t)

    def dma_out(self, out_ap, in_, q="sp", slow=False):
        kw = {"allow_slow_non_contiguous": True} if slow else {}
        self.op(q, lambda e: e.dma_start(out=out_ap, in_=in_.ap, **kw), reads=[in_.t], dma_tile=in_.t)


def build(TT=(384, 384, 384, 384, 384), taps=()):
    nc = bass.Bass("TRN2", target_bir_lowering=False)
    k = KB(nc)
    taps = set(taps)
    tapouts = {}

    def din(name, shape):
        return nc.dram_tensor(name, list(shape), F32, kind="ExternalInput").ap()

    def dout(name, shape):
        return nc.dram_tensor(name, list(shape), F32, kind="ExternalOutput").ap()

    xp = din("xp", [2048, D])
    xs = din("xs", [64, D])
    meta = din("meta", [16, D])
    sC = din("sC", [DEPTH, NS * 4, 128, 128])
    sn = din("sn", [DEPTH, NS * 4, 128])
    sm = din("sm", [DEPTH, NS, 4])
    sS = din("sS", [DEPTH, NS * 4, 128, 128])
    sgc = din("sgc", [DEPTH, NS * 3, 1536])
    sfc = din("sfc", [DEPTH, NS * 2, 2 * DFF])
    ln_emb_g = din("ln_emb_g", [1, D])
    ln_emb_b = din("ln_emb_b", [1, D])
    w_in = din("w_in", [DEPTH, D, NIN])
    gate_bias = din("gate_bias", [DEPTH, 8, 1])
    mnorm_w = din("mnorm_w", [DEPTH, 1, 512])
    gconv_w = din("gconv_w", [DEPTH, 4, 1536])
    A_log = din("A_log", [DEPTH, 4, 1])
    dt_bias = din("dt_bias", [DEPTH, 4, 1])
    gnorm_w = din("gnorm_w", [DEPTH, 1, 128])
    w_pa = din("w_pa", [DEPTH, 512, D])
    w_pb = din("w_pb", [DEPTH, 512, D])
    w_out = din("w_out", [DEPTH, D, D])
    ln1_g = din("ln1_g", [DEPTH, 1, D])
    ln1_b = din("ln1_b", [DEPTH, 1, D])
    w_up = din("w_up", [DEPTH, D, 2 * DFF])
    fconv_w = din("fconv_w", [DEPTH, 3, 2 * DFF])
    w_down = din("w_down", [DEPTH, DFF, D])
    ln2_g = din("ln2_g", [DEPTH, 1, D])
    ln2_b = din("ln2_b", [DEPTH, 1, D])

    yp = dout("yp", [2048, D])
    ys = dout("ys", [64, D])
    pC = dout("pC", [DEPTH, 4, 128, 128])
    pn = dout("pn", [DEPTH, 4, 128])
    pm = dout("pm", [DEPTH, 4, 1])
    pS = dout("pS", [DEPTH, 4, 128, 128])
    pgc = dout("pgc", [DEPTH, 3, 1536])
    pfc = dout("pfc", [DEPTH, 2, 2 * DFF])
    oC = dout("oC", [DEPTH, NS * 4, 128, 128])
    on = dout("on", [DEPTH, NS * 4, 128])
    om = dout("om", [DEPTH, NS, 4])
    oS = dout("oS", [DEPTH, NS * 4, 128, 128])
    ogc = dout("ogc", [DEPTH, NS * 3, 1536])
    ofc = dout("ofc", [DEPTH, NS * 2, 2 * DFF])

    def tap(name, view):
        if name not in taps:
            return
        shp = list(view.ap.shape)
        o = nc.dram_tensor("tap_" + name, shp, view.ap.dtype, kind="ExternalOutput").ap()
        tapouts[name] = shp
        k.dma_out(o, view, q="sp", slow=True)

    k.ps_tiles = [k.psum("ps%d" % i, [128, 512]) for i in range(6)]
    psd = k.psum("psd", [128, 1024])
    k.wb_tiles = [k.sb("wb%d" % i, [128, 8, 512], persist=True, dt=BF16) for i in range(4)]

    ident = k.sb("ident", [128, 128], persist=True)
    k.memset(ident.v, 0.0, eng="pool")
    k.op("pool", lambda e: e.affine_select(out=ident.ap, in_=ident.ap, pattern=[[-1, 128]], compare_op=ALU.not_equal,
                                           fill=1.0, base=0, channel_multiplier=1), reads=[ident], writes=[ident])
    aident = k.sb("aident", [128, 128], persist=True)
    k.ts(aident.v, ident.v, ALPHA, ALU.mult)
    ones = k.sb("ones", [128, 128], persist=True)
    k.memset(ones.v, 1.0)
    cst = k.sb("cst", [128, 8], persist=True)
    k.memset(cst[:, 4:5], LN_EPS)
    k.memset(cst[:, 0:1], 1.0)
    k.memset(cst[:, 1:2], NORM_EPS)
    k.memset(cst[:, 2:3], math.log(128.0 ** -0.5))
    k.memset(cst[:, 3:4], 0.0)

    def aff(t, pattern, op, fill, base, cm):
        k.op("pool", lambda e: e.affine_select(out=t.ap, in_=t.ap, pattern=pattern, compare_op=op, fill=fill,
                                               base=base, channel_multiplier=cm), reads=[t], writes=[t])

    m01T = k.sb("m01T", [128, 128], persist=True)
    k.memset(m01T.v, 1.0, eng="pool")
    aff(m01T, [[1, 128]], ALU.is_ge, 0.0, 0, -1)
    mn_st = k.sb("mn_st", [128, 128], persist=True)
    k.memset(mn_st.v, 0.0, eng="pool")
    aff(mn_st, [[-1, 128]], ALU.is_gt, NEG, 0, 1)
    mn_stT = k.sb("mn_stT", [128, 128], persist=True)
    k.memset(mn_stT.v, 0.0, eng="pool")
    aff(mn_stT, [[1, 128]], ALU.is_gt, NEG, 0, -1)
    mn_inT = k.sb("mn_inT", [128, 128], persist=True)
    k.memset(mn_inT.v, 0.0, eng="pool")
    aff(mn_inT, [[1, 128]], ALU.is_ge, NEG, 0, -1)
    indT = k.sb("indT", [16, 64], persist=True)
    k.memset(indT.v, 1.0, eng="pool")
    aff(indT, [[1, 64]], ALU.is_ge, 0.0, 0, -4)
    aff(indT, [[-1, 64]], ALU.is_ge, 0.0, 3, 4)
    ind = k.sb("ind", [64, 16], persist=True)
    k.memset(ind.v, 1.0, eng="pool")
    aff(ind, [[-4, 16]], ALU.is_ge, 0.0, 0, 1)
    aff(ind, [[4, 16]], ALU.is_ge, 0.0, 3, -1)
    same = k.sb("same", [64, 64], persist=True)
    p0 = k.ps()
    k.mm(p0[0:64, 0:64], indT.v, indT.v)
    k.cp(same.v, p0[0:64, 0:64])
    sneg = k.sb("sneg", [64, 64], persist=True)
    k.ts(sneg.v, same.v, -NEG, ALU.mult, NEG, ALU.add)
    b01T = k.sb("b01T", [64, 64], persist=True)
    k.tt(b01T.v, m01T[0:64, 0:64], same.v, ALU.mult)
    bn_st = k.sb("bn_st", [64, 64], persist=True)
    k.tt(bn_st.v, mn_st[0:64, 0:64], sneg.v, ALU.min)
    bn_stT = k.sb("bn_stT", [64, 64], persist=True)
    k.tt(bn_stT.v, mn_stT[0:64, 0:64], sneg.v, ALU.min)
    bn_inT = k.sb("bn_inT", [64, 64], persist=True)
    k.tt(bn_inT.v, mn_inT[0:64, 0:64], sneg.v, ALU.min)
    MASKS = {"c": (m01T, mn_st, mn_stT, mn_inT), "b": (b01T, bn_st, bn_stT, bn_inT)}
    oh = k.sb("oh", [4, 4, 128], persist=True)
    k.memset(oh.v, 0.0, eng="pool")
    aff(oh, [[-1, 4], [0, 128]], ALU.not_equal, 1.0, 0, 1)

    def load_fm(name, src2d, R, C):
        dst = k.sb(name, [128, C, R], persist=True)
        with k.scope():
            tmp = k.sb("ldfm", [R, C * 128])
            k.dma_in(tmp.v, src2d)
            c0 = 0
            while c0 < C:
                n = min(C - c0, 512 // R)
                p = k.ps()
                for c in range(n):
                    k.tr(p[:, c * R:(c + 1) * R], tmp[:, (c0 + c) * 128:(c0 + c + 1) * 128], ident)
                k.cp(dst[:, c0:c0 + n, :], p[:, 0:n * R].re("p (c r) -> p c r", r=R))
                c0 += n
        return dst

    def load_bc(name, src_row, n):
        dst = k.sb(name, [128, n], persist=True)
        k.dma_in(dst.v, src_row.to_broadcast([128, n]), slow=True)
        return dst

    g_emb = load_fm("g_emb", ln_emb_g, 1, 8)
    b_emb = load_fm("b_emb", ln_emb_b, 1, 8)
    LP = []
    for l in range(DEPTH):
        P = {}
        P["g1"] = load_fm("g1", ln1_g[l], 1, 8)
        P["b1"] = load_fm("b1", ln1_b[l], 1, 8)
        P["g2"] = load_fm("g2", ln2_g[l], 1, 8)
        P["b2"] = load_fm("b2", ln2_b[l], 1, 8)
        P["cwg"] = load_fm("cwg", gconv_w[l], 4, 12)
        P["cwf"] = load_fm("cwf", fconv_w[l], 3, 44)
        P["anw"] = load_bc("anw", mnorm_w[l], 512)
        P["gnw"] = load_bc("gnw", gnorm_w[l], 128)
        gbi = k.sb("gbi", [4, 1], persist=True)
        gbf = k.sb("gbf", [4, 1], persist=True)
        gal = k.sb("gal", [4, 1], persist=True)
        gdt = k.sb("gdt", [4, 1], persist=True)
        k.dma_in(gbi.v, gate_bias[l][0:4, :], slow=True)
        k.dma_in(gbf.v, gate_bias[l][4:8, :], slow=True)
        k.dma_in(gal.v, A_log[l], slow=True)
        k.dma_in(gdt.v, dt_bias[l], slow=True)
        k.ts(gbf.v, gbf.v, -1.0, ALU.mult)
        k.act(gal.v, gal.v, AF.Exp)
        k.ts(gal.v, gal.v, -1.0, ALU.mult)
        P["gb"] = (gbi, gbf, gal, gdt)
        Um = k.sb("Um", [128, 4, 129], persist=True)
        k.memset(Um.v, 0.0)
        mcur = k.sb("mcur", [4, 1], persist=True)
        k.memset(mcur.v, 0.0)
        S = k.sb("S", [128, 4, 128], persist=True)
        k.memset(S.v, 0.0)
        cg = k.sb("cg", [128, 12, 3], persist=True)
        k.memset(cg.v, 0.0)
        cf = k.sb("cf", [128, 44, 2], persist=True)
        k.memset(cf.v, 0.0)
        P.update(Um=Um, mcur=mcur, S=S, cg=cg, cf=cf)
        LP.append(P)
    g2bc = load_bc("g2bc", ln2_g[DEPTH - 1], D)
    b2bc = load_bc("b2bc", ln2_b[DEPTH - 1], D)

    NP0 = 128
    tiles = [dict(TT=80 + NP0, first=True, poff=80, tok0=0,
                  chunks=[dict(c0=0, c=16, kind="meta"), dict(c0=80, c=NP0, kind="p", tok0=0), dict(c0=16, c=64, kind="samp")],
                  groups=[(0, 16)] + [(16 + 4 * i, 4) for i in range(16)] + [(80, NP0)], tgs=[(0, 80), (80, NP0)],
                  pgroups=[(0, 16, [(0, 0, 16)]), (16, 64, [(2, 0, 64)]), (80, NP0, [(1, 0, NP0)])])]
    sizes = TT if isinstance(TT, (list, tuple)) else [TT] * (2048 // TT)
    assert sum(sizes) + NP0 == 2048
    t0_ = NP0
    for sz in sizes:
        tiles.append(dict(TT=sz, first=False, poff=0, chunks=[dict(c0=128 * j, c=128, kind="p", tok0=t0_ + 128 * j) for j in range(sz // 128)],
                          groups=[(128 * j, 128) for j in range(sz // 128)], tgs=[(128 * g, 128) for g in range(sz // 128)], tok0=t0_,
                          pgroups=[(128 * g, 128, [(g, 0, 128)]) for g in range(sz // 128)]))
        t0_ += sz

    scratch = {}
    cur = {"ti": 0, "n": 0}

    def wload_parts(parts, ncols):
        wb = k.wbuf()
        ktot = sum(kc for _, kc in parts)
        key = cur["n"]
        cur["n"] += 1
        if cur["ti"] == 0:
            k0 = 0
            for (src, kc) in parts:
                k.dma_in(wb[:, k0:k0 + kc, 0:ncols], src.rearrange("(k p) n -> p k n", p=128), q="pool")
                k0 += kc
            scr = nc.dram_tensor("wscr_%d" % key, [128, ktot, ncols], BF16).ap()
            st = T("scr%d" % key, None, {})
            scratch[key] = (scr, st)
            k.op("sp", lambda e: e.dma_start(out=scr, in_=wb.ap[:, 0:ktot, 0:ncols]), reads=[wb], writes=[st], dma_tile=wb)
        else:
            scr, st = scratch[key]
            k.op("sp", lambda e: e.dma_start(out=wb.ap[:, 0:ktot, 0:ncols], in_=scr), reads=[st], writes=[wb], dma_tile=wb)
        return wb

    def wload(src2d, kc, ncols):
        return wload_parts([(src2d, kc)], ncols)

    def proj_fm(wb, cb, xT, n, nk=8, k0=0):
        p = k.ps()
        for kk in range(nk):
            k.mm(p[:, 0:n], wb[:, k0 + kk, cb:cb + 128], xT[:, kk, 0:n], start=(kk == 0), stop=(kk == nk - 1))
        return p

    def proj_tm(wb, ncols, xT, c0, c):
        p = k.ps()
        for kk in range(8):
            k.mm(p[0:c, 0:ncols], xT[:, kk, c0:c0 + c], wb[:, kk, 0:ncols], start=(kk == 0), stop=(kk == 7))
        return p

    def rsqrt_small(out, in_, eps):
        n = in_.ap.shape[0]
        col = 1 if eps == NORM_EPS else 4
        k.act(out, in_, AF.Ln, bias=cst[0:n, col:col + 1])
        k.act(out, out, AF.Exp, scale=-0.5)

    def ln_chunk(src, c, g_fm, b_fm, dstT, c0, final_dst=None, dstB=None):
        with k.scope():
            st = k.sb("lnst", [64, 12])
            mv = k.sb("lnmv", [64, 2])
            rs = k.sb("lnrs", [64, 1])
            xn = k.sb("lnxn", [64, 1024])
            k.op("dve", lambda e: e.bn_stats(out=st.ap[0:c, 0:6], in_=src.ap[:, 0:512]), reads=[src.t], writes=[st])
            k.op("dve", lambda e: e.bn_stats(out=st.ap[0:c, 6:12], in_=src.ap[:, 512:1024]), reads=[src.t], writes=[st])
            k.op("dve", lambda e: e.bn_aggr(out=mv.ap[0:c, :], in_=st.ap[0:c, :]), reads=[st], writes=[mv])
            rsqrt_small(rs[0:c, :], mv[0:c, 1:2], LN_EPS)
            k.ts(xn[0:c, :], src, mv[0:c, 0:1], ALU.subtract, rs[0:c, 0:1], ALU.mult)
            if final_dst is not None:
                k.tt(xn[0:c, :], xn[0:c, :], g2bc[0:c, :], ALU.mult)
                k.tt(xn[0:c, :], xn[0:c, :], b2bc[0:c, :], ALU.add)
                k.dma_out(final_dst, xn[0:c, :])
            else:
                p = k.ps()
                for cc in range(8):
                    k.tr(p[:, cc * c:(cc + 1) * c], xn[0:c, cc * 128:(cc + 1) * 128], ident)
                for cc in range(8):
                    if cc % 2 == 0:
                        k.act(dstT[:, cc, c0:c0 + c], p[:, cc * c:(cc + 1) * c], AF.Identity,
                              bias=b_fm[:, cc, 0:1], scale=g_fm[:, cc, 0:1])
                    else:
                        k.ts(dstT[:, cc, c0:c0 + c], p[:, cc * c:(cc + 1) * c], g_fm[:, cc, 0:1], ALU.mult,
                             b_fm[:, cc, 0:1], ALU.add)
                k.cp(dstB[:, :, c0:c0 + c], dstT[:, :, c0:c0 + c])

    def fm_to_rows(src3, R, C, dram2d):
        with k.scope():
            ob = k.sb("f2r", [R, C * 128])
            c0 = 0
            while c0 < C:
                n = min(4, C - c0)
                p = k.ps()
                for c in range(n):
                    k.tr(p[0:R, c * 128:(c + 1) * 128], src3[:, c0 + c, :], ident)
                k.cp(ob[:, c0 * 128:(c0 + n) * 128], p[0:R, 0:n * 128])
                c0 += n
            k.dma_out(dram2d, ob.v)

    def bcast_rows(row4, n, name):
        R = k.sb("bcR", [4, n, 4])
        k.tt(R.v, row4.unsq(2).bc([4, n, 4]), ident[0:4, 0:4].unsq(1).bc([4, n, 4]), ALU.mult)
        p = k.ps()
        k.mm(p[:, 0:n * 4], ones[0:4, :], R.v.re("p a b -> p (a b)"))
        dst = k.sb(name, [128, n, 4])
        k.cp(dst.v, p[:, 0:n * 4].re("p (a b) -> p a b", b=4))
        return dst

    class RPool:
        def __init__(self, name, shape, n, dt=F32):
            self.tiles = [k.sb(name, shape, dt=dt) for _ in range(n)]
            self.i = 0

        def get(self):
            t = self.tiles[self.i % len(self.tiles)]
            self.i += 1
            return t

    def interleave(gens):
        gens = list(gens)
        while gens:
            nxt = []
            for g in gens:
                try:
                    next(g)
                    nxt.append(g)
                except StopIteration:
                    pass
            gens = nxt

    def rows_to_tm(rows, c0, c, name):
        p = k.ps()
        for i, r in enumerate(rows):
            k.tr(p[0:c, 4 * i:4 * i + 4], r[:, c0:c0 + c], ident)
        dst = k.sb(name, [128, 4 * len(rows)])
        k.cp(dst[0:c, :], p[0:c, 0:4 * len(rows)])
        return dst

    def mlstm_phase(l, tl, xT, hAT):
        P = LP[l]
        TTc = tl["TT"]
        chunks = tl["chunks"]
        CP = max(ch["c"] for ch in chunks)
        Um, mcur = P["Um"], P["mcur"]
        gbi, gbf, gal, gdt = P["gb"]
        qT = k.sb("qT", [128, 4, TTc])
        kT = k.sb("kT", [128, 4, TTc])
        wb = wload(w_in[l][:, A_Q:A_Q + 512], 8, 512)
        for h in range(4):
            p = proj_fm(wb, h * 128, xT, TTc)
            k.act(qT[:, h, :], p[:, 0:TTc], AF.Identity, scale=128.0 ** -0.5)
        wb = wload(w_in[l][:, A_K:A_K + 512], 8, 512)
        for h in range(4):
            p = proj_fm(wb, h * 128, xT, TTc)
            k.cp(kT[:, h, :], p[:, 0:TTc])
        kc = [k.sb("kc", [CP, 4, 128]) for _ in chunks]
        va = [k.sb("va", [CP, 4, 129]) for _ in chunks]
        ow = [k.sb("ow", [CP, 512]) for _ in chunks]
        for (g0, gn, mem) in tl["pgroups"]:
            p = proj_tm(wb, 512, xT, g0, gn)
            for (ci, r0, c) in mem:
                k.cp(kc[ci][0:c], p[r0:r0 + c, :].re("p (h d) -> p h d", h=4), eng="act")
        wb = wload(w_in[l][:, A_V:A_V + 512], 8, 512)
        for (g0, gn, mem) in tl["pgroups"]:
            p = proj_tm(wb, 512, xT, g0, gn)
            for (ci, r0, c) in mem:
                k.cp(va[ci][0:c, :, 0:128], p[r0:r0 + c, :].re("p (h d) -> p h d", h=4))
                k.memset(va[ci][0:c, :, 128:129], 1.0)
        wb = wload(w_in[l][:, A_O:A_O + 512], 8, 512)
        for (g0, gn, mem) in tl["pgroups"]:
            p = proj_tm(wb, 512, xT, g0, gn)
            for (ci, r0, c) in mem:
                k.act(ow[ci][0:c, :], p[r0:r0 + c, :], AF.Sigmoid)
                k.tt(ow[ci][0:c, :], ow[ci][0:c, :], P["anw"][0:c, :], ALU.mult)
        wgt = wload(w_in[l][:, A_I - 248:A_I + 8], 8, 256)
        pi = k.ps()
        for kk in range(8):
            k.mm(pi[0:4, 0:TTc], wgt[:, kk, 248:252], xT[:, kk, 0:TTc], start=(kk == 0), stop=(kk == 7))
        pf = k.ps()
        for kk in range(8):
            k.mm(pf[0:4, 0:TTc], wgt[:, kk, 252:256], xT[:, kk, 0:TTc], start=(kk == 0), stop=(kk == 7))
        igr = k.sb("igr", [4, TTc])
        lf = k.sb("lf", [4, TTc])
        b = k.sb("b", [4, TTc])
        a = k.sb("a", [4, TTc])
        Er = k.sb("Er", [4, TTc])
        Th = k.sb("Th", [4, TTc])
        k.act(igr.v, pi[0:4, 0:TTc], AF.Identity, bias=gbi[:, 0:1])
        k.act(lf.v, pf[0:4, 0:TTc], AF.Exp, bias=gbf[:, 0:1], scale=-1.0)
        k.act(lf.v, lf.v, AF.Ln, bias=cst[0:4, 0:1])
        k.ts(lf.v, lf.v, -1.0, ALU.mult)
        for (g0, gl) in tl["groups"]:
            k.scan(b[:, g0:g0 + gl], lf[:, g0:g0 + gl])
        k.tt(a.v, igr.v, b.v, ALU.subtract)
        nb = k.sb("nb", [4, TTc])
        k.ts(nb.v, b.v, -1.0, ALU.mult)
        nch = len(chunks)
        rr = k.sb("rr", [4, nch + 16])
        nrr = k.sb("nrr", [4, nch + 16])
        mprev = k.sb("mprev", [4, nch + 16])
        am = k.sb("am", [4, nch + 16])
        wrow = k.sb("wrow", [4, nch + 16])
        for j, ch in enumerate(chunks):
            c0, c = ch["c0"], ch["c"]
            if ch["kind"] == "samp":
                av = a[:, c0:c0 + 64].re("p (s j) -> p s j", j=4)
                bv = b[:, c0:c0 + 64].re("p (s j) -> p s j", j=4)
                k.red(am[:, j:j + 16], av, ALU.max)
                m0t = k.sb("m0t", [16, 4])
                k.dma_in(m0t.v, sm[l])
                pm0 = k.ps()
                k.tr(pm0[0:4, 0:16], m0t.v, ident)
                k.cp(mprev[:, j:j + 16], pm0[0:4, 0:16])
                k.tt(rr[:, j:j + 16], mprev[:, j:j + 16], am[:, j:j + 16], ALU.max)
                mnew = k.sb("mnew", [4, 16])
                k.tt(mnew.v, rr[:, j:j + 16], bv[:, :, 3], ALU.add)
                mex = k.sb("mex", [4, 16, 4])
                k.tt(mex.v, mnew.v.unsq(2).bc([4, 16, 4]), ident[0:4, 0:4].unsq(1).bc([4, 16, 4]), ALU.mult)
                pmo = k.ps()
                k.mm(pmo[0:1, 0:64], ones[0:4, 0:1], mex.v.re("p a b -> p (a b)"))
                mrow = k.sb("mrow", [1, 64])
                k.cp(mrow.v, pmo[0:1, 0:64])
                k.dma_out(om[l].rearrange("s h -> (s h)").rearrange("(o n) -> o n", o=1), mrow.v)
                k.ts(nrr[:, j:j + 16], rr[:, j:j + 16], -1.0, ALU.mult)
                k.tt(Er[:, c0:c0 + 64].re("p (s j) -> p s j", j=4), av, rr[:, j:j + 16].unsq(2).bc([4, 16, 4]), ALU.subtract)
                k.tt(Th[:, c0:c0 + 64].re("p (s j) -> p s j", j=4), nb[:, c0:c0 + 64].re("p (s j) -> p s j", j=4),
                     rr[:, j:j + 16].unsq(2).bc([4, 16, 4]), ALU.subtract)
                k.act(Er[:, c0:c0 + 64], Er[:, c0:c0 + 64], AF.Exp)
                k.act(Th[:, c0:c0 + 64], Th[:, c0:c0 + 64], AF.Exp)
                k.tt(wrow[:, j:j + 16], mprev[:, j:j + 16], rr[:, j:j + 16], ALU.subtract)
            else:
                k.red(am[:, j:j + 1], a[:, c0:c0 + c], ALU.max)
                k.cp(mprev[:, j:j + 1], mcur.v)
                k.tt(rr[:, j:j + 1], mcur.v, am[:, j:j + 1], ALU.max)
                k.tt(mcur.v, rr[:, j:j + 1], b[:, c0 + c - 1:c0 + c], ALU.add)
                k.ts(nrr[:, j:j + 1], rr[:, j:j + 1], -1.0, ALU.mult)
                k.act(Er[:, c0:c0 + c], a[:, c0:c0 + c], AF.Exp, bias=nrr[:, j:j + 1])
                k.act(Th[:, c0:c0 + c], nb[:, c0:c0 + c], AF.Exp, bias=nrr[:, j:j + 1])
                k.tt(wrow[:, j:j + 1], mprev[:, j:j + 1], rr[:, j:j + 1], ALU.subtract)
        nw = nch + 15 if chunks[-1]["kind"] == "samp" else nch
        k.act(wrow[:, 0:nw], wrow[:, 0:nw], AF.Exp)
        wbc = bcast_rows(wrow[:, 0:nw], nw, "wbc")
        tap("Er%d" % l, Er.v)
        tap("Th%d" % l, Th.v)
        for j, ch in enumerate(chunks):
            c0, c, kind = ch["c0"], ch["c"], ch["kind"]
            m01 = MASKS["b" if kind == "samp" else "c"][0]
            with k.scope():
                sc = rows_to_tm([Er.v, Th.v], c0, c, "scm")
                vE = k.sb("vE", [CP, 4, 129])
                k.tt(vE[0:c], va[j][0:c], sc[0:c, 0:4].unsq(2).bc([c, 4, 129]), ALU.mult)
                if kind != "samp":
                    offs = [(h // 2) * 512 + (h % 2) * 129 for h in range(4)]
                    psts = []
                    for h in range(4):
                        pst = k.ps()
                        k.mm(pst[0:c, 0:c], kT[:, h, c0:c0 + c], qT[:, h, c0:c0 + c])
                        psts.append(pst)
                    STl, Chl = [], []
                    for h in range(4):
                        STs = k.sb("STs", [CP, CP])
                        k.stt(STs[0:c, 0:c], psts[h][0:c, 0:c], sc[0:c, h:h + 1], m01[0:c, 0:c], ALU.mult, ALU.mult)
                        STl.append(STs)
                        Ch = k.sb("Ch", [128, 129])
                        k.ts(Ch.v, Um[:, h, :], wbc[:, j, h:h + 1], ALU.mult)
                        Chl.append(Ch)
                    for h in range(4):
                        k.mm(psd[0:c, offs[h]:offs[h] + 129], STl[h][0:c, 0:c], va[j][0:c, h, :], start=True, stop=False)
                        k.mm(psd[0:c, offs[h]:offs[h] + 129], qT[:, h, c0:c0 + c], Chl[h].v, start=False, stop=True)
                    pps = []
                    for h in range(4):
                        pp = k.ps()
                        k.mm(pp[:, 0:129], kc[j][0:c, h, :], vE[0:c, h, :])
                        pps.append(pp)
                    for h in range(4):
                        k.tt(Um[:, h, :], Chl[h].v, pps[h][:, 0:129], ALU.add)
                for h in (range(4) if kind == "samp" else []):
                    k.push()
                    off = (h // 2) * 512 + (h % 2) * 129
                    pst = k.ps()
                    k.mm(pst[0:c, 0:c], kT[:, h, c0:c0 + c], qT[:, h, c0:c0 + c])
                    STs = k.sb("STs", [CP, CP])
                    k.stt(STs[0:c, 0:c], pst[0:c, 0:c], sc[0:c, h:h + 1], m01[0:c, 0:c], ALU.mult, ALU.mult)
                    if kind != "samp":
                        pass
                    else:
                        qTm = k.sb("qTm", [128, 16, 64])
                        k.memset(qTm.v, 0.0)
                        for i in range(16):
                            k.cp(qTm[:, i, 4 * i:4 * i + 4], qT[:, h, c0 + 4 * i:c0 + 4 * i + 4])
                        k.mm(psd[0:c, off:off + 129], STs[0:c, 0:c], va[j][0:c, h, :], start=True, stop=False)
                        for g in range(4):
                            with k.scope():
                                Cg = k.sb("Cg", [128, 4, 129])
                                Cst = k.sb("Cst", [128, 4, 128])
                                k.dma_in(Cst.v, sC[l].rearrange("(s h) d v -> h d s v", h=4)[h][:, 4 * g:4 * g + 4, :])
                                k.cp(Cg[:, :, 0:128], Cst.v, eng="act")
                                nrow = k.sb("nrow", [4, 128])
                                k.dma_in(nrow.v, sn[l].rearrange("(s h) d -> h s d", h=4)[h][4 * g:4 * g + 4, :])
                                pn_ = k.ps()
                                k.tr(pn_[:, 0:4], nrow.v, ident)
                                k.cp(Cg[:, :, 128], pn_[:, 0:4])
                                wv = wbc[:, j + 4 * g:j + 4 * g + 4, h:h + 1]
                                k.tt(Cg.v, Cg.v, wv.bc([128, 4, 129]), ALU.mult)
                                for ii in range(4):
                                    i = 4 * g + ii
                                    k.mm(psd[0:c, off:off + 129], qTm[:, i, :], Cg[:, ii, :], start=False,
                                         stop=(i == 15))
                                vEx = k.sb("vEx", [64, 4, 129])
                                k.tt(vEx.v, vE[0:64, h, :].unsq(1).bc([64, 4, 129]),
                                     ind[:, 4 * g:4 * g + 4].unsq(2).bc([64, 4, 129]), ALU.mult)
                                vf = vEx.v.re("p a b -> p (a b)")
                                pu = k.ps()
                                pu2 = k.ps()
                                k.mm(pu[:, 0:512], kc[j][0:64, h, :], vf[:, 0:512])
                                k.mm(pu2[:, 0:4], kc[j][0:64, h, :], vf[:, 512:516])
                                Cf = Cg.v.re("p a b -> p (a b)")
                                k.tt(Cf[:, 0:512], Cf[:, 0:512], pu[:, 0:512], ALU.add)
                                k.tt(Cf[:, 512:516], Cf[:, 512:516], pu2[:, 0:4], ALU.add)
                                k.dma_out(oC[l].rearrange("(s h) d v -> h d s v", h=4)[h][:, 4 * g:4 * g + 4, :], Cg[:, :, 0:128])
                                pn2 = k.ps()
                                k.tr(pn2[0:4, 0:128], Cg[:, :, 128], ident)
                                nout = k.sb("nout", [4, 128])
                                k.cp(nout.v, pn2[0:4, 0:128])
                                k.dma_out(on[l].rearrange("(s h) d -> h s d", h=4)[h][4 * g:4 * g + 4, :], nout.v)
                    k.pop()
                numv = psd[0:c, :].re("p (a r) -> p a r", a=2)[:, :, 0:258].re("p a (h q) -> p a h q", h=2)
                den = k.sb("den", [CP, 2, 2])
                k.cp(den[0:c], numv[:, :, :, 128])
                dn = k.sb("dn", [CP, 4])
                denf = den[0:c].re("p a b -> p (a b)")
                k.ts(dn[0:c], denf, -1.0, ALU.mult)
                k.tt(dn[0:c], dn[0:c], denf, ALU.max)
                k.tt(dn[0:c], dn[0:c], sc[0:c, 4:8], ALU.max)
                rden = k.sb("rden", [CP, 4])
                k.op("dve", lambda e: e.reciprocal(out=rden.ap[0:c], in_=dn.ap[0:c]), reads=[dn], writes=[rden])
                st = k.sb("hst", [CP, 4, 6])
                mv = k.sb("hmv", [CP, 4, 2])
                for h in range(4):
                    off = (h // 2) * 512 + (h % 2) * 129
                    k.op("dve", lambda e, h=h, off=off: e.bn_stats(out=st.ap[0:c, h, :], in_=psd.ap[0:c, off:off + 128]),
                         reads=[psd], writes=[st])
                for h in range(4):
                    k.op("dve", lambda e, h=h: e.bn_aggr(out=mv.ap[0:c, h, :], in_=st.ap[0:c, h, :]), reads=[st], writes=[mv])
                t1 = k.sb("t1", [CP, 4])
                k.tt(t1[0:c], rden[0:c], rden[0:c], ALU.mult)
                k.tt(t1[0:c], t1[0:c], mv[0:c, :, 1], ALU.mult)
                rsq = k.sb("rsq", [CP, 4])
                rsqrt_small(rsq[0:c], t1[0:c], NORM_EPS)
                k.tt(rsq[0:c], rsq[0:c], rden[0:c], ALU.mult)
                hA = k.sb("hA", [CP, 512])
                for h in range(4):
                    off = (h // 2) * 512 + (h % 2) * 129
                    k.ts(hA[0:c, h * 128:(h + 1) * 128], psd[0:c, off:off + 128], mv[0:c, h, 0:1], ALU.subtract,
                         rsq[0:c, h:h + 1], ALU.mult)
                k.tt(hA[0:c], hA[0:c], ow[j][0:c], ALU.mult)
                pt = k.ps()
                for h in range(4):
                    k.tr(pt[:, h * c:(h + 1) * c], hA[0:c, h * 128:(h + 1) * 128], ident)
                k.cp(hAT[:, :, c0:c0 + c], pt[:, 0:4 * c].re("p (h t) -> p h t", h=4), eng="act")

    def gdn_phase(l, tl, xT, hBT):
        P = LP[l]
        TTc = tl["TT"]
        chunks = tl["chunks"]
        CP = max(ch["c"] for ch in chunks)
        S, cg, cwg = P["S"], P["cg"], P["cwg"]
        gbi, gbf, gal, gdt = P["gb"]
        qkvT = k.sb("qkvT", [128, 12, TTc])
        k.push()
        pools = None if tl["first"] else (RPool("cext", [128, TTc + 3], 3), RPool("cacc", [128, TTc], 3))
        sqp = RPool("csq", [128, TTc], 2)
        for blk in range(3):
            wb = wload(w_in[l][:, B_QKV + 512 * blk:B_QKV + 512 * (blk + 1)], 8, 512)
            for q4 in range(4):
                cc = blk * 4 + q4
                p = proj_fm(wb, q4 * 128, xT, TTc)
                with k.scope():
                    acc = conv_fm(p, cc, 4, cg, cwg, sgc[l], ogc[l], tl, "c", pools)
                    k.act(qkvT[:, cc, :], acc.v, AF.Silu)
                    if cc < 8:
                        sq = sqp.get()
                        k.tt(sq.v, qkvT[:, cc, :], qkvT[:, cc, :], ALU.mult, eng=("dve" if tl["first"] else "pool"))
                        pq = k.ps()
                        k.mm(pq[:, 0:TTc], ones.v, sq.v)
                        k.act(sq.v, pq[:, 0:TTc], AF.Ln, bias=cst[:, 1:2])
                        k.act(sq.v, sq.v, AF.Exp, scale=-0.5, bias=(cst[:, 2:3] if cc < 4 else cst[:, 3:4]))
                        k.tt(qkvT[:, cc, :], qkvT[:, cc, :], sq.v, ALU.mult)
        k.pop()
        wb = wload(w_in[l][:, B_Z:B_Z + 512], 8, 512)
        wz = [k.sb("wz", [CP, 4, 128]) for _ in chunks]
        for (g0, gn, mem) in tl["pgroups"]:
            p = proj_tm(wb, 512, xT, g0, gn)
            for (ci, r0, c) in mem:
                k.act(wz[ci][0:c].re("p h d -> p (h d)"), p[r0:r0 + c, :], AF.Silu)
                k.tt(wz[ci][0:c], wz[ci][0:c], P["gnw"][0:c, :].unsq(1).bc([c, 4, 128]), ALU.mult)
        wgt = wload(w_in[l][:, B_BETA - 248:B_BETA + 8], 8, 256)
        pbt = k.ps()
        for kk in range(8):
            k.mm(pbt[0:4, 0:TTc], wgt[:, kk, 248:252], xT[:, kk, 0:TTc], start=(kk == 0), stop=(kk == 7))
        pa_ = k.ps()
        for kk in range(8):
            k.mm(pa_[0:4, 0:TTc], wgt[:, kk, 252:256], xT[:, kk, 0:TTc], start=(kk == 0), stop=(kk == 7))
        spb = k.sb("spb", [4, TTc])
        k.act(spb.v, pbt[0:4, 0:TTc], AF.Exp, scale=-1.0)
        k.act(spb.v, spb.v, AF.Ln, bias=cst[0:4, 0:1])
        g = k.sb("g", [4, TTc])
        k.act(g.v, pa_[0:4, 0:TTc], AF.Exp, bias=gdt[:, 0:1])
        k.act(g.v, g.v, AF.Ln, bias=cst[0:4, 0:1])
        k.ts(g.v, g.v, gal[:, 0:1], ALU.mult)
        G = k.sb("G", [4, TTc])
        for (g0, gl) in tl["groups"]:
            k.scan(G[:, g0:g0 + gl], g[:, g0:g0 + gl])
        Gb = k.sb("Gb", [4, TTc])
        k.tt(Gb.v, G.v, spb.v, ALU.subtract)
        nG = k.sb("nG", [4, TTc])
        k.ts(nG.v, G.v, -1.0, ALU.mult)
        r_beta = k.sb("r_beta", [4, TTc])
        k.act(r_beta.v, spb.v, AF.Exp, scale=-1.0)
        r_bg = k.sb("r_bg", [4, TTc])
        k.act(r_bg.v, Gb.v, AF.Exp)
        r_eg = k.sb("r_eg", [4, TTc])
        k.act(r_eg.v, G.v, AF.Exp)
        r_kd = k.sb("r_kd", [4, TTc])
        nch = len(chunks)
        ge = k.sb("ge", [4, nch + 16])
        for j, ch in enumerate(chunks):
            c0, c = ch["c0"], ch["c"]
            if ch["kind"] == "samp":
                Gv = G[:, c0:c0 + 64].re("p (s j) -> p s j", j=4)
                k.cp(ge[:, j:j + 16], Gv[:, :, 3])
                k.tt(r_kd[:, c0:c0 + 64].re("p (s j) -> p s j", j=4), nG[:, c0:c0 + 64].re("p (s j) -> p s j", j=4),
                     ge[:, j:j + 16].unsq(2).bc([4, 16, 4]), ALU.add)
            else:
                k.cp(ge[:, j:j + 1], G[:, c0 + c - 1:c0 + c])
                k.ts(r_kd[:, c0:c0 + c], nG[:, c0:c0 + c], ge[:, j:j + 1], ALU.add)
        k.act(r_kd.v, r_kd.v, AF.Exp)
        nw = nch + 15 if chunks[-1]["kind"] == "samp" else nch
        k.act(ge[:, 0:nw], ge[:, 0:nw], AF.Exp)
        gbc = bcast_rows(ge[:, 0:nw], nw, "gbc")
        for j, ch in enumerate(chunks):
            c0, c, kind = ch["c0"], ch["c"], ch["kind"]
            _, m_st, m_stT, m_inT = MASKS["b" if kind == "samp" else "c"]
            nsq = {128: 6, 64: 5, 16: 3}[c] if kind != "samp" else 1
            with k.scope():
                sc = rows_to_tm([r_beta.v, r_bg.v, r_eg.v, r_kd.v], c0, c, "scg")
                rv = k.sb("rv", [CP, 4, 128])
                rk = k.sb("rk", [CP, 4, 128])
                kd = k.sb("kd", [CP, 4, 128])
                k.push()
                kcn = k.sb("kcn", [CP, 4, 128])
                vcn = k.sb("vcn", [CP, 4, 128])
                p = k.ps()
                for h in range(4):
                    k.tr(p[0:c, h * 128:(h + 1) * 128], qkvT[:, 4 + h, c0:c0 + c], ident)
                k.cp(kcn[0:c].re("p h d -> p (h d)"), p[0:c, :], eng="act")
                p = k.ps()
                for h in range(4):
                    k.tr(p[0:c, h * 128:(h + 1) * 128], qkvT[:, 8 + h, c0:c0 + c], ident)
                k.cp(vcn[0:c].re("p h d -> p (h d)"), p[0:c, :])
                k.tt(rv[0:c], vcn[0:c], sc[0:c, 0:4].unsq(2).bc([c, 4, 128]), ALU.mult)
                k.tt(rk[0:c], kcn[0:c], sc[0:c, 4:8].unsq(2).bc([c, 4, 128]), ALU.mult)
                k.tt(kd[0:c], kcn[0:c], sc[0:c, 12:16].unsq(2).bc([c, 4, 128]), ALU.mult)
                k.pop()
                ob = k.sb("ob", [CP, 4, 128])

                def head(h):
                    if kind == "samp":
                        k.push()
                    qh = qkvT[:, h, c0:c0 + c]
                    kh = qkvT[:, 4 + h, c0:c0 + c]
                    pe_ = k.ps()
                    k.mm(pe_[0:c, 0:c], Gb[:, c0:c0 + c], oh[:, h, 0:c], start=True, stop=False)
                    k.mm(pe_[0:c, 0:c], oh[:, h, 0:c], nG[:, c0:c0 + c], start=False, stop=False)
                    k.mm(pe_[0:c, 0:c], ident[0:c, 0:c], m_st[0:c, 0:c], start=False, stop=True)
                    k.mm(pe_[0:c, c:2 * c], oh[:, h, 0:c], Gb[:, c0:c0 + c], start=True, stop=False)
                    k.mm(pe_[0:c, c:2 * c], nG[:, c0:c0 + c], oh[:, h, 0:c], start=False, stop=False)
                    k.mm(pe_[0:c, c:2 * c], ident[0:c, 0:c], m_stT[0:c, 0:c], start=False, stop=True)
                    k.mm(pe_[0:c, 2 * c:3 * c], oh[:, h, 0:c], G[:, c0:c0 + c], start=True, stop=False)
                    k.mm(pe_[0:c, 2 * c:3 * c], nG[:, c0:c0 + c], oh[:, h, 0:c], start=False, stop=False)
                    k.mm(pe_[0:c, 2 * c:3 * c], ident[0:c, 0:c], m_inT[0:c, 0:c], start=False, stop=True)
                    Ex = k.sb("Ex", [CP, 3 * CP])
                    k.act(Ex[0:c, 0:3 * c], pe_[0:c, 0:3 * c], AF.Exp)
                    yield
                    psc = k.ps()
                    k.mm(psc[0:c, 0:c], kh, kh)
                    k.mm(psc[0:c, c:2 * c], kh, kh)
                    k.mm(psc[0:c, 2 * c:3 * c], kh, qh)
                    M3 = Ex
                    k.tt(M3[0:c, 0:3 * c], psc[0:c, 0:3 * c], Ex[0:c, 0:3 * c], ALU.mult)
                    TTs = [k.sb("TTm", [CP, CP]), k.sb("TTm", [CP, CP])]
                    P2s = [k.sb("P2", [CP, 2 * CP]), k.sb("P2", [CP, 2 * CP])]
                    TTm = TTs[0]
                    k.tt(TTm[0:c, 0:c], ident[0:c, 0:c], M3[0:c, c:2 * c], ALU.subtract)
                    yield
                    Pw = M3
                    for q in range(nsq):
                        pp = k.ps()
                        k.mm(pp[0:c, 0:c], Pw[0:c, c:2 * c], Pw[0:c, 0:c])
                        P2 = P2s[q % 2]
                        if q < nsq - 1:
                            k.mm(pp[0:c, c:2 * c], Pw[0:c, 0:c], Pw[0:c, c:2 * c])
                            k.cp(P2[0:c, 0:2 * c], pp[0:c, 0:2 * c], eng="act")
                        else:
                            k.cp(P2[0:c, 0:c], pp[0:c, 0:c], eng="act")
                        yield
                        pt_ = k.ps()
                        k.mm(pt_[0:c, 0:c], P2[0:c, 0:c], TTm[0:c, 0:c])
                        Tn = TTs[(q + 1) % 2]
                        k.tt(Tn[0:c, 0:c], TTm[0:c, 0:c], pt_[0:c, 0:c], ALU.add)
                        TTm = Tn
                        Pw = P2
                        yield
                    pw = k.ps()
                    k.mm(pw[:, 0:c], rk[0:c, h, :], TTm[0:c, 0:c])
                    WTn = k.sb("WTn", [128, CP])
                    k.ts(WTn[:, 0:c], pw[:, 0:c], -1.0, ALU.mult)
                    yield
                    pu = k.ps()
                    po = k.ps()
                    if kind != "samp":
                        k.mm(pu[0:c, 0:128], TTm[0:c, 0:c], rv[0:c, h, :], start=True, stop=False)
                        k.mm(pu[0:c, 0:128], WTn[:, 0:c], S[:, h, :], start=False, stop=True)
                        Us = P2s[1][:, 0:128]
                        k.cp(Us[0:c], pu[0:c, 0:128], eng="act")
                        k.mm(po[0:c, 0:128], qh, S[:, h, :])
                        o1 = P2s[0][:, 0:128]
                        k.act(o1[0:c], po[0:c, 0:128], AF.Identity, scale=sc[0:c, 8 + h:9 + h])
                        yield
                        po2 = k.ps()
                        k.mm(po2[0:c, 0:128], M3[0:c, 2 * c:3 * c], Us[0:c])
                        k.tt(ob[0:c, h, :], o1[0:c], po2[0:c, 0:128], ALU.add)
                        pS_ = k.ps()
                        k.mm(pS_[:, 0:128], kd[0:c, h, :], Us[0:c])
                        k.stt(S[:, h, :], S[:, h, :], gbc[:, j, h:h + 1], pS_[:, 0:128], ALU.mult, ALU.add)
                    else:
                        WTm = k.sb("WTm", [128, 16, 64])
                        qTm = k.sb("qTm", [128, 16, 64])
                        k.memset(WTm.v, 0.0)
                        k.memset(qTm.v, 0.0)
                        for i in range(16):
                            k.cp(WTm[:, i, 4 * i:4 * i + 4], WTn[:, 4 * i:4 * i + 4])
                            k.cp(qTm[:, i, 4 * i:4 * i + 4], qkvT[:, h, c0 + 4 * i:c0 + 4 * i + 4], eng="act")
                        Sg = []
                        for gq in range(4):
                            t = k.sb("Sg", [128, 4, 128])
                            k.dma_in(t.v, sS[l].rearrange("(s h) d v -> h d s v", h=4)[h][:, 4 * gq:4 * gq + 4, :])
                            Sg.append(t)
                        k.mm(pu[0:c, 0:128], TTm[0:c, 0:c], rv[0:c, h, :], start=True, stop=False)
                        for i in range(16):
                            k.mm(pu[0:c, 0:128], WTm[:, i, :], Sg[i // 4][:, i % 4, :], start=False, stop=(i == 15))
                        Us = k.sb("Us", [CP, 128])
                        k.cp(Us[0:c], pu[0:c, 0:128], eng="act")
                        for i in range(16):
                            k.mm(po[0:c, 0:128], qTm[:, i, :], Sg[i // 4][:, i % 4, :], start=(i == 0), stop=(i == 15))
                        o1 = k.sb("o1", [CP, 128])
                        k.act(o1[0:c], po[0:c, 0:128], AF.Identity, scale=sc[0:c, 8 + h:9 + h])
                        po2 = k.ps()
                        k.mm(po2[0:c, 0:128], M3[0:c, 2 * c:3 * c], Us[0:c])
                        k.tt(ob[0:c, h, :], o1[0:c], po2[0:c, 0:128], ALU.add)
                        for gq in range(4):
                            Ux = k.sb("Ux", [64, 4, 128])
                            k.tt(Ux.v, Us[0:64].unsq(1).bc([64, 4, 128]), ind[:, 4 * gq:4 * gq + 4].unsq(2).bc([64, 4, 128]), ALU.mult)
                            pS_ = k.ps()
                            k.mm(pS_[:, 0:512], kd[0:64, h, :], Ux.v.re("p a b -> p (a b)"))
                            k.tt(Sg[gq].v, Sg[gq].v, gbc[:, j + 4 * gq:j + 4 * gq + 4, h:h + 1].bc([128, 4, 128]), ALU.mult)
                            k.tt(Sg[gq].v, Sg[gq].v, pS_[:, 0:512].re("p (a b) -> p a b", a=4), ALU.add)
                            k.dma_out(oS[l].rearrange("(s h) d v -> h d s v", h=4)[h][:, 4 * gq:4 * gq + 4, :], Sg[gq].v)
                    if kind == "samp":
                        k.pop()

                if kind == "samp":
                    for h in range(4):
                        for _ in head(h):
                            pass
                else:
                    interleave([head(h) for h in range(4)])
                sq = rv
                k.tt(sq[0:c], ob[0:c], ob[0:c], ALU.mult)
                ss = rk[:, 0, 0:4]
                k.red(ss[0:c], sq[0:c], ALU.add)
                k.ts(ss[0:c], ss[0:c], 1.0 / 128.0, ALU.mult)
                rs = kd[:, 0, 0:4]
                rsqrt_small(rs[0:c], ss[0:c], NORM_EPS)
                k.tt(ob[0:c], ob[0:c], rs[0:c].unsq(2).bc([c, 4, 128]), ALU.mult)
                k.tt(ob[0:c], ob[0:c], wz[j][0:c], ALU.mult)
                pt = k.ps()
                for h in range(4):
                    k.tr(pt[:, h * c:(h + 1) * c], ob[0:c, h, :], ident)
                k.cp(hBT[:, :, c0:c0 + c], pt[:, 0:4 * c].re("p (h t) -> p h t", h=4), eng="act")

    def ln_rows(src, n, g_fm, b_fm, dst32, dstB, c0, final=None):
        with k.scope():
            st = k.sb("lnst", [128, 12])
            mv = k.sb("lnmv", [128, 2])
            rs = k.sb("lnrs", [128, 1])
            xn = k.sb("lnxn", [128, 1024])
            k.op("dve", lambda e: e.bn_stats(out=st.ap[0:n, 0:6], in_=src.ap[:, 0:512]), reads=[src.t], writes=[st])
            k.op("dve", lambda e: e.bn_stats(out=st.ap[0:n, 6:12], in_=src.ap[:, 512:1024]), reads=[src.t], writes=[st])
            k.op("dve", lambda e: e.bn_aggr(out=mv.ap[0:n, :], in_=st.ap[0:n, :]), reads=[st], writes=[mv])
            rsqrt_small(rs[0:n, :], mv[0:n, 1:2], LN_EPS)
            k.ts(xn[0:n, :], src, mv[0:n, 0:1], ALU.subtract, rs[0:n, 0:1], ALU.mult)
            if final is not None:
                k.tt(xn[0:n, :], xn[0:n, :], g2bc[0:n, :], ALU.mult)
                k.tt(xn[0:n, :], xn[0:n, :], b2bc[0:n, :], ALU.add)
                for (dst, r0, r1) in final:
                    k.dma_out(dst, xn[r0:r1, :])
            else:
                for half in range(2):
                    p = k.ps()
                    for q in range(4):
                        cc = half * 4 + q
                        k.tr(p[:, q * n:(q + 1) * n], xn[0:n, cc * 128:(cc + 1) * 128], ident)
                    for q in range(4):
                        cc = half * 4 + q
                        if q % 2 == 0:
                            k.act(dst32[:, cc, c0:c0 + n], p[:, q * n:(q + 1) * n], AF.Identity,
                                  bias=b_fm[:, cc, 0:1], scale=g_fm[:, cc, 0:1])
                        else:
                            k.ts(dst32[:, cc, c0:c0 + n], p[:, q * n:(q + 1) * n], g_fm[:, cc, 0:1], ALU.mult,
                                 b_fm[:, cc, 0:1], ALU.add)
                k.cp(dstB[:, :, c0:c0 + n], dst32[:, :, c0:c0 + n], eng="act")

    def merge_phase(l, tl, xT, xT32, hAT, hBT, x1T32, x1T):
        P = LP[l]
        TTc = tl["TT"]
        yT = k.sb("yT", [128, 8, TTc], dt=BF16)
        for hh in range(2):
            wpp = wload_parts([(w_pa[l][:, hh * 512:(hh + 1) * 512], 4), (w_pb[l][:, hh * 512:(hh + 1) * 512], 4)], 512)
            wga = wload(w_in[l][:, G_MERGE + hh * 512:G_MERGE + (hh + 1) * 512], 8, 512)
            ga = []
            for q4 in range(4):
                p = proj_fm(wga, q4 * 128, xT, TTc)
                t = k.sb("ga", [128, TTc])
                k.act(t.v, p[:, 0:TTc], AF.Sigmoid)
                ppa = k.ps()
                for kk in range(4):
                    k.mm(ppa[:, 0:TTc], wpp[:, kk, q4 * 128:(q4 + 1) * 128], hAT[:, kk, :], start=(kk == 0), stop=(kk == 3))
                k.tt(t.v, t.v, ppa[:, 0:TTc], ALU.mult)
                ga.append(t)
            wgb = wload(w_in[l][:, G_MERGE + D + hh * 512:G_MERGE + D + (hh + 1) * 512], 8, 512)
            for q4 in range(4):
                n = hh * 4 + q4
                p = proj_fm(wgb, q4 * 128, xT, TTc)
                t = k.sb("gbm", [128, TTc])
                k.act(t.v, p[:, 0:TTc], AF.Sigmoid)
                ppb = k.ps()
                for kk in range(4):
                    k.mm(ppb[:, 0:TTc], wpp[:, 4 + kk, q4 * 128:(q4 + 1) * 128], hBT[:, kk, :], start=(kk == 0), stop=(kk == 3))
                k.tt(t.v, t.v, ppb[:, 0:TTc], ALU.mult)
                k.tt(yT[:, n, :], t.v, ga[q4].v, ALU.add)
        tgs = tl["tgs"]
        osb = [k.sb("osb", [128, 1024]) for _ in tgs]
        for hh in range(2):
            wb = wload(w_out[l][:, hh * 512:(hh + 1) * 512], 8, 512)
            for j, (c0, n) in enumerate(tgs):
                p = k.ps()
                for n8 in range(8):
                    k.mm(p[0:n, 0:512], yT[:, n8, c0:c0 + n], wb[:, n8, :], start=(n8 == 0), stop=False)
                for q4 in range(4):
                    k.mm(p[0:n, q4 * 128:(q4 + 1) * 128], xT32[:, hh * 4 + q4, c0:c0 + n], aident.v, start=False, stop=(q4 == 3))
                k.cp(osb[j][0:n, hh * 512:(hh + 1) * 512], p[0:n, 0:512], eng="act")
        for j, (c0, n) in enumerate(tgs):
            ln_rows(osb[j][0:n, :], n, P["g1"], P["b1"], x1T32, x1T, c0)

    def conv_fm(p, cc, nt, carry, cw, s_in, s_out, tl, tag, pools=None):
        TTc = tl["TT"]
        hh = nt - 1
        acc = pools[1].get() if pools is not None else k.sb(tag + "acc", [128, TTc])
        if tl["first"]:
            ext = k.sb(tag + "ext", [128, 16 + hh])
            k.cp(ext[:, 0:hh], carry[:, cc, :])
            k.cp(ext[:, hh:16 + hh], p[:, 0:16], eng="act")
            exs = k.sb(tag + "exs", [128, 16, 4 + hh])
            hist = k.sb(tag + "hist", [16 * hh, 128])
            k.dma_in(hist.v, s_in[:, cc * 128:(cc + 1) * 128])
            ph = k.ps()
            k.tr(ph[:, 0:16 * hh], hist.v, ident)
            k.cp(exs[:, :, 0:hh], ph[:, 0:16 * hh].re("p (s j) -> p s j", j=hh))
            k.cp(exs[:, :, hh:4 + hh], p[:, 16:80].re("p (s j) -> p s j", j=4), eng="act")
            k.cp(carry[:, cc, :], ext[:, 16:16 + hh])
            pc = k.ps()
            nst = k.sb(tag + "nst", [128, 16, hh])
            k.cp(nst.v, exs[:, :, 4:4 + hh])
            k.tr(pc[0:16 * hh, 0:128], nst.v.re("p s j -> p (s j)"), ident)
            ob = k.sb(tag + "ob", [16 * hh, 128])
            k.cp(ob.v, pc[0:16 * hh, 0:128])
            k.dma_out(s_out[:, cc * 128:(cc + 1) * 128], ob.v)
            av = acc[:, 0:16]
            k.ts(av, ext[:, 0:16], cw[:, cc, 0:1], ALU.mult)
            for jj in range(1, nt):
                k.stt(av, ext[:, jj:jj + 16], cw[:, cc, jj:jj + 1], av, ALU.mult, ALU.add)
            av = acc[:, 16:80].re("p (s j) -> p s j", j=4)
            k.ts(av, exs[:, :, 0:4], cw[:, cc, 0:1], ALU.mult)
            for jj in range(1, nt):
                k.stt(av, exs[:, :, jj:jj + 4], cw[:, cc, jj:jj + 1], av, ALU.mult, ALU.add)
            if TTc > 80:
                n2 = TTc - 80
                ext2 = k.sb(tag + "ext2", [128, n2 + hh])
                k.cp(ext2[:, 0:hh], carry[:, cc, :])
                k.cp(ext2[:, hh:n2 + hh], p[:, 80:TTc], eng="act")
                k.cp(carry[:, cc, :], ext2[:, n2:n2 + hh])
                av = acc[:, 80:TTc]
                k.ts(av, ext2[:, 0:n2], cw[:, cc, 0:1], ALU.mult)
                for jj in range(1, nt):
                    k.stt(av, ext2[:, jj:jj + n2], cw[:, cc, jj:jj + 1], av, ALU.mult, ALU.add)
        else:
            ext = pools[0].get()
            k.cp(ext[:, 0:hh], carry[:, cc, :])
            k.cp(ext[:, hh:TTc + hh], p[:, 0:TTc], eng="act")
            k.act(acc.v, p[:, 0:TTc], AF.Identity, scale=cw[:, cc, hh:hh + 1])
            k.cp(carry[:, cc, :], ext[:, TTc:TTc + hh])
            for jj in range(0, hh):
                k.stt(acc.v, ext[:, jj:jj + TTc], cw[:, cc, jj:jj + 1], acc.v, ALU.mult, ALU.add)
        return acc

    def ffn_phase(l, tl, x1T, x1T32, x2T32, x2T, last):
        P = LP[l]
        TTc = tl["TT"]
        cf, cwf = P["cf"], P["cwf"]
        hT = k.sb("hT", [128, 22, TTc], dt=BF16)
        k.push()
        pools = None if tl["first"] else (RPool("fext", [128, TTc + 2], 4), RPool("facc", [128, TTc], 4))
        for blk in range(6):
            ncols = 512 if blk < 5 else 256
            wb1 = wload(w_up[l][:, blk * 512:blk * 512 + ncols], 8, ncols)
            wb2 = wload(w_up[l][:, DFF + blk * 512:DFF + blk * 512 + ncols], 8, ncols)
            for q4 in range(ncols // 128):
                f = blk * 4 + q4
                with k.scope():
                    p1 = proj_fm(wb1, q4 * 128, x1T, TTc)
                    a1 = conv_fm(p1, f, 3, cf, cwf, sfc[l], ofc[l], tl, "f", pools)
                    k.act(a1.v, a1.v, AF.Silu)
                    p2 = proj_fm(wb2, q4 * 128, x1T, TTc)
                    a2 = conv_fm(p2, 22 + f, 3, cf, cwf, sfc[l], ofc[l], tl, "f", pools)
                    k.tt(hT[:, f, :], a1.v, a2.v, ALU.mult, eng=("dve" if tl["first"] else "pool"))
        k.pop()
        tgs = tl["tgs"]
        osb = [k.sb("osb2", [128, 1024]) for _ in tgs]
        kgs = [(0, 8), (8, 8), (16, 6)]
        for hh in range(2):
            pts = [k.ps() for _ in tgs]
            for (k0, nk) in kgs:
                wb = wload(w_down[l][k0 * 128:(k0 + nk) * 128, hh * 512:(hh + 1) * 512], nk, 512)
                for j, (c0, n) in enumerate(tgs):
                    for kk in range(nk):
                        k.mm(pts[j][0:n, 0:512], hT[:, k0 + kk, c0:c0 + n], wb[:, kk, :], start=(k0 + kk == 0), stop=False,
                             sig=(kk == nk - 1))
            for j, (c0, n) in enumerate(tgs):
                for q4 in range(4):
                    k.mm(pts[j][0:n, q4 * 128:(q4 + 1) * 128], x1T32[:, hh * 4 + q4, c0:c0 + n], aident.v, start=False, stop=(q4 == 3))
                k.cp(osb[j][0:n, hh * 512:(hh + 1) * 512], pts[j][0:n, 0:512], eng="act")
        for j, (c0, n) in enumerate(tgs):
            fin = None
            if last:
                if tl["first"] and c0 == 0:
                    fin = [(ys[:, :], 16, 80)]
                else:
                    tok0 = tl["tok0"] + c0 - tl["poff"]
                    fin = [(yp[tok0:tok0 + n, :], 0, n)]
            ln_rows(osb[j][0:n, :], n, P["g2"], P["b2"], x2T32, x2T, c0, final=fin)

    for ti, tl in enumerate(tiles):
        TTc = tl["TT"]
        cur["ti"] = ti
        cur["n"] = 0
        k.conservative = False
        with k.scope():
            xT32 = k.sb("xT32", [128, 8, TTc])
            xT = k.sb("xT", [128, 8, TTc], dt=BF16)
            for (c0, n) in tl["tgs"]:
                with k.scope():
                    xin = k.sb("xin", [128, 1024])
                    if tl["first"] and c0 == 0:
                        k.dma_in(xin[0:16, :], meta[:, :])
                        k.dma_in(xin[16:80, :], xs[:, :])
                    else:
                        tk = tl["tok0"] + c0 - tl["poff"]
                        k.dma_in(xin[0:n, :], xp[tk:tk + n, :])
                    ln_rows(xin[0:n, :], n, g_emb, b_emb, xT32, xT, c0)
            if ti == 0:
                tap("xT0", xT32.v)
                tap("xTb0", xT.v)
            xpairs = [(xT32, xT), (k.sb("xB32", [128, 8, TTc]), k.sb("xB", [128, 8, TTc], dt=BF16)),
                      (k.sb("xC32", [128, 8, TTc]), k.sb("xC", [128, 8, TTc], dt=BF16))]
            for l in range(DEPTH):
                (x1T32, x1T) = xpairs[1]
                (x2T32, x2T) = xpairs[2] if l % 2 == 0 else xpairs[0]
                with k.scope():
                    hAT = k.sb("hAT", [128, 4, TTc], dt=BF16)
                    hBT = k.sb("hBT", [128, 4, TTc], dt=BF16)
                    with k.scope():
                        mlstm_phase(l, tl, xT, hAT)
                    if ti == 0:
                        tap("hAT%d" % l, hAT.v)
                    with k.scope():
                        gdn_phase(l, tl, xT, hBT)
                    if ti == 0:
                        tap("hBT%d" % l, hBT.v)
                    with k.scope():
                        merge_phase(l, tl, xT, xT32, hAT, hBT, x1T32, x1T)
                if ti == 0:
                    tap("x1T%d" % l, x1T32.v)
                with k.scope():
                    ffn_phase(l, tl, x1T, x1T32, x2T32, x2T, last=(l == DEPTH - 1))
                if ti == 0 and l == 0:
                    tap("x2T%d" % l, x2T32.v)
                xT, xT32 = x2T, x2T32
    for l in range(DEPTH):
        P = LP[l]
        k.dma_out(pC[l].rearrange("h d v -> d h v"), P["Um"][:, :, 0:128])
        with k.scope():
            pt = k.ps()
            k.tr(pt[0:4, 0:128], P["Um"][:, :, 128], ident)
            nr = k.sb("pnr", [4, 128])
            k.cp(nr.v, pt[0:4, 0:128])
            k.dma_out(pn[l], nr.v)
        with k.scope():
            ptm = k.ps()
            k.tr(ptm[0:1, 0:4], P["mcur"].v, ident)
            mr = k.sb("pmr", [1, 4])
            k.cp(mr.v, ptm[0:1, 0:4])
            k.dma_out(pm[l].rearrange("h o -> o h"), mr.v)
        k.dma_out(pS[l].rearrange("h d v -> d h v"), P["S"].v)
        fm_to_rows(P["cg"].v, 3, 12, pgc[l])
        fm_to_rows(P["cf"].v, 2, 44, pfc[l])
    k.finish()
    return nc, tapouts


_CACHE = {}


def _prep_inputs(inp, c):
    f = lambda a: np.ascontiguousarray(np.asarray(a, dtype=np.float32))
    s0, s1 = NS * c, NS * (c + 1)
    m = {
        "xp": f(inp["x_prompt"][c]),
        "xs": f(inp["x_sample"][s0:s1]).reshape(64, D),
        "meta": f(inp["meta_tokens"]),
        "sC": f(inp["state_mlstm_C"][:, s0:s1]).reshape(DEPTH, NS * 4, 128, 128),
        "sn": f(inp["state_mlstm_n"][:, s0:s1]).reshape(DEPTH, NS * 4, 128),
        "sm": f(inp["state_mlstm_m"][:, s0:s1]),
        "sS": f(inp["state_gdn_S"][:, s0:s1]).reshape(DEPTH, NS * 4, 128, 128),
        "sgc": f(inp["state_gdn_conv"][:, s0:s1]).reshape(DEPTH, NS * 3, 1536),
        "sfc": f(inp["state_ffn_conv"][:, s0:s1]).reshape(DEPTH, NS * 2, 2 * DFF),
        "ln_emb_g": f(inp["ln_emb_g"]).reshape(1, D),
        "ln_emb_b": f(inp["ln_emb_b"]).reshape(1, D),
        "w_in": f(inp["w_in"]),
        "gate_bias": f(inp["mlstm_gate_bias"]).reshape(DEPTH, 8, 1),
        "mnorm_w": f(inp["mlstm_norm_w"]).reshape(DEPTH, 1, 512),
        "gconv_w": f(inp["gdn_conv_w"]),
        "A_log": f(inp["gdn_A_log"]).reshape(DEPTH, 4, 1),
        "dt_bias": f(inp["gdn_dt_bias"]).reshape(DEPTH, 4, 1),
        "gnorm_w": f(inp["gdn_norm_w"]).reshape(DEPTH, 1, 128),
        "w_pa": f(inp["w_branch_a"]),
        "w_pb": f(inp["w_branch_b"]),
        "w_out": f(inp["w_out"]),
        "ln1_g": f(inp["ln1_g"]).reshape(DEPTH, 1, D),
        "ln1_b": f(inp["ln1_b"]).reshape(DEPTH, 1, D),
        "w_up": f(inp["w_up"]),
        "fconv_w": f(inp["ffn_conv_w"]),
        "w_down": f(inp["w_down"]),
        "ln2_g": f(inp["ln2_g"]).reshape(DEPTH, 1, D),
        "ln2_b": f(inp["ln2_b"]).reshape(DEPTH, 1, D),
    }
    return m


def kernel(**inp):
    if "nc" not in _CACHE:
        _CACHE["nc"] = build()[0]
    nc = _CACHE["nc"]
    in_maps = [_prep_inputs(inp, c) for c in range(8)]
    res = run_bass_kernel_spmd(nc, in_maps, core_ids=list(range(8)))
    R = res.results
    st = lambda name: np.stack([np.asarray(R[c][name]) for c in range(8)], axis=1)
    cat = lambda name, shp: np.concatenate([np.asarray(R[c][name]).reshape(shp) for c in range(8)], axis=1)
    y_prompt = np.stack([np.asarray(R[c]["yp"]) for c in range(8)], axis=0)
    y_sample = np.concatenate([np.asarray(R[c]["ys"]).reshape(NS, 4, D) for c in range(8)], axis=0)
    outs = (
        y_prompt, y_sample,
        st("pC"), st("pn"), st("pm").reshape(DEPTH, 8, 4), st("pS"), st("pgc"), st("pfc"),
        cat("oC", (DEPTH, NS, 4, 128, 128)), cat("on", (DEPTH, NS, 4, 128)), cat("om", (DEPTH, NS, 4)),
        cat("oS", (DEPTH, NS, 4, 128, 128)), cat("ogc", (DEPTH, NS, 3, 1536)), cat("ofc", (DEPTH, NS, 2, 2 * DFF)),
    )
    return tuple(np.ascontiguousarray(o, dtype=np.float32) for o in outs)
```
